# Optimizing a Trainium2 kernel written in Bass

```python
import math
import jax
import jax.numpy as jnp
from jax import lax
import numpy as np

D_MODEL = 1024
BATCH = 16
SEQ = 256
DEPTH = 2
DEC_BATCH = 2
DEC_SEQ = 4096
PAST_LEN = 512

GRID_W = 64
HD = 64
SCALE = 1.0 / math.sqrt(HD)
BLK = 128
H_A = 8
KV_A = 2
G_A = H_A // KV_A
WINDOW = 128
D_A = H_A * HD
D_LRU = 512
LRU_BLOCKS = 8
LRU_BW = D_LRU // LRU_BLOCKS
CONV_W = 4
LRU_C = 8.0
H_C = 4
DV_C = 2 * HD
D_C = H_C * DV_C
N_BRANCH = 3
ROPE_BASE = 10000.0
ROPE_FREQS = HD // 4
EPS = 1e-6
SPLIT_SIZES = (D_A, KV_A * HD, KV_A * HD, D_A,
               D_LRU, D_LRU,
               H_C * 2 * HD, H_C * 2 * HD, D_C, D_C,
               N_BRANCH * D_MODEL)
IN_COLS = sum(SPLIT_SIZES)

kernel_name = "hybrid_dit_swa_rglru_diffattn_step"


def rmsnorm(x, g):
    xf = x.astype(jnp.float32)
    y = xf * lax.rsqrt(jnp.mean(xf * xf, axis=-1, keepdims=True) + EPS)
    return (y * g.astype(jnp.float32)).astype(x.dtype)


def split_cols(p):
    idx = [int(v) for v in np.cumsum(SPLIT_SIZES)[:-1]]
    return jnp.split(p, idx, axis=-1)


def modulation(cvec, mod_w, mod_b):
    m = jax.nn.silu(cvec) @ mod_w + mod_b
    m = m.reshape(-1, 1, 3 * D_MODEL)
    return jnp.split(m, 3, axis=-1)


def axial_rope_tables(T):
    n_rows = T // GRID_W
    row = jnp.repeat(jnp.arange(n_rows), GRID_W).astype(jnp.float32)
    col = jnp.tile(jnp.arange(GRID_W), n_rows).astype(jnp.float32)
    inv = jnp.power(ROPE_BASE, -jnp.arange(ROPE_FREQS, dtype=jnp.float32) / ROPE_FREQS)
    ang_r = row[:, None] * inv
    ang_c = col[:, None] * inv
    return (jnp.cos(ang_r), jnp.sin(ang_r), jnp.cos(ang_c), jnp.sin(ang_c))


def _rotate(x, cos, sin):
    x1, x2 = x[..., :ROPE_FREQS], x[..., ROPE_FREQS:]
    return jnp.concatenate([x1 * cos - x2 * sin, x2 * cos + x1 * sin], axis=-1)


def apply_axial_rope(x, tabs):
    shape = (x.shape[1],) + (1,) * (x.ndim - 3) + (ROPE_FREQS,)
    cr, sr, cc, sc = [t.reshape(shape).astype(x.dtype) for t in tabs]
    return jnp.concatenate([_rotate(x[..., :HD // 2], cr, sr),
                            _rotate(x[..., HD // 2:], cc, sc)], axis=-1)


def mixer_front(x, mod, norm_g, w_in):
    shift, scale, gate = mod
    h = rmsnorm(x, norm_g) * (1 + scale) + shift
    return split_cols(h @ w_in), gate


def mixer_tail(x, gate, oa, g_a, ob, g_b, oc, g_c, merge, P):
    B, T, _ = x.shape
    m = jax.nn.sigmoid(merge).reshape(B, T, N_BRANCH, D_MODEL)
    y = (m[:, :, 0] * ((oa * jax.nn.silu(g_a)) @ P['w_br_a'])
         + m[:, :, 1] * ((ob * jax.nn.silu(g_b)) @ P['w_br_b'])
         + m[:, :, 2] * ((oc * jax.nn.silu(g_c)) @ P['w_br_c']))
    return x + gate * (y @ P['w_out'])


def attn_a_context(q, k, v, sink):
    B, T = q.shape[:2]
    nb = T // BLK
    qb = jnp.moveaxis(q.reshape(B, nb, BLK, KV_A, G_A, HD), 1, 0)
    snk = sink.astype(jnp.float32).reshape(1, KV_A, G_A, 1)

    def block(qblk):
        s = jnp.einsum('bqkgd,bskd->bkgqs', qblk, k).astype(jnp.float32) * SCALE
        m = jnp.maximum(jnp.max(s, axis=-1), snk)[..., None]
        e = jnp.exp(s - m)
        p = e / (jnp.sum(e, axis=-1, keepdims=True) + jnp.exp(snk[..., None] - m))
        return jnp.einsum('bkgqs,bskd->bqkgd', p.astype(v.dtype), v)

    o = lax.map(block, qb)
    return jnp.moveaxis(o, 0, 1).reshape(B, T, D_A)


def attn_a_latent(q, k, v, k_ctx, v_ctx, sink):
    B, T = q.shape[:2]
    nb = T // BLK
    qb = jnp.moveaxis(q.reshape(B, nb, BLK, KV_A, G_A, HD), 1, 0)

    def band(t):
        tp = jnp.pad(t, ((0, 0), (BLK, BLK), (0, 0), (0, 0))).reshape(B, nb + 2, BLK, KV_A, HD)
        w = jnp.concatenate([tp[:, :-2], tp[:, 1:-1], tp[:, 2:]], axis=2)
        return jnp.moveaxis(w, 1, 0)

    kw, vw = band(k), band(v)
    kj = jnp.arange(3 * BLK) - BLK
    rel = kj[None, :] - jnp.arange(BLK)[:, None]
    pos = jnp.arange(nb)[:, None] * BLK + kj[None, :]
    mask = (jnp.abs(rel)[None] <= WINDOW) & ((pos >= 0) & (pos < T))[:, None, :]
    snk = sink.astype(jnp.float32).reshape(1, KV_A, G_A, 1)

    def block(args):
        qblk, kblk, vblk, mblk = args
        s_w = jnp.einsum('bqkgd,bskd->bkgqs', qblk, kblk).astype(jnp.float32) * SCALE
        s_w = jnp.where(mblk, s_w, -jnp.inf)
        s_c = jnp.einsum('bqkgd,bskd->bkgqs', qblk, k_ctx).astype(jnp.float32) * SCALE
        m = jnp.maximum(jnp.maximum(jnp.max(s_w, axis=-1), jnp.max(s_c, axis=-1)), snk)[..., None]
        e_w = jnp.exp(s_w - m)
        e_c = jnp.exp(s_c - m)
        denom = (jnp.sum(e_w, axis=-1, keepdims=True) + jnp.sum(e_c, axis=-1, keepdims=True)
                 + jnp.exp(snk[..., None] - m))
        return (jnp.einsum('bkgqs,bskd->bqkgd', (e_w / denom).astype(v.dtype), vblk)
                + jnp.einsum('bkgqs,bskd->bqkgd', (e_c / denom).astype(v.dtype), v_ctx))

    o = lax.map(block, (qb, kw, vw, mask))
    return jnp.moveaxis(o, 0, 1).reshape(B, T, D_A)


def centred_conv(x, w, b):
    T = x.shape[1]
    xp = jnp.pad(x, ((0, 0), (CONV_W // 2, CONV_W - 1 - CONV_W // 2), (0, 0)))
    y = b
    for j in range(CONV_W):
        y = y + xp[:, j:j + T] * w[j]
    return y


def rglru_coeffs(x, wa, ba, wx, bx, lam):
    B, T, _ = x.shape
    xb = x.reshape(B, T, LRU_BLOCKS, LRU_BW)
    r = jax.nn.sigmoid(jnp.einsum('btnc,ncd->btnd', xb, wa).reshape(B, T, D_LRU) + ba)
    i = jax.nn.sigmoid(jnp.einsum('btnc,ncd->btnd', xb, wx).reshape(B, T, D_LRU) + bx)
    log_a = -LRU_C * r.astype(jnp.float32) * jax.nn.softplus(-lam.astype(jnp.float32))
    a = jnp.exp(log_a)
    b = jnp.sqrt(-jnp.expm1(2.0 * log_a)) * (i * x).astype(jnp.float32)
    return a, b


def linear_scan(a, b, h0, reverse):
    def combine(left, right):
        a_l, b_l = left
        a_r, b_r = right
        return a_l * a_r, a_r * b_l + b_r
    a_cum, b_cum = lax.associative_scan(combine, (a, b), axis=1, reverse=reverse)
    return a_cum * h0[:, None] + b_cum


def rglru_bidir(x, h0, P):
    y = jnp.zeros(x.shape, jnp.float32)
    finals = []
    for d in range(2):
        a, b = rglru_coeffs(x, P['lru_wa'][d], P['lru_ba'][d], P['lru_wx'][d],
                            P['lru_bx'][d], P['lru_lam'][d])
        h = linear_scan(a, b, h0[:, d].astype(jnp.float32), reverse=(d == 1))
        y = y + h
        finals.append(h[:, -1] if d == 0 else h[:, 0])
    return y.astype(x.dtype), jnp.stack(finals, axis=1).astype(x.dtype)


def diff_lambda(lq1, lk1, lq2, lk2, lam_init):
    f = lambda t: t.astype(jnp.float32)
    return jnp.exp(jnp.sum(f(lq1) * f(lk1))) - jnp.exp(jnp.sum(f(lq2) * f(lk2))) + lam_init


def diff_attn(q, k, v, lam, subln, lam_init):
    B, T = q.shape[:2]
    nb = T // BLK
    qb = jnp.moveaxis(q.reshape(B, nb, BLK, H_C, 2, HD), 1, 0)

    def block(qblk):
        s = jnp.einsum('bqhmd,bshmd->bhmqs', qblk, k).astype(jnp.float32) * SCALE
        p = jax.nn.softmax(s, axis=-1)
        pd = p[:, :, 0] - lam * p[:, :, 1]
        return jnp.einsum('bhqs,bshe->bqhe', pd.astype(v.dtype), v)

    o = jnp.moveaxis(lax.map(block, qb), 0, 1).reshape(B, T, H_C, DV_C)
    o = rmsnorm(o, subln) * (1 - lam_init)
    return o.reshape(B, T, D_C)


def project_heads(x, q_a, k_a, v_a, q_c, k_c, v_c, P):
    B, T, _ = x.shape
    q_a = rmsnorm(q_a.reshape(B, T, H_A, HD), P['qn_a'])
    k_a = rmsnorm(k_a.reshape(B, T, KV_A, HD), P['kn_a'])
    v_a = v_a.reshape(B, T, KV_A, HD)
    q_c = rmsnorm(q_c.reshape(B, T, H_C, 2, HD), P['qn_c'])
    k_c = rmsnorm(k_c.reshape(B, T, H_C, 2, HD), P['kn_c'])
    v_c = v_c.reshape(B, T, H_C, DV_C)
    return q_a, k_a, v_a, q_c, k_c, v_c


def context_layer(x, mod, P, lam, lam_init):
    (q_a, k_a, v_a, g_a, x_b, g_b, q_c, k_c, v_c, g_c, merge), gate = mixer_front(x, mod, P['norm_g'], P['w_in'])
    q_a, k_a, v_a, q_c, k_c, v_c = project_heads(x, q_a, k_a, v_a, q_c, k_c, v_c, P)
    oa = attn_a_context(q_a, k_a, v_a, P['sink_a'])
    xb = centred_conv(x_b, P['conv_w'], P['conv_b'])
    h0 = jnp.zeros((x.shape[0], 2, D_LRU), jnp.float32)
    ob, st = rglru_bidir(xb, h0, P)
    oc = diff_attn(q_c, k_c, v_c, lam, P['subln_c'], lam_init)
    x = mixer_tail(x, gate, oa, g_a, ob, g_b, oc, g_c, merge, P)
    return x, (k_a, v_a, k_c, v_c, st)


def latent_layer(x, mod, P, lam, lam_init, tabs, ck_a, cv_a, ck_c, cv_c, st):
    (q_a, k_a, v_a, g_a, x_b, g_b, q_c, k_c, v_c, g_c, merge), gate = mixer_front(x, mod, P['norm_g'], P['w_in'])
    q_a, k_a, v_a, q_c, k_c, v_c = project_heads(x, q_a, k_a, v_a, q_c, k_c, v_c, P)
    q_a, k_a = apply_axial_rope(q_a, tabs), apply_axial_rope(k_a, tabs)
    oa = attn_a_latent(q_a, k_a, v_a, ck_a, cv_a, P['sink_a'])
    xb = centred_conv(x_b, P['conv_w'], P['conv_b'])
    ob, _ = rglru_bidir(xb, st, P)
    q_c, k_c = apply_axial_rope(q_c, tabs), apply_axial_rope(k_c, tabs)
    k_all = jnp.concatenate([k_c, ck_c.astype(k_c.dtype)], axis=1)
    v_all = jnp.concatenate([v_c, cv_c.astype(v_c.dtype)], axis=1)
    oc = diff_attn(q_c, k_all, v_all, lam, P['subln_c'], lam_init)
    return mixer_tail(x, gate, oa, g_a, ob, g_b, oc, g_c, merge, P)


def setup_inputs(seed: int = 0) -> dict:
    key = jax.random.key(seed)
    ks = jax.random.split(key, 40)
    n = lambda i, shape, s: jax.random.normal(ks[i], shape, jnp.float32) * s
    u = jax.random.uniform(ks[20], (DEPTH, 2, D_LRU), jnp.float32, 0.9, 0.999)
    s_lam = jnp.power(u, 1.0 / LRU_C)
    return {
        "x_prompt": n(0, (BATCH, SEQ, D_MODEL), 1.0),
        "x_sample": n(1, (DEC_BATCH, DEC_SEQ, D_MODEL), 1.0),
        "cache_a_k": n(2, (DEC_BATCH, DEPTH, PAST_LEN, KV_A, HD), 1.0),
        "cache_a_v": n(3, (DEC_BATCH, DEPTH, PAST_LEN, KV_A, HD), 1.0),
        "cache_c_k": n(4, (DEC_BATCH, DEPTH, PAST_LEN, H_C, 2, HD), 1.0),
        "cache_c_v": n(5, (DEC_BATCH, DEPTH, PAST_LEN, H_C, DV_C), 1.0),
        "state_lru": n(6, (DEC_BATCH, DEPTH, 2, D_LRU), 0.5),
        "c": n(7, (DEC_BATCH, D_MODEL), 1.0),
        "c_ctx": n(8, (D_MODEL,), 1.0),
        "norm_g": 1.0 + n(9, (DEPTH, D_MODEL), 0.02),
        "mod_w": n(10, (DEPTH, D_MODEL, 3 * D_MODEL), D_MODEL ** -0.5),
        "mod_b": n(11, (DEPTH, 3 * D_MODEL), 0.01),
        "w_in": n(12, (DEPTH, D_MODEL, IN_COLS), D_MODEL ** -0.5),
        "qn_a": 1.0 + n(13, (DEPTH, HD), 0.02),
        "kn_a": 1.0 + n(14, (DEPTH, HD), 0.02),
        "sink_a": n(15, (DEPTH, H_A), 0.5),
        "conv_w": n(16, (DEPTH, CONV_W, D_LRU), CONV_W ** -0.5),
        "conv_b": n(17, (DEPTH, D_LRU), 0.01),
        "lru_wa": n(18, (DEPTH, 2, LRU_BLOCKS, LRU_BW, LRU_BW), LRU_BW ** -0.5),
        "lru_ba": n(19, (DEPTH, 2, D_LRU), 0.01),
        "lru_wx": n(21, (DEPTH, 2, LRU_BLOCKS, LRU_BW, LRU_BW), LRU_BW ** -0.5),
        "lru_bx": n(22, (DEPTH, 2, D_LRU), 0.01),
        "lru_lam": jnp.log(s_lam) - jnp.log1p(-s_lam),
        "qn_c": 1.0 + n(23, (DEPTH, HD), 0.02),
        "kn_c": 1.0 + n(24, (DEPTH, HD), 0.02),
        "lam_q1": n(25, (DEPTH, HD), 0.1),
        "lam_k1": n(26, (DEPTH, HD), 0.1),
        "lam_q2": n(27, (DEPTH, HD), 0.1),
        "lam_k2": n(28, (DEPTH, HD), 0.1),
        "subln_c": 1.0 + n(29, (DEPTH, DV_C), 0.02),
        "w_br_a": n(30, (DEPTH, D_A, D_MODEL), D_A ** -0.5),
        "w_br_b": n(31, (DEPTH, D_LRU, D_MODEL), D_LRU ** -0.5),
        "w_br_c": n(32, (DEPTH, D_C, D_MODEL), D_C ** -0.5),
        "w_out": n(33, (DEPTH, D_MODEL, D_MODEL), D_MODEL ** -0.5),
    }


def reference(x_prompt, x_sample, cache_a_k, cache_a_v, cache_c_k, cache_c_v, state_lru,
              c, c_ctx, norm_g, mod_w, mod_b, w_in, qn_a, kn_a, sink_a, conv_w, conv_b,
              lru_wa, lru_ba, lru_wx, lru_bx, lru_lam, qn_c, kn_c, lam_q1, lam_k1,
              lam_q2, lam_k2, subln_c, w_br_a, w_br_b, w_br_c, w_out):
    tabs = axial_rope_tables(x_sample.shape[1])
    xp, xs = x_prompt, x_sample
    ka_l, va_l, kc_l, vc_l, st_l = [], [], [], [], []
    for l in range(DEPTH):
        P = {'norm_g': norm_g[l], 'w_in': w_in[l], 'qn_a': qn_a[l], 'kn_a': kn_a[l],
             'sink_a': sink_a[l], 'conv_w': conv_w[l], 'conv_b': conv_b[l],
             'lru_wa': lru_wa[l], 'lru_ba': lru_ba[l], 'lru_wx': lru_wx[l],
             'lru_bx': lru_bx[l], 'lru_lam': lru_lam[l], 'qn_c': qn_c[l], 'kn_c': kn_c[l],
             'subln_c': subln_c[l], 'w_br_a': w_br_a[l], 'w_br_b': w_br_b[l],
             'w_br_c': w_br_c[l], 'w_out': w_out[l]}
        lam_init = 0.8 - 0.6 * math.exp(-0.3 * l)
        lam = diff_lambda(lam_q1[l], lam_k1[l], lam_q2[l], lam_k2[l], lam_init)
        xp, (ka, va, kc, vc, st) = context_layer(xp, modulation(c_ctx, mod_w[l], mod_b[l]), P, lam, lam_init)
        ka_l.append(ka)
        va_l.append(va)
        kc_l.append(kc)
        vc_l.append(vc)
        st_l.append(st)
        xs = latent_layer(xs, modulation(c, mod_w[l], mod_b[l]), P, lam, lam_init, tabs,
                          cache_a_k[:, l], cache_a_v[:, l], cache_c_k[:, l], cache_c_v[:, l],
                          state_lru[:, l])
    return (xp, xs, jnp.stack(ka_l, axis=1), jnp.stack(va_l, axis=1), jnp.stack(kc_l, axis=1),
            jnp.stack(vc_l, axis=1), jnp.stack(st_l, axis=1))
```

```python
import contextlib
import math
import numpy as np
import concourse.bass as bass
import concourse.mybir as mybir
from concourse.bass_utils import run_bass_kernel_spmd

F32 = mybir.dt.float32
BF16 = mybir.dt.bfloat16
AF = mybir.ActivationFunctionType
ALU = mybir.AluOpType
AX = mybir.AxisListType

ENGS = ("pe", "act", "dve", "pool", "sp")

DEPTH = 2
SCALE = 0.125
EPS = 1e-6
NTOK = 1536
KC_OFF, VC_OFF, KA_OFF, VA_OFF = 0, 0, 0, 2048
XBW = 1545
SEGS = [(0, 256, 0), (259, 256, 256), (518, 1024, 512)]

_SM_LAYER = [("ng", 8), ("modb", 24), ("qna", 1), ("kna", 1), ("qnc", 1), ("knc", 1), ("convw", 16),
             ("convb", 4), ("ba", 8), ("bx", 8), ("lam", 8), ("subln", 1), ("sink", 8),
             ("lq1", 64), ("lk1", 64), ("lq2", 64), ("lk2", 64), ("h0", 8)]
SM = {}
_o = 0
for _l in range(DEPTH):
    for _n, _w in _SM_LAYER:
        SM[(_n, _l)] = (_o, _w)
        _o += _w
SM["c"] = (_o, 16); _o += 16
SM["sel"] = (_o, 12); _o += 12
NSM = _o

WP1N = 1024 + 4096 * 3 + 5120
WP2_Q = 16384
WP2N = WP2_Q + 8 * 4096 + 8192


class Res:
    __slots__ = ("name", "last_w", "readers")

    def __init__(self, name=""):
        self.name = name
        self.last_w = None
        self.readers = {}


class Sched:
    NDMA = 8

    def __init__(self, nc, stack):
        self.nc = nc
        self.streams = {e: [] for e in ENGS}
        self.count = {e: 0 for e in ENGS}
        self.seen = {e: {} for e in ENGS}
        self.sems = {}
        self.resmap = {}
        for e in ENGS:
            self.sems[e] = stack.enter_context(nc.semaphore("prog_" + e))
        self.dma_k = {}
        for q in ("sp", "act", "pool"):
            self.dma_k[q] = 0
            for i in range(self.NDMA):
                self.sems[("dma", q, i)] = stack.enter_context(nc.semaphore(f"dma_{q}_{i}"))
        self.sems["cc"] = stack.enter_context(nc.semaphore("cc"))
        self.cc_count = 0
        self.out_events = []

    def _res(self, x):
        if isinstance(x, Res):
            return x
        name = x.tensor.name
        r = self.resmap.get(name)
        if r is None:
            r = Res(name)
            self.resmap[name] = r
        return r

    def _collect(self, reads, writes):
        waits = {}

        def add(ev):
            if ev is None:
                return
            k, v = ev
            if waits.get(k, 0) < v:
                waits[k] = v
        for r in reads:
            add(r.last_w)
        for w in writes:
            add(w.last_w)
            for k, v in w.readers.items():
                add((k, v))
        return waits

    def _emit_waits(self, eng, waits):
        for k, v in waits.items():
            if k == eng:
                if eng == "pe":
                    continue
                if v <= self.count[eng] - 6:
                    continue
            if self.seen[eng].get(k, 0) >= v:
                continue
            self.seen[eng][k] = v
            self.streams[eng].append(("wait", k, v))

    def _commit(self, ev, reads, writes):
        k, v = ev
        for r in reads:
            if r.readers.get(k, 0) < v:
                r.readers[k] = v
        for w in writes:
            w.last_w = ev
            w.readers = {}

    def op(self, eng, fn, reads=(), writes=()):
        reads = [self._res(x) for x in reads]
        writes = [self._res(x) for x in writes]
        writes = writes + [r for r in reads if r.name.startswith("bank") and r not in writes]
        self._emit_waits(eng, self._collect(reads, writes))
        self.count[eng] += 1
        ev = (eng, self.count[eng])
        self.streams[eng].append(("op", fn, eng, 1))
        self._commit(ev, reads, writes)
        return ev

    def dma(self, q, out, in_, is_output=False, extra_reads=(), **kw):
        reads = [self._res(in_)] + [self._res(x) for x in extra_reads]
        writes = [self._res(out)]
        waits = self._collect(reads, writes)
        k = self.dma_k[q]
        self.dma_k[q] += 1
        s = ("dma", q, k % self.NDMA)
        gen = k // self.NDMA
        if gen > 0 and waits.get(s, 0) < 16 * gen:
            waits[s] = 16 * gen
        self._emit_waits(q, waits)
        ev = (s, 16 * (gen + 1))
        self.streams[q].append(("op", lambda e: e.dma_start(out=out, in_=in_, **kw), s, 16))
        self._commit(ev, reads, writes)
        if is_output:
            self.out_events.append(ev)
        return ev

    def collective(self, groups, in_t, out_t):
        reads = [self._res(in_t.ap())]
        writes = [self._res(out_t.ap())]
        self._emit_waits("pool", self._collect(reads, writes))
        self.cc_count += 1
        ev = ("cc", self.cc_count)
        self.streams["pool"].append(("op", lambda e: e.collective_compute(
            "AllGather", ALU.bypass, replica_groups=groups, ins=[in_t.ap().opt()],
            outs=[out_t.ap().opt()]), "cc", 1))
        self._commit(ev, reads, writes)
        return ev

    def _all_events(self):
        waits = {}
        for e in ENGS:
            if self.count[e] > 0:
                waits[e] = self.count[e]
        for q in ("sp", "act", "pool"):
            k = self.dma_k[q]
            for i in range(self.NDMA):
                n = (k // self.NDMA) + (1 if i < k % self.NDMA else 0)
                if n > 0:
                    waits[("dma", q, i)] = 16 * n
        if self.cc_count:
            waits["cc"] = self.cc_count
        return waits

    def barrier(self, skip_cc=False):
        waits = self._all_events()
        if skip_cc:
            waits.pop("cc", None)
        for e in ENGS:
            for k, v in waits.items():
                if k == e:
                    continue
                if self.seen[e].get(k, 0) >= v:
                    continue
                self.seen[e][k] = v
                self.streams[e].append(("wait", k, v))

    def finish(self):
        self.barrier()

    def replay(self):
        nc = self.nc
        sems = self.sems
        streams = self.streams

        def run(engobj, name):
            for item in streams[name]:
                if item[0] == "wait":
                    engobj.wait_ge(sems[item[1]], item[2])
                else:
                    _, fn, semk, inc = item
                    fn(engobj).then_inc(sems[semk], inc)

        with nc.Block() as block:
            @block.tensor
            def _(e):
                run(e, "pe")

            @block.scalar
            def _(e):
                run(e, "act")

            @block.vector
            def _(e):
                run(e, "dve")

            @block.gpsimd
            def _(e):
                run(e, "pool")

            @block.sync
            def _(e):
                run(e, "sp")


class WPlan:
    def __init__(self, S, slotsA, slotsB, plan):
        self.S, self.A, self.B, self.plan = S, list(slotsA), list(slotsB), plan
        self.slot_of = {}
        self.live = {}
        self.cur = 0
        self.got = []
        self.markers_passed = 0

    def _pump(self):
        j = 0
        unpassed = 0
        seen_markers = 0
        for j in range(len(self.plan)):
            e = self.plan[j]
            if e[0] == 'KV':
                seen_markers += 1
                if seen_markers > self.markers_passed:
                    unpassed += 1
                    if unpassed > 1:
                        return
                continue
            if j in self.slot_of or j < self.cur:
                continue
            allowed = self.A + (self.B if unpassed == 0 else [])
            free = [t for t in allowed if t.name not in self.live]
            if not free:
                return
            t = free[0]
            _, src, P, n = e[0:4]
            self.S.dma("pool", t[0:P, 0:n], src, max_dma_last_dim=4096)
            self.slot_of[j] = t
            self.live[t.name] = j

    def _release(self, idx):
        t = self.slot_of.get(idx)
        if t is not None and self.live.get(t.name) == idx:
            del self.live[t.name]

    def get(self, src_off=None):
        while self.plan[self.cur][0] == 'KV':
            self.cur += 1
        i = self.cur
        if src_off is not None:
            assert self.plan[i][4] == src_off, (i, self.plan[i][4], src_off)
        while len(self.got) >= 2:
            self._release(self.got.pop(0))
        if i not in self.slot_of:
            self._pump()
        assert i in self.slot_of, ("no slot for load", i)
        self.cur += 1
        self.got.append(i)
        self._pump()
        return self.slot_of[i]

    def release_all(self):
        while self.got:
            self._release(self.got.pop(0))

    def kv_begin(self):
        self.release_all()
        self._pump()

    def kv_end(self):
        self.markers_passed += 1
        self._pump()


class K:
    def __init__(self, S):
        self.S = S

    @staticmethod
    def _aps(*xs):
        return [x for x in xs if x is not None and not isinstance(x, (int, float))]

    def act(self, out, in_, func, scale=1.0, bias=None):
        kw = {}
        if bias is not None:
            kw["bias"] = bias
        self.S.op("act", lambda e: e.activation(out=out, in_=in_, func=func, scale=scale, **kw),
                  reads=self._aps(in_, scale, bias), writes=[out])

    def ts(self, out, in0, s1, s2=None, op0=ALU.mult, op1=None, eng="dve"):
        kw = {}
        if op1 is not None:
            kw["op1"] = op1
        self.S.op(eng, lambda e: e.tensor_scalar(out=out, in0=in0, scalar1=s1, scalar2=s2, op0=op0, **kw),
                  reads=self._aps(in0, s1, s2), writes=[out])

    def tt(self, out, in0, in1, op, eng="dve"):
        self.S.op(eng, lambda e: e.tensor_tensor(out=out, in0=in0, in1=in1, op=op),
                  reads=[in0, in1], writes=[out])

    def stt(self, out, in0, scalar, in1, op0, op1, eng="dve"):
        self.S.op(eng, lambda e: e.scalar_tensor_tensor(out=out, in0=in0, scalar=scalar, in1=in1, op0=op0, op1=op1),
                  reads=self._aps(in0, scalar, in1), writes=[out])

    def recip(self, out, in_):
        self.S.op("dve", lambda e: e.reciprocal(out=out, in_=in_), reads=[in_], writes=[out])

    def copy(self, out, in_, eng="dve"):
        if in_.tensor.name.startswith("bank"):
            self.ts(out, in_, 1.0, None, op0=ALU.mult, eng=eng)
            return
        self.S.op(eng, lambda e: e.tensor_copy(out=out, in_=in_), reads=[in_], writes=[out])

    def memset(self, ap, val, eng="dve"):
        self.S.op(eng, lambda e: e.memset(ap, val), writes=[ap])

    def rsum(self, out, in_):
        self.S.op("dve", lambda e: e.reduce_sum(out=out, in_=in_, axis=AX.X), reads=[in_], writes=[out])

    def scan(self, out, a, b, init):
        self.S.op("dve", lambda e: e.tensor_tensor_scan(out=out, data0=a, data1=b, initial=init,
                                                        op0=ALU.mult, op1=ALU.add),
                  reads=self._aps(a, b, init), writes=[out])

    def mm(self, out, lhsT, rhs, start=True, stop=True):
        self.S.op("pe", lambda e: e.matmul(out, lhsT=lhsT, rhs=rhs, start=start, stop=stop),
                  reads=[lhsT, rhs], writes=[out])


def build_program():
    nc = bass.Bass("TRN2", target_bir_lowering=False)
    din = lambda n, s, dt=F32: nc.dram_tensor(n, s, dt, kind="ExternalInput")
    dout = lambda n, s: nc.dram_tensor(n, s, F32, kind="ExternalOutput")
    xT_in = din("xT", [1024, NTOK])
    small_in = din("small", [128, NSM])
    wmod = [din(f"wmod{l}", [128, 24576]) for l in range(DEPTH)]
    wp1 = [din(f"wp1{l}", [128, WP1N]) for l in range(DEPTH)]
    wp2 = [din(f"wp2{l}", [128, WP2N]) for l in range(DEPTH)]
    wba = [din(f"wba{l}", [64, 8192]) for l in range(DEPTH)]
    glru_in = din("glru", [128, 4096])
    rmat_in = din("rmat", [128, 128])
    cos_in = din("cosT", [128, 1024])
    sin_in = din("sinT", [128, 1024])
    msk_in = din("masks", [128, 3072])
    ckc_in = [din(f"ckc{l}", [128, 2048]) for l in range(DEPTH)]
    cvc_in = [din(f"cvc{l}", [128, 2048]) for l in range(DEPTH)]
    cka_in = [din(f"cka{l}", [64, 1024]) for l in range(DEPTH)]
    cva_in = [din(f"cva{l}", [128, 512]) for l in range(DEPTH)]

    yT_out = dout("yT", [1024, NTOK])
    o_ka = dout("o_ka", [DEPTH * 2 * 64, 512])
    o_kc = dout("o_kc", [DEPTH * 4 * 128, 512])
    o_va = dout("o_va", [DEPTH * 512, 128])
    o_vc = dout("o_vc", [DEPTH * 512, 512])
    o_st = dout("o_st", [128, 32])

    ga_in = nc.dram_tensor("ga_in", [128, 4096], BF16)
    ga_out = nc.dram_tensor("ga_out", [512, 4096], BF16)
    gb_in = nc.dram_tensor("gb_in", [128, 4096], BF16)
    gb_out = nc.dram_tensor("gb_out", [512, 4096], BF16)
    gc_in = nc.dram_tensor("gc_in", [128, 3072], BF16)
    gc_out = nc.dram_tensor("gc_out", [512, 3072], BF16)
    g2_in = nc.dram_tensor("g2_in", [128, 12], F32)
    g2_out = nc.dram_tensor("g2_out", [512, 12], F32)
    g3_in = nc.dram_tensor("g3_in", [128, 16], F32)
    g3_out = nc.dram_tensor("g3_out", [512, 16], F32)
    spill_a = nc.dram_tensor("spill_a", [128, 8 * 1024], F32)
    spill_b = nc.dram_tensor("spill_b", [128, 8 * 1024], F32)
    GROUPS = [[0, 1, 2, 3], [4, 5, 6, 7]]

    with contextlib.ExitStack() as st:
        S = Sched(nc, st)
        k = K(S)
        _cnt = [0]

        def T(stack, shape, dt=F32, name=None):
            _cnt[0] += 1
            return stack.enter_context(nc.sbuf_tensor(f"{name or 't'}_{_cnt[0]}", shape, dt))

        banks = [st.enter_context(nc.psum_tensor(f"bank{i}", [128, 512], F32)) for i in range(8)]

        XT = T(st, [128, 8, NTOK], F32, "XT")
        OSGB = T(st, [128, 4, NTOK], BF16, "OSGB")
        ONES = T(st, [128, 128], BF16, "ONES")
        BONES = T(st, [128, 128], BF16, "BONES")
        RM = T(st, [128, 128], BF16, "RM")
        COS = T(st, [128, 1024], F32, "COS")
        SIN = T(st, [128, 1024], F32, "SIN")
        MSK = T(st, [128, 6, 512], BF16, "MSK")
        SMALL = T(st, [128, NSM], F32, "SMALL")
        EPST = T(st, [128, 1], F32, "EPST")
        MODV = [T(st, [128, 24, 2], F32, "MODV") for _ in range(DEPTH)]
        GS = [T(st, [128, 8, 2], F32, "GS") for _ in range(DEPTH)]
        DER = [T(st, [128, 64], F32, "DER") for _ in range(DEPTH)]
        KAW = T(st, [128, 2, 1280], BF16, "KAW")
        VAW = T(st, [128, 10, 128], BF16, "VAW")
        VAWS = T(st, [128, 10, 128], BF16, "VAWS")
        KACTX = T(st, [128, 2, 512], BF16, "KACTX")
        KCCTX = T(st, [128, 4, 512], BF16, "KCCTX")
        VACTX = T(st, [128, 4, 128], BF16, "VACTX")
        VACTXS = T(st, [128, 4, 128], BF16, "VACTXS")
        VCCTX = T(st, [128, 4, 512], BF16, "VCCTX")
        STO = T(st, [128, 32], F32, "STO")
        WSL = [T(st, [128, 4096], BF16, "WSL") for _ in range(2)]
        ws_i = [0]

        def sm(name, l=None):
            o, w = SM[(name, l)] if l is not None else SM[name]
            return SMALL[:, o:o + w]

        def wload(src_ap, P, n):
            slot = WSL[ws_i[0] % len(WSL)]
            ws_i[0] += 1
            S.dma("pool", slot[0:P, 0:n], src_ap, max_dma_last_dim=4096)
            return slot

        bank_i = [0]

        def nb(pool=(0, 1, 2, 3)):
            b = banks[pool[bank_i[0] % len(pool)]]
            bank_i[0] += 1
            return b

        D_NBA, D_NBX, D_CL, D_C2, D_ESINK, D_NLAM, D_SUBG = 0, 8, 16, 24, 32, 40, 41

        S.dma("sp", XT[:], xT_in.ap().rearrange("(k p) t -> p k t", p=128))
        S.dma("sp", SMALL[:], small_in.ap())
        S.dma("sp", COS[:], cos_in.ap())
        S.dma("sp", SIN[:], sin_in.ap())
        S.dma("pool", RM[:], rmat_in.ap())
        S.dma("pool", MSK[:].rearrange("p a b -> p (a b)"), msk_in.ap(), max_dma_last_dim=4096)
        k.memset(ONES[:], 1.0)
        k.memset(BONES[:], 0.0)
        k.memset(BONES[0:64, 0:64], 1.0)
        k.memset(BONES[64:128, 64:128], 1.0)
        k.memset(EPST[:], EPS)
        ONE_P = T(st, [128, 1], F32, "ONE_P")
        k.memset(ONE_P[:], 1.0)
        k.memset(KAW[:], 0.0)
        k.memset(VAW[:], 0.0)
        k.memset(VAWS[:], 0.0)
        k.memset(KACTX[:], 0.0)
        k.memset(WSL[0][:, 0:2048], 0.0)
        S.dma("sp", gc_in.ap()[64:128, 0:2048], WSL[0][64:128, 0:2048])

        with contextlib.ExitStack() as ph:
            SCF = T(ph, [128, 16], F32, "SCF")
            SCF2 = T(ph, [128, 16], F32, "SCF2")
            SCB = T(ph, [128, 8, 2], BF16, "SCB")
            TMPS = T(ph, [128, 64], F32, "TMPS")
            TMPS2 = T(ph, [128, 64], F32, "TMPS2")
            cT = sm("c")
            k.act(SCF[:], cT, AF.Exp, scale=-1.0)
            k.ts(SCF[:], SCF[:], 1.0, None, op0=ALU.add)
            k.recip(SCF2[:], SCF[:])
            k.tt(SCB[:].rearrange("p a b -> p (a b)"), cT, SCF2[:], ALU.mult)
            for l in range(DEPTH):
                ps = nb()
                for g in range(6):
                    w = wload(wmod[l].ap()[:, g * 4096:(g + 1) * 4096], 128, 4096)
                    wv = w[:, :].rearrange("p (c k m) -> p c k m", c=4, k=8)
                    for c4 in range(4):
                        cc = g * 4 + c4
                        for kk in range(8):
                            k.mm(ps[:, cc * 2:cc * 2 + 2], wv[:, c4, kk, :], SCB[:, kk, :],
                                 start=(kk == 0), stop=(kk == 7))
                psv = ps[:, 0:48].rearrange("p (c v) -> p c v", v=2)
                for v in range(2):
                    k.tt(MODV[l][:, :, v], psv[:, :, v], sm("modb", l), ALU.add)
                    k.stt(GS[l][:, :, v], MODV[l][:, 8:16, v], 1.0, sm("ng", l), ALU.add, ALU.mult)
                D = DER[l]
                k.ts(D[:, D_NBA:D_NBA + 8], sm("ba", l), -1.0, None, op0=ALU.mult)
                k.ts(D[:, D_NBX:D_NBX + 8], sm("bx", l), -1.0, None, op0=ALU.mult)
                k.act(TMPS[:, 0:8], sm("lam", l), AF.Exp, scale=-1.0)
                k.ts(TMPS[:, 0:8], TMPS[:, 0:8], 1.0, None, op0=ALU.add)
                k.act(TMPS2[:, 0:8], TMPS[:, 0:8], AF.Ln)
                k.ts(D[:, D_CL:D_CL + 8], TMPS2[:, 0:8], -8.0, None, op0=ALU.mult)
                k.ts(D[:, D_C2:D_C2 + 8], TMPS2[:, 0:8], -16.0, None, op0=ALU.mult)
                k.act(D[:, D_ESINK:D_ESINK + 8], sm("sink", l), AF.Exp)
                lam_init = 0.8 - 0.6 * math.exp(-0.3 * l)
                k.tt(TMPS[:, 0:64], sm("lq1", l), sm("lk1", l), ALU.mult)
                k.rsum(TMPS2[:, 8:9], TMPS[:, 0:64])
                k.tt(TMPS[:, 0:64], sm("lq2", l), sm("lk2", l), ALU.mult)
                k.rsum(TMPS2[:, 9:10], TMPS[:, 0:64])
                k.act(TMPS2[:, 10:12], TMPS2[:, 8:10], AF.Exp)
                k.tt(TMPS2[:, 12:13], TMPS2[:, 11:12], TMPS2[:, 10:11], ALU.subtract)
                k.ts(D[:, D_NLAM:D_NLAM + 1], TMPS2[:, 12:13], -lam_init, None, op0=ALU.add)
                k.ts(D[:, D_SUBG:D_SUBG + 1], sm("subln", l), 1.0 - lam_init, None, op0=ALU.mult)
            S.barrier()

        def front(ph_t, l, b):
            HT, SQ, RSTD, FTMP = ph_t["HT"], ph_t["SQ"], ph_t["RSTD"], ph_t["FTMP"]
            v = 0 if b == 0 else 1
            cols = slice(b * 512, (b + 1) * 512)
            ps = banks[7]
            for kk in range(8):
                sq = SQ[kk % 2]
                if kk % 2 == 0:
                    k.act(sq[:], XT[:, kk, cols], AF.Square)
                else:
                    k.tt(sq[:], XT[:, kk, cols], XT[:, kk, cols], ALU.mult, eng=("pool" if kk % 4 == 1 else "dve"))
                k.mm(ps[:], ONES[:], sq[:], start=(kk == 0), stop=(kk == 7))
            k.act(RSTD[:], ps[:], AF.Ln, scale=1.0 / 1024.0, bias=EPST[:])
            k.act(RSTD[:], RSTD[:], AF.Exp, scale=-0.5)
            for kk in range(8):
                ft = FTMP[kk % 2]
                k.tt(ft[:], XT[:, kk, cols], RSTD[:], ALU.mult)
                k.act(HT[:, kk, :], ft[:], AF.Identity, scale=GS[l][:, kk, v:v + 1], bias=MODV[l][:, kk, v:v + 1])

        def proj(ps, w3, nk, M, rhs_fn, N=512, P=128):
            if isinstance(w3, tuple):
                flat, base = w3
                for kk in range(nk):
                    k.mm(ps[0:128, 0:N], flat[0:P, base + kk * 64:base + kk * 64 + 128], rhs_fn(kk),
                         start=(kk == 0), stop=(kk == nk - 1))
                return
            for kk in range(nk):
                k.mm(ps[0:M, 0:N], w3[0:P, kk, 0:M], rhs_fn(kk), start=(kk == 0), stop=(kk == nk - 1))

        qk_i = [0]

        def qknorm(ph_t, ps, P, gain, rope_cols, dest_fn):
            i = qk_i[0] % 2
            qk_i[0] += 1
            X32, SQB, LNV, XN = ph_t["QX"][i], ph_t["QS"][i], ph_t["QL"][i], ph_t["QN"][i]
            k.act(X32[0:P, :], ps[0:P, :], AF.Copy)
            k.act(SQB[0:P, :], ps[0:P, :], AF.Square)
            ss = nb((4, 5))
            k.mm(ss[0:P, :], BONES[0:P, 0:P], SQB[0:P, :])
            k.act(LNV[0:P, :], ss[0:P, :], AF.Ln, scale=1.0 / 64.0, bias=EPST[0:P, :])
            k.act(LNV[0:P, :], LNV[0:P, :], AF.Exp, scale=-0.5)
            k.stt(XN[0:P, :], X32[0:P, :], gain, LNV[0:P, :], ALU.mult, ALU.mult)
            if rope_cols is None:
                dest_fn(XN)
                return
            k.copy(SQB[0:P, :], XN[0:P, :], eng="pool")
            rx = nb((4, 5))
            k.mm(rx[0:P, :], RM[0:P, 0:P], SQB[0:P, :])
            k.tt(X32[0:P, :], XN[0:P, :], COS[0:P, rope_cols], ALU.mult, eng="pool")
            k.tt(LNV[0:P, :], rx[0:P, :], SIN[0:P, rope_cols], ALU.mult)
            dest_fn((X32, LNV))

        def qk_pipeline(ph_t, items, rope_cols, rhs_fn):
            n = len(items)
            st = [None] * n
            base = qk_i[0]
            qk_i[0] += n

            def tiles(i):
                j = (base + i) % 2
                return ph_t["QX"][j], ph_t["QS"][j], ph_t["QL"][j], ph_t["QN"][j]

            def stage_a(i):
                w3, P, gain, dest = items[i]
                ps = nb()
                proj(ps, w3, 8, P, rhs_fn)
                st[i] = ps

            def stage_b(i):
                w3, P, gain, dest = items[i]
                X32, SQB, LNV, XN = tiles(i)
                ps = st[i]
                k.act(SQB[0:P, :], ps[0:P, :], AF.Square)
                ss = nb((4, 5))
                k.mm(ss[0:P, :], BONES[0:P, 0:P], SQB[0:P, :])
                k.act(LNV[0:P, :], ss[0:P, :], AF.Ln, scale=1.0 / 64.0, bias=EPST[0:P, :])
                k.act(LNV[0:P, :], LNV[0:P, :], AF.Exp, scale=-0.5)
                k.stt(XN[0:P, :], ps[0:P, :], gain, LNV[0:P, :], ALU.mult, ALU.mult)
                if rope_cols is None:
                    dest(XN)

            def stage_c0(i):
                w3, P, gain, dest = items[i]
                X32, SQB, LNV, XN = tiles(i)
                k.act(SQB[0:P, :], XN[0:P, :], AF.Copy)

            def stage_c1(i):
                w3, P, gain, dest = items[i]
                X32, SQB, LNV, XN = tiles(i)
                rx = nb((6, 7))
                k.mm(rx[0:P, :], RM[0:P, 0:P], SQB[0:P, :])
                k.tt(X32[0:P, :], XN[0:P, :], COS[0:P, rope_cols], ALU.mult, eng="pool")
                k.tt(LNV[0:P, :], rx[0:P, :], SIN[0:P, rope_cols], ALU.mult)
                dest((X32, LNV))

            for step in range(n + 2):
                if rope_cols is not None and 0 <= step - 2 < n:
                    stage_c0(step - 2)
                if step < n:
                    stage_a(step)
                if 0 <= step - 1 < n:
                    stage_b(step - 1)
                if rope_cols is not None and 0 <= step - 2 < n:
                    stage_c1(step - 2)

        def silu_to(ph_t, ps, P, out_ap):
            i = qk_i[0] % 2
            qk_i[0] += 1
            E = ph_t["QX"][i]
            k.act(E[0:P, :], ps[0:P, :], AF.Exp, scale=-1.0)
            k.act(E[0:P, :], E[0:P, :], AF.Ln, bias=ONE_P[0:P, :])
            k.act(E[0:P, :], E[0:P, :], AF.Exp, scale=-1.0)
            k.tt(out_ap, ps[0:P, :], E[0:P, :], ALU.mult)

        def common_tiles(ph):
            sq = [T(ph, [128, 512], BF16, "SQ") for _ in range(2)]
            qn = [T(ph, [128, 512], F32, "QN") for _ in range(2)]
            return {
                "HT": T(ph, [128, 8, 512], BF16, "HT"),
                "SQ": sq,
                "RSTD": T(ph, [128, 512], F32, "RSTD"),
                "FTMP": qn,
                "QX": [T(ph, [128, 512], F32, "QX") for _ in range(2)],
                "QS": sq,
                "QL": [T(ph, [128, 512], F32, "QL") for _ in range(2)],
                "QN": qn,
            }

        class _Stop(Exception):
            pass

        def layers():
          for l in range(DEPTH):
            D = DER[l]
            if STOP == "setup":
                return
            with contextlib.ExitStack() as ph0:
              XB = T(ph0, [128, 4, XBW], F32, "XB")
              with contextlib.ExitStack() as ph:
                pt = common_tiles(ph)
                HT = pt["HT"]
                KST = [T(ph, [128, 512], BF16, "KST") for _ in range(2)]
                VTMP = [T(ph, [128, 512], F32, "VTMP") for _ in range(1)]
                VST = [T(ph, [128, 512], BF16, "VST") for _ in range(2)]
                VATMP = T(ph, [128, 128], F32, "VATMP")
                k.memset(XB[:], 0.0)
                kst_i = [0]
                W1 = {}
                for (o_, n_) in ((0, 1024), (1024, 4096), (5120, 4096), (9216, 4096), (13312, 4096), (17408, 1024)):
                    t_ = T(ph, [128, n_ + (64 if o_ == 0 else 0)], BF16, "W1")
                    if o_ == 0:
                        k.memset(t_[:, n_:n_ + 64], 0.0)
                    S.dma("pool", t_[:, 0:n_], wp1[l].ap()[:, o_:o_ + n_], max_dma_last_dim=4096)
                    W1[o_] = t_

                class _W1:
                    def get(self, off):
                        return W1[off]
                wpl = _W1()
                for b in range(3):
                    lat = b > 0
                    front(pt, l, b)
                    rope_cols = slice((b - 1) * 512, b * 512) if lat else None
                    hrhs = lambda kk: HT[:, kk, :]
                    if STOP == "front":
                        return
                    w = wpl.get(0)
                    w4 = w[:, 0:1024].rearrange("p (c k m) -> p c k m", c=2, k=8)
                    qitems = []
                    for kv in range(2):
                        if not lat:
                            def dest(XN, kv=kv):
                                S.dma("sp", o_ka.ap()[(l * 2 + kv) * 64:(l * 2 + kv + 1) * 64, :], XN[0:64, :],
                                      is_output=True)
                                k.copy(KACTX[0:64, kv, :], XN[0:64, :])
                        else:
                            def dest(t, kv=kv, b=b):
                                dst = KAW[0:64, kv, 128 + (b - 1) * 512:128 + b * 512]
                                k.tt(dst, t[0][0:64, :], t[1][0:64, :], ALU.add)
                                S.dma("sp", gc_in.ap()[0:64, KA_OFF + kv * 1024 + (b - 1) * 512:KA_OFF + kv * 1024 + b * 512], dst)
                        qitems.append(((w, kv * 512), 64, sm("kna", l)[0:64, :], dest))
                    qk_pipeline(pt, qitems, rope_cols, hrhs)
                    if STOP == "ka":
                        return
                    w = wpl.get(1024)
                    w4 = w[:, 0:4096].rearrange("p (c k m) -> p c k m", c=4, k=8)
                    qitems = []
                    for h in range(4):
                        if not lat:
                            def dest(XN, h=h):
                                S.dma("sp", o_kc.ap()[(l * 4 + h) * 128:(l * 4 + h + 1) * 128, :], XN[:, :], is_output=True)
                                k.copy(KCCTX[:, h, :], XN[:, :])
                        else:
                            def dest(t, h=h, b=b):
                                ks = KST[kst_i[0] % 2]
                                kst_i[0] += 1
                                k.tt(ks[:], t[0][:, :], t[1][:, :], ALU.add)
                                S.dma("sp", ga_in.ap()[:, KC_OFF + h * 1024 + (b - 1) * 512:KC_OFF + h * 1024 + b * 512], ks[:])
                        qitems.append((w4[:, h], 128, sm("knc", l), dest))
                    qk_pipeline(pt, qitems, rope_cols, hrhs)
                    if STOP == "kc":
                        return
                    w = wpl.get(5120)
                    w4 = w[:, 0:4096].rearrange("p (c k m) -> p c k m", c=4, k=8)
                    for ch in range(4):
                        ps = nb()
                        proj(ps, w4[:, ch], 8, 128, hrhs)
                        if b == 0:
                            k.act(XB[:, ch, 2:258], ps[:, 0:256], AF.Copy)
                            k.act(XB[:, ch, 261:517], ps[:, 256:512], AF.Copy)
                        else:
                            c0 = 520 + (b - 1) * 512
                            k.act(XB[:, ch, c0:c0 + 512], ps[:, :], AF.Copy)
                    if STOP == "xb":
                        return
                    w = wpl.get(9216)
                    w4 = w[:, 0:4096].rearrange("p (c k m) -> p c k m", c=4, k=8)
                    for ch in range(4):
                        ps = nb()
                        proj(ps, w4[:, ch], 8, 128, hrhs)
                        silu_to(pt, ps, 128, OSGB[:, ch, b * 512:(b + 1) * 512])
                    if STOP == "gb":
                        return
                    w = wpl.get(13312)
                    w2 = wpl.get(17408)
                    wv = w[:, 0:4096].rearrange("p (k n) -> p k n", k=8)
                    wva = w2[:, 0:1024].rearrange("p (k n) -> p k n", k=8)
                    for tt_ in range(4):
                        tok = slice(tt_ * 128, (tt_ + 1) * 128)
                        psc = nb((4, 5))
                        psa = banks[6]
                        for kk in range(8):
                            k.mm(psc[:, :], HT[:, kk, tok], wv[:, kk, :], start=(kk == 0), stop=(kk == 7))
                        if 'a' in KV:
                            continue
                        for kk in range(8):
                            k.mm(psa[:, 0:128], HT[:, kk, tok], wva[:, kk, :], start=(kk == 0), stop=(kk == 7))
                        if 'b' in KV:
                            continue
                        if not lat:
                            vt = VTMP[0]
                            if 'e' not in KV:
                                k.act(vt[:], psc[:, :], AF.Copy)
                            if 'c' not in KV:
                                S.dma("sp", o_vc.ap()[l * 512 + tt_ * 128:l * 512 + (tt_ + 1) * 128, :], vt[:], is_output=True)
                            if 'f' not in KV:
                                k.copy(VCCTX[:, tt_, :], psc[:, :])
                            if 'd' in KV:
                                continue
                            k.act(VATMP[:], psa[:, 0:128], AF.Copy)
                            if 'c' not in KV:
                                S.dma("sp", o_va.ap()[l * 512 + tt_ * 128:l * 512 + (tt_ + 1) * 128, :], VATMP[:], is_output=True)
                            k.copy(VACTX[:, tt_, :], psa[:, 0:128])
                            k.copy(VACTXS[:, tt_, 0:64], psa[:, 64:128])
                            k.copy(VACTXS[:, tt_, 64:128], psa[:, 0:64])
                        else:
                            Tt = (b - 1) * 4 + tt_
                            vs = VST[tt_ % 2]
                            k.act(vs[:], psc[:, :], AF.Copy)
                            S.dma("sp", gb_in.ap()[:, VC_OFF + Tt * 512:VC_OFF + (Tt + 1) * 512], vs[:])
                            k.copy(VAW[:, 1 + Tt, :], psa[:, 0:128])
                            k.copy(VAWS[:, 1 + Tt, 0:64], psa[:, 64:128])
                            k.copy(VAWS[:, 1 + Tt, 64:128], psa[:, 0:64])
                            S.dma("sp", gc_in.ap()[:, VA_OFF + Tt * 128:VA_OFF + (Tt + 1) * 128], VAW[:, 1 + Tt, :])
                    if STOP == "v":
                        return
                if STOP == "p1":
                    return
                G2S = T(ph, [128, 12], F32, "G2S")
                for ch in range(4):
                    k.copy(G2S[:, ch * 3:ch * 3 + 1], XB[:, ch, 520:521])
                    k.copy(G2S[:, ch * 3 + 1:ch * 3 + 3], XB[:, ch, 520 + 1022:520 + 1024])
                S.dma("sp", g2_in.ap(), G2S[:])
                S.collective(GROUPS, g2_in, g2_out)
                S.collective(GROUPS, gc_in, gc_out)
                S.collective(GROUPS, ga_in, ga_out)
                S.collective(GROUPS, gb_in, gb_out)
                S.barrier(skip_cc=True)
              if STOP == "cc":
                  return
              with contextlib.ExitStack() as ph:
                GL = T(ph, [128, 16, 128], BF16, "GL")
                S.dma("pool", GL[:].rearrange("p a b -> p (a b)"), glru_in.ap()[:, l * 2048:(l + 1) * 2048], max_dma_last_dim=4096)
                Us = [T(ph, [128, 1024], F32, "U") for _ in range(2)]
                UBs = [T(ph, [128, 1024], BF16, "UB") for _ in range(2)]
                AA = [T(ph, [128, 1024], F32, "AA") for _ in range(2)]
                BB = [T(ph, [128, 1024], F32, "BB") for _ in range(2)]
                HH = [T(ph, [128, 1024], F32, "HH") for _ in range(2)]
                LT = [[T(ph, [128, 512], F32, "LT") for _ in range(4)] for _ in range(2)]
                ONE_T = T(ph, [128, 1], F32, "ONE_T")
                k.memset(ONE_T[:], 1.0)
                u_i = [0]
                HAL = T(ph, [128, 4, 12], F32, "HAL")
                TOT = T(ph, [128, 16], F32, "TOT")
                RS = T(ph, [128, 8], F32, "RS")
                TOTG = T(ph, [128, 4, 16], F32, "TOTG")
                CH = T(ph, [128, 40], F32, "CH")
                HIN = T(ph, [128, 8], F32, "HIN")
                sel = sm("sel")

                def lat_halo():
                    S.dma("sp", HAL[:], g2_out.ap().rearrange("(r p) c -> p r c", p=128))
                    for ch in range(4):
                        pre = XB[:, ch, 518:520]
                        post = XB[:, ch, 1544:1545]
                        for r in range(4):
                            k.stt(pre, HAL[:, r, ch * 3 + 1:ch * 3 + 3], sel[:, r:r + 1], pre, ALU.mult, ALU.add)
                            k.stt(post, HAL[:, r, ch * 3:ch * 3 + 1], sel[:, 4 + r:5 + r], post, ALU.mult, ALU.add)

                def lru_gates(ch, s0, Tn, U, UB, want_rsum):
                    npc = (Tn + 511) // 512
                    for pc in range(npc):
                        n = min(512, Tn)
                        cs = slice(pc * 512, pc * 512 + n)
                        seqs = []
                        for dr in range(2):
                            nba = D[:, D_NBA + dr * 4 + ch:D_NBA + dr * 4 + ch + 1]
                            nbx = D[:, D_NBX + dr * 4 + ch:D_NBX + dr * 4 + ch + 1]
                            cl = D[:, D_CL + dr * 4 + ch:D_CL + dr * 4 + ch + 1]
                            c2 = D[:, D_C2 + dr * 4 + ch:D_C2 + dr * 4 + ch + 1]
                            L0, L1, L2, L3 = [t[:, 0:n] for t in LT[dr]]
                            zr = banks[(pc % 2) * 4 + dr * 2]
                            zi = banks[(pc % 2) * 4 + dr * 2 + 1]
                            ops = [
                                lambda zr=zr, dr=dr: k.mm(zr[:, 0:n], GL[:, (dr * 2 + 0) * 4 + ch, :], UB[:, cs]),
                                lambda zi=zi, dr=dr: k.mm(zi[:, 0:n], GL[:, (dr * 2 + 1) * 4 + ch, :], UB[:, cs]),
                                lambda L0=L0, zr=zr, nba=nba: k.act(L0, zr[:, 0:n], AF.Exp, scale=-1.0, bias=nba),
                                lambda L1=L1, zi=zi, nbx=nbx: k.act(L1, zi[:, 0:n], AF.Exp, scale=-1.0, bias=nbx),
                                lambda L0=L0: k.act(L0, L0, AF.Ln, bias=ONE_T[:]),
                                lambda L1=L1: k.act(L1, L1, AF.Ln, bias=ONE_T[:]),
                                lambda L0=L0: k.act(L0, L0, AF.Exp, scale=-1.0),
                                lambda L0=L0, dr=dr, cl=cl: k.act(AA[dr][:, cs], L0, AF.Exp, scale=cl),
                                lambda L2=L2, L0=L0, c2=c2: k.ts(L2, L0, c2, None, op0=ALU.mult),
                                lambda L3=L3, L2=L2: k.ts(L3, L2, 1.0 / 24.0, 1.0 / 6.0, op0=ALU.mult, op1=ALU.add),
                                lambda L3=L3, L2=L2: k.tt(L3, L3, L2, ALU.mult),
                                lambda L3=L3, L2=L2: k.stt(L3, L3, 0.5, L2, ALU.add, ALU.mult),
                                lambda L3=L3, L2=L2: k.stt(L3, L3, 1.0, L2, ALU.add, ALU.mult),
                                lambda L3=L3: k.act(L3, L3, AF.Ln, scale=-1.0),
                                lambda L3=L3, L1=L1: k.stt(L3, L3, 0.5, L1, ALU.mult, ALU.subtract),
                                lambda L3=L3: k.act(L3, L3, AF.Exp),
                                lambda L3=L3, dr=dr: k.tt(BB[dr][:, cs], L3, U[:, cs], ALU.mult),
                            ]
                            if want_rsum:
                                ops.insert(8, lambda L0=L0, dr=dr, pc=pc: k.rsum(RS[:, dr * 2 + pc:dr * 2 + pc + 1], L0))
                            seqs.append(ops)
                        for o0, o1 in zip(*seqs):
                            o0()
                            o1()

                def conv_u(ch, s0, Tn):
                    U = Us[u_i[0] % 2]
                    UB = UBs[u_i[0] % 2]
                    u_i[0] += 1
                    cw = sm("convw", l)
                    k.ts(U[:, 0:Tn], XB[:, ch, s0:s0 + Tn], cw[:, ch * 4:ch * 4 + 1], sm("convb", l)[:, ch:ch + 1],
                         op0=ALU.mult, op1=ALU.add)
                    for j in range(1, 4):
                        k.stt(U[:, 0:Tn], XB[:, ch, s0 + j:s0 + j + Tn], cw[:, ch * 4 + j:ch * 4 + j + 1], U[:, 0:Tn],
                              ALU.mult, ALU.add)
                    k.copy(UB[:, 0:Tn], U[:, 0:Tn])
                    return U, UB

                def do_scan(dr, Tn, init):
                    if dr == 0:
                        k.scan(HH[0][:, 0:Tn], AA[0][:, 0:Tn], BB[0][:, 0:Tn], init)
                    else:
                        k.scan(HH[1][:, 0:Tn][:, ::-1], AA[1][:, 0:Tn][:, ::-1], BB[1][:, 0:Tn][:, ::-1], init)

                def lru_seg(ch, si):
                        s0, Tn, tk0 = SEGS[si]
                        U, UB = conv_u(ch, s0, Tn)
                        lru_gates(ch, s0, Tn, U, UB, si == 2)
                        for dr in range(2):
                            do_scan(dr, Tn, 0.0)
                            endc = Tn - 1 if dr == 0 else 0
                            if si < 2:
                                col = ((l * 2 + si) * 2 + dr) * 4 + ch
                                k.copy(STO[:, col:col + 1], HH[dr][:, endc:endc + 1])
                            else:
                                k.copy(TOT[:, dr * 8 + 4 + ch:dr * 8 + 5 + ch], HH[dr][:, endc:endc + 1])
                                k.tt(RS[:, 4 + dr:5 + dr], RS[:, dr * 2:dr * 2 + 1], RS[:, dr * 2 + 1:dr * 2 + 2], ALU.add)
                                k.act(TOT[:, dr * 8 + ch:dr * 8 + ch + 1], RS[:, 4 + dr:5 + dr], AF.Exp,
                                      scale=D[:, D_CL + dr * 4 + ch:D_CL + dr * 4 + ch + 1])
                                sl = slice((ch * 2 + dr) * 1024, (ch * 2 + dr + 1) * 1024)
                                S.dma("sp", spill_a.ap()[:, sl], AA[dr][:, :])
                                S.dma("sp", spill_b.ap()[:, sl], BB[dr][:, :])
                        if si < 2:
                            k.tt(HH[0][:, 0:Tn], HH[0][:, 0:Tn], HH[1][:, 0:Tn], ALU.add, eng="pool")
                            k.tt(OSGB[:, ch, tk0:tk0 + Tn], HH[0][:, 0:Tn], OSGB[:, ch, tk0:tk0 + Tn], ALU.mult, eng="pool")
                for ch in range(2):
                    lru_seg(ch, 0)
                    lru_seg(ch, 1)
                lat_halo()
                for ch in range(4):
                    lru_seg(ch, 2)
                S.dma("sp", g3_in.ap(), TOT[:])
                S.collective(GROUPS, g3_in, g3_out)
                for ch in range(2, 4):
                    lru_seg(ch, 0)
                    lru_seg(ch, 1)
                S.dma("sp", TOTG[:], g3_out.ap().rearrange("(r p) c -> p r c", p=128))
                h0 = sm("h0", l)
                k.copy(CH[:, 0:4], h0[:, 0:4])
                for r in range(3):
                    k.tt(CH[:, (r + 1) * 4:(r + 2) * 4], TOTG[:, r, 0:4], CH[:, r * 4:(r + 1) * 4], ALU.mult)
                    k.tt(CH[:, (r + 1) * 4:(r + 2) * 4], CH[:, (r + 1) * 4:(r + 2) * 4], TOTG[:, r, 4:8], ALU.add)
                k.copy(CH[:, 16 + 12:16 + 16], h0[:, 4:8])
                for r in (3, 2, 1):
                    k.tt(CH[:, 16 + (r - 1) * 4:16 + r * 4], TOTG[:, r, 8:12], CH[:, 16 + r * 4:16 + (r + 1) * 4], ALU.mult)
                    k.tt(CH[:, 16 + (r - 1) * 4:16 + r * 4], CH[:, 16 + (r - 1) * 4:16 + r * 4], TOTG[:, r, 12:16], ALU.add)
                k.memset(HIN[:], 0.0)
                for r in range(4):
                    k.stt(HIN[:, 0:4], CH[:, r * 4:(r + 1) * 4], sel[:, 8 + r:9 + r], HIN[:, 0:4], ALU.mult, ALU.add)
                    k.stt(HIN[:, 4:8], CH[:, 16 + r * 4:16 + (r + 1) * 4], sel[:, 8 + r:9 + r], HIN[:, 4:8], ALU.mult, ALU.add)
                s0, Tn, tk0 = SEGS[2]
                for ch in range(4):
                    for dr in range(2):
                        sl = slice((ch * 2 + dr) * 1024, (ch * 2 + dr + 1) * 1024)
                        S.dma("sp", AA[dr][:, :], spill_a.ap()[:, sl])
                        S.dma("sp", BB[dr][:, :], spill_b.ap()[:, sl])
                        do_scan(dr, Tn, HIN[:, dr * 4 + ch:dr * 4 + ch + 1])
                    k.tt(HH[0][:, :], HH[0][:, :], HH[1][:, :], ALU.add, eng="pool")
                    k.tt(OSGB[:, ch, tk0:tk0 + Tn], HH[0][:, :], OSGB[:, ch, tk0:tk0 + Tn], ALU.mult, eng="pool")
                S.barrier()

            if STOP == "lru":
                return
            with contextlib.ExitStack() as ph:
                pt = common_tiles(ph)
                HT = pt["HT"]
                QA = T(ph, [128, 8, 512], BF16, "QA")
                OSA = T(ph, [64, 8, 512], BF16, "OSA")
                QCZ = T(ph, [128, 4, 2, 512], BF16, "QCZ")
                OSC = T(ph, [128, 4, 512], BF16, "OSC")
                YT = T(ph, [128, 8, 512], BF16, "YT")
                KCALL = T(ph, [128, 36 * 128], BF16, "KCALL")
                VCALL = T(ph, [128, 36 * 128], BF16, "VCALL")
                plan = []
                for b_ in range(3):
                    for o_ in (0, 2048, 4096, 8192, 10240, 12288):
                        n_ = 2048 if o_ in (0, 2048, 8192, 10240) else 4096
                        plan.append(('L', wp2[l].ap()[:, o_:o_ + n_], 128, n_, o_))
                    plan.append(('KV',))
                    for cc_ in range(8):
                        o_ = WP2_Q + cc_ * 4096
                        plan.append(('L', wp2[l].ap()[:, o_:o_ + 4096], 128, 4096, o_))
                        plan.append(('L', wba[l].ap()[:, cc_ * 1024:(cc_ + 1) * 1024], 64, 1024, -1 - cc_))
                    for g_ in range(2):
                        o_ = WP2_Q + 8 * 4096 + g_ * 4096
                        plan.append(('L', wp2[l].ap()[:, o_:o_ + 4096], 128, 4096, o_))
                for t_ in WSL + [KCALL, VCALL]:
                    k.memset(t_[:, 2048:2112], 0.0)
                wpl = WPlan(S, WSL, [KCALL, VCALL], plan)
                CKA = T(ph, [128, 2, 512], BF16, "CKA")
                CVA = T(ph, [128, 4, 128], BF16, "CVA")
                CVAS = T(ph, [128, 4, 128], BF16, "CVAS")
                ET = [T(ph, [128, 512], BF16, "ET") for _ in range(3)]
                PT1, PT2 = pt["QX"][0], pt["QX"][1]
                PT3 = pt["RSTD"][:, 0:256]
                PT5 = pt["RSTD"][:, 256:512]
                PT4 = pt["SQ"][0][:, 0:256]
                MT = [pt["QX"][0], pt["QX"][1], pt["QL"][0]]
                MT2 = [pt["QL"][1], pt["QN"][0], pt["QN"][1]]
                YACC = pt["RSTD"]
                k.memset(QCZ[:], 0.0)
                k.memset(CKA[:], 0.0)
                k.memset(QA[64:128, :, :], 0.0)
                S.dma("pool", CKA[0:64, :, :].rearrange("p a b -> p (a b)"), cka_in[l].ap(), max_dma_last_dim=4096)
                S.dma("pool", CVA[:].rearrange("p a b -> p (a b)"), cva_in[l].ap(), max_dma_last_dim=4096)
                cva3 = cva_in[l].ap().rearrange("p (t n) -> p t n", n=128)
                S.dma("pool", CVAS[:, :, 0:64], cva3[:, :, 64:128])
                S.dma("pool", CVAS[:, :, 64:128], cva3[:, :, 0:64])
                g1a, g1b, g1c = ga_out.ap(), gb_out.ap(), gc_out.ap()
                HALK = T(ph, [64, 2, 4, 2, 128], BF16, "HALK")
                HALV = T(ph, [128, 2, 4, 128], BF16, "HALV")
                sel = sm("sel")

                def build_halos():
                    for kv in range(2):
                        base = KA_OFF + kv * 1024
                        S.dma("sp", HALK[:, 0, :, kv, :], g1c[:, base + 896:base + 1024].rearrange("(r p) c -> p r c", p=128)[0:64])
                        S.dma("sp", HALK[:, 1, :, kv, :], g1c[:, base:base + 128].rearrange("(r p) c -> p r c", p=128)[0:64])
                    S.dma("sp", HALV[:, 0, :, :], g1c[:, VA_OFF + 896:VA_OFF + 1024].rearrange("(r p) c -> p r c", p=128))
                    S.dma("sp", HALV[:, 1, :, :], g1c[:, VA_OFF:VA_OFF + 128].rearrange("(r p) c -> p r c", p=128))
                    k.memset(KAW[:, :, 0:128], 0.0)
                    k.memset(KAW[:, :, 1152:1280], 0.0)
                    k.memset(VAW[:, 0, :], 0.0)
                    k.memset(VAW[:, 9, :], 0.0)
                    k.memset(VAWS[:, 0, :], 0.0)
                    k.memset(VAWS[:, 9, :], 0.0)
                    for r in range(4):
                        for kv in range(2):
                            k.stt(KAW[0:64, kv, 0:128], HALK[:, 0, r, kv, :], sel[0:64, r:r + 1], KAW[0:64, kv, 0:128], ALU.mult, ALU.add)
                            k.stt(KAW[0:64, kv, 1152:1280], HALK[:, 1, r, kv, :], sel[0:64, 4 + r:5 + r], KAW[0:64, kv, 1152:1280], ALU.mult, ALU.add)
                        k.stt(VAW[:, 0, :], HALV[:, 0, r, :], sel[:, r:r + 1], VAW[:, 0, :], ALU.mult, ALU.add)
                        k.stt(VAW[:, 9, :], HALV[:, 1, r, :], sel[:, 4 + r:5 + r], VAW[:, 9, :], ALU.mult, ALU.add)
                        for (d0, s0_) in ((0, 64), (64, 0)):
                            k.stt(VAWS[:, 0, d0:d0 + 64], HALV[:, 0, r, s0_:s0_ + 64], sel[:, r:r + 1], VAWS[:, 0, d0:d0 + 64], ALU.mult, ALU.add)
                            k.stt(VAWS[:, 9, d0:d0 + 64], HALV[:, 1, r, s0_:s0_ + 64], sel[:, 4 + r:5 + r], VAWS[:, 9, d0:d0 + 64], ALU.mult, ALU.add)


                et_i = [0]
                acc_i = [0]

                def attend_all(units):
                    steps = [(ui, j, len(u[1])) for ui, u in enumerate(units) for j in range(len(u[1]))]
                    LOOK = 2
                    pend = []
                    deferred = []

                    def issue(si):
                        ui, j, n = steps[si]
                        ps = nb((0, 1, 2))
                        units[ui][0](ps, units[ui][1][j][0])
                        pend.append(ps)
                    for si in range(min(LOOK, len(steps))):
                        issue(si)
                    for si, (ui, j, n) in enumerate(steps):
                        qk_fn, tiles, Mv, post_fn = units[ui]
                        _, vap, mask = tiles[j]
                        ps = pend.pop(0)
                        et = ET[et_i[0] % 3]
                        et_i[0] += 1
                        k.act(et[:], ps[:], AF.Exp, scale=SCALE)
                        if mask is not None:
                            k.tt(et[:], et[:], mask, ALU.mult, eng=("pool" if si % 2 else "dve"))
                        while deferred and deferred[0][0] <= si:
                            deferred.pop(0)[1]()
                        if si + LOOK < len(steps):
                            issue(si + LOOK)
                        a_ = (acc_i[0] + ui) % 2
                        accO = banks[3 + 2 * a_]
                        accD = banks[4 + 2 * a_]
                        k.mm(accO[0:Mv, :], vap, et[:], start=(j == 0), stop=(j == n - 1))
                        k.mm(accD[0:Mv, :], ONES[:, 0:Mv], et[:], start=(j == 0), stop=(j == n - 1))
                        if j == n - 1:
                            deferred.append((si + 3, lambda post_fn=post_fn, accO=accO, accD=accD: post_fn(accO, accD)))
                    for _, fn in deferred:
                        fn()
                    acc_i[0] += len(units)

                for b in range(3):
                    lat = b > 0
                    v = 1 if lat else 0
                    cols = slice(b * 512, (b + 1) * 512)
                    front(pt, l, b)
                    rope_cols = slice((b - 1) * 512, b * 512) if lat else None
                    hrhs = lambda kk: HT[:, kk, :]
                    for g in range(2):
                        w = wpl.get(g * 2048)
                        w4 = w[:, 0:2048].rearrange("p (c k m) -> p c k m", c=4, k=8)
                        qitems = []
                        for c4 in range(4):
                            h = g * 4 + c4
                            if not lat:
                                def dest(XN, h=h):
                                    k.copy(QA[0:64, h, :], XN[0:64, :])
                            else:
                                def dest(t, h=h):
                                    k.tt(QA[0:64, h, :], t[0][0:64, :], t[1][0:64, :], ALU.add)
                            qitems.append(((w, c4 * 512), 64, sm("qna", l)[0:64, :], dest))
                        qk_pipeline(pt, qitems, rope_cols, hrhs)
                    w = wpl.get(4096)
                    w4 = w[:, 0:4096].rearrange("p (c k m) -> p c k m", c=4, k=8)
                    qitems = []
                    for h in range(4):
                        if not lat:
                            def dest(XN, h=h):
                                k.copy(QCZ[0:64, h, 0, :], XN[0:64, :])
                                k.copy(QCZ[64:128, h, 1, :], XN[64:128, :])
                        else:
                            def dest(t, h=h):
                                k.tt(QCZ[0:64, h, 0, :], t[0][0:64, :], t[1][0:64, :], ALU.add)
                                k.tt(QCZ[64:128, h, 1, :], t[0][64:128, :], t[1][64:128, :], ALU.add)
                        qitems.append((w4[:, h], 128, sm("qnc", l), dest))
                    qk_pipeline(pt, qitems, rope_cols, hrhs)
                    for g in range(2):
                        w = wpl.get(8192 + g * 2048)
                        w4 = w[:, 0:2048].rearrange("p (c k m) -> p c k m", c=4, k=8)
                        for c4 in range(4):
                            h = g * 4 + c4
                            ps = nb()
                            proj(ps, (w, c4 * 512), 8, 64, hrhs)
                            silu_to(pt, ps, 64, OSA[0:64, h, :])
                    w = wpl.get(12288)
                    w4 = w[:, 0:4096].rearrange("p (c k m) -> p c k m", c=4, k=8)
                    for h in range(4):
                        ps = nb()
                        proj(ps, w4[:, h], 8, 128, hrhs)
                        silu_to(pt, ps, 128, OSC[:, h, :])

                    if b == 1:
                        build_halos()
                    wpl.kv_begin()
                    nqb = 2
                    units = []
                    for qb in range(nqb):
                        qc = slice(qb * 256, (qb + 1) * 256)
                        for hp in range(4):
                            kv = hp // 2
                            h0_ = hp * 2
                            tiles = []
                            if not lat:
                                for kt in range(2):
                                    c0 = qb * 256 + kt * 128
                                    tiles.append((KACTX[:, kv, c0:c0 + 128], (VACTX if kv == 0 else VACTXS)[:, qb * 2 + kt, :], None))
                            else:
                                t0 = (b - 1) * 4 + qb * 2
                                for wdx in range(4):
                                    mi = [0 if t0 == 0 else 1, 2, 3, 5 if t0 == 6 else 4][wdx]
                                    tiles.append((KAW[:, kv, (t0 + wdx) * 128:(t0 + wdx + 1) * 128],
                                                  (VAW if kv == 0 else VAWS)[:, t0 + wdx, :], MSK[:, mi, :]))
                                for c in range(4):
                                    tiles.append((CKA[:, kv, c * 128:(c + 1) * 128], (CVA if kv == 0 else CVAS)[:, c, :], None))

                            def qk_a(ps, kap, h0_=h0_, qc=qc):
                                k.mm(ps[:, :], kap, QA[:, h0_:h0_ + 2, qc])

                            def post_a(accO, accD, h0_=h0_, qc=qc):
                                for hh in range(2):
                                    cs = slice(hh * 256, (hh + 1) * 256)
                                    k.act(PT1[0:64, cs], accD[0:64, cs], AF.Ln,
                                          bias=D[0:64, D_ESINK + h0_ + hh:D_ESINK + h0_ + hh + 1])
                                k.act(PT2[0:64, :], PT1[0:64, :], AF.Exp, scale=-1.0)
                                k.tt(PT1[0:64, :], accO[0:64, :], PT2[0:64, :], ALU.mult)
                                k.tt(OSA[0:64, h0_:h0_ + 2, qc], PT1[0:64, :].rearrange("p (a b) -> p a b", a=2),
                                     OSA[0:64, h0_:h0_ + 2, qc], ALU.mult)
                            units.append((qk_a, tiles, 128, post_a))
                    attend_all(units)

                    for h in range(4):
                        units = []
                        if lat:
                            S.dma("sp", KCALL[:, 0:4096].rearrange("p (r c) -> p r c", r=4),
                                  g1a[:, KC_OFF + h * 1024:KC_OFF + (h + 1) * 1024].rearrange("(r p) c -> p r c", p=128))
                            S.dma("pool", KCALL[:, 4096:4608], ckc_in[l].ap()[:, h * 512:(h + 1) * 512], max_dma_last_dim=2048)
                            for r in range(4):
                                S.dma("sp", VCALL[:, r * 1024:(r + 1) * 1024].rearrange("p (t n) -> p t n", n=128),
                                      g1b[r * 128:(r + 1) * 128, VC_OFF:VC_OFF + 4096].rearrange("p (t n) -> p t n", n=512)[:, :, h * 128:(h + 1) * 128])
                            S.dma("pool", VCALL[:, 4096:4608].rearrange("p (t n) -> p t n", n=128),
                                  cvc_in[l].ap().rearrange("p (t n) -> p t n", n=512)[:, :, h * 128:(h + 1) * 128])
                        for qb in range(2):
                            qc = slice(qb * 256, (qb + 1) * 256)
                            tiles = []
                            if not lat:
                                for kt in range(2):
                                    c0 = qb * 256 + kt * 128
                                    tiles.append((KCCTX[:, h, c0:c0 + 128], VCCTX[:, qb * 2 + kt, h * 128:(h + 1) * 128], None))
                            else:
                                for j in range(36):
                                    tiles.append((KCALL[:, j * 128:(j + 1) * 128], VCALL[:, j * 128:(j + 1) * 128], None))

                            def qk_c(ps, kap, h=h, qc=qc):
                                k.mm(ps[:, :], kap, QCZ[:, h, :, qc])

                            def post_c(accO, accD, h=h, qc=qc):
                                k.act(PT1[:, :], accD[:, :], AF.Ln)
                                k.act(PT1[:, :], PT1[:, :], AF.Exp, scale=-1.0)
                                k.tt(PT2[:, :], accO[:, :], PT1[:, :], ALU.mult)
                                k.stt(PT3, PT2[:, 256:512], D[:, D_NLAM:D_NLAM + 1], PT2[:, 0:256], ALU.mult, ALU.add)
                                k.tt(PT4, PT3, PT3, ALU.mult)
                                ss = banks[7]
                                k.mm(ss[:, 0:256], ONES[:, :], PT4)
                                k.act(PT5, ss[:, 0:256], AF.Ln, scale=1.0 / 128.0, bias=EPST[:])
                                k.act(PT5, PT5, AF.Exp, scale=-0.5)
                                k.stt(PT3, PT3, D[:, D_SUBG:D_SUBG + 1], PT5, ALU.mult, ALU.mult)
                                k.tt(OSC[:, h, qc], PT3, OSC[:, h, qc], ALU.mult)
                            units.append((qk_c, tiles, 128, post_c))
                        attend_all(units)

                    wpl.kv_end()
                    for cc in range(8):
                        base = WP2_Q + cc * 4096
                        w = wpl.get(base)
                        wa = wpl.get(-1 - cc)
                        wm = w[:, 0:3072].rearrange("p (i k m) -> p i k m", i=3, k=8)
                        wb = w[:, 3072:3584].rearrange("p (k m) -> p k m", k=4)
                        wc = w[:, 3584:4096].rearrange("p (k m) -> p k m", k=4)
                        wa3 = wa[0:64, 0:1024].rearrange("p (k m) -> p k m", k=8)
                        pa, pb, pc = (banks[0], banks[1], banks[2]) if cc % 2 == 0 else (banks[5], banks[6], banks[7])
                        zs = [banks[3], banks[4], banks[3]]
                        proj(zs[0], wm[:, 0], 8, 128, hrhs)
                        proj(zs[1], wm[:, 1], 8, 128, hrhs)
                        k.act(MT[0][:], zs[0][:], AF.Exp, scale=-1.0)
                        proj(zs[2], wm[:, 2], 8, 128, hrhs)
                        k.act(MT[1][:], zs[1][:], AF.Exp, scale=-1.0)
                        k.act(MT[2][:], zs[2][:], AF.Exp, scale=-1.0)
                        proj(pa, wa3, 8, 128, lambda kk: OSA[0:64, kk, :], P=64)
                        proj(pb, wb, 4, 128, lambda kk: OSGB[:, kk, cols])
                        proj(pc, wc, 4, 128, lambda kk: OSC[:, kk, :])
                        for i, pbr in enumerate((pa, pb, pc)):
                            k.act(MT[i][:], MT[i][:], AF.Ln, bias=ONE_P[:])
                            k.act(MT[i][:], MT[i][:], AF.Exp, scale=-1.0)
                            k.tt(MT2[i][:], pbr[:], MT[i][:], ALU.mult)
                        k.tt(YACC[:], MT2[0][:], MT2[1][:], ALU.add)
                        k.tt(YT[:, cc, :], YACC[:], MT2[2][:], ALU.add)
                    for g in range(2):
                        base = WP2_Q + 8 * 4096 + g * 4096
                        w = wpl.get(base)
                        w4 = w[:, 0:4096].rearrange("p (c k m) -> p c k m", c=4, k=8)
                        for c4 in range(4):
                            cc = g * 4 + c4
                            ps = nb((6, 7))
                            proj(ps, w4[:, c4], 8, 128, lambda kk: YT[:, kk, :])
                            k.stt(XT[:, cc, cols], ps[:, :], MODV[l][:, 16 + cc, v:v + 1], XT[:, cc, cols], ALU.mult, ALU.add)
                S.barrier()

        layers()
        S.barrier()
        S.dma("sp", yT_out.ap().rearrange("(k p) t -> p k t", p=128), XT[:], is_output=True)
        S.dma("sp", o_st.ap(), STO[:], is_output=True)
        S.finish()
        S.replay()
    return nc


def _fm(w, M):
    K_, n = w.shape
    a = w.reshape(K_ // 128, 128, n // M, M).transpose(1, 2, 0, 3)
    return np.ascontiguousarray(a).reshape(128, -1)


_PROG = None
import os
STOP = os.environ.get('KSTOP', '')
KV = os.environ.get('KV', '')


def kernel(**inp):
    global _PROG
    f32 = lambda a: np.ascontiguousarray(np.asarray(a, dtype=np.float32))
    I = {k_: f32(v) for k_, v in inp.items()}
    w_in = I["w_in"]
    shared = {}
    for l in range(DEPTH):
        W = w_in[l]
        q_a, k_a, v_a, g_a = W[:, 0:512], W[:, 512:640], W[:, 640:768], W[:, 768:1280]
        x_b, g_b = W[:, 1280:1792], W[:, 1792:2304]
        q_c, k_c, v_c, g_c = W[:, 2304:2816], W[:, 2816:3328], W[:, 3328:3840], W[:, 3840:4352]
        merge = W[:, 4352:7424]
        vtm = lambda w: np.ascontiguousarray(w.reshape(8, 128, -1).transpose(1, 0, 2)).reshape(128, -1)
        shared[f"wp1{l}"] = np.concatenate([_fm(k_a, 64), _fm(k_c, 128), _fm(x_b, 128), _fm(g_b, 128),
                                            vtm(v_c), vtm(v_a)], axis=1)
        parts = [_fm(q_a, 64), _fm(q_c, 128), _fm(g_a, 64), _fm(g_c, 128)]
        for cc in range(8):
            for i in range(3):
                parts.append(_fm(merge[:, i * 1024 + cc * 128:i * 1024 + (cc + 1) * 128], 128))
            parts.append(_fm(I["w_br_b"][l][:, cc * 128:(cc + 1) * 128], 128))
            parts.append(_fm(I["w_br_c"][l][:, cc * 128:(cc + 1) * 128], 128))
        parts.append(_fm(I["w_out"][l], 128))
        shared[f"wp2{l}"] = np.concatenate(parts, axis=1)
        wa = I["w_br_a"][l]
        shared[f"wba{l}"] = np.ascontiguousarray(
            wa.reshape(8, 64, 8, 128).transpose(1, 2, 0, 3)).reshape(64, -1)
        shared[f"wmod{l}"] = _fm(I["mod_w"][l], 128)
    gl = np.zeros((128, DEPTH, 2, 2, 4, 128), np.float32)
    for l in range(DEPTH):
        for dr in range(2):
            for gi, nm in enumerate(("lru_wa", "lru_wx")):
                Wg = I[nm][l, dr]
                for ch in range(4):
                    for hb in range(2):
                        gl[hb * 64:(hb + 1) * 64, l, dr, gi, ch, hb * 64:(hb + 1) * 64] = Wg[ch * 2 + hb]
    shared["glru"] = gl.reshape(128, -1)
    R = np.zeros((128, 128), np.float32)
    for dp in range(128):
        if dp % 32 < 16:
            R[dp + 16, dp] = -1.0
        else:
            R[dp - 16, dp] = 1.0
    shared["rmat"] = R
    inv = np.power(np.float32(10000.0), -np.arange(16, dtype=np.float32) / np.float32(16)).astype(np.float32)
    a_ = np.arange(128)[:, None]
    b_ = np.arange(128)[None, :]
    ge = (a_ >= b_).astype(np.float32)
    le = (a_ <= b_).astype(np.float32)
    one = np.ones((128, 128), np.float32)
    zero = np.zeros((128, 128), np.float32)
    M0 = np.concatenate([ge, zero], 1)
    M1 = np.concatenate([one, ge], 1)
    M2 = np.concatenate([le, one], 1)
    M3 = np.concatenate([zero, le], 1)

    in_maps = []
    for c in range(8):
        s, j = c // 4, c % 4
        m = dict(shared)
        xs = np.concatenate([I["x_prompt"][2 * c], I["x_prompt"][2 * c + 1],
                             I["x_sample"][s, 1024 * j:1024 * (j + 1)]], axis=0)
        m["xT"] = np.ascontiguousarray(xs.T)
        sm_ = np.zeros((128, NSM), np.float32)

        def put(key, arr):
            o, w = SM[key]
            sm_[:, o:o + w] = arr
        col = lambda v, n: np.ascontiguousarray(v.reshape(n, 128).T)
        for l in range(DEPTH):
            put(("ng", l), col(I["norm_g"][l], 8))
            put(("modb", l), col(I["mod_b"][l], 24))
            put(("qna", l), np.tile(I["qn_a"][l], 2)[:, None])
            put(("kna", l), np.tile(I["kn_a"][l], 2)[:, None])
            put(("qnc", l), np.tile(I["qn_c"][l], 2)[:, None])
            put(("knc", l), np.tile(I["kn_c"][l], 2)[:, None])
            cw = I["conv_w"][l]
            put(("convw", l), np.ascontiguousarray(cw.reshape(4, 4, 128).transpose(2, 1, 0)).reshape(128, 16))
            put(("convb", l), col(I["conv_b"][l], 4))
            for nm, key in (("lru_ba", "ba"), ("lru_bx", "bx"), ("lru_lam", "lam")):
                put((key, l), np.ascontiguousarray(I[nm][l].reshape(2, 4, 128).transpose(2, 0, 1)).reshape(128, 8))
            put(("subln", l), I["subln_c"][l][:, None])
            put(("sink", l), np.broadcast_to(I["sink_a"][l][None, :], (128, 8)))
            for nm, key in (("lam_q1", "lq1"), ("lam_k1", "lk1"), ("lam_q2", "lq2"), ("lam_k2", "lk2")):
                put((key, l), np.broadcast_to(I[nm][l][None, :], (128, 64)))
            put(("h0", l), np.ascontiguousarray(I["state_lru"][s, l].reshape(2, 4, 128).transpose(2, 0, 1)).reshape(128, 8))
        cc_ = np.stack([col(I["c_ctx"], 8), col(I["c"][s], 8)], axis=2).reshape(128, 16)
        put("c", cc_)
        sel = np.zeros((128, 12), np.float32)
        if j > 0:
            sel[:, j - 1] = 1.0
        if j < 3:
            sel[:, 4 + j + 1] = 1.0
        sel[:, 8 + j] = 1.0
        put("sel", sel)
        m["small"] = sm_
        t = 1024 * j + np.arange(1024)
        row = (t // 64).astype(np.float32)
        colp = (t % 64).astype(np.float32)
        ang = np.zeros((64, 1024), np.float32)
        for d in range(64):
            ang[d] = (row if d < 32 else colp) * inv[d % 16]
        ang = np.concatenate([ang, ang], 0)
        m["cosT"] = np.cos(ang).astype(np.float32)
        m["sinT"] = np.sin(ang).astype(np.float32)
        first = M0 if j > 0 else np.zeros_like(M0)
        last = M3 if j < 3 else np.zeros_like(M3)
        msk = np.stack([first, M0, M1, M2, M3, last], 0)
        msk = np.concatenate([msk, msk], 2)
        m["masks"] = np.ascontiguousarray(msk.transpose(1, 0, 2)).reshape(128, -1)
        for l in range(DEPTH):
            ck = I["cache_c_k"][s, l]
            m[f"ckc{l}"] = np.ascontiguousarray(ck.transpose(2, 3, 1, 0)).reshape(128, 4 * 512)
            cv = I["cache_c_v"][s, l]
            m[f"cvc{l}"] = np.ascontiguousarray(cv.reshape(4, 128, 512).transpose(1, 0, 2)).reshape(128, -1)
            ka = I["cache_a_k"][s, l]
            m[f"cka{l}"] = np.ascontiguousarray(ka.transpose(2, 1, 0)).reshape(64, -1)
            va = I["cache_a_v"][s, l]
            m[f"cva{l}"] = np.ascontiguousarray(va.reshape(4, 128, 128).transpose(1, 0, 2)).reshape(128, -1)
        in_maps.append(m)

    if _PROG is None:
        _PROG = build_program()
    res = run_bass_kernel_spmd(_PROG, in_maps, core_ids=list(range(8)))
    R_ = res.results

    y_prompt = np.zeros((16, 256, 1024), np.float32)
    y_sample = np.zeros((2, 4096, 1024), np.float32)
    n_ka = np.zeros((16, 2, 256, 2, 64), np.float32)
    n_va = np.zeros((16, 2, 256, 2, 64), np.float32)
    n_kc = np.zeros((16, 2, 256, 4, 2, 64), np.float32)
    n_vc = np.zeros((16, 2, 256, 4, 128), np.float32)
    n_st = np.zeros((16, 2, 2, 512), np.float32)
    for c in range(8):
        s, j = c // 4, c % 4
        r = R_[c]
        y = np.asarray(r["yT"]).T
        y_prompt[2 * c] = y[0:256]
        y_prompt[2 * c + 1] = y[256:512]
        y_sample[s, 1024 * j:1024 * (j + 1)] = y[512:]
        ka = np.asarray(r["o_ka"]).reshape(2, 2, 64, 2, 256)
        kc = np.asarray(r["o_kc"]).reshape(2, 4, 2, 64, 2, 256)
        va = np.asarray(r["o_va"]).reshape(2, 2, 256, 2, 64)
        vc = np.asarray(r["o_vc"]).reshape(2, 2, 256, 4, 128)
        stt_ = np.asarray(r["o_st"]).reshape(128, 2, 2, 2, 4)
        for sq in range(2):
            bi = 2 * c + sq
            n_ka[bi] = ka[:, :, :, sq, :].transpose(0, 3, 1, 2)
            n_kc[bi] = kc[:, :, :, :, sq, :].transpose(0, 4, 1, 2, 3)
            n_va[bi] = va[:, sq]
            n_vc[bi] = vc[:, sq]
            n_st[bi] = stt_[:, :, sq].transpose(1, 2, 3, 0).reshape(2, 2, 512)
    return (y_prompt, y_sample, n_ka, n_va, n_kc, n_vc, n_st)
```

```python
import contextlib
import math
import numpy as np
import concourse.bass as bass
import concourse.mybir as mybir
from concourse.bass_utils import run_bass_kernel_spmd

F32 = mybir.dt.float32
BF16 = mybir.dt.bfloat16
AF = mybir.ActivationFunctionType
ALU = mybir.AluOpType
AX = mybir.AxisListType

ENGS = ("pe", "act", "dve", "pool", "sp")

DEPTH = 2
SCALE = 0.125
EPS = 1e-6
NTOK = 1536
KC_OFF, VC_OFF, KA_OFF, VA_OFF = 0, 0, 0, 2048
XBW = 1545
SEGS = [(0, 256, 0), (259, 256, 256), (518, 1024, 512)]

_SM_LAYER = [("ng", 8), ("modb", 24), ("qna", 1), ("kna", 1), ("qnc", 1), ("knc", 1), ("convw", 16),
             ("convb", 4), ("ba", 8), ("bx", 8), ("lam", 8), ("subln", 1), ("sink", 8),
             ("lq1", 64), ("lk1", 64), ("lq2", 64), ("lk2", 64), ("h0", 8)]
SM = {}
_o = 0
for _l in range(DEPTH):
    for _n, _w in _SM_LAYER:
        SM[(_n, _l)] = (_o, _w)
        _o += _w
SM["c"] = (_o, 16); _o += 16
SM["sel"] = (_o, 12); _o += 12
NSM = _o

WP1N = 1024 + 4096 * 3 + 5120
WP2_Q = 16384
WP2N = WP2_Q + 8 * 4096 + 8192


class Res:
    __slots__ = ("name", "last_w", "readers")

    def __init__(self, name=""):
        self.name = name
        self.last_w = None
        self.readers = {}


class Sched:
    NDMA = 8

    def __init__(self, nc, stack):
        self.nc = nc
        self.streams = {e: [] for e in ENGS}
        self.count = {e: 0 for e in ENGS}
        self.seen = {e: {} for e in ENGS}
        self.sems = {}
        self.resmap = {}
        self.split = {}
        for e in ENGS:
            self.sems[e] = stack.enter_context(nc.semaphore("prog_" + e))
        self.dma_k = {}
        for q in ("sp", "act", "pool"):
            self.dma_k[q] = 0
            for i in range(self.NDMA):
                self.sems[("dma", q, i)] = stack.enter_context(nc.semaphore(f"dma_{q}_{i}"))
        self.sems["cc"] = stack.enter_context(nc.semaphore("cc"))
        self.cc_count = 0
        self.out_events = []

    def _named(self, name):
        r = self.resmap.get(name)
        if r is None:
            r = Res(name)
            self.resmap[name] = r
        return r

    def _res(self, x):
        if isinstance(x, Res):
            return x
        return self._named(x.tensor.name)

    def _resl(self, x):
        if isinstance(x, Res):
            return [x]
        name = x.tensor.name
        b = self.split.get(name)
        if b is None:
            return [self._named(name)]
        try:
            ap = [list(e) for e in x.ap]
            col0 = int(x.offset) % int(ap[0][0])
            hi = col0 + 1
            for st_, cnt in ap[1:]:
                assert st_ >= 0
                hi += (int(cnt) - 1) * int(st_)
        except Exception:
            col0, hi = 0, 1 << 30
        out = []
        if col0 < b:
            out.append(self._named(name + ".A"))
        if hi > b:
            out.append(self._named(name + ".B"))
        return out

    def _collect(self, reads, writes):
        waits = {}

        def add(ev):
            if ev is None:
                return
            k, v = ev
            if waits.get(k, 0) < v:
                waits[k] = v
        for r in reads:
            add(r.last_w)
        for w in writes:
            add(w.last_w)
            for k, v in w.readers.items():
                add((k, v))
        return waits

    def _emit_waits(self, eng, waits):
        for k, v in waits.items():
            if k == eng:
                if eng == "pe":
                    continue
                if v <= self.count[eng] - 6:
                    continue
            if self.seen[eng].get(k, 0) >= v:
                continue
            self.seen[eng][k] = v
            self.streams[eng].append(("wait", k, v))

    def _commit(self, ev, reads, writes):
        k, v = ev
        for r in reads:
            if r.readers.get(k, 0) < v:
                r.readers[k] = v
        for w in writes:
            w.last_w = ev
            w.readers = {}

    def op(self, eng, fn, reads=(), writes=()):
        reads = [r for x in reads for r in self._resl(x)]
        writes = [r for x in writes for r in self._resl(x)]
        writes = writes + [r for r in reads if r.name.startswith("bank") and r not in writes]
        self._emit_waits(eng, self._collect(reads, writes))
        self.count[eng] += 1
        ev = (eng, self.count[eng])
        self.streams[eng].append(("op", fn, eng, 1))
        self._commit(ev, reads, writes)
        return ev

    def dma(self, q, out, in_, is_output=False, extra_reads=(), **kw):
        reads = self._resl(in_) + [r for x in extra_reads for r in self._resl(x)]
        writes = self._resl(out)
        waits = self._collect(reads, writes)
        k = self.dma_k[q]
        self.dma_k[q] += 1
        s = ("dma", q, k % self.NDMA)
        gen = k // self.NDMA
        if gen > 0 and waits.get(s, 0) < 16 * gen:
            waits[s] = 16 * gen
        self._emit_waits(q, waits)
        ev = (s, 16 * (gen + 1))
        self.streams[q].append(("op", lambda e: e.dma_start(out=out, in_=in_, **kw), s, 16))
        self._commit(ev, reads, writes)
        if is_output:
            self.out_events.append(ev)
        return ev

    def collective(self, groups, in_t, out_t):
        reads = [self._res(in_t.ap())]
        writes = [self._res(out_t.ap())]
        self._emit_waits("pool", self._collect(reads, writes))
        self.cc_count += 1
        ev = ("cc", self.cc_count)
        self.streams["pool"].append(("op", lambda e: e.collective_compute(
            "AllGather", ALU.bypass, replica_groups=groups, ins=[in_t.ap().opt()],
            outs=[out_t.ap().opt()]), "cc", 1))
        self._commit(ev, reads, writes)
        return ev

    def _all_events(self):
        waits = {}
        for e in ENGS:
            if self.count[e] > 0:
                waits[e] = self.count[e]
        for q in ("sp", "act", "pool"):
            k = self.dma_k[q]
            for i in range(self.NDMA):
                n = (k // self.NDMA) + (1 if i < k % self.NDMA else 0)
                if n > 0:
                    waits[("dma", q, i)] = 16 * n
        if self.cc_count:
            waits["cc"] = self.cc_count
        return waits

    def barrier(self, skip_cc=False):
        waits = self._all_events()
        if skip_cc:
            waits.pop("cc", None)
        for e in ENGS:
            for k, v in waits.items():
                if k == e:
                    continue
                if self.seen[e].get(k, 0) >= v:
                    continue
                self.seen[e][k] = v
                self.streams[e].append(("wait", k, v))

    def finish(self):
        self.barrier()

    def replay(self):
        nc = self.nc
        sems = self.sems
        streams = self.streams

        def run(engobj, name):
            for item in streams[name]:
                if item[0] == "wait":
                    engobj.wait_ge(sems[item[1]], item[2])
                else:
                    _, fn, semk, inc = item
                    fn(engobj).then_inc(sems[semk], inc)

        with nc.Block() as block:
            @block.tensor
            def _(e):
                run(e, "pe")

            @block.scalar
            def _(e):
                run(e, "act")

            @block.vector
            def _(e):
                run(e, "dve")

            @block.gpsimd
            def _(e):
                run(e, "pool")

            @block.sync
            def _(e):
                run(e, "sp")


class WPlan:
    def __init__(self, S, slotsA, slotsB, plan):
        self.S, self.A, self.B, self.plan = S, list(slotsA), list(slotsB), plan
        self.slot_of = {}
        self.live = {}
        self.cur = 0
        self.got = []
        self.markers_passed = 0

    def _pump(self):
        j = 0
        unpassed = 0
        seen_markers = 0
        for j in range(len(self.plan)):
            e = self.plan[j]
            if e[0] == 'KV':
                seen_markers += 1
                if seen_markers > self.markers_passed:
                    unpassed += 1
                    if unpassed > 1:
                        return
                continue
            if j in self.slot_of or j < self.cur:
                continue
            allowed = self.A + (self.B if unpassed == 0 else [])
            free = [t for t in allowed if t.name not in self.live]
            if not free:
                return
            t = free[0]
            _, src, P, n = e[0:4]
            self.S.dma("pool", t[0:P, 0:n], src, max_dma_last_dim=4096)
            self.slot_of[j] = t
            self.live[t.name] = j

    def _release(self, idx):
        t = self.slot_of.get(idx)
        if t is not None and self.live.get(t.name) == idx:
            del self.live[t.name]

    def get(self, src_off=None):
        while self.plan[self.cur][0] == 'KV':
            self.cur += 1
        i = self.cur
        if src_off is not None:
            assert self.plan[i][4] == src_off, (i, self.plan[i][4], src_off)
        while len(self.got) >= 2:
            self._release(self.got.pop(0))
        if i not in self.slot_of:
            self._pump()
        assert i in self.slot_of, ("no slot for load", i)
        self.cur += 1
        self.got.append(i)
        self._pump()
        return self.slot_of[i]

    def release_all(self):
        while self.got:
            self._release(self.got.pop(0))

    def kv_begin(self):
        self.release_all()
        self._pump()

    def kv_end(self):
        self.markers_passed += 1
        self._pump()


class K:
    def __init__(self, S):
        self.S = S

    @staticmethod
    def _aps(*xs):
        return [x for x in xs if x is not None and not isinstance(x, (int, float))]

    def act(self, out, in_, func, scale=1.0, bias=None):
        kw = {}
        if bias is not None:
            kw["bias"] = bias
        self.S.op("act", lambda e: e.activation(out=out, in_=in_, func=func, scale=scale, **kw),
                  reads=self._aps(in_, scale, bias), writes=[out])

    def ts(self, out, in0, s1, s2=None, op0=ALU.mult, op1=None, eng="dve"):
        kw = {}
        if op1 is not None:
            kw["op1"] = op1
        self.S.op(eng, lambda e: e.tensor_scalar(out=out, in0=in0, scalar1=s1, scalar2=s2, op0=op0, **kw),
                  reads=self._aps(in0, s1, s2), writes=[out])

    def tt(self, out, in0, in1, op, eng="dve"):
        self.S.op(eng, lambda e: e.tensor_tensor(out=out, in0=in0, in1=in1, op=op),
                  reads=[in0, in1], writes=[out])

    def stt(self, out, in0, scalar, in1, op0, op1, eng="dve"):
        self.S.op(eng, lambda e: e.scalar_tensor_tensor(out=out, in0=in0, scalar=scalar, in1=in1, op0=op0, op1=op1),
                  reads=self._aps(in0, scalar, in1), writes=[out])

    def recip(self, out, in_):
        self.S.op("dve", lambda e: e.reciprocal(out=out, in_=in_), reads=[in_], writes=[out])

    def copy(self, out, in_, eng="dve"):
        if in_.tensor.name.startswith("bank"):
            self.ts(out, in_, 1.0, None, op0=ALU.mult, eng=eng)
            return
        self.S.op(eng, lambda e: e.tensor_copy(out=out, in_=in_), reads=[in_], writes=[out])

    def memset(self, ap, val, eng="dve"):
        self.S.op(eng, lambda e: e.memset(ap, val), writes=[ap])

    def rsum(self, out, in_):
        self.S.op("dve", lambda e: e.reduce_sum(out=out, in_=in_, axis=AX.X), reads=[in_], writes=[out])

    def scan(self, out, a, b, init):
        self.S.op("dve", lambda e: e.tensor_tensor_scan(out=out, data0=a, data1=b, initial=init,
                                                        op0=ALU.mult, op1=ALU.add),
                  reads=self._aps(a, b, init), writes=[out])

    def mm(self, out, lhsT, rhs, start=True, stop=True):
        self.S.op("pe", lambda e: e.matmul(out, lhsT=lhsT, rhs=rhs, start=start, stop=stop),
                  reads=[lhsT, rhs], writes=[out])


def build_program():
    nc = bass.Bass("TRN2", target_bir_lowering=False)
    din = lambda n, s, dt=F32: nc.dram_tensor(n, s, dt, kind="ExternalInput")
    dout = lambda n, s: nc.dram_tensor(n, s, F32, kind="ExternalOutput")
    xT_in = din("xT", [1024, NTOK])
    small_in = din("small", [128, NSM])
    wmod = [din(f"wmod{l}", [128, 24576]) for l in range(DEPTH)]
    wp1 = [din(f"wp1{l}", [128, WP1N]) for l in range(DEPTH)]
    wp2 = [din(f"wp2{l}", [128, WP2N]) for l in range(DEPTH)]
    wba = [din(f"wba{l}", [64, 8192]) for l in range(DEPTH)]
    glru_in = din("glru", [128, 4096])
    rmat_in = din("rmat", [128, 128])
    cos_in = din("cosT", [128, 1024])
    sin_in = din("sinT", [128, 1024])
    msk_in = din("masks", [128, 3072])
    ckc_in = [din(f"ckc{l}", [128, 2048]) for l in range(DEPTH)]
    cvc_in = [din(f"cvc{l}", [128, 2048]) for l in range(DEPTH)]
    cka_in = [din(f"cka{l}", [64, 1024]) for l in range(DEPTH)]
    cva_in = [din(f"cva{l}", [128, 512]) for l in range(DEPTH)]

    yT_out = dout("yT", [1024, NTOK])
    o_ka = dout("o_ka", [DEPTH * 2 * 64, 512])
    o_kc = dout("o_kc", [DEPTH * 4 * 128, 512])
    o_va = dout("o_va", [DEPTH * 512, 128])
    o_vc = dout("o_vc", [DEPTH * 512, 512])
    o_st = dout("o_st", [128, 32])

    ga_in = nc.dram_tensor("ga_in", [128, 4096], BF16)
    ga_out = nc.dram_tensor("ga_out", [512, 4096], BF16)
    gb_in = nc.dram_tensor("gb_in", [128, 4096], BF16)
    gb_out = nc.dram_tensor("gb_out", [512, 4096], BF16)
    gc_in = nc.dram_tensor("gc_in", [128, 3072], BF16)
    gc_out = nc.dram_tensor("gc_out", [512, 3072], BF16)
    g2_in = nc.dram_tensor("g2_in", [128, 12], F32)
    g2_out = nc.dram_tensor("g2_out", [512, 12], F32)
    g3_in = nc.dram_tensor("g3_in", [128, 16], F32)
    g3_out = nc.dram_tensor("g3_out", [512, 16], F32)
    spill_a = nc.dram_tensor("spill_a", [128, 8 * 1024], F32)
    spill_b = nc.dram_tensor("spill_b", [128, 8 * 1024], F32)
    GROUPS = [[0, 1, 2, 3], [4, 5, 6, 7]]

    with contextlib.ExitStack() as st:
        S = Sched(nc, st)
        k = K(S)
        _cnt = [0]

        def T(stack, shape, dt=F32, name=None):
            _cnt[0] += 1
            return stack.enter_context(nc.sbuf_tensor(f"{name or 't'}_{_cnt[0]}", shape, dt))

        banks = [st.enter_context(nc.psum_tensor(f"bank{i}", [128, 512], F32)) for i in range(8)]

        XT = T(st, [128, 8, NTOK], F32, "XT")
        OSGB = T(st, [128, 4, NTOK], BF16, "OSGB")
        ONES = T(st, [128, 128], BF16, "ONES")
        BONES = T(st, [128, 128], BF16, "BONES")
        RM = T(st, [128, 128], BF16, "RM")
        COS = T(st, [128, 1024], F32, "COS")
        SIN = T(st, [128, 1024], F32, "SIN")
        MSK = T(st, [128, 6, 512], BF16, "MSK")
        SMALL = T(st, [128, NSM], F32, "SMALL")
        EPST = T(st, [128, 1], F32, "EPST")
        MODV = [T(st, [128, 24, 2], F32, "MODV") for _ in range(DEPTH)]
        GS = [T(st, [128, 8, 2], F32, "GS") for _ in range(DEPTH)]
        DER = [T(st, [128, 64], F32, "DER") for _ in range(DEPTH)]
        KAW = T(st, [128, 2, 1280], BF16, "KAW")
        VAW = T(st, [128, 10, 128], BF16, "VAW")
        VAWS = T(st, [128, 10, 128], BF16, "VAWS")
        KACTX = T(st, [128, 2, 512], BF16, "KACTX")
        KCCTX = T(st, [128, 4, 512], BF16, "KCCTX")
        VACTX = T(st, [128, 4, 128], BF16, "VACTX")
        VACTXS = T(st, [128, 4, 128], BF16, "VACTXS")
        VCCTX = T(st, [128, 4, 512], BF16, "VCCTX")
        STO = T(st, [128, 32], F32, "STO")
        WSL = [T(st, [128, 4096], BF16, "WSL") for _ in range(2)]
        ws_i = [0]

        def sm(name, l=None):
            o, w = SM[(name, l)] if l is not None else SM[name]
            return SMALL[:, o:o + w]

        def wload(src_ap, P, n):
            slot = WSL[ws_i[0] % len(WSL)]
            ws_i[0] += 1
            S.dma("pool", slot[0:P, 0:n], src_ap, max_dma_last_dim=4096)
            return slot

        bank_i = [0]

        def nb(pool=(0, 1, 2, 3)):
            b = banks[pool[bank_i[0] % len(pool)]]
            bank_i[0] += 1
            return b

        D_NBA, D_NBX, D_CL, D_C2, D_ESINK, D_NLAM, D_SUBG = 0, 8, 16, 24, 32, 40, 41

        S.dma("sp", XT[:], xT_in.ap().rearrange("(k p) t -> p k t", p=128))
        S.dma("sp", SMALL[:], small_in.ap())
        S.dma("sp", COS[:], cos_in.ap())
        S.dma("sp", SIN[:], sin_in.ap())
        S.dma("pool", RM[:], rmat_in.ap())
        S.dma("pool", MSK[:].rearrange("p a b -> p (a b)"), msk_in.ap(), max_dma_last_dim=4096)
        k.memset(ONES[:], 1.0)
        k.memset(BONES[:], 0.0)
        k.memset(BONES[0:64, 0:64], 1.0)
        k.memset(BONES[64:128, 64:128], 1.0)
        k.memset(EPST[:], EPS)
        ONE_P = T(st, [128, 1], F32, "ONE_P")
        k.memset(ONE_P[:], 1.0)
        k.memset(KAW[:], 0.0)
        k.memset(VAW[:], 0.0)
        k.memset(VAWS[:], 0.0)
        k.memset(KACTX[:], 0.0)
        k.memset(WSL[0][:, 0:2048], 0.0)
        S.dma("sp", gc_in.ap()[64:128, 0:2048], WSL[0][64:128, 0:2048])

        with contextlib.ExitStack() as ph:
            SCF = T(ph, [128, 16], F32, "SCF")
            SCF2 = T(ph, [128, 16], F32, "SCF2")
            SCB = T(ph, [128, 8, 2], BF16, "SCB")
            TMPS = T(ph, [128, 64], F32, "TMPS")
            TMPS2 = T(ph, [128, 64], F32, "TMPS2")
            cT = sm("c")
            k.act(SCF[:], cT, AF.Exp, scale=-1.0)
            k.ts(SCF[:], SCF[:], 1.0, None, op0=ALU.add)
            k.recip(SCF2[:], SCF[:])
            k.tt(SCB[:].rearrange("p a b -> p (a b)"), cT, SCF2[:], ALU.mult)
            for l in range(DEPTH):
                ps = nb()
                for g in range(6):
                    w = wload(wmod[l].ap()[:, g * 4096:(g + 1) * 4096], 128, 4096)
                    wv = w[:, :].rearrange("p (c k m) -> p c k m", c=4, k=8)
                    for c4 in range(4):
                        cc = g * 4 + c4
                        for kk in range(8):
                            k.mm(ps[:, cc * 2:cc * 2 + 2], wv[:, c4, kk, :], SCB[:, kk, :],
                                 start=(kk == 0), stop=(kk == 7))
                psv = ps[:, 0:48].rearrange("p (c v) -> p c v", v=2)
                for v in range(2):
                    k.tt(MODV[l][:, :, v], psv[:, :, v], sm("modb", l), ALU.add)
                    k.stt(GS[l][:, :, v], MODV[l][:, 8:16, v], 1.0, sm("ng", l), ALU.add, ALU.mult)
                D = DER[l]
                k.ts(D[:, D_NBA:D_NBA + 8], sm("ba", l), -1.0, None, op0=ALU.mult)
                k.ts(D[:, D_NBX:D_NBX + 8], sm("bx", l), -1.0, None, op0=ALU.mult)
                k.act(TMPS[:, 0:8], sm("lam", l), AF.Exp, scale=-1.0)
                k.ts(TMPS[:, 0:8], TMPS[:, 0:8], 1.0, None, op0=ALU.add)
                k.act(TMPS2[:, 0:8], TMPS[:, 0:8], AF.Ln)
                k.ts(D[:, D_CL:D_CL + 8], TMPS2[:, 0:8], -8.0, None, op0=ALU.mult)
                k.ts(D[:, D_C2:D_C2 + 8], TMPS2[:, 0:8], -16.0, None, op0=ALU.mult)
                k.act(D[:, D_ESINK:D_ESINK + 8], sm("sink", l), AF.Exp)
                lam_init = 0.8 - 0.6 * math.exp(-0.3 * l)
                k.tt(TMPS[:, 0:64], sm("lq1", l), sm("lk1", l), ALU.mult)
                k.rsum(TMPS2[:, 8:9], TMPS[:, 0:64])
                k.tt(TMPS[:, 0:64], sm("lq2", l), sm("lk2", l), ALU.mult)
                k.rsum(TMPS2[:, 9:10], TMPS[:, 0:64])
                k.act(TMPS2[:, 10:12], TMPS2[:, 8:10], AF.Exp)
                k.tt(TMPS2[:, 12:13], TMPS2[:, 11:12], TMPS2[:, 10:11], ALU.subtract)
                k.ts(D[:, D_NLAM:D_NLAM + 1], TMPS2[:, 12:13], -lam_init, None, op0=ALU.add)
                k.ts(D[:, D_SUBG:D_SUBG + 1], sm("subln", l), 1.0 - lam_init, None, op0=ALU.mult)
            S.barrier()

        def front(ph_t, l, b):
            HT, SQ, RSTD, FTMP = ph_t["HT"], ph_t["SQ"], ph_t["RSTD"], ph_t["FTMP"]
            v = 0 if b == 0 else 1
            cols = slice(b * 512, (b + 1) * 512)
            ps = banks[7]
            for kk in range(8):
                sq = SQ[kk % 2]
                if kk % 2 == 0:
                    k.act(sq[:], XT[:, kk, cols], AF.Square)
                else:
                    k.tt(sq[:], XT[:, kk, cols], XT[:, kk, cols], ALU.mult, eng=("pool" if kk % 4 == 1 else "dve"))
                k.mm(ps[:], ONES[:], sq[:], start=(kk == 0), stop=(kk == 7))
            k.act(RSTD[:], ps[:], AF.Ln, scale=1.0 / 1024.0, bias=EPST[:])
            k.act(RSTD[:], RSTD[:], AF.Exp, scale=-0.5)
            for kk in range(8):
                ft = FTMP[kk % 2]
                k.tt(ft[:], XT[:, kk, cols], RSTD[:], ALU.mult)
                k.act(HT[:, kk, :], ft[:], AF.Identity, scale=GS[l][:, kk, v:v + 1], bias=MODV[l][:, kk, v:v + 1])

        def proj(ps, w3, nk, M, rhs_fn, N=512, P=128):
            if isinstance(w3, tuple):
                flat, base = w3
                for kk in range(nk):
                    k.mm(ps[0:128, 0:N], flat[0:P, base + kk * 64:base + kk * 64 + 128], rhs_fn(kk),
                         start=(kk == 0), stop=(kk == nk - 1))
                return
            for kk in range(nk):
                k.mm(ps[0:M, 0:N], w3[0:P, kk, 0:M], rhs_fn(kk), start=(kk == 0), stop=(kk == nk - 1))

        qk_i = [0]

        def qknorm(ph_t, ps, P, gain, rope_cols, dest_fn):
            i = qk_i[0] % 2
            qk_i[0] += 1
            X32, SQB, LNV, XN = ph_t["QX"][i], ph_t["QS"][i], ph_t["QL"][i], ph_t["QN"][i]
            k.act(X32[0:P, :], ps[0:P, :], AF.Copy)
            k.act(SQB[0:P, :], ps[0:P, :], AF.Square)
            ss = nb((4, 5))
            k.mm(ss[0:P, :], BONES[0:P, 0:P], SQB[0:P, :])
            k.act(LNV[0:P, :], ss[0:P, :], AF.Ln, scale=1.0 / 64.0, bias=EPST[0:P, :])
            k.act(LNV[0:P, :], LNV[0:P, :], AF.Exp, scale=-0.5)
            k.stt(XN[0:P, :], X32[0:P, :], gain, LNV[0:P, :], ALU.mult, ALU.mult)
            if rope_cols is None:
                dest_fn(XN)
                return
            k.copy(SQB[0:P, :], XN[0:P, :], eng="pool")
            rx = nb((4, 5))
            k.mm(rx[0:P, :], RM[0:P, 0:P], SQB[0:P, :])
            k.tt(X32[0:P, :], XN[0:P, :], COS[0:P, rope_cols], ALU.mult, eng="pool")
            k.tt(LNV[0:P, :], rx[0:P, :], SIN[0:P, rope_cols], ALU.mult)
            dest_fn((X32, LNV))

        def qk_pipeline(ph_t, items, rope_cols, rhs_fn):
            n = len(items)
            st = [None] * n
            base = qk_i[0]
            qk_i[0] += n

            def tiles(i):
                j = (base + i) % 2
                return ph_t["QX"][j], ph_t["QS"][j], ph_t["QL"][j], ph_t["QN"][j]

            def stage_a(i):
                w3, P, gain, dest = items[i]
                ps = nb()
                proj(ps, w3, 8, P, rhs_fn)
                st[i] = ps

            def stage_b(i):
                w3, P, gain, dest = items[i]
                X32, SQB, LNV, XN = tiles(i)
                ps = st[i]
                k.act(SQB[0:P, :], ps[0:P, :], AF.Square)
                ss = nb((4, 5))
                k.mm(ss[0:P, :], BONES[0:P, 0:P], SQB[0:P, :])
                k.act(LNV[0:P, :], ss[0:P, :], AF.Ln, scale=1.0 / 64.0, bias=EPST[0:P, :])
                k.act(LNV[0:P, :], LNV[0:P, :], AF.Exp, scale=-0.5)
                k.stt(XN[0:P, :], ps[0:P, :], gain, LNV[0:P, :], ALU.mult, ALU.mult)
                if rope_cols is None:
                    dest(XN)

            def stage_c0(i):
                w3, P, gain, dest = items[i]
                X32, SQB, LNV, XN = tiles(i)
                k.act(SQB[0:P, :], XN[0:P, :], AF.Copy)

            def stage_c1(i):
                w3, P, gain, dest = items[i]
                X32, SQB, LNV, XN = tiles(i)
                rx = nb((6, 7))
                k.mm(rx[0:P, :], RM[0:P, 0:P], SQB[0:P, :])
                k.tt(X32[0:P, :], XN[0:P, :], COS[0:P, rope_cols], ALU.mult, eng="pool")
                k.tt(LNV[0:P, :], rx[0:P, :], SIN[0:P, rope_cols], ALU.mult)
                dest((X32, LNV))

            for step in range(n + 2):
                if rope_cols is not None and 0 <= step - 2 < n:
                    stage_c0(step - 2)
                if step < n:
                    stage_a(step)
                if 0 <= step - 1 < n:
                    stage_b(step - 1)
                if rope_cols is not None and 0 <= step - 2 < n:
                    stage_c1(step - 2)

        def silu_to(ph_t, ps, P, out_ap):
            i = qk_i[0] % 2
            qk_i[0] += 1
            E = ph_t["QX"][i]
            k.act(E[0:P, :], ps[0:P, :], AF.Exp, scale=-1.0)
            k.act(E[0:P, :], E[0:P, :], AF.Ln, bias=ONE_P[0:P, :])
            k.act(E[0:P, :], E[0:P, :], AF.Exp, scale=-1.0)
            k.tt(out_ap, ps[0:P, :], E[0:P, :], ALU.mult)

        def common_tiles(ph):
            sq = [T(ph, [128, 512], BF16, "SQ") for _ in range(2)]
            qn = [T(ph, [128, 512], F32, "QN") for _ in range(2)]
            return {
                "HT": T(ph, [128, 8, 512], BF16, "HT"),
                "SQ": sq,
                "RSTD": T(ph, [128, 512], F32, "RSTD"),
                "FTMP": qn,
                "QX": [T(ph, [128, 512], F32, "QX") for _ in range(2)],
                "QS": sq,
                "QL": [T(ph, [128, 512], F32, "QL") for _ in range(2)],
                "QN": qn,
            }

        class _Stop(Exception):
            pass

        def layers():
          for l in range(DEPTH):
            D = DER[l]
            if STOP == "setup":
                return
            with contextlib.ExitStack() as ph0:
              XB = T(ph0, [128, 4, XBW], F32, "XB")
              with contextlib.ExitStack() as ph:
                pt = common_tiles(ph)
                HT = pt["HT"]
                KST = [T(ph, [128, 512], BF16, "KST") for _ in range(2)]
                VTMP = [T(ph, [128, 512], F32, "VTMP") for _ in range(1)]
                VST = [T(ph, [128, 512], BF16, "VST") for _ in range(2)]
                VATMP = T(ph, [128, 128], F32, "VATMP")
                k.memset(XB[:], 0.0)
                kst_i = [0]
                W1 = {}
                for (o_, n_) in ((0, 1024), (1024, 4096), (5120, 4096), (9216, 4096), (13312, 4096), (17408, 1024)):
                    t_ = T(ph, [128, n_ + (64 if o_ == 0 else 0)], BF16, "W1")
                    if o_ == 0:
                        k.memset(t_[:, n_:n_ + 64], 0.0)
                    S.dma("pool", t_[:, 0:n_], wp1[l].ap()[:, o_:o_ + n_], max_dma_last_dim=4096)
                    W1[o_] = t_

                class _W1:
                    def get(self, off):
                        return W1[off]
                wpl = _W1()
                for b in range(3):
                    lat = b > 0
                    front(pt, l, b)
                    rope_cols = slice((b - 1) * 512, b * 512) if lat else None
                    hrhs = lambda kk: HT[:, kk, :]
                    if STOP == "front":
                        return
                    w = wpl.get(0)
                    w4 = w[:, 0:1024].rearrange("p (c k m) -> p c k m", c=2, k=8)
                    qitems = []
                    for kv in range(2):
                        if not lat:
                            def dest(XN, kv=kv):
                                S.dma("sp", o_ka.ap()[(l * 2 + kv) * 64:(l * 2 + kv + 1) * 64, :], XN[0:64, :],
                                      is_output=True)
                                k.copy(KACTX[0:64, kv, :], XN[0:64, :])
                        else:
                            def dest(t, kv=kv, b=b):
                                dst = KAW[0:64, kv, 128 + (b - 1) * 512:128 + b * 512]
                                k.tt(dst, t[0][0:64, :], t[1][0:64, :], ALU.add)
                                S.dma("sp", gc_in.ap()[0:64, KA_OFF + kv * 1024 + (b - 1) * 512:KA_OFF + kv * 1024 + b * 512], dst)
                        qitems.append(((w, kv * 512), 64, sm("kna", l)[0:64, :], dest))
                    qk_pipeline(pt, qitems, rope_cols, hrhs)
                    if STOP == "ka":
                        return
                    w = wpl.get(1024)
                    w4 = w[:, 0:4096].rearrange("p (c k m) -> p c k m", c=4, k=8)
                    qitems = []
                    for h in range(4):
                        if not lat:
                            def dest(XN, h=h):
                                S.dma("sp", o_kc.ap()[(l * 4 + h) * 128:(l * 4 + h + 1) * 128, :], XN[:, :], is_output=True)
                                k.copy(KCCTX[:, h, :], XN[:, :])
                        else:
                            def dest(t, h=h, b=b):
                                ks = KST[kst_i[0] % 2]
                                kst_i[0] += 1
                                k.tt(ks[:], t[0][:, :], t[1][:, :], ALU.add)
                                S.dma("sp", ga_in.ap()[:, KC_OFF + h * 1024 + (b - 1) * 512:KC_OFF + h * 1024 + b * 512], ks[:])
                        qitems.append((w4[:, h], 128, sm("knc", l), dest))
                    qk_pipeline(pt, qitems, rope_cols, hrhs)
                    if STOP == "kc":
                        return
                    w = wpl.get(5120)
                    w4 = w[:, 0:4096].rearrange("p (c k m) -> p c k m", c=4, k=8)
                    for ch in range(4):
                        ps = nb()
                        proj(ps, w4[:, ch], 8, 128, hrhs)
                        if b == 0:
                            k.act(XB[:, ch, 2:258], ps[:, 0:256], AF.Copy)
                            k.act(XB[:, ch, 261:517], ps[:, 256:512], AF.Copy)
                        else:
                            c0 = 520 + (b - 1) * 512
                            k.act(XB[:, ch, c0:c0 + 512], ps[:, :], AF.Copy)
                    if STOP == "xb":
                        return
                    w = wpl.get(9216)
                    w4 = w[:, 0:4096].rearrange("p (c k m) -> p c k m", c=4, k=8)
                    for ch in range(4):
                        ps = nb()
                        proj(ps, w4[:, ch], 8, 128, hrhs)
                        silu_to(pt, ps, 128, OSGB[:, ch, b * 512:(b + 1) * 512])
                    if STOP == "gb":
                        return
                    w = wpl.get(13312)
                    w2 = wpl.get(17408)
                    wv = w[:, 0:4096].rearrange("p (k n) -> p k n", k=8)
                    wva = w2[:, 0:1024].rearrange("p (k n) -> p k n", k=8)
                    for tt_ in range(4):
                        tok = slice(tt_ * 128, (tt_ + 1) * 128)
                        psc = nb((4, 5))
                        psa = banks[6]
                        for kk in range(8):
                            k.mm(psc[:, :], HT[:, kk, tok], wv[:, kk, :], start=(kk == 0), stop=(kk == 7))
                        if 'a' in KV:
                            continue
                        for kk in range(8):
                            k.mm(psa[:, 0:128], HT[:, kk, tok], wva[:, kk, :], start=(kk == 0), stop=(kk == 7))
                        if 'b' in KV:
                            continue
                        if not lat:
                            vt = VTMP[0]
                            if 'e' not in KV:
                                k.act(vt[:], psc[:, :], AF.Copy)
                            if 'c' not in KV:
                                S.dma("sp", o_vc.ap()[l * 512 + tt_ * 128:l * 512 + (tt_ + 1) * 128, :], vt[:], is_output=True)
                            if 'f' not in KV:
                                k.copy(VCCTX[:, tt_, :], psc[:, :])
                            if 'd' in KV:
                                continue
                            k.act(VATMP[:], psa[:, 0:128], AF.Copy)
                            if 'c' not in KV:
                                S.dma("sp", o_va.ap()[l * 512 + tt_ * 128:l * 512 + (tt_ + 1) * 128, :], VATMP[:], is_output=True)
                            k.copy(VACTX[:, tt_, :], psa[:, 0:128])
                            k.copy(VACTXS[:, tt_, 0:64], psa[:, 64:128])
                            k.copy(VACTXS[:, tt_, 64:128], psa[:, 0:64])
                        else:
                            Tt = (b - 1) * 4 + tt_
                            vs = VST[tt_ % 2]
                            k.act(vs[:], psc[:, :], AF.Copy)
                            S.dma("sp", gb_in.ap()[:, VC_OFF + Tt * 512:VC_OFF + (Tt + 1) * 512], vs[:])
                            k.copy(VAW[:, 1 + Tt, :], psa[:, 0:128])
                            k.copy(VAWS[:, 1 + Tt, 0:64], psa[:, 64:128])
                            k.copy(VAWS[:, 1 + Tt, 64:128], psa[:, 0:64])
                            S.dma("sp", gc_in.ap()[:, VA_OFF + Tt * 128:VA_OFF + (Tt + 1) * 128], VAW[:, 1 + Tt, :])
                    if STOP == "v":
                        return
                if STOP == "p1":
                    return
                G2S = T(ph, [128, 12], F32, "G2S")
                for ch in range(4):
                    k.copy(G2S[:, ch * 3:ch * 3 + 1], XB[:, ch, 520:521])
                    k.copy(G2S[:, ch * 3 + 1:ch * 3 + 3], XB[:, ch, 520 + 1022:520 + 1024])
                S.dma("sp", g2_in.ap(), G2S[:])
                S.collective(GROUPS, g2_in, g2_out)
                S.collective(GROUPS, gc_in, gc_out)
                S.collective(GROUPS, ga_in, ga_out)
                S.collective(GROUPS, gb_in, gb_out)
                S.barrier(skip_cc=True)
              if STOP == "cc":
                  return
              with contextlib.ExitStack() as ph:
                GL = T(ph, [128, 16, 128], BF16, "GL")
                S.dma("pool", GL[:].rearrange("p a b -> p (a b)"), glru_in.ap()[:, l * 2048:(l + 1) * 2048], max_dma_last_dim=4096)
                Us = [T(ph, [128, 1024], F32, "U") for _ in range(2)]
                UBs = [T(ph, [128, 1024], BF16, "UB") for _ in range(2)]
                AA = [T(ph, [128, 1024], F32, "AA") for _ in range(2)]
                BB = [T(ph, [128, 1024], F32, "BB") for _ in range(2)]
                HH = [T(ph, [128, 1024], F32, "HH") for _ in range(2)]
                LT = [[T(ph, [128, 512], F32, "LT") for _ in range(4)] for _ in range(2)]
                ONE_T = T(ph, [128, 1], F32, "ONE_T")
                k.memset(ONE_T[:], 1.0)
                u_i = [0]
                HAL = T(ph, [128, 4, 12], F32, "HAL")
                TOT = T(ph, [128, 16], F32, "TOT")
                RS = T(ph, [128, 8], F32, "RS")
                TOTG = T(ph, [128, 4, 16], F32, "TOTG")
                CH = T(ph, [128, 40], F32, "CH")
                HIN = T(ph, [128, 8], F32, "HIN")
                sel = sm("sel")

                def lat_halo():
                    S.dma("sp", HAL[:], g2_out.ap().rearrange("(r p) c -> p r c", p=128))
                    for ch in range(4):
                        pre = XB[:, ch, 518:520]
                        post = XB[:, ch, 1544:1545]
                        for r in range(4):
                            k.stt(pre, HAL[:, r, ch * 3 + 1:ch * 3 + 3], sel[:, r:r + 1], pre, ALU.mult, ALU.add)
                            k.stt(post, HAL[:, r, ch * 3:ch * 3 + 1], sel[:, 4 + r:5 + r], post, ALU.mult, ALU.add)

                def lru_gates(ch, s0, Tn, U, UB, want_rsum):
                    npc = (Tn + 511) // 512
                    for pc in range(npc):
                        n = min(512, Tn)
                        cs = slice(pc * 512, pc * 512 + n)
                        seqs = []
                        for dr in range(2):
                            nba = D[:, D_NBA + dr * 4 + ch:D_NBA + dr * 4 + ch + 1]
                            nbx = D[:, D_NBX + dr * 4 + ch:D_NBX + dr * 4 + ch + 1]
                            cl = D[:, D_CL + dr * 4 + ch:D_CL + dr * 4 + ch + 1]
                            c2 = D[:, D_C2 + dr * 4 + ch:D_C2 + dr * 4 + ch + 1]
                            L0, L1, L2, L3 = [t[:, 0:n] for t in LT[dr]]
                            zr = banks[(pc % 2) * 4 + dr * 2]
                            zi = banks[(pc % 2) * 4 + dr * 2 + 1]
                            ops = [
                                lambda zr=zr, dr=dr: k.mm(zr[:, 0:n], GL[:, (dr * 2 + 0) * 4 + ch, :], UB[:, cs]),
                                lambda zi=zi, dr=dr: k.mm(zi[:, 0:n], GL[:, (dr * 2 + 1) * 4 + ch, :], UB[:, cs]),
                                lambda L0=L0, zr=zr, nba=nba: k.act(L0, zr[:, 0:n], AF.Exp, scale=-1.0, bias=nba),
                                lambda L1=L1, zi=zi, nbx=nbx: k.act(L1, zi[:, 0:n], AF.Exp, scale=-1.0, bias=nbx),
                                lambda L0=L0: k.act(L0, L0, AF.Ln, bias=ONE_T[:]),
                                lambda L1=L1: k.act(L1, L1, AF.Ln, bias=ONE_T[:]),
                                lambda L0=L0: k.act(L0, L0, AF.Exp, scale=-1.0),
                                lambda L0=L0, dr=dr, cl=cl: k.act(AA[dr][:, cs], L0, AF.Exp, scale=cl),
                                lambda L2=L2, L0=L0, c2=c2: k.ts(L2, L0, c2, None, op0=ALU.mult),
                                lambda L3=L3, L2=L2: k.ts(L3, L2, 1.0 / 24.0, 1.0 / 6.0, op0=ALU.mult, op1=ALU.add),
                                lambda L3=L3, L2=L2: k.tt(L3, L3, L2, ALU.mult),
                                lambda L3=L3, L2=L2: k.stt(L3, L3, 0.5, L2, ALU.add, ALU.mult),
                                lambda L3=L3, L2=L2: k.stt(L3, L3, 1.0, L2, ALU.add, ALU.mult),
                                lambda L3=L3: k.act(L3, L3, AF.Ln, scale=-1.0),
                                lambda L3=L3, L1=L1: k.stt(L3, L3, 0.5, L1, ALU.mult, ALU.subtract),
                                lambda L3=L3: k.act(L3, L3, AF.Exp),
                                lambda L3=L3, dr=dr: k.tt(BB[dr][:, cs], L3, U[:, cs], ALU.mult),
                            ]
                            if want_rsum:
                                ops.insert(8, lambda L0=L0, dr=dr, pc=pc: k.rsum(RS[:, dr * 2 + pc:dr * 2 + pc + 1], L0))
                            seqs.append(ops)
                        for o0, o1 in zip(*seqs):
                            o0()
                            o1()

                def conv_u(ch, s0, Tn):
                    U = Us[u_i[0] % 2]
                    UB = UBs[u_i[0] % 2]
                    u_i[0] += 1
                    cw = sm("convw", l)
                    k.ts(U[:, 0:Tn], XB[:, ch, s0:s0 + Tn], cw[:, ch * 4:ch * 4 + 1], sm("convb", l)[:, ch:ch + 1],
                         op0=ALU.mult, op1=ALU.add)
                    for j in range(1, 4):
                        k.stt(U[:, 0:Tn], XB[:, ch, s0 + j:s0 + j + Tn], cw[:, ch * 4 + j:ch * 4 + j + 1], U[:, 0:Tn],
                              ALU.mult, ALU.add)
                    k.copy(UB[:, 0:Tn], U[:, 0:Tn])
                    return U, UB

                def do_scan(dr, Tn, init):
                    if dr == 0:
                        k.scan(HH[0][:, 0:Tn], AA[0][:, 0:Tn], BB[0][:, 0:Tn], init)
                    else:
                        k.scan(HH[1][:, 0:Tn][:, ::-1], AA[1][:, 0:Tn][:, ::-1], BB[1][:, 0:Tn][:, ::-1], init)

                def lru_seg(ch, si):
                        s0, Tn, tk0 = SEGS[si]
                        U, UB = conv_u(ch, s0, Tn)
                        lru_gates(ch, s0, Tn, U, UB, si == 2)
                        for dr in range(2):
                            do_scan(dr, Tn, 0.0)
                            endc = Tn - 1 if dr == 0 else 0
                            if si < 2:
                                col = ((l * 2 + si) * 2 + dr) * 4 + ch
                                k.copy(STO[:, col:col + 1], HH[dr][:, endc:endc + 1])
                            else:
                                k.copy(TOT[:, dr * 8 + 4 + ch:dr * 8 + 5 + ch], HH[dr][:, endc:endc + 1])
                                k.tt(RS[:, 4 + dr:5 + dr], RS[:, dr * 2:dr * 2 + 1], RS[:, dr * 2 + 1:dr * 2 + 2], ALU.add)
                                k.act(TOT[:, dr * 8 + ch:dr * 8 + ch + 1], RS[:, 4 + dr:5 + dr], AF.Exp,
                                      scale=D[:, D_CL + dr * 4 + ch:D_CL + dr * 4 + ch + 1])
                                sl = slice((ch * 2 + dr) * 1024, (ch * 2 + dr + 1) * 1024)
                                S.dma("sp", spill_a.ap()[:, sl], AA[dr][:, :])
                                S.dma("sp", spill_b.ap()[:, sl], BB[dr][:, :])
                        if si < 2:
                            k.tt(HH[0][:, 0:Tn], HH[0][:, 0:Tn], HH[1][:, 0:Tn], ALU.add, eng="pool")
                            k.tt(OSGB[:, ch, tk0:tk0 + Tn], HH[0][:, 0:Tn], OSGB[:, ch, tk0:tk0 + Tn], ALU.mult, eng="pool")
                for ch in range(2):
                    lru_seg(ch, 0)
                    lru_seg(ch, 1)
                lat_halo()
                for ch in range(4):
                    lru_seg(ch, 2)
                S.dma("sp", g3_in.ap(), TOT[:])
                S.collective(GROUPS, g3_in, g3_out)
                for ch in range(2, 4):
                    lru_seg(ch, 0)
                    lru_seg(ch, 1)
                S.dma("sp", TOTG[:], g3_out.ap().rearrange("(r p) c -> p r c", p=128))
                h0 = sm("h0", l)
                k.copy(CH[:, 0:4], h0[:, 0:4])
                for r in range(3):
                    k.tt(CH[:, (r + 1) * 4:(r + 2) * 4], TOTG[:, r, 0:4], CH[:, r * 4:(r + 1) * 4], ALU.mult)
                    k.tt(CH[:, (r + 1) * 4:(r + 2) * 4], CH[:, (r + 1) * 4:(r + 2) * 4], TOTG[:, r, 4:8], ALU.add)
                k.copy(CH[:, 16 + 12:16 + 16], h0[:, 4:8])
                for r in (3, 2, 1):
                    k.tt(CH[:, 16 + (r - 1) * 4:16 + r * 4], TOTG[:, r, 8:12], CH[:, 16 + r * 4:16 + (r + 1) * 4], ALU.mult)
                    k.tt(CH[:, 16 + (r - 1) * 4:16 + r * 4], CH[:, 16 + (r - 1) * 4:16 + r * 4], TOTG[:, r, 12:16], ALU.add)
                k.memset(HIN[:], 0.0)
                for r in range(4):
                    k.stt(HIN[:, 0:4], CH[:, r * 4:(r + 1) * 4], sel[:, 8 + r:9 + r], HIN[:, 0:4], ALU.mult, ALU.add)
                    k.stt(HIN[:, 4:8], CH[:, 16 + r * 4:16 + (r + 1) * 4], sel[:, 8 + r:9 + r], HIN[:, 4:8], ALU.mult, ALU.add)
                s0, Tn, tk0 = SEGS[2]
                for ch in range(4):
                    for dr in range(2):
                        sl = slice((ch * 2 + dr) * 1024, (ch * 2 + dr + 1) * 1024)
                        S.dma("sp", AA[dr][:, :], spill_a.ap()[:, sl])
                        S.dma("sp", BB[dr][:, :], spill_b.ap()[:, sl])
                        do_scan(dr, Tn, HIN[:, dr * 4 + ch:dr * 4 + ch + 1])
                    k.tt(HH[0][:, :], HH[0][:, :], HH[1][:, :], ALU.add, eng="pool")
                    k.tt(OSGB[:, ch, tk0:tk0 + Tn], HH[0][:, :], OSGB[:, ch, tk0:tk0 + Tn], ALU.mult, eng="pool")
                S.barrier()

            if STOP == "lru":
                return
            with contextlib.ExitStack() as ph:
                pt = common_tiles(ph)
                HT = pt["HT"]
                QA = T(ph, [128, 8, 512], BF16, "QA")
                OSA = T(ph, [64, 8, 512], BF16, "OSA")
                QCZ = T(ph, [128, 4, 2, 512], BF16, "QCZ")
                OSC = T(ph, [128, 4, 512], BF16, "OSC")
                YT = T(ph, [128, 8, 512], BF16, "YT")
                KCALL = T(ph, [128, 36 * 128], BF16, "KCALL")
                VCALL = T(ph, [128, 36 * 128], BF16, "VCALL")
                plan = []
                for b_ in range(3):
                    for o_ in (0, 2048, 4096, 8192, 10240, 12288):
                        n_ = 2048 if o_ in (0, 2048, 8192, 10240) else 4096
                        plan.append(('L', wp2[l].ap()[:, o_:o_ + n_], 128, n_, o_))
                    plan.append(('KV',))
                    for cc_ in range(8):
                        o_ = WP2_Q + cc_ * 4096
                        plan.append(('L', wp2[l].ap()[:, o_:o_ + 4096], 128, 4096, o_))
                        plan.append(('L', wba[l].ap()[:, cc_ * 1024:(cc_ + 1) * 1024], 64, 1024, -1 - cc_))
                    for g_ in range(2):
                        o_ = WP2_Q + 8 * 4096 + g_ * 4096
                        plan.append(('L', wp2[l].ap()[:, o_:o_ + 4096], 128, 4096, o_))
                S.split[KCALL.name] = 2048
                S.split[VCALL.name] = 2048
                for t_ in WSL + [KCALL, VCALL]:
                    k.memset(t_[:, 2048:2112], 0.0)
                wpl = WPlan(S, WSL, [KCALL, VCALL], plan)
                CKA = T(ph, [128, 2, 512], BF16, "CKA")
                CVA = T(ph, [128, 4, 128], BF16, "CVA")
                CVAS = T(ph, [128, 4, 128], BF16, "CVAS")
                ET = [T(ph, [128, 512], BF16, "ET") for _ in range(3)]
                PT1, PT2 = pt["QX"][0], pt["QX"][1]
                PT3 = pt["RSTD"][:, 0:256]
                PT5 = pt["RSTD"][:, 256:512]
                PT4 = pt["SQ"][0][:, 0:256]
                MT = [pt["QX"][0], pt["QX"][1], pt["QL"][0]]
                MT2 = [pt["QL"][1], pt["QN"][0], pt["QN"][1]]
                YACC = pt["RSTD"]
                k.memset(QCZ[:], 0.0)
                k.memset(CKA[:], 0.0)
                k.memset(QA[64:128, :, :], 0.0)
                S.dma("pool", CKA[0:64, :, :].rearrange("p a b -> p (a b)"), cka_in[l].ap(), max_dma_last_dim=4096)
                S.dma("pool", CVA[:].rearrange("p a b -> p (a b)"), cva_in[l].ap(), max_dma_last_dim=4096)
                cva3 = cva_in[l].ap().rearrange("p (t n) -> p t n", n=128)
                S.dma("pool", CVAS[:, :, 0:64], cva3[:, :, 64:128])
                S.dma("pool", CVAS[:, :, 64:128], cva3[:, :, 0:64])
                g1a, g1b, g1c = ga_out.ap(), gb_out.ap(), gc_out.ap()
                HALK = T(ph, [64, 2, 4, 2, 128], BF16, "HALK")
                HALV = T(ph, [128, 2, 4, 128], BF16, "HALV")
                sel = sm("sel")

                def build_halos():
                    for kv in range(2):
                        base = KA_OFF + kv * 1024
                        S.dma("sp", HALK[:, 0, :, kv, :], g1c[:, base + 896:base + 1024].rearrange("(r p) c -> p r c", p=128)[0:64])
                        S.dma("sp", HALK[:, 1, :, kv, :], g1c[:, base:base + 128].rearrange("(r p) c -> p r c", p=128)[0:64])
                    S.dma("sp", HALV[:, 0, :, :], g1c[:, VA_OFF + 896:VA_OFF + 1024].rearrange("(r p) c -> p r c", p=128))
                    S.dma("sp", HALV[:, 1, :, :], g1c[:, VA_OFF:VA_OFF + 128].rearrange("(r p) c -> p r c", p=128))
                    k.memset(KAW[:, :, 0:128], 0.0)
                    k.memset(KAW[:, :, 1152:1280], 0.0)
                    k.memset(VAW[:, 0, :], 0.0)
                    k.memset(VAW[:, 9, :], 0.0)
                    k.memset(VAWS[:, 0, :], 0.0)
                    k.memset(VAWS[:, 9, :], 0.0)
                    for r in range(4):
                        for kv in range(2):
                            k.stt(KAW[0:64, kv, 0:128], HALK[:, 0, r, kv, :], sel[0:64, r:r + 1], KAW[0:64, kv, 0:128], ALU.mult, ALU.add)
                            k.stt(KAW[0:64, kv, 1152:1280], HALK[:, 1, r, kv, :], sel[0:64, 4 + r:5 + r], KAW[0:64, kv, 1152:1280], ALU.mult, ALU.add)
                        k.stt(VAW[:, 0, :], HALV[:, 0, r, :], sel[:, r:r + 1], VAW[:, 0, :], ALU.mult, ALU.add)
                        k.stt(VAW[:, 9, :], HALV[:, 1, r, :], sel[:, 4 + r:5 + r], VAW[:, 9, :], ALU.mult, ALU.add)
                        for (d0, s0_) in ((0, 64), (64, 0)):
                            k.stt(VAWS[:, 0, d0:d0 + 64], HALV[:, 0, r, s0_:s0_ + 64], sel[:, r:r + 1], VAWS[:, 0, d0:d0 + 64], ALU.mult, ALU.add)
                            k.stt(VAWS[:, 9, d0:d0 + 64], HALV[:, 1, r, s0_:s0_ + 64], sel[:, 4 + r:5 + r], VAWS[:, 9, d0:d0 + 64], ALU.mult, ALU.add)


                et_i = [0]
                acc_i = [0]

                def attend_all(units, order=None):
                    if order is None:
                        order = [(ui, j) for ui, u in enumerate(units) for j in range(len(u[1]))]
                    first = {}
                    last = {}
                    for si_, (ui_, j_) in enumerate(order):
                        first.setdefault(ui_, si_)
                        last[ui_] = si_
                    steps = [(ui_, j_, len(units[ui_][1])) for (ui_, j_) in order]
                    LOOK = 2
                    pend = []
                    deferred = []

                    def issue(si):
                        ui, j, n = steps[si]
                        ps = nb((0, 1, 2))
                        units[ui][0](ps, units[ui][1][j][0])
                        pend.append(ps)
                    for si in range(min(LOOK, len(steps))):
                        issue(si)
                    for si, (ui, j, n) in enumerate(steps):
                        qk_fn, tiles, Mv, post_fn = units[ui]
                        _, vap, mask = tiles[j]
                        ps = pend.pop(0)
                        et = ET[et_i[0] % 3]
                        et_i[0] += 1
                        k.act(et[:], ps[:], AF.Exp, scale=SCALE)
                        if mask is not None:
                            k.tt(et[:], et[:], mask, ALU.mult, eng=("pool" if si % 2 else "dve"))
                        while deferred and deferred[0][0] <= si:
                            deferred.pop(0)[1]()
                        if si + LOOK < len(steps):
                            issue(si + LOOK)
                        a_ = (acc_i[0] + ui) % 2
                        accO = banks[3 + 2 * a_]
                        accD = banks[4 + 2 * a_]
                        k.mm(accO[0:Mv, :], vap, et[:], start=(si == first[ui]), stop=(si == last[ui]))
                        k.mm(accD[0:Mv, :], ONES[:, 0:Mv], et[:], start=(si == first[ui]), stop=(si == last[ui]))
                        if si == last[ui]:
                            deferred.append((si + 3, lambda post_fn=post_fn, accO=accO, accD=accD: post_fn(accO, accD)))
                    for _, fn in deferred:
                        fn()
                    acc_i[0] += len(units)

                for b in range(3):
                    lat = b > 0
                    v = 1 if lat else 0
                    cols = slice(b * 512, (b + 1) * 512)
                    front(pt, l, b)
                    rope_cols = slice((b - 1) * 512, b * 512) if lat else None
                    hrhs = lambda kk: HT[:, kk, :]
                    for g in range(2):
                        w = wpl.get(g * 2048)
                        w4 = w[:, 0:2048].rearrange("p (c k m) -> p c k m", c=4, k=8)
                        qitems = []
                        for c4 in range(4):
                            h = g * 4 + c4
                            if not lat:
                                def dest(XN, h=h):
                                    k.copy(QA[0:64, h, :], XN[0:64, :])
                            else:
                                def dest(t, h=h):
                                    k.tt(QA[0:64, h, :], t[0][0:64, :], t[1][0:64, :], ALU.add)
                            qitems.append(((w, c4 * 512), 64, sm("qna", l)[0:64, :], dest))
                        qk_pipeline(pt, qitems, rope_cols, hrhs)
                    w = wpl.get(4096)
                    w4 = w[:, 0:4096].rearrange("p (c k m) -> p c k m", c=4, k=8)
                    qitems = []
                    for h in range(4):
                        if not lat:
                            def dest(XN, h=h):
                                k.copy(QCZ[0:64, h, 0, :], XN[0:64, :])
                                k.copy(QCZ[64:128, h, 1, :], XN[64:128, :])
                        else:
                            def dest(t, h=h):
                                k.tt(QCZ[0:64, h, 0, :], t[0][0:64, :], t[1][0:64, :], ALU.add)
                                k.tt(QCZ[64:128, h, 1, :], t[0][64:128, :], t[1][64:128, :], ALU.add)
                        qitems.append((w4[:, h], 128, sm("qnc", l), dest))
                    qk_pipeline(pt, qitems, rope_cols, hrhs)
                    for g in range(2):
                        w = wpl.get(8192 + g * 2048)
                        w4 = w[:, 0:2048].rearrange("p (c k m) -> p c k m", c=4, k=8)
                        for c4 in range(4):
                            h = g * 4 + c4
                            ps = nb()
                            proj(ps, (w, c4 * 512), 8, 64, hrhs)
                            silu_to(pt, ps, 64, OSA[0:64, h, :])
                    w = wpl.get(12288)
                    w4 = w[:, 0:4096].rearrange("p (c k m) -> p c k m", c=4, k=8)
                    for h in range(4):
                        ps = nb()
                        proj(ps, w4[:, h], 8, 128, hrhs)
                        silu_to(pt, ps, 128, OSC[:, h, :])

                    if b == 1:
                        build_halos()
                    wpl.kv_begin()
                    nqb = 2
                    units = []
                    for qb in range(nqb):
                        qc = slice(qb * 256, (qb + 1) * 256)
                        for hp in range(4):
                            kv = hp // 2
                            h0_ = hp * 2
                            tiles = []
                            if not lat:
                                for kt in range(2):
                                    c0 = qb * 256 + kt * 128
                                    tiles.append((KACTX[:, kv, c0:c0 + 128], (VACTX if kv == 0 else VACTXS)[:, qb * 2 + kt, :], None))
                            else:
                                t0 = (b - 1) * 4 + qb * 2
                                for wdx in range(4):
                                    mi = [0 if t0 == 0 else 1, 2, 3, 5 if t0 == 6 else 4][wdx]
                                    tiles.append((KAW[:, kv, (t0 + wdx) * 128:(t0 + wdx + 1) * 128],
                                                  (VAW if kv == 0 else VAWS)[:, t0 + wdx, :], MSK[:, mi, :]))
                                for c in range(4):
                                    tiles.append((CKA[:, kv, c * 128:(c + 1) * 128], (CVA if kv == 0 else CVAS)[:, c, :], None))

                            def qk_a(ps, kap, h0_=h0_, qc=qc):
                                k.mm(ps[:, :], kap, QA[:, h0_:h0_ + 2, qc])

                            def post_a(accO, accD, h0_=h0_, qc=qc):
                                for hh in range(2):
                                    cs = slice(hh * 256, (hh + 1) * 256)
                                    k.act(PT1[0:64, cs], accD[0:64, cs], AF.Ln,
                                          bias=D[0:64, D_ESINK + h0_ + hh:D_ESINK + h0_ + hh + 1])
                                k.act(PT2[0:64, :], PT1[0:64, :], AF.Exp, scale=-1.0)
                                k.tt(PT1[0:64, :], accO[0:64, :], PT2[0:64, :], ALU.mult)
                                k.tt(OSA[0:64, h0_:h0_ + 2, qc], PT1[0:64, :].rearrange("p (a b) -> p a b", a=2),
                                     OSA[0:64, h0_:h0_ + 2, qc], ALU.mult)
                            units.append((qk_a, tiles, 128, post_a))
                    attend_all(units)

                    for h in range(4):
                        units = []
                        if lat:
                            src_k = g1a[:, KC_OFF + h * 1024:KC_OFF + (h + 1) * 1024].rearrange("(r p) c -> p r c", p=128)
                            S.dma("sp", KCALL[:, 0:2048].rearrange("p (r c) -> p r c", r=2), src_k[:, 0:2, :])
                            for r in (0, 1):
                                S.dma("sp", VCALL[:, r * 1024:(r + 1) * 1024].rearrange("p (t n) -> p t n", n=128),
                                      g1b[r * 128:(r + 1) * 128, VC_OFF:VC_OFF + 4096].rearrange("p (t n) -> p t n", n=512)[:, :, h * 128:(h + 1) * 128])
                            S.dma("sp", KCALL[:, 2048:4096].rearrange("p (r c) -> p r c", r=2), src_k[:, 2:4, :])
                            for r in (2, 3):
                                S.dma("sp", VCALL[:, r * 1024:(r + 1) * 1024].rearrange("p (t n) -> p t n", n=128),
                                      g1b[r * 128:(r + 1) * 128, VC_OFF:VC_OFF + 4096].rearrange("p (t n) -> p t n", n=512)[:, :, h * 128:(h + 1) * 128])
                            S.dma("pool", KCALL[:, 4096:4608], ckc_in[l].ap()[:, h * 512:(h + 1) * 512], max_dma_last_dim=2048)
                            S.dma("pool", VCALL[:, 4096:4608].rearrange("p (t n) -> p t n", n=128),
                                  cvc_in[l].ap().rearrange("p (t n) -> p t n", n=512)[:, :, h * 128:(h + 1) * 128])
                        for qb in range(2):
                            qc = slice(qb * 256, (qb + 1) * 256)
                            tiles = []
                            if not lat:
                                for kt in range(2):
                                    c0 = qb * 256 + kt * 128
                                    tiles.append((KCCTX[:, h, c0:c0 + 128], VCCTX[:, qb * 2 + kt, h * 128:(h + 1) * 128], None))
                            else:
                                for j in range(36):
                                    tiles.append((KCALL[:, j * 128:(j + 1) * 128], VCALL[:, j * 128:(j + 1) * 128], None))

                            def qk_c(ps, kap, h=h, qc=qc):
                                k.mm(ps[:, :], kap, QCZ[:, h, :, qc])

                            def post_c(accO, accD, h=h, qc=qc):
                                k.act(PT1[:, :], accD[:, :], AF.Ln)
                                k.act(PT1[:, :], PT1[:, :], AF.Exp, scale=-1.0)
                                k.tt(PT2[:, :], accO[:, :], PT1[:, :], ALU.mult)
                                k.stt(PT3, PT2[:, 256:512], D[:, D_NLAM:D_NLAM + 1], PT2[:, 0:256], ALU.mult, ALU.add)
                                k.tt(PT4, PT3, PT3, ALU.mult)
                                ss = banks[7]
                                k.mm(ss[:, 0:256], ONES[:, :], PT4)
                                k.act(PT5, ss[:, 0:256], AF.Ln, scale=1.0 / 128.0, bias=EPST[:])
                                k.act(PT5, PT5, AF.Exp, scale=-0.5)
                                k.stt(PT3, PT3, D[:, D_SUBG:D_SUBG + 1], PT5, ALU.mult, ALU.mult)
                                k.tt(OSC[:, h, qc], PT3, OSC[:, h, qc], ALU.mult)
                            units.append((qk_c, tiles, 128, post_c))
                        if lat:
                            order = ([(0, j) for j in range(16)] + [(1, j) for j in range(16)]
                                     + [(0, j) for j in range(16, 36)] + [(1, j) for j in range(16, 36)])
                            attend_all(units, order)
                        else:
                            attend_all(units)

                    wpl.kv_end()
                    for cc in range(8):
                        base = WP2_Q + cc * 4096
                        w = wpl.get(base)
                        wa = wpl.get(-1 - cc)
                        wm = w[:, 0:3072].rearrange("p (i k m) -> p i k m", i=3, k=8)
                        wb = w[:, 3072:3584].rearrange("p (k m) -> p k m", k=4)
                        wc = w[:, 3584:4096].rearrange("p (k m) -> p k m", k=4)
                        wa3 = wa[0:64, 0:1024].rearrange("p (k m) -> p k m", k=8)
                        pa, pb, pc = (banks[0], banks[1], banks[2]) if cc % 2 == 0 else (banks[5], banks[6], banks[7])
                        zs = [banks[3], banks[4], banks[3]]
                        proj(zs[0], wm[:, 0], 8, 128, hrhs)
                        proj(zs[1], wm[:, 1], 8, 128, hrhs)
                        k.act(MT[0][:], zs[0][:], AF.Exp, scale=-1.0)
                        proj(zs[2], wm[:, 2], 8, 128, hrhs)
                        k.act(MT[1][:], zs[1][:], AF.Exp, scale=-1.0)
                        k.act(MT[2][:], zs[2][:], AF.Exp, scale=-1.0)
                        proj(pa, wa3, 8, 128, lambda kk: OSA[0:64, kk, :], P=64)
                        proj(pb, wb, 4, 128, lambda kk: OSGB[:, kk, cols])
                        proj(pc, wc, 4, 128, lambda kk: OSC[:, kk, :])
                        for i, pbr in enumerate((pa, pb, pc)):
                            k.act(MT[i][:], MT[i][:], AF.Ln, bias=ONE_P[:])
                            k.act(MT[i][:], MT[i][:], AF.Exp, scale=-1.0)
                            k.tt(MT2[i][:], pbr[:], MT[i][:], ALU.mult)
                        k.tt(YACC[:], MT2[0][:], MT2[1][:], ALU.add)
                        k.tt(YT[:, cc, :], YACC[:], MT2[2][:], ALU.add)
                    for g in range(2):
                        base = WP2_Q + 8 * 4096 + g * 4096
                        w = wpl.get(base)
                        w4 = w[:, 0:4096].rearrange("p (c k m) -> p c k m", c=4, k=8)
                        for c4 in range(4):
                            cc = g * 4 + c4
                            ps = nb((6, 7))
                            proj(ps, w4[:, c4], 8, 128, lambda kk: YT[:, kk, :])
                            k.stt(XT[:, cc, cols], ps[:, :], MODV[l][:, 16 + cc, v:v + 1], XT[:, cc, cols], ALU.mult, ALU.add)
                S.barrier()

        layers()
        S.barrier()
        S.dma("sp", yT_out.ap().rearrange("(k p) t -> p k t", p=128), XT[:], is_output=True)
        S.dma("sp", o_st.ap(), STO[:], is_output=True)
        S.finish()
        S.replay()
    return nc


def _fm(w, M):
    K_, n = w.shape
    a = w.reshape(K_ // 128, 128, n // M, M).transpose(1, 2, 0, 3)
    return np.ascontiguousarray(a).reshape(128, -1)


_PROG = None
import os
STOP = os.environ.get('KSTOP', '')
KV = os.environ.get('KV', '')


def kernel(**inp):
    global _PROG
    f32 = lambda a: np.ascontiguousarray(np.asarray(a, dtype=np.float32))
    I = {k_: f32(v) for k_, v in inp.items()}
    w_in = I["w_in"]
    shared = {}
    for l in range(DEPTH):
        W = w_in[l]
        q_a, k_a, v_a, g_a = W[:, 0:512], W[:, 512:640], W[:, 640:768], W[:, 768:1280]
        x_b, g_b = W[:, 1280:1792], W[:, 1792:2304]
        q_c, k_c, v_c, g_c = W[:, 2304:2816], W[:, 2816:3328], W[:, 3328:3840], W[:, 3840:4352]
        merge = W[:, 4352:7424]
        vtm = lambda w: np.ascontiguousarray(w.reshape(8, 128, -1).transpose(1, 0, 2)).reshape(128, -1)
        shared[f"wp1{l}"] = np.concatenate([_fm(k_a, 64), _fm(k_c, 128), _fm(x_b, 128), _fm(g_b, 128),
                                            vtm(v_c), vtm(v_a)], axis=1)
        parts = [_fm(q_a, 64), _fm(q_c, 128), _fm(g_a, 64), _fm(g_c, 128)]
        for cc in range(8):
            for i in range(3):
                parts.append(_fm(merge[:, i * 1024 + cc * 128:i * 1024 + (cc + 1) * 128], 128))
            parts.append(_fm(I["w_br_b"][l][:, cc * 128:(cc + 1) * 128], 128))
            parts.append(_fm(I["w_br_c"][l][:, cc * 128:(cc + 1) * 128], 128))
        parts.append(_fm(I["w_out"][l], 128))
        shared[f"wp2{l}"] = np.concatenate(parts, axis=1)
        wa = I["w_br_a"][l]
        shared[f"wba{l}"] = np.ascontiguousarray(
            wa.reshape(8, 64, 8, 128).transpose(1, 2, 0, 3)).reshape(64, -1)
        shared[f"wmod{l}"] = _fm(I["mod_w"][l], 128)
    gl = np.zeros((128, DEPTH, 2, 2, 4, 128), np.float32)
    for l in range(DEPTH):
        for dr in range(2):
            for gi, nm in enumerate(("lru_wa", "lru_wx")):
                Wg = I[nm][l, dr]
                for ch in range(4):
                    for hb in range(2):
                        gl[hb * 64:(hb + 1) * 64, l, dr, gi, ch, hb * 64:(hb + 1) * 64] = Wg[ch * 2 + hb]
    shared["glru"] = gl.reshape(128, -1)
    R = np.zeros((128, 128), np.float32)
    for dp in range(128):
        if dp % 32 < 16:
            R[dp + 16, dp] = -1.0
        else:
            R[dp - 16, dp] = 1.0
    shared["rmat"] = R
    inv = np.power(np.float32(10000.0), -np.arange(16, dtype=np.float32) / np.float32(16)).astype(np.float32)
    a_ = np.arange(128)[:, None]
    b_ = np.arange(128)[None, :]
    ge = (a_ >= b_).astype(np.float32)
    le = (a_ <= b_).astype(np.float32)
    one = np.ones((128, 128), np.float32)
    zero = np.zeros((128, 128), np.float32)
    M0 = np.concatenate([ge, zero], 1)
    M1 = np.concatenate([one, ge], 1)
    M2 = np.concatenate([le, one], 1)
    M3 = np.concatenate([zero, le], 1)

    in_maps = []
    for c in range(8):
        s, j = c // 4, c % 4
        m = dict(shared)
        xs = np.concatenate([I["x_prompt"][2 * c], I["x_prompt"][2 * c + 1],
                             I["x_sample"][s, 1024 * j:1024 * (j + 1)]], axis=0)
        m["xT"] = np.ascontiguousarray(xs.T)
        sm_ = np.zeros((128, NSM), np.float32)

        def put(key, arr):
            o, w = SM[key]
            sm_[:, o:o + w] = arr
        col = lambda v, n: np.ascontiguousarray(v.reshape(n, 128).T)
        for l in range(DEPTH):
            put(("ng", l), col(I["norm_g"][l], 8))
            put(("modb", l), col(I["mod_b"][l], 24))
            put(("qna", l), np.tile(I["qn_a"][l], 2)[:, None])
            put(("kna", l), np.tile(I["kn_a"][l], 2)[:, None])
            put(("qnc", l), np.tile(I["qn_c"][l], 2)[:, None])
            put(("knc", l), np.tile(I["kn_c"][l], 2)[:, None])
            cw = I["conv_w"][l]
            put(("convw", l), np.ascontiguousarray(cw.reshape(4, 4, 128).transpose(2, 1, 0)).reshape(128, 16))
            put(("convb", l), col(I["conv_b"][l], 4))
            for nm, key in (("lru_ba", "ba"), ("lru_bx", "bx"), ("lru_lam", "lam")):
                put((key, l), np.ascontiguousarray(I[nm][l].reshape(2, 4, 128).transpose(2, 0, 1)).reshape(128, 8))
            put(("subln", l), I["subln_c"][l][:, None])
            put(("sink", l), np.broadcast_to(I["sink_a"][l][None, :], (128, 8)))
            for nm, key in (("lam_q1", "lq1"), ("lam_k1", "lk1"), ("lam_q2", "lq2"), ("lam_k2", "lk2")):
                put((key, l), np.broadcast_to(I[nm][l][None, :], (128, 64)))
            put(("h0", l), np.ascontiguousarray(I["state_lru"][s, l].reshape(2, 4, 128).transpose(2, 0, 1)).reshape(128, 8))
        cc_ = np.stack([col(I["c_ctx"], 8), col(I["c"][s], 8)], axis=2).reshape(128, 16)
        put("c", cc_)
        sel = np.zeros((128, 12), np.float32)
        if j > 0:
            sel[:, j - 1] = 1.0
        if j < 3:
            sel[:, 4 + j + 1] = 1.0
        sel[:, 8 + j] = 1.0
        put("sel", sel)
        m["small"] = sm_
        t = 1024 * j + np.arange(1024)
        row = (t // 64).astype(np.float32)
        colp = (t % 64).astype(np.float32)
        ang = np.zeros((64, 1024), np.float32)
        for d in range(64):
            ang[d] = (row if d < 32 else colp) * inv[d % 16]
        ang = np.concatenate([ang, ang], 0)
        m["cosT"] = np.cos(ang).astype(np.float32)
        m["sinT"] = np.sin(ang).astype(np.float32)
        first = M0 if j > 0 else np.zeros_like(M0)
        last = M3 if j < 3 else np.zeros_like(M3)
        msk = np.stack([first, M0, M1, M2, M3, last], 0)
        msk = np.concatenate([msk, msk], 2)
        m["masks"] = np.ascontiguousarray(msk.transpose(1, 0, 2)).reshape(128, -1)
        for l in range(DEPTH):
            ck = I["cache_c_k"][s, l]
            m[f"ckc{l}"] = np.ascontiguousarray(ck.transpose(2, 3, 1, 0)).reshape(128, 4 * 512)
            cv = I["cache_c_v"][s, l]
            m[f"cvc{l}"] = np.ascontiguousarray(cv.reshape(4, 128, 512).transpose(1, 0, 2)).reshape(128, -1)
            ka = I["cache_a_k"][s, l]
            m[f"cka{l}"] = np.ascontiguousarray(ka.transpose(2, 1, 0)).reshape(64, -1)
            va = I["cache_a_v"][s, l]
            m[f"cva{l}"] = np.ascontiguousarray(va.reshape(4, 128, 128).transpose(1, 0, 2)).reshape(128, -1)
        in_maps.append(m)

    if _PROG is None:
        _PROG = build_program()
    res = run_bass_kernel_spmd(_PROG, in_maps, core_ids=list(range(8)))
    R_ = res.results

    y_prompt = np.zeros((16, 256, 1024), np.float32)
    y_sample = np.zeros((2, 4096, 1024), np.float32)
    n_ka = np.zeros((16, 2, 256, 2, 64), np.float32)
    n_va = np.zeros((16, 2, 256, 2, 64), np.float32)
    n_kc = np.zeros((16, 2, 256, 4, 2, 64), np.float32)
    n_vc = np.zeros((16, 2, 256, 4, 128), np.float32)
    n_st = np.zeros((16, 2, 2, 512), np.float32)
    for c in range(8):
        s, j = c // 4, c % 4
        r = R_[c]
        y = np.asarray(r["yT"]).T
        y_prompt[2 * c] = y[0:256]
        y_prompt[2 * c + 1] = y[256:512]
        y_sample[s, 1024 * j:1024 * (j + 1)] = y[512:]
        ka = np.asarray(r["o_ka"]).reshape(2, 2, 64, 2, 256)
        kc = np.asarray(r["o_kc"]).reshape(2, 4, 2, 64, 2, 256)
        va = np.asarray(r["o_va"]).reshape(2, 2, 256, 2, 64)
        vc = np.asarray(r["o_vc"]).reshape(2, 2, 256, 4, 128)
        stt_ = np.asarray(r["o_st"]).reshape(128, 2, 2, 2, 4)
        for sq in range(2):
            bi = 2 * c + sq
            n_ka[bi] = ka[:, :, :, sq, :].transpose(0, 3, 1, 2)
            n_kc[bi] = kc[:, :, :, :, sq, :].transpose(0, 4, 1, 2, 3)
            n_va[bi] = va[:, sq]
            n_vc[bi] = vc[:, sq]
            n_st[bi] = stt_[:, :, sq].transpose(1, 2, 3, 0).reshape(2, 2, 512)
    return (y_prompt, y_sample, n_ka, n_va, n_kc, n_vc, n_st)
```

```python
import contextlib
import math
import numpy as np
import concourse.bass as bass
import concourse.mybir as mybir
from concourse.bass_utils import run_bass_kernel_spmd

F32 = mybir.dt.float32
BF16 = mybir.dt.bfloat16
AF = mybir.ActivationFunctionType
ALU = mybir.AluOpType
AX = mybir.AxisListType

ENGS = ("pe", "act", "dve", "pool", "sp")

DEPTH = 2
SCALE = 0.125
EPS = 1e-6
NTOK = 1536
KC_OFF, VC_OFF, KA_OFF, VA_OFF = 0, 0, 0, 2048
XBW = 1545
SEGS = [(0, 256, 0), (259, 256, 256), (518, 1024, 512)]

_SM_LAYER = [("ng", 8), ("modb", 24), ("qna", 1), ("kna", 1), ("qnc", 1), ("knc", 1), ("convw", 16),
             ("convb", 4), ("ba", 8), ("bx", 8), ("lam", 8), ("subln", 1), ("sink", 8),
             ("lq1", 64), ("lk1", 64), ("lq2", 64), ("lk2", 64), ("h0", 8)]
SM = {}
_o = 0
for _l in range(DEPTH):
    for _n, _w in _SM_LAYER:
        SM[(_n, _l)] = (_o, _w)
        _o += _w
SM["c"] = (_o, 16); _o += 16
SM["sel"] = (_o, 12); _o += 12
NSM = _o

WP1N = 1024 + 4096 * 3 + 5120
WP2_Q = 16384
WP2N = WP2_Q + 8 * 4096 + 8192


class Res:
    __slots__ = ("name", "last_w", "readers")

    def __init__(self, name=""):
        self.name = name
        self.last_w = None
        self.readers = {}


class Sched:
    NDMA = 8

    def __init__(self, nc, stack):
        self.nc = nc
        self.streams = {e: [] for e in ENGS}
        self.count = {e: 0 for e in ENGS}
        self.seen = {e: {} for e in ENGS}
        self.sems = {}
        self.resmap = {}
        self.split = {}
        for e in ENGS:
            self.sems[e] = stack.enter_context(nc.semaphore("prog_" + e))
        self.dma_k = {}
        for q in ("sp", "act", "pool"):
            self.dma_k[q] = 0
            for i in range(self.NDMA):
                self.sems[("dma", q, i)] = stack.enter_context(nc.semaphore(f"dma_{q}_{i}"))
        self.sems["cc"] = stack.enter_context(nc.semaphore("cc"))
        self.cc_count = 0
        self.out_events = []

    def _named(self, name):
        r = self.resmap.get(name)
        if r is None:
            r = Res(name)
            self.resmap[name] = r
        return r

    def _res(self, x):
        if isinstance(x, Res):
            return x
        return self._named(x.tensor.name)

    def _resl(self, x):
        if isinstance(x, Res):
            return [x]
        name = x.tensor.name
        b = self.split.get(name)
        if b is None:
            return [self._named(name)]
        try:
            ap = [list(e) for e in x.ap]
            col0 = int(x.offset) % int(ap[0][0])
            hi = col0 + 1
            for st_, cnt in ap[1:]:
                assert st_ >= 0
                hi += (int(cnt) - 1) * int(st_)
        except Exception:
            col0, hi = 0, 1 << 30
        out = []
        if col0 < b:
            out.append(self._named(name + ".A"))
        if hi > b:
            out.append(self._named(name + ".B"))
        return out

    def _collect(self, reads, writes):
        waits = {}

        def add(ev):
            if ev is None:
                return
            k, v = ev
            if waits.get(k, 0) < v:
                waits[k] = v
        for r in reads:
            add(r.last_w)
        for w in writes:
            add(w.last_w)
            for k, v in w.readers.items():
                add((k, v))
        return waits

    def _emit_waits(self, eng, waits):
        for k, v in waits.items():
            if k == eng:
                if eng == "pe":
                    continue
                if v <= self.count[eng] - 6:
                    continue
            if self.seen[eng].get(k, 0) >= v:
                continue
            self.seen[eng][k] = v
            self.streams[eng].append(("wait", k, v))

    def _commit(self, ev, reads, writes):
        k, v = ev
        for r in reads:
            if r.readers.get(k, 0) < v:
                r.readers[k] = v
        for w in writes:
            w.last_w = ev
            w.readers = {}

    def op(self, eng, fn, reads=(), writes=()):
        reads = [r for x in reads for r in self._resl(x)]
        writes = [r for x in writes for r in self._resl(x)]
        writes = writes + [r for r in reads if r.name.startswith("bank") and r not in writes]
        self._emit_waits(eng, self._collect(reads, writes))
        self.count[eng] += 1
        ev = (eng, self.count[eng])
        self.streams[eng].append(("op", fn, eng, 1))
        self._commit(ev, reads, writes)
        return ev

    def dma(self, q, out, in_, is_output=False, extra_reads=(), **kw):
        reads = self._resl(in_) + [r for x in extra_reads for r in self._resl(x)]
        writes = self._resl(out)
        waits = self._collect(reads, writes)
        k = self.dma_k[q]
        self.dma_k[q] += 1
        s = ("dma", q, k % self.NDMA)
        gen = k // self.NDMA
        if gen > 0 and waits.get(s, 0) < 16 * gen:
            waits[s] = 16 * gen
        self._emit_waits(q, waits)
        ev = (s, 16 * (gen + 1))
        self.streams[q].append(("op", lambda e: e.dma_start(out=out, in_=in_, **kw), s, 16))
        self._commit(ev, reads, writes)
        if is_output:
            self.out_events.append(ev)
        return ev

    def collective(self, groups, in_t, out_t):
        reads = [self._res(in_t.ap())]
        writes = [self._res(out_t.ap())]
        self._emit_waits("pool", self._collect(reads, writes))
        self.cc_count += 1
        ev = ("cc", self.cc_count)
        self.streams["pool"].append(("op", lambda e: e.collective_compute(
            "AllGather", ALU.bypass, replica_groups=groups, ins=[in_t.ap().opt()],
            outs=[out_t.ap().opt()]), "cc", 1))
        self._commit(ev, reads, writes)
        return ev

    def _all_events(self):
        waits = {}
        for e in ENGS:
            if self.count[e] > 0:
                waits[e] = self.count[e]
        for q in ("sp", "act", "pool"):
            k = self.dma_k[q]
            for i in range(self.NDMA):
                n = (k // self.NDMA) + (1 if i < k % self.NDMA else 0)
                if n > 0:
                    waits[("dma", q, i)] = 16 * n
        if self.cc_count:
            waits["cc"] = self.cc_count
        return waits

    def barrier(self, skip_cc=False):
        waits = self._all_events()
        if skip_cc:
            waits.pop("cc", None)
        for e in ENGS:
            for k, v in waits.items():
                if k == e:
                    continue
                if self.seen[e].get(k, 0) >= v:
                    continue
                self.seen[e][k] = v
                self.streams[e].append(("wait", k, v))

    def finish(self):
        self.barrier()

    def replay(self):
        nc = self.nc
        sems = self.sems
        streams = self.streams

        def run(engobj, name):
            for item in streams[name]:
                if item[0] == "wait":
                    engobj.wait_ge(sems[item[1]], item[2])
                else:
                    _, fn, semk, inc = item
                    fn(engobj).then_inc(sems[semk], inc)

        with nc.Block() as block:
            @block.tensor
            def _(e):
                run(e, "pe")

            @block.scalar
            def _(e):
                run(e, "act")

            @block.vector
            def _(e):
                run(e, "dve")

            @block.gpsimd
            def _(e):
                run(e, "pool")

            @block.sync
            def _(e):
                run(e, "sp")


class WPlan:
    def __init__(self, S, slotsA, slotsB, plan):
        self.S, self.A, self.B, self.plan = S, list(slotsA), list(slotsB), plan
        self.slot_of = {}
        self.live = {}
        self.cur = 0
        self.got = []
        self.markers_passed = 0

    def _pump(self):
        j = 0
        unpassed = 0
        seen_markers = 0
        for j in range(len(self.plan)):
            e = self.plan[j]
            if e[0] == 'KV':
                seen_markers += 1
                if seen_markers > self.markers_passed:
                    unpassed += 1
                    if unpassed > 1:
                        return
                continue
            if j in self.slot_of or j < self.cur:
                continue
            allowed = self.A + (self.B if unpassed == 0 else [])
            free = [t for t in allowed if t.name not in self.live]
            if not free:
                return
            t = free[0]
            _, src, P, n = e[0:4]
            self.S.dma("pool", t[0:P, 0:n], src, max_dma_last_dim=4096)
            self.slot_of[j] = t
            self.live[t.name] = j

    def _release(self, idx):
        t = self.slot_of.get(idx)
        if t is not None and self.live.get(t.name) == idx:
            del self.live[t.name]

    def get(self, src_off=None):
        while self.plan[self.cur][0] == 'KV':
            self.cur += 1
        i = self.cur
        if src_off is not None:
            assert self.plan[i][4] == src_off, (i, self.plan[i][4], src_off)
        while len(self.got) >= 2:
            self._release(self.got.pop(0))
        if i not in self.slot_of:
            self._pump()
        assert i in self.slot_of, ("no slot for load", i)
        self.cur += 1
        self.got.append(i)
        self._pump()
        return self.slot_of[i]

    def release_all(self):
        while self.got:
            self._release(self.got.pop(0))

    def kv_begin(self):
        self.release_all()
        self._pump()

    def kv_end(self):
        self.markers_passed += 1
        self._pump()


class K:
    def __init__(self, S):
        self.S = S

    @staticmethod
    def _aps(*xs):
        return [x for x in xs if x is not None and not isinstance(x, (int, float))]

    def act(self, out, in_, func, scale=1.0, bias=None):
        kw = {}
        if bias is not None:
            kw["bias"] = bias
        self.S.op("act", lambda e: e.activation(out=out, in_=in_, func=func, scale=scale, **kw),
                  reads=self._aps(in_, scale, bias), writes=[out])

    def ts(self, out, in0, s1, s2=None, op0=ALU.mult, op1=None, eng="dve"):
        kw = {}
        if op1 is not None:
            kw["op1"] = op1
        self.S.op(eng, lambda e: e.tensor_scalar(out=out, in0=in0, scalar1=s1, scalar2=s2, op0=op0, **kw),
                  reads=self._aps(in0, s1, s2), writes=[out])

    def tt(self, out, in0, in1, op, eng="dve"):
        self.S.op(eng, lambda e: e.tensor_tensor(out=out, in0=in0, in1=in1, op=op),
                  reads=[in0, in1], writes=[out])

    def stt(self, out, in0, scalar, in1, op0, op1, eng="dve"):
        self.S.op(eng, lambda e: e.scalar_tensor_tensor(out=out, in0=in0, scalar=scalar, in1=in1, op0=op0, op1=op1),
                  reads=self._aps(in0, scalar, in1), writes=[out])

    def recip(self, out, in_):
        self.S.op("dve", lambda e: e.reciprocal(out=out, in_=in_), reads=[in_], writes=[out])

    def copy(self, out, in_, eng="dve"):
        if in_.tensor.name.startswith("bank"):
            self.ts(out, in_, 1.0, None, op0=ALU.mult, eng=eng)
            return
        self.S.op(eng, lambda e: e.tensor_copy(out=out, in_=in_), reads=[in_], writes=[out])

    def memset(self, ap, val, eng="dve"):
        self.S.op(eng, lambda e: e.memset(ap, val), writes=[ap])

    def rsum(self, out, in_):
        self.S.op("dve", lambda e: e.reduce_sum(out=out, in_=in_, axis=AX.X), reads=[in_], writes=[out])

    def scan(self, out, a, b, init):
        self.S.op("dve", lambda e: e.tensor_tensor_scan(out=out, data0=a, data1=b, initial=init,
                                                        op0=ALU.mult, op1=ALU.add),
                  reads=self._aps(a, b, init), writes=[out])

    def mm(self, out, lhsT, rhs, start=True, stop=True):
        self.S.op("pe", lambda e: e.matmul(out, lhsT=lhsT, rhs=rhs, start=start, stop=stop),
                  reads=[lhsT, rhs], writes=[out])


def build_program():
    nc = bass.Bass("TRN2", target_bir_lowering=False)
    din = lambda n, s, dt=F32: nc.dram_tensor(n, s, dt, kind="ExternalInput")
    dout = lambda n, s: nc.dram_tensor(n, s, F32, kind="ExternalOutput")
    xT_in = din("xT", [1024, NTOK])
    small_in = din("small", [128, NSM])
    wmod = [din(f"wmod{l}", [128, 24576]) for l in range(DEPTH)]
    wp1 = [din(f"wp1{l}", [128, WP1N]) for l in range(DEPTH)]
    wp2 = [din(f"wp2{l}", [128, WP2N]) for l in range(DEPTH)]
    wba = [din(f"wba{l}", [64, 8192]) for l in range(DEPTH)]
    glru_in = din("glru", [128, 4096])
    rmat_in = din("rmat", [128, 128])
    cos_in = din("cosT", [128, 1024])
    sin_in = din("sinT", [128, 1024])
    msk_in = din("masks", [128, 3072])
    ckc_in = [din(f"ckc{l}", [128, 2048]) for l in range(DEPTH)]
    cvc_in = [din(f"cvc{l}", [128, 2048]) for l in range(DEPTH)]
    cka_in = [din(f"cka{l}", [64, 1024]) for l in range(DEPTH)]
    cva_in = [din(f"cva{l}", [128, 512]) for l in range(DEPTH)]

    yT_out = dout("yT", [1024, NTOK])
    o_ka = dout("o_ka", [DEPTH * 2 * 64, 512])
    o_kc = dout("o_kc", [DEPTH * 4 * 128, 512])
    o_va = dout("o_va", [DEPTH * 512, 128])
    o_vc = dout("o_vc", [DEPTH * 512, 512])
    o_st = dout("o_st", [128, 32])

    ga_in = nc.dram_tensor("ga_in", [128, 4096], BF16)
    ga_out = nc.dram_tensor("ga_out", [512, 4096], BF16)
    gb_in = nc.dram_tensor("gb_in", [128, 4096], BF16)
    gb_out = nc.dram_tensor("gb_out", [512, 4096], BF16)
    gc_in = nc.dram_tensor("gc_in", [128, 3072], BF16)
    gc_out = nc.dram_tensor("gc_out", [512, 3072], BF16)
    g2_in = nc.dram_tensor("g2_in", [128, 12], F32)
    g2_out = nc.dram_tensor("g2_out", [512, 12], F32)
    g3_in = nc.dram_tensor("g3_in", [128, 16], F32)
    g3_out = nc.dram_tensor("g3_out", [512, 16], F32)
    spill_a = nc.dram_tensor("spill_a", [128, 8 * 1024], F32)
    spill_b = nc.dram_tensor("spill_b", [128, 8 * 1024], F32)
    GROUPS = [[0, 1, 2, 3], [4, 5, 6, 7]]

    with contextlib.ExitStack() as st:
        S = Sched(nc, st)
        k = K(S)
        _cnt = [0]

        def T(stack, shape, dt=F32, name=None):
            _cnt[0] += 1
            return stack.enter_context(nc.sbuf_tensor(f"{name or 't'}_{_cnt[0]}", shape, dt))

        banks = [st.enter_context(nc.psum_tensor(f"bank{i}", [128, 512], F32)) for i in range(8)]

        XT = T(st, [128, 8, NTOK], F32, "XT")
        OSGB = T(st, [128, 4, NTOK], BF16, "OSGB")
        ONES = T(st, [128, 128], BF16, "ONES")
        BONES = T(st, [128, 128], BF16, "BONES")
        RM = T(st, [128, 128], BF16, "RM")
        COS = T(st, [128, 1024], F32, "COS")
        SIN = T(st, [128, 1024], F32, "SIN")
        MSK = T(st, [128, 6, 512], BF16, "MSK")
        SMALL = T(st, [128, NSM], F32, "SMALL")
        EPST = T(st, [128, 1], F32, "EPST")
        MODV = [T(st, [128, 24, 2], F32, "MODV") for _ in range(DEPTH)]
        GS = [T(st, [128, 8, 2], F32, "GS") for _ in range(DEPTH)]
        DER = [T(st, [128, 64], F32, "DER") for _ in range(DEPTH)]
        KAW = T(st, [128, 2, 1280], BF16, "KAW")
        VAW = T(st, [128, 10, 128], BF16, "VAW")
        VAWS = T(st, [128, 10, 128], BF16, "VAWS")
        KACTX = T(st, [128, 2, 512], BF16, "KACTX")
        KCCTX = T(st, [128, 4, 512], BF16, "KCCTX")
        VACTX = T(st, [128, 4, 128], BF16, "VACTX")
        VACTXS = T(st, [128, 4, 128], BF16, "VACTXS")
        VCCTX = T(st, [128, 4, 512], BF16, "VCCTX")
        STO = T(st, [128, 32], F32, "STO")
        WSL = [T(st, [128, 4096], BF16, "WSL") for _ in range(2)]
        ws_i = [0]

        def sm(name, l=None):
            o, w = SM[(name, l)] if l is not None else SM[name]
            return SMALL[:, o:o + w]

        def wload(src_ap, P, n):
            slot = WSL[ws_i[0] % len(WSL)]
            ws_i[0] += 1
            S.dma("pool", slot[0:P, 0:n], src_ap, max_dma_last_dim=4096)
            return slot

        bank_i = [0]

        def nb(pool=(0, 1, 2, 3)):
            b = banks[pool[bank_i[0] % len(pool)]]
            bank_i[0] += 1
            return b

        D_NBA, D_NBX, D_CL, D_C2, D_ESINK, D_NLAM, D_SUBG = 0, 8, 16, 24, 32, 40, 41

        S.dma("sp", XT[:], xT_in.ap().rearrange("(k p) t -> p k t", p=128))
        S.dma("sp", SMALL[:], small_in.ap())
        S.dma("sp", COS[:], cos_in.ap())
        S.dma("sp", SIN[:], sin_in.ap())
        S.dma("pool", RM[:], rmat_in.ap())
        S.dma("pool", MSK[:].rearrange("p a b -> p (a b)"), msk_in.ap(), max_dma_last_dim=4096)
        k.memset(ONES[:], 1.0)
        k.memset(BONES[:], 0.0)
        k.memset(BONES[0:64, 0:64], 1.0)
        k.memset(BONES[64:128, 64:128], 1.0)
        k.memset(EPST[:], EPS)
        ONE_P = T(st, [128, 1], F32, "ONE_P")
        k.memset(ONE_P[:], 1.0)
        k.memset(KAW[:], 0.0)
        k.memset(VAW[:], 0.0)
        k.memset(VAWS[:], 0.0)
        k.memset(KACTX[:], 0.0)
        k.memset(WSL[0][:, 0:2048], 0.0)
        S.dma("sp", gc_in.ap()[64:128, 0:2048], WSL[0][64:128, 0:2048])

        with contextlib.ExitStack() as ph:
            SCF = T(ph, [128, 16], F32, "SCF")
            SCF2 = T(ph, [128, 16], F32, "SCF2")
            SCB = T(ph, [128, 8, 2], BF16, "SCB")
            TMPS = T(ph, [128, 64], F32, "TMPS")
            TMPS2 = T(ph, [128, 64], F32, "TMPS2")
            cT = sm("c")
            k.act(SCF[:], cT, AF.Exp, scale=-1.0)
            k.ts(SCF[:], SCF[:], 1.0, None, op0=ALU.add)
            k.recip(SCF2[:], SCF[:])
            k.tt(SCB[:].rearrange("p a b -> p (a b)"), cT, SCF2[:], ALU.mult)
            for l in range(DEPTH):
                ps = nb()
                for g in range(6):
                    w = wload(wmod[l].ap()[:, g * 4096:(g + 1) * 4096], 128, 4096)
                    wv = w[:, :].rearrange("p (c k m) -> p c k m", c=4, k=8)
                    for c4 in range(4):
                        cc = g * 4 + c4
                        for kk in range(8):
                            k.mm(ps[:, cc * 2:cc * 2 + 2], wv[:, c4, kk, :], SCB[:, kk, :],
                                 start=(kk == 0), stop=(kk == 7))
                psv = ps[:, 0:48].rearrange("p (c v) -> p c v", v=2)
                for v in range(2):
                    k.tt(MODV[l][:, :, v], psv[:, :, v], sm("modb", l), ALU.add)
                    k.stt(GS[l][:, :, v], MODV[l][:, 8:16, v], 1.0, sm("ng", l), ALU.add, ALU.mult)
                D = DER[l]
                k.ts(D[:, D_NBA:D_NBA + 8], sm("ba", l), -1.0, None, op0=ALU.mult)
                k.ts(D[:, D_NBX:D_NBX + 8], sm("bx", l), -1.0, None, op0=ALU.mult)
                k.act(TMPS[:, 0:8], sm("lam", l), AF.Exp, scale=-1.0)
                k.ts(TMPS[:, 0:8], TMPS[:, 0:8], 1.0, None, op0=ALU.add)
                k.act(TMPS2[:, 0:8], TMPS[:, 0:8], AF.Ln)
                k.ts(D[:, D_CL:D_CL + 8], TMPS2[:, 0:8], -8.0, None, op0=ALU.mult)
                k.ts(D[:, D_C2:D_C2 + 8], TMPS2[:, 0:8], -16.0, None, op0=ALU.mult)
                k.act(D[:, D_ESINK:D_ESINK + 8], sm("sink", l), AF.Exp)
                lam_init = 0.8 - 0.6 * math.exp(-0.3 * l)
                k.tt(TMPS[:, 0:64], sm("lq1", l), sm("lk1", l), ALU.mult)
                k.rsum(TMPS2[:, 8:9], TMPS[:, 0:64])
                k.tt(TMPS[:, 0:64], sm("lq2", l), sm("lk2", l), ALU.mult)
                k.rsum(TMPS2[:, 9:10], TMPS[:, 0:64])
                k.act(TMPS2[:, 10:12], TMPS2[:, 8:10], AF.Exp)
                k.tt(TMPS2[:, 12:13], TMPS2[:, 11:12], TMPS2[:, 10:11], ALU.subtract)
                k.ts(D[:, D_NLAM:D_NLAM + 1], TMPS2[:, 12:13], -lam_init, None, op0=ALU.add)
                k.ts(D[:, D_SUBG:D_SUBG + 1], sm("subln", l), 1.0 - lam_init, None, op0=ALU.mult)
            S.barrier()

        def front(ph_t, l, b):
            HT, SQ, RSTD, FTMP = ph_t["HT"], ph_t["SQ"], ph_t["RSTD"], ph_t["FTMP"]
            v = 0 if b == 0 else 1
            cols = slice(b * 512, (b + 1) * 512)
            ps = banks[7]
            for kk in range(8):
                sq = SQ[kk % 2]
                if kk % 2 == 0:
                    k.act(sq[:], XT[:, kk, cols], AF.Square)
                else:
                    k.tt(sq[:], XT[:, kk, cols], XT[:, kk, cols], ALU.mult, eng=("pool" if kk % 4 == 1 else "dve"))
                k.mm(ps[:], ONES[:], sq[:], start=(kk == 0), stop=(kk == 7))
            k.act(RSTD[:], ps[:], AF.Ln, scale=1.0 / 1024.0, bias=EPST[:])
            k.act(RSTD[:], RSTD[:], AF.Exp, scale=-0.5)
            for kk in range(8):
                ft = FTMP[kk % 2]
                k.tt(ft[:], XT[:, kk, cols], RSTD[:], ALU.mult)
                k.act(HT[:, kk, :], ft[:], AF.Identity, scale=GS[l][:, kk, v:v + 1], bias=MODV[l][:, kk, v:v + 1])

        def proj(ps, w3, nk, M, rhs_fn, N=512, P=128):
            if isinstance(w3, tuple):
                flat, base = w3
                for kk in range(nk):
                    k.mm(ps[0:128, 0:N], flat[0:P, base + kk * 64:base + kk * 64 + 128], rhs_fn(kk),
                         start=(kk == 0), stop=(kk == nk - 1))
                return
            for kk in range(nk):
                k.mm(ps[0:M, 0:N], w3[0:P, kk, 0:M], rhs_fn(kk), start=(kk == 0), stop=(kk == nk - 1))

        qk_i = [0]

        def qknorm(ph_t, ps, P, gain, rope_cols, dest_fn):
            i = qk_i[0] % 2
            qk_i[0] += 1
            X32, SQB, LNV, XN = ph_t["QX"][i], ph_t["QS"][i], ph_t["QL"][i], ph_t["QN"][i]
            k.act(X32[0:P, :], ps[0:P, :], AF.Copy)
            k.act(SQB[0:P, :], ps[0:P, :], AF.Square)
            ss = nb((4, 5))
            k.mm(ss[0:P, :], BONES[0:P, 0:P], SQB[0:P, :])
            k.act(LNV[0:P, :], ss[0:P, :], AF.Ln, scale=1.0 / 64.0, bias=EPST[0:P, :])
            k.act(LNV[0:P, :], LNV[0:P, :], AF.Exp, scale=-0.5)
            k.stt(XN[0:P, :], X32[0:P, :], gain, LNV[0:P, :], ALU.mult, ALU.mult)
            if rope_cols is None:
                dest_fn(XN)
                return
            k.copy(SQB[0:P, :], XN[0:P, :], eng="pool")
            rx = nb((4, 5))
            k.mm(rx[0:P, :], RM[0:P, 0:P], SQB[0:P, :])
            k.tt(X32[0:P, :], XN[0:P, :], COS[0:P, rope_cols], ALU.mult, eng="pool")
            k.tt(LNV[0:P, :], rx[0:P, :], SIN[0:P, rope_cols], ALU.mult)
            dest_fn((X32, LNV))

        def qk_pipeline(ph_t, items, rope_cols, rhs_fn):
            n = len(items)
            st = [None] * n
            base = qk_i[0]
            qk_i[0] += n

            def tiles(i):
                j = (base + i) % 2
                return ph_t["QX"][j], ph_t["QS"][j], ph_t["QL"][j], ph_t["QN"][j]

            def stage_a(i):
                w3, P, gain, dest = items[i]
                ps = nb()
                proj(ps, w3, 8, P, rhs_fn)
                st[i] = ps

            def stage_b(i):
                w3, P, gain, dest = items[i]
                X32, SQB, LNV, XN = tiles(i)
                ps = st[i]
                k.act(SQB[0:P, :], ps[0:P, :], AF.Square)
                ss = nb((4, 5))
                k.mm(ss[0:P, :], BONES[0:P, 0:P], SQB[0:P, :])
                k.act(LNV[0:P, :], ss[0:P, :], AF.Ln, scale=1.0 / 64.0, bias=EPST[0:P, :])
                k.act(LNV[0:P, :], LNV[0:P, :], AF.Exp, scale=-0.5)
                k.stt(XN[0:P, :], ps[0:P, :], gain, LNV[0:P, :], ALU.mult, ALU.mult)
                if rope_cols is None:
                    dest(XN)

            def stage_c0(i):
                w3, P, gain, dest = items[i]
                X32, SQB, LNV, XN = tiles(i)
                k.act(SQB[0:P, :], XN[0:P, :], AF.Copy)

            def stage_c1(i):
                w3, P, gain, dest = items[i]
                X32, SQB, LNV, XN = tiles(i)
                rx = nb((6, 7))
                k.mm(rx[0:P, :], RM[0:P, 0:P], SQB[0:P, :])
                k.tt(X32[0:P, :], XN[0:P, :], COS[0:P, rope_cols], ALU.mult, eng="pool")
                k.tt(LNV[0:P, :], rx[0:P, :], SIN[0:P, rope_cols], ALU.mult)
                dest((X32, LNV))

            for step in range(n + 2):
                if rope_cols is not None and 0 <= step - 2 < n:
                    stage_c0(step - 2)
                if step < n:
                    stage_a(step)
                if 0 <= step - 1 < n:
                    stage_b(step - 1)
                if rope_cols is not None and 0 <= step - 2 < n:
                    stage_c1(step - 2)

        def silu_to(ph_t, ps, P, out_ap):
            i = qk_i[0] % 2
            qk_i[0] += 1
            E = ph_t["QX"][i]
            k.act(E[0:P, :], ps[0:P, :], AF.Exp, scale=-1.0)
            k.act(E[0:P, :], E[0:P, :], AF.Ln, bias=ONE_P[0:P, :])
            k.act(E[0:P, :], E[0:P, :], AF.Exp, scale=-1.0)
            k.tt(out_ap, ps[0:P, :], E[0:P, :], ALU.mult)

        def common_tiles(ph):
            sq = [T(ph, [128, 512], BF16, "SQ") for _ in range(2)]
            qn = [T(ph, [128, 512], F32, "QN") for _ in range(2)]
            return {
                "HT": T(ph, [128, 8, 512], BF16, "HT"),
                "SQ": sq,
                "RSTD": T(ph, [128, 512], F32, "RSTD"),
                "FTMP": qn,
                "QX": [T(ph, [128, 512], F32, "QX") for _ in range(2)],
                "QS": sq,
                "QL": [T(ph, [128, 512], F32, "QL") for _ in range(2)],
                "QN": qn,
            }

        class _Stop(Exception):
            pass

        def layers():
          for l in range(DEPTH):
            D = DER[l]
            if STOP == "setup":
                return
            with contextlib.ExitStack() as ph0:
              XB = T(ph0, [128, 4, XBW], F32, "XB")
              with contextlib.ExitStack() as ph:
                pt = common_tiles(ph)
                HT = pt["HT"]
                KST = [T(ph, [128, 512], BF16, "KST") for _ in range(2)]
                VTMP = [T(ph, [128, 512], F32, "VTMP") for _ in range(1)]
                VST = [T(ph, [128, 512], BF16, "VST") for _ in range(2)]
                VATMP = T(ph, [128, 128], F32, "VATMP")
                k.memset(XB[:], 0.0)
                kst_i = [0]
                W1 = {}
                for (o_, n_) in ((0, 1024), (1024, 4096), (5120, 4096), (9216, 4096), (13312, 4096), (17408, 1024)):
                    t_ = T(ph, [128, n_ + (64 if o_ == 0 else 0)], BF16, "W1")
                    if o_ == 0:
                        k.memset(t_[:, n_:n_ + 64], 0.0)
                    S.dma("pool", t_[:, 0:n_], wp1[l].ap()[:, o_:o_ + n_], max_dma_last_dim=4096)
                    W1[o_] = t_

                class _W1:
                    def get(self, off):
                        return W1[off]
                wpl = _W1()
                for b in range(3):
                    lat = b > 0
                    front(pt, l, b)
                    rope_cols = slice((b - 1) * 512, b * 512) if lat else None
                    hrhs = lambda kk: HT[:, kk, :]
                    if STOP == "front":
                        return
                    w = wpl.get(0)
                    w4 = w[:, 0:1024].rearrange("p (c k m) -> p c k m", c=2, k=8)
                    qitems = []
                    for kv in range(2):
                        if not lat:
                            def dest(XN, kv=kv):
                                S.dma("sp", o_ka.ap()[(l * 2 + kv) * 64:(l * 2 + kv + 1) * 64, :], XN[0:64, :],
                                      is_output=True)
                                k.copy(KACTX[0:64, kv, :], XN[0:64, :])
                        else:
                            def dest(t, kv=kv, b=b):
                                dst = KAW[0:64, kv, 128 + (b - 1) * 512:128 + b * 512]
                                k.tt(dst, t[0][0:64, :], t[1][0:64, :], ALU.add)
                                S.dma("sp", gc_in.ap()[0:64, KA_OFF + kv * 1024 + (b - 1) * 512:KA_OFF + kv * 1024 + b * 512], dst)
                        qitems.append(((w, kv * 512), 64, sm("kna", l)[0:64, :], dest))
                    qk_pipeline(pt, qitems, rope_cols, hrhs)
                    if STOP == "ka":
                        return
                    w = wpl.get(1024)
                    w4 = w[:, 0:4096].rearrange("p (c k m) -> p c k m", c=4, k=8)
                    qitems = []
                    for h in range(4):
                        if not lat:
                            def dest(XN, h=h):
                                S.dma("sp", o_kc.ap()[(l * 4 + h) * 128:(l * 4 + h + 1) * 128, :], XN[:, :], is_output=True)
                                k.copy(KCCTX[:, h, :], XN[:, :])
                        else:
                            def dest(t, h=h, b=b):
                                ks = KST[kst_i[0] % 2]
                                kst_i[0] += 1
                                k.tt(ks[:], t[0][:, :], t[1][:, :], ALU.add)
                                S.dma("sp", ga_in.ap()[:, KC_OFF + h * 1024 + (b - 1) * 512:KC_OFF + h * 1024 + b * 512], ks[:])
                        qitems.append((w4[:, h], 128, sm("knc", l), dest))
                    qk_pipeline(pt, qitems, rope_cols, hrhs)
                    if STOP == "kc":
                        return
                    w = wpl.get(5120)
                    w4 = w[:, 0:4096].rearrange("p (c k m) -> p c k m", c=4, k=8)
                    for ch in range(4):
                        ps = nb()
                        proj(ps, w4[:, ch], 8, 128, hrhs)
                        if b == 0:
                            k.act(XB[:, ch, 2:258], ps[:, 0:256], AF.Copy)
                            k.act(XB[:, ch, 261:517], ps[:, 256:512], AF.Copy)
                        else:
                            c0 = 520 + (b - 1) * 512
                            k.act(XB[:, ch, c0:c0 + 512], ps[:, :], AF.Copy)
                    if STOP == "xb":
                        return
                    w = wpl.get(9216)
                    w4 = w[:, 0:4096].rearrange("p (c k m) -> p c k m", c=4, k=8)
                    for ch in range(4):
                        ps = nb()
                        proj(ps, w4[:, ch], 8, 128, hrhs)
                        silu_to(pt, ps, 128, OSGB[:, ch, b * 512:(b + 1) * 512])
                    if STOP == "gb":
                        return
                    w = wpl.get(13312)
                    w2 = wpl.get(17408)
                    wv = w[:, 0:4096].rearrange("p (k n) -> p k n", k=8)
                    wva = w2[:, 0:1024].rearrange("p (k n) -> p k n", k=8)
                    for tt_ in range(4):
                        tok = slice(tt_ * 128, (tt_ + 1) * 128)
                        psc = nb((4, 5))
                        psa = banks[6]
                        for kk in range(8):
                            k.mm(psc[:, :], HT[:, kk, tok], wv[:, kk, :], start=(kk == 0), stop=(kk == 7))
                        if 'a' in KV:
                            continue
                        for kk in range(8):
                            k.mm(psa[:, 0:128], HT[:, kk, tok], wva[:, kk, :], start=(kk == 0), stop=(kk == 7))
                        if 'b' in KV:
                            continue
                        if not lat:
                            vt = VTMP[0]
                            if 'e' not in KV:
                                k.act(vt[:], psc[:, :], AF.Copy)
                            if 'c' not in KV:
                                S.dma("sp", o_vc.ap()[l * 512 + tt_ * 128:l * 512 + (tt_ + 1) * 128, :], vt[:], is_output=True)
                            if 'f' not in KV:
                                k.copy(VCCTX[:, tt_, :], psc[:, :])
                            if 'd' in KV:
                                continue
                            k.act(VATMP[:], psa[:, 0:128], AF.Copy)
                            if 'c' not in KV:
                                S.dma("sp", o_va.ap()[l * 512 + tt_ * 128:l * 512 + (tt_ + 1) * 128, :], VATMP[:], is_output=True)
                            k.copy(VACTX[:, tt_, :], psa[:, 0:128])
                            k.copy(VACTXS[:, tt_, 0:64], psa[:, 64:128])
                            k.copy(VACTXS[:, tt_, 64:128], psa[:, 0:64])
                        else:
                            Tt = (b - 1) * 4 + tt_
                            vs = VST[tt_ % 2]
                            k.act(vs[:], psc[:, :], AF.Copy)
                            S.dma("sp", gb_in.ap()[:, VC_OFF + Tt * 512:VC_OFF + (Tt + 1) * 512], vs[:])
                            k.copy(VAW[:, 1 + Tt, :], psa[:, 0:128])
                            k.copy(VAWS[:, 1 + Tt, 0:64], psa[:, 64:128])
                            k.copy(VAWS[:, 1 + Tt, 64:128], psa[:, 0:64])
                            S.dma("sp", gc_in.ap()[:, VA_OFF + Tt * 128:VA_OFF + (Tt + 1) * 128], VAW[:, 1 + Tt, :])
                    if STOP == "v":
                        return
                if STOP == "p1":
                    return
                G2S = T(ph, [128, 12], F32, "G2S")
                for ch in range(4):
                    k.copy(G2S[:, ch * 3:ch * 3 + 1], XB[:, ch, 520:521])
                    k.copy(G2S[:, ch * 3 + 1:ch * 3 + 3], XB[:, ch, 520 + 1022:520 + 1024])
                S.dma("sp", g2_in.ap(), G2S[:])
                S.collective(GROUPS, g2_in, g2_out)
                S.collective(GROUPS, gc_in, gc_out)
                S.collective(GROUPS, ga_in, ga_out)
                S.collective(GROUPS, gb_in, gb_out)
                S.barrier(skip_cc=True)
              if STOP == "cc":
                  return
              with contextlib.ExitStack() as ph:
                GL = T(ph, [128, 16, 128], BF16, "GL")
                S.dma("pool", GL[:].rearrange("p a b -> p (a b)"), glru_in.ap()[:, l * 2048:(l + 1) * 2048], max_dma_last_dim=4096)
                Us = [T(ph, [128, 1024], F32, "U") for _ in range(2)]
                UBs = [T(ph, [128, 1024], BF16, "UB") for _ in range(2)]
                AA = [T(ph, [128, 1024], F32, "AA") for _ in range(2)]
                BB = [T(ph, [128, 1024], F32, "BB") for _ in range(2)]
                HH = [T(ph, [128, 1024], F32, "HH") for _ in range(2)]
                LT = [[T(ph, [128, 512], F32, "LT") for _ in range(4)] for _ in range(2)]
                ONE_T = T(ph, [128, 1], F32, "ONE_T")
                k.memset(ONE_T[:], 1.0)
                u_i = [0]
                HAL = T(ph, [128, 4, 12], F32, "HAL")
                TOT = T(ph, [128, 16], F32, "TOT")
                RS = T(ph, [128, 8], F32, "RS")
                TOTG = T(ph, [128, 4, 16], F32, "TOTG")
                CH = T(ph, [128, 40], F32, "CH")
                HIN = T(ph, [128, 8], F32, "HIN")
                sel = sm("sel")

                def lat_halo():
                    S.dma("sp", HAL[:], g2_out.ap().rearrange("(r p) c -> p r c", p=128))
                    for ch in range(4):
                        pre = XB[:, ch, 518:520]
                        post = XB[:, ch, 1544:1545]
                        for r in range(4):
                            k.stt(pre, HAL[:, r, ch * 3 + 1:ch * 3 + 3], sel[:, r:r + 1], pre, ALU.mult, ALU.add)
                            k.stt(post, HAL[:, r, ch * 3:ch * 3 + 1], sel[:, 4 + r:5 + r], post, ALU.mult, ALU.add)

                def lru_gates(ch, s0, Tn, U, UB, want_rsum):
                    npc = (Tn + 511) // 512
                    for pc in range(npc):
                        n = min(512, Tn)
                        cs = slice(pc * 512, pc * 512 + n)
                        seqs = []
                        for dr in range(2):
                            nba = D[:, D_NBA + dr * 4 + ch:D_NBA + dr * 4 + ch + 1]
                            nbx = D[:, D_NBX + dr * 4 + ch:D_NBX + dr * 4 + ch + 1]
                            cl = D[:, D_CL + dr * 4 + ch:D_CL + dr * 4 + ch + 1]
                            c2 = D[:, D_C2 + dr * 4 + ch:D_C2 + dr * 4 + ch + 1]
                            L0, L1, L2, L3 = [t[:, 0:n] for t in LT[dr]]
                            zr = banks[(pc % 2) * 4 + dr * 2]
                            zi = banks[(pc % 2) * 4 + dr * 2 + 1]
                            ops = [
                                lambda zr=zr, dr=dr: k.mm(zr[:, 0:n], GL[:, (dr * 2 + 0) * 4 + ch, :], UB[:, cs]),
                                lambda zi=zi, dr=dr: k.mm(zi[:, 0:n], GL[:, (dr * 2 + 1) * 4 + ch, :], UB[:, cs]),
                                lambda L0=L0, zr=zr, nba=nba: k.act(L0, zr[:, 0:n], AF.Exp, scale=-1.0, bias=nba),
                                lambda L1=L1, zi=zi, nbx=nbx: k.act(L1, zi[:, 0:n], AF.Exp, scale=-1.0, bias=nbx),
                                lambda L0=L0: k.act(L0, L0, AF.Ln, bias=ONE_T[:]),
                                lambda L1=L1: k.act(L1, L1, AF.Ln, bias=ONE_T[:]),
                                lambda L0=L0: k.act(L0, L0, AF.Exp, scale=-1.0),
                                lambda L0=L0, dr=dr, cl=cl: k.act(AA[dr][:, cs], L0, AF.Exp, scale=cl),
                                lambda L2=L2, L0=L0, c2=c2: k.ts(L2, L0, c2, None, op0=ALU.mult),
                                lambda L3=L3, L2=L2: k.ts(L3, L2, 1.0 / 24.0, 1.0 / 6.0, op0=ALU.mult, op1=ALU.add),
                                lambda L3=L3, L2=L2: k.tt(L3, L3, L2, ALU.mult),
                                lambda L3=L3, L2=L2: k.stt(L3, L3, 0.5, L2, ALU.add, ALU.mult),
                                lambda L3=L3, L2=L2: k.stt(L3, L3, 1.0, L2, ALU.add, ALU.mult),
                                lambda L3=L3: k.act(L3, L3, AF.Ln, scale=-1.0),
                                lambda L3=L3, L1=L1: k.stt(L3, L3, 0.5, L1, ALU.mult, ALU.subtract),
                                lambda L3=L3: k.act(L3, L3, AF.Exp),
                                lambda L3=L3, dr=dr: k.tt(BB[dr][:, cs], L3, U[:, cs], ALU.mult),
                            ]
                            if want_rsum:
                                ops.insert(8, lambda L0=L0, dr=dr, pc=pc: k.rsum(RS[:, dr * 2 + pc:dr * 2 + pc + 1], L0))
                            seqs.append(ops)
                        for o0, o1 in zip(*seqs):
                            o0()
                            o1()

                def conv_u(ch, s0, Tn):
                    U = Us[u_i[0] % 2]
                    UB = UBs[u_i[0] % 2]
                    u_i[0] += 1
                    cw = sm("convw", l)
                    k.ts(U[:, 0:Tn], XB[:, ch, s0:s0 + Tn], cw[:, ch * 4:ch * 4 + 1], sm("convb", l)[:, ch:ch + 1],
                         op0=ALU.mult, op1=ALU.add)
                    for j in range(1, 4):
                        k.stt(U[:, 0:Tn], XB[:, ch, s0 + j:s0 + j + Tn], cw[:, ch * 4 + j:ch * 4 + j + 1], U[:, 0:Tn],
                              ALU.mult, ALU.add)
                    k.copy(UB[:, 0:Tn], U[:, 0:Tn])
                    return U, UB

                def do_scan(dr, Tn, init):
                    if dr == 0:
                        k.scan(HH[0][:, 0:Tn], AA[0][:, 0:Tn], BB[0][:, 0:Tn], init)
                    else:
                        k.scan(HH[1][:, 0:Tn][:, ::-1], AA[1][:, 0:Tn][:, ::-1], BB[1][:, 0:Tn][:, ::-1], init)

                def lru_seg(ch, si):
                        s0, Tn, tk0 = SEGS[si]
                        U, UB = conv_u(ch, s0, Tn)
                        lru_gates(ch, s0, Tn, U, UB, si == 2)
                        for dr in range(2):
                            do_scan(dr, Tn, 0.0)
                            endc = Tn - 1 if dr == 0 else 0
                            if si < 2:
                                col = ((l * 2 + si) * 2 + dr) * 4 + ch
                                k.copy(STO[:, col:col + 1], HH[dr][:, endc:endc + 1])
                            else:
                                k.copy(TOT[:, dr * 8 + 4 + ch:dr * 8 + 5 + ch], HH[dr][:, endc:endc + 1])
                                k.tt(RS[:, 4 + dr:5 + dr], RS[:, dr * 2:dr * 2 + 1], RS[:, dr * 2 + 1:dr * 2 + 2], ALU.add)
                                k.act(TOT[:, dr * 8 + ch:dr * 8 + ch + 1], RS[:, 4 + dr:5 + dr], AF.Exp,
                                      scale=D[:, D_CL + dr * 4 + ch:D_CL + dr * 4 + ch + 1])
                                sl = slice((ch * 2 + dr) * 1024, (ch * 2 + dr + 1) * 1024)
                                S.dma("sp", spill_a.ap()[:, sl], AA[dr][:, :])
                                S.dma("sp", spill_b.ap()[:, sl], BB[dr][:, :])
                        if si < 2:
                            k.tt(HH[0][:, 0:Tn], HH[0][:, 0:Tn], HH[1][:, 0:Tn], ALU.add, eng="pool")
                            k.tt(OSGB[:, ch, tk0:tk0 + Tn], HH[0][:, 0:Tn], OSGB[:, ch, tk0:tk0 + Tn], ALU.mult, eng="pool")
                for ch in range(2):
                    lru_seg(ch, 0)
                    lru_seg(ch, 1)
                lat_halo()
                for ch in range(4):
                    lru_seg(ch, 2)
                S.dma("sp", g3_in.ap(), TOT[:])
                S.collective(GROUPS, g3_in, g3_out)
                for ch in range(2, 4):
                    lru_seg(ch, 0)
                    lru_seg(ch, 1)
                S.dma("sp", TOTG[:], g3_out.ap().rearrange("(r p) c -> p r c", p=128))
                h0 = sm("h0", l)
                k.copy(CH[:, 0:4], h0[:, 0:4])
                for r in range(3):
                    k.tt(CH[:, (r + 1) * 4:(r + 2) * 4], TOTG[:, r, 0:4], CH[:, r * 4:(r + 1) * 4], ALU.mult)
                    k.tt(CH[:, (r + 1) * 4:(r + 2) * 4], CH[:, (r + 1) * 4:(r + 2) * 4], TOTG[:, r, 4:8], ALU.add)
                k.copy(CH[:, 16 + 12:16 + 16], h0[:, 4:8])
                for r in (3, 2, 1):
                    k.tt(CH[:, 16 + (r - 1) * 4:16 + r * 4], TOTG[:, r, 8:12], CH[:, 16 + r * 4:16 + (r + 1) * 4], ALU.mult)
                    k.tt(CH[:, 16 + (r - 1) * 4:16 + r * 4], CH[:, 16 + (r - 1) * 4:16 + r * 4], TOTG[:, r, 12:16], ALU.add)
                k.memset(HIN[:], 0.0)
                for r in range(4):
                    k.stt(HIN[:, 0:4], CH[:, r * 4:(r + 1) * 4], sel[:, 8 + r:9 + r], HIN[:, 0:4], ALU.mult, ALU.add)
                    k.stt(HIN[:, 4:8], CH[:, 16 + r * 4:16 + (r + 1) * 4], sel[:, 8 + r:9 + r], HIN[:, 4:8], ALU.mult, ALU.add)
                s0, Tn, tk0 = SEGS[2]
                for ch in range(4):
                    for dr in range(2):
                        sl = slice((ch * 2 + dr) * 1024, (ch * 2 + dr + 1) * 1024)
                        S.dma("sp", AA[dr][:, :], spill_a.ap()[:, sl])
                        S.dma("sp", BB[dr][:, :], spill_b.ap()[:, sl])
                        do_scan(dr, Tn, HIN[:, dr * 4 + ch:dr * 4 + ch + 1])
                    k.tt(HH[0][:, :], HH[0][:, :], HH[1][:, :], ALU.add, eng="pool")
                    k.tt(OSGB[:, ch, tk0:tk0 + Tn], HH[0][:, :], OSGB[:, ch, tk0:tk0 + Tn], ALU.mult, eng="pool")
                S.barrier()

            if STOP == "lru":
                return
            with contextlib.ExitStack() as ph:
                pt = common_tiles(ph)
                HT = pt["HT"]
                QA = T(ph, [128, 8, 512], BF16, "QA")
                OSA = T(ph, [64, 8, 512], BF16, "OSA")
                QCZ = T(ph, [128, 4, 2, 512], BF16, "QCZ")
                OSC = T(ph, [128, 4, 512], BF16, "OSC")
                YT = T(ph, [128, 8, 512], BF16, "YT")
                KCALL = T(ph, [128, 36 * 128], BF16, "KCALL")
                VCALL = T(ph, [128, 36 * 128], BF16, "VCALL")
                plan = []
                for b_ in range(3):
                    for o_ in (0, 2048, 4096, 8192, 10240, 12288):
                        n_ = 2048 if o_ in (0, 2048, 8192, 10240) else 4096
                        plan.append(('L', wp2[l].ap()[:, o_:o_ + n_], 128, n_, o_))
                    plan.append(('KV',))
                    for cc_ in range(8):
                        o_ = WP2_Q + cc_ * 4096
                        plan.append(('L', wp2[l].ap()[:, o_:o_ + 4096], 128, 4096, o_))
                        plan.append(('L', wba[l].ap()[:, cc_ * 1024:(cc_ + 1) * 1024], 64, 1024, -1 - cc_))
                    for g_ in range(2):
                        o_ = WP2_Q + 8 * 4096 + g_ * 4096
                        plan.append(('L', wp2[l].ap()[:, o_:o_ + 4096], 128, 4096, o_))
                S.split[KCALL.name] = 2048
                S.split[VCALL.name] = 2048
                for t_ in WSL + [KCALL, VCALL]:
                    k.memset(t_[:, 2048:2112], 0.0)
                wpl = WPlan(S, WSL, [KCALL, VCALL], plan)
                CKA = T(ph, [128, 2, 512], BF16, "CKA")
                CVA = T(ph, [128, 4, 128], BF16, "CVA")
                CVAS = T(ph, [128, 4, 128], BF16, "CVAS")
                ET = [T(ph, [128, 512], BF16, "ET") for _ in range(3)]
                PT1, PT2 = pt["QX"][0], pt["QX"][1]
                PT3 = pt["RSTD"][:, 0:256]
                PT5 = pt["RSTD"][:, 256:512]
                PT4 = pt["SQ"][0][:, 0:256]
                MT = [pt["QX"][0], pt["QX"][1], pt["QL"][0]]
                MT2 = [pt["QL"][1], pt["QN"][0], pt["QN"][1]]
                YACC = pt["RSTD"]
                k.memset(QCZ[:], 0.0)
                k.memset(CKA[:], 0.0)
                k.memset(QA[64:128, :, :], 0.0)
                S.dma("pool", CKA[0:64, :, :].rearrange("p a b -> p (a b)"), cka_in[l].ap(), max_dma_last_dim=4096)
                S.dma("pool", CVA[:].rearrange("p a b -> p (a b)"), cva_in[l].ap(), max_dma_last_dim=4096)
                cva3 = cva_in[l].ap().rearrange("p (t n) -> p t n", n=128)
                S.dma("pool", CVAS[:, :, 0:64], cva3[:, :, 64:128])
                S.dma("pool", CVAS[:, :, 64:128], cva3[:, :, 0:64])
                g1a, g1b, g1c = ga_out.ap(), gb_out.ap(), gc_out.ap()
                HALK = T(ph, [64, 2, 4, 2, 128], BF16, "HALK")
                HALV = T(ph, [128, 2, 4, 128], BF16, "HALV")
                sel = sm("sel")

                def build_halos():
                    for kv in range(2):
                        base = KA_OFF + kv * 1024
                        S.dma("sp", HALK[:, 0, :, kv, :], g1c[:, base + 896:base + 1024].rearrange("(r p) c -> p r c", p=128)[0:64])
                        S.dma("sp", HALK[:, 1, :, kv, :], g1c[:, base:base + 128].rearrange("(r p) c -> p r c", p=128)[0:64])
                    S.dma("sp", HALV[:, 0, :, :], g1c[:, VA_OFF + 896:VA_OFF + 1024].rearrange("(r p) c -> p r c", p=128))
                    S.dma("sp", HALV[:, 1, :, :], g1c[:, VA_OFF:VA_OFF + 128].rearrange("(r p) c -> p r c", p=128))
                    k.memset(KAW[:, :, 0:128], 0.0)
                    k.memset(KAW[:, :, 1152:1280], 0.0)
                    k.memset(VAW[:, 0, :], 0.0)
                    k.memset(VAW[:, 9, :], 0.0)
                    k.memset(VAWS[:, 0, :], 0.0)
                    k.memset(VAWS[:, 9, :], 0.0)
                    for r in range(4):
                        for kv in range(2):
                            k.stt(KAW[0:64, kv, 0:128], HALK[:, 0, r, kv, :], sel[0:64, r:r + 1], KAW[0:64, kv, 0:128], ALU.mult, ALU.add)
                            k.stt(KAW[0:64, kv, 1152:1280], HALK[:, 1, r, kv, :], sel[0:64, 4 + r:5 + r], KAW[0:64, kv, 1152:1280], ALU.mult, ALU.add)
                        k.stt(VAW[:, 0, :], HALV[:, 0, r, :], sel[:, r:r + 1], VAW[:, 0, :], ALU.mult, ALU.add)
                        k.stt(VAW[:, 9, :], HALV[:, 1, r, :], sel[:, 4 + r:5 + r], VAW[:, 9, :], ALU.mult, ALU.add)
                        for (d0, s0_) in ((0, 64), (64, 0)):
                            k.stt(VAWS[:, 0, d0:d0 + 64], HALV[:, 0, r, s0_:s0_ + 64], sel[:, r:r + 1], VAWS[:, 0, d0:d0 + 64], ALU.mult, ALU.add)
                            k.stt(VAWS[:, 9, d0:d0 + 64], HALV[:, 1, r, s0_:s0_ + 64], sel[:, 4 + r:5 + r], VAWS[:, 9, d0:d0 + 64], ALU.mult, ALU.add)


                et_i = [0]
                acc_i = [0]
                esum_i = [0]
                S.split[HALV.name] = 512
                _hv = HALV[:].rearrange("p a r c -> p (a r c)")
                ESUM = [_hv[:, 0:512], _hv[:, 512:1024]]

                def attend_all(units, order=None):
                    if order is None:
                        order = [(ui, j) for ui, u in enumerate(units) for j in range(len(u[1]))]
                    first = {}
                    last = {}
                    for si_, (ui_, j_) in enumerate(order):
                        first.setdefault(ui_, si_)
                        last[ui_] = si_
                    steps = [(ui_, j_, len(units[ui_][1])) for (ui_, j_) in order]
                    LOOK = 2
                    pend = []
                    deferred = []
                    role = [None] * len(steps)
                    si_ = 0
                    while si_ < len(steps):
                        if si_ + 1 < len(steps) and steps[si_ + 1][0] == steps[si_][0]:
                            role[si_], role[si_ + 1] = 'first', 'second'
                            si_ += 2
                        else:
                            role[si_] = 'single'
                            si_ += 1
                    dfirst, dlast = {}, {}
                    for si_, (ui_, j_, n_) in enumerate(steps):
                        if role[si_] != 'first':
                            dfirst.setdefault(ui_, si_)
                            dlast[ui_] = si_
                    pendD = []
                    prev_et = [None]

                    def issue(si):
                        ui, j, n = steps[si]
                        ps = nb((0, 1, 2))
                        units[ui][0](ps, units[ui][1][j][0])
                        pend.append(ps)
                    for si in range(min(LOOK, len(steps))):
                        issue(si)
                    for si, (ui, j, n) in enumerate(steps):
                        qk_fn, tiles, Mv, post_fn = units[ui]
                        _, vap, mask = tiles[j]
                        ps = pend.pop(0)
                        et = ET[et_i[0] % 3]
                        et_i[0] += 1
                        k.act(et[:], ps[:], AF.Exp, scale=SCALE)
                        if mask is not None:
                            k.tt(et[:], et[:], mask, ALU.mult, eng=("pool" if si % 2 else "dve"))
                        a_ = (acc_i[0] + ui) % 2
                        accO = banks[3 + 2 * a_]
                        accD = banks[4 + 2 * a_]
                        dsrc = None
                        if role[si] == 'second':
                            es = ESUM[esum_i[0] % 2]
                            esum_i[0] += 1
                            k.tt(es, prev_et[0][:], et[:], ALU.add)
                            dsrc = es
                        elif role[si] == 'single':
                            dsrc = et[:]
                        prev_et[0] = et
                        while deferred and deferred[0][0] <= si:
                            deferred.pop(0)[1]()
                        if si + LOOK < len(steps):
                            issue(si + LOOK)
                        k.mm(accO[0:Mv, :], vap, et[:], start=(si == first[ui]), stop=(si == last[ui]))
                        while pendD and pendD[0][0] <= si:
                            pendD.pop(0)[1]()
                        if dsrc is not None:
                            fn = (lambda accD=accD, Mv=Mv, dsrc=dsrc, st_=(si == dfirst[ui]), sp_=(si == dlast[ui]):
                                  k.mm(accD[0:Mv, :], ONES[:, 0:Mv], dsrc, start=st_, stop=sp_))
                            if role[si] == 'single':
                                fn()
                            else:
                                pendD.append((si + 1, fn))
                        if si == last[ui]:
                            deferred.append((si + 3, lambda post_fn=post_fn, accO=accO, accD=accD: post_fn(accO, accD)))
                    for _, fn in pendD:
                        fn()
                    for _, fn in deferred:
                        fn()
                    acc_i[0] += len(units)

                for b in range(3):
                    lat = b > 0
                    v = 1 if lat else 0
                    cols = slice(b * 512, (b + 1) * 512)
                    front(pt, l, b)
                    rope_cols = slice((b - 1) * 512, b * 512) if lat else None
                    hrhs = lambda kk: HT[:, kk, :]
                    for g in range(2):
                        w = wpl.get(g * 2048)
                        w4 = w[:, 0:2048].rearrange("p (c k m) -> p c k m", c=4, k=8)
                        qitems = []
                        for c4 in range(4):
                            h = g * 4 + c4
                            if not lat:
                                def dest(XN, h=h):
                                    k.copy(QA[0:64, h, :], XN[0:64, :])
                            else:
                                def dest(t, h=h):
                                    k.tt(QA[0:64, h, :], t[0][0:64, :], t[1][0:64, :], ALU.add)
                            qitems.append(((w, c4 * 512), 64, sm("qna", l)[0:64, :], dest))
                        qk_pipeline(pt, qitems, rope_cols, hrhs)
                    w = wpl.get(4096)
                    w4 = w[:, 0:4096].rearrange("p (c k m) -> p c k m", c=4, k=8)
                    qitems = []
                    for h in range(4):
                        if not lat:
                            def dest(XN, h=h):
                                k.copy(QCZ[0:64, h, 0, :], XN[0:64, :])
                                k.copy(QCZ[64:128, h, 1, :], XN[64:128, :])
                        else:
                            def dest(t, h=h):
                                k.tt(QCZ[0:64, h, 0, :], t[0][0:64, :], t[1][0:64, :], ALU.add)
                                k.tt(QCZ[64:128, h, 1, :], t[0][64:128, :], t[1][64:128, :], ALU.add)
                        qitems.append((w4[:, h], 128, sm("qnc", l), dest))
                    qk_pipeline(pt, qitems, rope_cols, hrhs)
                    for g in range(2):
                        w = wpl.get(8192 + g * 2048)
                        w4 = w[:, 0:2048].rearrange("p (c k m) -> p c k m", c=4, k=8)
                        for c4 in range(4):
                            h = g * 4 + c4
                            ps = nb()
                            proj(ps, (w, c4 * 512), 8, 64, hrhs)
                            silu_to(pt, ps, 64, OSA[0:64, h, :])
                    w = wpl.get(12288)
                    w4 = w[:, 0:4096].rearrange("p (c k m) -> p c k m", c=4, k=8)
                    for h in range(4):
                        ps = nb()
                        proj(ps, w4[:, h], 8, 128, hrhs)
                        silu_to(pt, ps, 128, OSC[:, h, :])

                    if b == 1:
                        build_halos()
                    wpl.kv_begin()
                    nqb = 2
                    units = []
                    for qb in range(nqb):
                        qc = slice(qb * 256, (qb + 1) * 256)
                        for hp in range(4):
                            kv = hp // 2
                            h0_ = hp * 2
                            tiles = []
                            if not lat:
                                for kt in range(2):
                                    c0 = qb * 256 + kt * 128
                                    tiles.append((KACTX[:, kv, c0:c0 + 128], (VACTX if kv == 0 else VACTXS)[:, qb * 2 + kt, :], None))
                            else:
                                t0 = (b - 1) * 4 + qb * 2
                                for wdx in range(4):
                                    mi = [0 if t0 == 0 else 1, 2, 3, 5 if t0 == 6 else 4][wdx]
                                    tiles.append((KAW[:, kv, (t0 + wdx) * 128:(t0 + wdx + 1) * 128],
                                                  (VAW if kv == 0 else VAWS)[:, t0 + wdx, :], MSK[:, mi, :]))
                                for c in range(4):
                                    tiles.append((CKA[:, kv, c * 128:(c + 1) * 128], (CVA if kv == 0 else CVAS)[:, c, :], None))

                            def qk_a(ps, kap, h0_=h0_, qc=qc):
                                k.mm(ps[:, :], kap, QA[:, h0_:h0_ + 2, qc])

                            def post_a(accO, accD, h0_=h0_, qc=qc):
                                for hh in range(2):
                                    cs = slice(hh * 256, (hh + 1) * 256)
                                    k.act(PT1[0:64, cs], accD[0:64, cs], AF.Ln,
                                          bias=D[0:64, D_ESINK + h0_ + hh:D_ESINK + h0_ + hh + 1])
                                k.act(PT2[0:64, :], PT1[0:64, :], AF.Exp, scale=-1.0)
                                k.tt(PT1[0:64, :], accO[0:64, :], PT2[0:64, :], ALU.mult)
                                k.tt(OSA[0:64, h0_:h0_ + 2, qc], PT1[0:64, :].rearrange("p (a b) -> p a b", a=2),
                                     OSA[0:64, h0_:h0_ + 2, qc], ALU.mult)
                            units.append((qk_a, tiles, 128, post_a))
                    attend_all(units)

                    for h in range(4):
                        units = []
                        if lat:
                            src_k = g1a[:, KC_OFF + h * 1024:KC_OFF + (h + 1) * 1024].rearrange("(r p) c -> p r c", p=128)
                            S.dma("sp", KCALL[:, 0:2048].rearrange("p (r c) -> p r c", r=2), src_k[:, 0:2, :])
                            for r in (0, 1):
                                S.dma("sp", VCALL[:, r * 1024:(r + 1) * 1024].rearrange("p (t n) -> p t n", n=128),
                                      g1b[r * 128:(r + 1) * 128, VC_OFF:VC_OFF + 4096].rearrange("p (t n) -> p t n", n=512)[:, :, h * 128:(h + 1) * 128])
                            S.dma("sp", KCALL[:, 2048:4096].rearrange("p (r c) -> p r c", r=2), src_k[:, 2:4, :])
                            for r in (2, 3):
                                S.dma("sp", VCALL[:, r * 1024:(r + 1) * 1024].rearrange("p (t n) -> p t n", n=128),
                                      g1b[r * 128:(r + 1) * 128, VC_OFF:VC_OFF + 4096].rearrange("p (t n) -> p t n", n=512)[:, :, h * 128:(h + 1) * 128])
                            S.dma("pool", KCALL[:, 4096:4608], ckc_in[l].ap()[:, h * 512:(h + 1) * 512], max_dma_last_dim=2048)
                            S.dma("pool", VCALL[:, 4096:4608].rearrange("p (t n) -> p t n", n=128),
                                  cvc_in[l].ap().rearrange("p (t n) -> p t n", n=512)[:, :, h * 128:(h + 1) * 128])
                        for qb in range(2):
                            qc = slice(qb * 256, (qb + 1) * 256)
                            tiles = []
                            if not lat:
                                for kt in range(2):
                                    c0 = qb * 256 + kt * 128
                                    tiles.append((KCCTX[:, h, c0:c0 + 128], VCCTX[:, qb * 2 + kt, h * 128:(h + 1) * 128], None))
                            else:
                                for j in range(36):
                                    tiles.append((KCALL[:, j * 128:(j + 1) * 128], VCALL[:, j * 128:(j + 1) * 128], None))

                            def qk_c(ps, kap, h=h, qc=qc):
                                k.mm(ps[:, :], kap, QCZ[:, h, :, qc])

                            def post_c(accO, accD, h=h, qc=qc):
                                k.act(PT1[:, :], accD[:, :], AF.Ln)
                                k.act(PT1[:, :], PT1[:, :], AF.Exp, scale=-1.0)
                                k.tt(PT2[:, :], accO[:, :], PT1[:, :], ALU.mult)
                                k.stt(PT3, PT2[:, 256:512], D[:, D_NLAM:D_NLAM + 1], PT2[:, 0:256], ALU.mult, ALU.add)
                                k.tt(PT4, PT3, PT3, ALU.mult)
                                ss = banks[7]
                                k.mm(ss[:, 0:256], ONES[:, :], PT4)
                                k.act(PT5, ss[:, 0:256], AF.Ln, scale=1.0 / 128.0, bias=EPST[:])
                                k.act(PT5, PT5, AF.Exp, scale=-0.5)
                                k.stt(PT3, PT3, D[:, D_SUBG:D_SUBG + 1], PT5, ALU.mult, ALU.mult)
                                k.tt(OSC[:, h, qc], PT3, OSC[:, h, qc], ALU.mult)
                            units.append((qk_c, tiles, 128, post_c))
                        if lat:
                            order = ([(0, j) for j in range(16)] + [(1, j) for j in range(16)]
                                     + [(0, j) for j in range(16, 36)] + [(1, j) for j in range(16, 36)])
                            attend_all(units, order)
                        else:
                            attend_all(units)

                    wpl.kv_end()
                    for cc in range(8):
                        base = WP2_Q + cc * 4096
                        w = wpl.get(base)
                        wa = wpl.get(-1 - cc)
                        wm = w[:, 0:3072].rearrange("p (i k m) -> p i k m", i=3, k=8)
                        wb = w[:, 3072:3584].rearrange("p (k m) -> p k m", k=4)
                        wc = w[:, 3584:4096].rearrange("p (k m) -> p k m", k=4)
                        wa3 = wa[0:64, 0:1024].rearrange("p (k m) -> p k m", k=8)
                        pa, pb, pc = (banks[0], banks[1], banks[2]) if cc % 2 == 0 else (banks[5], banks[6], banks[7])
                        zs = [banks[3], banks[4], banks[3]]
                        proj(zs[0], wm[:, 0], 8, 128, hrhs)
                        proj(zs[1], wm[:, 1], 8, 128, hrhs)
                        k.act(MT[0][:], zs[0][:], AF.Exp, scale=-1.0)
                        proj(zs[2], wm[:, 2], 8, 128, hrhs)
                        k.act(MT[1][:], zs[1][:], AF.Exp, scale=-1.0)
                        k.act(MT[2][:], zs[2][:], AF.Exp, scale=-1.0)
                        proj(pa, wa3, 8, 128, lambda kk: OSA[0:64, kk, :], P=64)
                        proj(pb, wb, 4, 128, lambda kk: OSGB[:, kk, cols])
                        proj(pc, wc, 4, 128, lambda kk: OSC[:, kk, :])
                        for i, pbr in enumerate((pa, pb, pc)):
                            k.act(MT[i][:], MT[i][:], AF.Ln, bias=ONE_P[:])
                            k.act(MT[i][:], MT[i][:], AF.Exp, scale=-1.0)
                            k.tt(MT2[i][:], pbr[:], MT[i][:], ALU.mult)
                        k.tt(YACC[:], MT2[0][:], MT2[1][:], ALU.add)
                        k.tt(YT[:, cc, :], YACC[:], MT2[2][:], ALU.add)
                    for g in range(2):
                        base = WP2_Q + 8 * 4096 + g * 4096
                        w = wpl.get(base)
                        w4 = w[:, 0:4096].rearrange("p (c k m) -> p c k m", c=4, k=8)
                        for c4 in range(4):
                            cc = g * 4 + c4
                            ps = nb((6, 7))
                            proj(ps, w4[:, c4], 8, 128, lambda kk: YT[:, kk, :])
                            k.stt(XT[:, cc, cols], ps[:, :], MODV[l][:, 16 + cc, v:v + 1], XT[:, cc, cols], ALU.mult, ALU.add)
                S.barrier()

        layers()
        S.barrier()
        S.dma("sp", yT_out.ap().rearrange("(k p) t -> p k t", p=128), XT[:], is_output=True)
        S.dma("sp", o_st.ap(), STO[:], is_output=True)
        S.finish()
        S.replay()
    return nc


def _fm(w, M):
    K_, n = w.shape
    a = w.reshape(K_ // 128, 128, n // M, M).transpose(1, 2, 0, 3)
    return np.ascontiguousarray(a).reshape(128, -1)


_PROG = None
import os
STOP = os.environ.get('KSTOP', '')
KV = os.environ.get('KV', '')


def kernel(**inp):
    global _PROG
    f32 = lambda a: np.ascontiguousarray(np.asarray(a, dtype=np.float32))
    I = {k_: f32(v) for k_, v in inp.items()}
    w_in = I["w_in"]
    shared = {}
    for l in range(DEPTH):
        W = w_in[l]
        q_a, k_a, v_a, g_a = W[:, 0:512], W[:, 512:640], W[:, 640:768], W[:, 768:1280]
        x_b, g_b = W[:, 1280:1792], W[:, 1792:2304]
        q_c, k_c, v_c, g_c = W[:, 2304:2816], W[:, 2816:3328], W[:, 3328:3840], W[:, 3840:4352]
        merge = W[:, 4352:7424]
        vtm = lambda w: np.ascontiguousarray(w.reshape(8, 128, -1).transpose(1, 0, 2)).reshape(128, -1)
        shared[f"wp1{l}"] = np.concatenate([_fm(k_a, 64), _fm(k_c, 128), _fm(x_b, 128), _fm(g_b, 128),
                                            vtm(v_c), vtm(v_a)], axis=1)
        parts = [_fm(q_a, 64), _fm(q_c, 128), _fm(g_a, 64), _fm(g_c, 128)]
        for cc in range(8):
            for i in range(3):
                parts.append(_fm(merge[:, i * 1024 + cc * 128:i * 1024 + (cc + 1) * 128], 128))
            parts.append(_fm(I["w_br_b"][l][:, cc * 128:(cc + 1) * 128], 128))
            parts.append(_fm(I["w_br_c"][l][:, cc * 128:(cc + 1) * 128], 128))
        parts.append(_fm(I["w_out"][l], 128))
        shared[f"wp2{l}"] = np.concatenate(parts, axis=1)
        wa = I["w_br_a"][l]
        shared[f"wba{l}"] = np.ascontiguousarray(
            wa.reshape(8, 64, 8, 128).transpose(1, 2, 0, 3)).reshape(64, -1)
        shared[f"wmod{l}"] = _fm(I["mod_w"][l], 128)
    gl = np.zeros((128, DEPTH, 2, 2, 4, 128), np.float32)
    for l in range(DEPTH):
        for dr in range(2):
            for gi, nm in enumerate(("lru_wa", "lru_wx")):
                Wg = I[nm][l, dr]
                for ch in range(4):
                    for hb in range(2):
                        gl[hb * 64:(hb + 1) * 64, l, dr, gi, ch, hb * 64:(hb + 1) * 64] = Wg[ch * 2 + hb]
    shared["glru"] = gl.reshape(128, -1)
    R = np.zeros((128, 128), np.float32)
    for dp in range(128):
        if dp % 32 < 16:
            R[dp + 16, dp] = -1.0
        else:
            R[dp - 16, dp] = 1.0
    shared["rmat"] = R
    inv = np.power(np.float32(10000.0), -np.arange(16, dtype=np.float32) / np.float32(16)).astype(np.float32)
    a_ = np.arange(128)[:, None]
    b_ = np.arange(128)[None, :]
    ge = (a_ >= b_).astype(np.float32)
    le = (a_ <= b_).astype(np.float32)
    one = np.ones((128, 128), np.float32)
    zero = np.zeros((128, 128), np.float32)
    M0 = np.concatenate([ge, zero], 1)
    M1 = np.concatenate([one, ge], 1)
    M2 = np.concatenate([le, one], 1)
    M3 = np.concatenate([zero, le], 1)

    in_maps = []
    for c in range(8):
        s, j = c // 4, c % 4
        m = dict(shared)
        xs = np.concatenate([I["x_prompt"][2 * c], I["x_prompt"][2 * c + 1],
                             I["x_sample"][s, 1024 * j:1024 * (j + 1)]], axis=0)
        m["xT"] = np.ascontiguousarray(xs.T)
        sm_ = np.zeros((128, NSM), np.float32)

        def put(key, arr):
            o, w = SM[key]
            sm_[:, o:o + w] = arr
        col = lambda v, n: np.ascontiguousarray(v.reshape(n, 128).T)
        for l in range(DEPTH):
            put(("ng", l), col(I["norm_g"][l], 8))
            put(("modb", l), col(I["mod_b"][l], 24))
            put(("qna", l), np.tile(I["qn_a"][l], 2)[:, None])
            put(("kna", l), np.tile(I["kn_a"][l], 2)[:, None])
            put(("qnc", l), np.tile(I["qn_c"][l], 2)[:, None])
            put(("knc", l), np.tile(I["kn_c"][l], 2)[:, None])
            cw = I["conv_w"][l]
            put(("convw", l), np.ascontiguousarray(cw.reshape(4, 4, 128).transpose(2, 1, 0)).reshape(128, 16))
            put(("convb", l), col(I["conv_b"][l], 4))
            for nm, key in (("lru_ba", "ba"), ("lru_bx", "bx"), ("lru_lam", "lam")):
                put((key, l), np.ascontiguousarray(I[nm][l].reshape(2, 4, 128).transpose(2, 0, 1)).reshape(128, 8))
            put(("subln", l), I["subln_c"][l][:, None])
            put(("sink", l), np.broadcast_to(I["sink_a"][l][None, :], (128, 8)))
            for nm, key in (("lam_q1", "lq1"), ("lam_k1", "lk1"), ("lam_q2", "lq2"), ("lam_k2", "lk2")):
                put((key, l), np.broadcast_to(I[nm][l][None, :], (128, 64)))
            put(("h0", l), np.ascontiguousarray(I["state_lru"][s, l].reshape(2, 4, 128).transpose(2, 0, 1)).reshape(128, 8))
        cc_ = np.stack([col(I["c_ctx"], 8), col(I["c"][s], 8)], axis=2).reshape(128, 16)
        put("c", cc_)
        sel = np.zeros((128, 12), np.float32)
        if j > 0:
            sel[:, j - 1] = 1.0
        if j < 3:
            sel[:, 4 + j + 1] = 1.0
        sel[:, 8 + j] = 1.0
        put("sel", sel)
        m["small"] = sm_
        t = 1024 * j + np.arange(1024)
        row = (t // 64).astype(np.float32)
        colp = (t % 64).astype(np.float32)
        ang = np.zeros((64, 1024), np.float32)
        for d in range(64):
            ang[d] = (row if d < 32 else colp) * inv[d % 16]
        ang = np.concatenate([ang, ang], 0)
        m["cosT"] = np.cos(ang).astype(np.float32)
        m["sinT"] = np.sin(ang).astype(np.float32)
        first = M0 if j > 0 else np.zeros_like(M0)
        last = M3 if j < 3 else np.zeros_like(M3)
        msk = np.stack([first, M0, M1, M2, M3, last], 0)
        msk = np.concatenate([msk, msk], 2)
        m["masks"] = np.ascontiguousarray(msk.transpose(1, 0, 2)).reshape(128, -1)
        for l in range(DEPTH):
            ck = I["cache_c_k"][s, l]
            m[f"ckc{l}"] = np.ascontiguousarray(ck.transpose(2, 3, 1, 0)).reshape(128, 4 * 512)
            cv = I["cache_c_v"][s, l]
            m[f"cvc{l}"] = np.ascontiguousarray(cv.reshape(4, 128, 512).transpose(1, 0, 2)).reshape(128, -1)
            ka = I["cache_a_k"][s, l]
            m[f"cka{l}"] = np.ascontiguousarray(ka.transpose(2, 1, 0)).reshape(64, -1)
            va = I["cache_a_v"][s, l]
            m[f"cva{l}"] = np.ascontiguousarray(va.reshape(4, 128, 128).transpose(1, 0, 2)).reshape(128, -1)
        in_maps.append(m)

    if _PROG is None:
        _PROG = build_program()
    res = run_bass_kernel_spmd(_PROG, in_maps, core_ids=list(range(8)))
    R_ = res.results

    y_prompt = np.zeros((16, 256, 1024), np.float32)
    y_sample = np.zeros((2, 4096, 1024), np.float32)
    n_ka = np.zeros((16, 2, 256, 2, 64), np.float32)
    n_va = np.zeros((16, 2, 256, 2, 64), np.float32)
    n_kc = np.zeros((16, 2, 256, 4, 2, 64), np.float32)
    n_vc = np.zeros((16, 2, 256, 4, 128), np.float32)
    n_st = np.zeros((16, 2, 2, 512), np.float32)
    for c in range(8):
        s, j = c // 4, c % 4
        r = R_[c]
        y = np.asarray(r["yT"]).T
        y_prompt[2 * c] = y[0:256]
        y_prompt[2 * c + 1] = y[256:512]
        y_sample[s, 1024 * j:1024 * (j + 1)] = y[512:]
        ka = np.asarray(r["o_ka"]).reshape(2, 2, 64, 2, 256)
        kc = np.asarray(r["o_kc"]).reshape(2, 4, 2, 64, 2, 256)
        va = np.asarray(r["o_va"]).reshape(2, 2, 256, 2, 64)
        vc = np.asarray(r["o_vc"]).reshape(2, 2, 256, 4, 128)
        stt_ = np.asarray(r["o_st"]).reshape(128, 2, 2, 2, 4)
        for sq in range(2):
            bi = 2 * c + sq
            n_ka[bi] = ka[:, :, :, sq, :].transpose(0, 3, 1, 2)
            n_kc[bi] = kc[:, :, :, :, sq, :].transpose(0, 4, 1, 2, 3)
            n_va[bi] = va[:, sq]
            n_vc[bi] = vc[:, sq]
            n_st[bi] = stt_[:, :, sq].transpose(1, 2, 3, 0).reshape(2, 2, 512)
    return (y_prompt, y_sample, n_ka, n_va, n_kc, n_vc, n_st)
```

```python
import contextlib
import math
import numpy as np
import concourse.bass as bass
import concourse.mybir as mybir
from concourse.bass_utils import run_bass_kernel_spmd

F32 = mybir.dt.float32
BF16 = mybir.dt.bfloat16
AF = mybir.ActivationFunctionType
ALU = mybir.AluOpType
AX = mybir.AxisListType

ENGS = ("pe", "act", "dve", "pool", "sp")

DEPTH = 2
SCALE = 0.125
EPS = 1e-6
NTOK = 1536
KC_OFF, VC_OFF, KA_OFF, VA_OFF = 0, 0, 0, 2048
XBW = 1545
SEGS = [(0, 256, 0), (259, 256, 256), (518, 1024, 512)]

_SM_LAYER = [("ng", 8), ("modb", 24), ("qna", 1), ("kna", 1), ("qnc", 1), ("knc", 1), ("convw", 16),
             ("convb", 4), ("ba", 8), ("bx", 8), ("lam", 8), ("subln", 1), ("sink", 8),
             ("lq1", 64), ("lk1", 64), ("lq2", 64), ("lk2", 64), ("h0", 8)]
SM = {}
_o = 0
for _l in range(DEPTH):
    for _n, _w in _SM_LAYER:
        SM[(_n, _l)] = (_o, _w)
        _o += _w
SM["c"] = (_o, 16); _o += 16
SM["sel"] = (_o, 12); _o += 12
NSM = _o

WP1N = 1024 + 4096 * 3 + 5120
WP2_Q = 16384
WP2N = WP2_Q + 8 * 4096 + 8192


class Res:
    __slots__ = ("name", "last_w", "readers")

    def __init__(self, name=""):
        self.name = name
        self.last_w = None
        self.readers = {}


class Sched:
    NDMA = 8

    def __init__(self, nc, stack):
        self.nc = nc
        self.streams = {e: [] for e in ENGS}
        self.count = {e: 0 for e in ENGS}
        self.seen = {e: {} for e in ENGS}
        self.sems = {}
        self.resmap = {}
        self.split = {}
        for e in ENGS:
            self.sems[e] = stack.enter_context(nc.semaphore("prog_" + e))
        self.dma_k = {}
        for q in ("sp", "act", "pool"):
            self.dma_k[q] = 0
            for i in range(self.NDMA):
                self.sems[("dma", q, i)] = stack.enter_context(nc.semaphore(f"dma_{q}_{i}"))
        self.sems["cc"] = stack.enter_context(nc.semaphore("cc"))
        self.cc_count = 0
        self.out_events = []

    def _named(self, name):
        r = self.resmap.get(name)
        if r is None:
            r = Res(name)
            self.resmap[name] = r
        return r

    def _res(self, x):
        if isinstance(x, Res):
            return x
        return self._named(x.tensor.name)

    def _resl(self, x):
        if isinstance(x, Res):
            return [x]
        name = x.tensor.name
        b = self.split.get(name)
        if b is None:
            return [self._named(name)]
        try:
            ap = [list(e) for e in x.ap]
            col0 = int(x.offset) % int(ap[0][0])
            hi = col0 + 1
            for st_, cnt in ap[1:]:
                assert st_ >= 0
                hi += (int(cnt) - 1) * int(st_)
        except Exception:
            col0, hi = 0, 1 << 30
        out = []
        if col0 < b:
            out.append(self._named(name + ".A"))
        if hi > b:
            out.append(self._named(name + ".B"))
        return out

    def _collect(self, reads, writes):
        waits = {}

        def add(ev):
            if ev is None:
                return
            k, v = ev
            if waits.get(k, 0) < v:
                waits[k] = v
        for r in reads:
            add(r.last_w)
        for w in writes:
            add(w.last_w)
            for k, v in w.readers.items():
                add((k, v))
        return waits

    def _emit_waits(self, eng, waits):
        for k, v in waits.items():
            if k == eng:
                if eng == "pe":
                    continue
                if v <= self.count[eng] - 6:
                    continue
            if self.seen[eng].get(k, 0) >= v:
                continue
            self.seen[eng][k] = v
            self.streams[eng].append(("wait", k, v))

    def _commit(self, ev, reads, writes):
        k, v = ev
        for r in reads:
            if r.readers.get(k, 0) < v:
                r.readers[k] = v
        for w in writes:
            w.last_w = ev
            w.readers = {}

    def op(self, eng, fn, reads=(), writes=()):
        reads = [r for x in reads for r in self._resl(x)]
        writes = [r for x in writes for r in self._resl(x)]
        writes = writes + [r for r in reads if r.name.startswith("bank") and r not in writes]
        self._emit_waits(eng, self._collect(reads, writes))
        self.count[eng] += 1
        ev = (eng, self.count[eng])
        self.streams[eng].append(("op", fn, eng, 1))
        self._commit(ev, reads, writes)
        return ev

    def dma(self, q, out, in_, is_output=False, extra_reads=(), **kw):
        reads = self._resl(in_) + [r for x in extra_reads for r in self._resl(x)]
        writes = self._resl(out)
        waits = self._collect(reads, writes)
        k = self.dma_k[q]
        self.dma_k[q] += 1
        s = ("dma", q, k % self.NDMA)
        gen = k // self.NDMA
        if gen > 0 and waits.get(s, 0) < 16 * gen:
            waits[s] = 16 * gen
        self._emit_waits(q, waits)
        ev = (s, 16 * (gen + 1))
        self.streams[q].append(("op", lambda e: e.dma_start(out=out, in_=in_, **kw), s, 16))
        self._commit(ev, reads, writes)
        if is_output:
            self.out_events.append(ev)
        return ev

    def collective(self, groups, in_t, out_t):
        reads = [self._res(in_t.ap())]
        writes = [self._res(out_t.ap())]
        self._emit_waits("pool", self._collect(reads, writes))
        self.cc_count += 1
        ev = ("cc", self.cc_count)
        self.streams["pool"].append(("op", lambda e: e.collective_compute(
            "AllGather", ALU.bypass, replica_groups=groups, ins=[in_t.ap().opt()],
            outs=[out_t.ap().opt()]), "cc", 1))
        self._commit(ev, reads, writes)
        return ev

    def _all_events(self):
        waits = {}
        for e in ENGS:
            if self.count[e] > 0:
                waits[e] = self.count[e]
        for q in ("sp", "act", "pool"):
            k = self.dma_k[q]
            for i in range(self.NDMA):
                n = (k // self.NDMA) + (1 if i < k % self.NDMA else 0)
                if n > 0:
                    waits[("dma", q, i)] = 16 * n
        if self.cc_count:
            waits["cc"] = self.cc_count
        return waits

    def barrier(self, skip_cc=False):
        waits = self._all_events()
        if skip_cc:
            waits.pop("cc", None)
        for e in ENGS:
            for k, v in waits.items():
                if k == e:
                    continue
                if self.seen[e].get(k, 0) >= v:
                    continue
                self.seen[e][k] = v
                self.streams[e].append(("wait", k, v))

    def finish(self):
        self.barrier()

    def replay(self):
        nc = self.nc
        sems = self.sems
        streams = self.streams

        def run(engobj, name):
            for item in streams[name]:
                if item[0] == "wait":
                    engobj.wait_ge(sems[item[1]], item[2])
                else:
                    _, fn, semk, inc = item
                    fn(engobj).then_inc(sems[semk], inc)

        with nc.Block() as block:
            @block.tensor
            def _(e):
                run(e, "pe")

            @block.scalar
            def _(e):
                run(e, "act")

            @block.vector
            def _(e):
                run(e, "dve")

            @block.gpsimd
            def _(e):
                run(e, "pool")

            @block.sync
            def _(e):
                run(e, "sp")


class WPlan:
    def __init__(self, S, slotsA, slotsB, plan):
        self.S, self.A, self.B, self.plan = S, list(slotsA), list(slotsB), plan
        self.slot_of = {}
        self.live = {}
        self.cur = 0
        self.got = []
        self.markers_passed = 0

    def _pump(self):
        j = 0
        unpassed = 0
        seen_markers = 0
        for j in range(len(self.plan)):
            e = self.plan[j]
            if e[0] == 'KV':
                seen_markers += 1
                if seen_markers > self.markers_passed:
                    unpassed += 1
                    if unpassed > 1:
                        return
                continue
            if j in self.slot_of or j < self.cur:
                continue
            allowed = self.A + (self.B if unpassed == 0 else [])
            free = [t for t in allowed if t.name not in self.live]
            if not free:
                return
            t = free[0]
            _, src, P, n = e[0:4]
            self.S.dma("pool", t[0:P, 0:n], src, max_dma_last_dim=4096)
            self.slot_of[j] = t
            self.live[t.name] = j

    def _release(self, idx):
        t = self.slot_of.get(idx)
        if t is not None and self.live.get(t.name) == idx:
            del self.live[t.name]

    def get(self, src_off=None):
        while self.plan[self.cur][0] == 'KV':
            self.cur += 1
        i = self.cur
        if src_off is not None:
            assert self.plan[i][4] == src_off, (i, self.plan[i][4], src_off)
        while len(self.got) >= 2:
            self._release(self.got.pop(0))
        if i not in self.slot_of:
            self._pump()
        assert i in self.slot_of, ("no slot for load", i)
        self.cur += 1
        self.got.append(i)
        self._pump()
        return self.slot_of[i]

    def release_all(self):
        while self.got:
            self._release(self.got.pop(0))

    def kv_begin(self):
        self.release_all()
        self._pump()

    def kv_end(self):
        self.markers_passed += 1
        self._pump()


class K:
    def __init__(self, S):
        self.S = S

    @staticmethod
    def _aps(*xs):
        return [x for x in xs if x is not None and not isinstance(x, (int, float))]

    def act(self, out, in_, func, scale=1.0, bias=None):
        kw = {}
        if bias is not None:
            kw["bias"] = bias
        self.S.op("act", lambda e: e.activation(out=out, in_=in_, func=func, scale=scale, **kw),
                  reads=self._aps(in_, scale, bias), writes=[out])

    def ts(self, out, in0, s1, s2=None, op0=ALU.mult, op1=None, eng="dve"):
        kw = {}
        if op1 is not None:
            kw["op1"] = op1
        self.S.op(eng, lambda e: e.tensor_scalar(out=out, in0=in0, scalar1=s1, scalar2=s2, op0=op0, **kw),
                  reads=self._aps(in0, s1, s2), writes=[out])

    def tt(self, out, in0, in1, op, eng="dve"):
        self.S.op(eng, lambda e: e.tensor_tensor(out=out, in0=in0, in1=in1, op=op),
                  reads=[in0, in1], writes=[out])

    def stt(self, out, in0, scalar, in1, op0, op1, eng="dve"):
        self.S.op(eng, lambda e: e.scalar_tensor_tensor(out=out, in0=in0, scalar=scalar, in1=in1, op0=op0, op1=op1),
                  reads=self._aps(in0, scalar, in1), writes=[out])

    def recip(self, out, in_):
        self.S.op("dve", lambda e: e.reciprocal(out=out, in_=in_), reads=[in_], writes=[out])

    def copy(self, out, in_, eng="dve"):
        if in_.tensor.name.startswith("bank"):
            self.ts(out, in_, 1.0, None, op0=ALU.mult, eng=eng)
            return
        self.S.op(eng, lambda e: e.tensor_copy(out=out, in_=in_), reads=[in_], writes=[out])

    def memset(self, ap, val, eng="dve"):
        self.S.op(eng, lambda e: e.memset(ap, val), writes=[ap])

    def rsum(self, out, in_):
        self.S.op("dve", lambda e: e.reduce_sum(out=out, in_=in_, axis=AX.X), reads=[in_], writes=[out])

    def scan(self, out, a, b, init):
        self.S.op("dve", lambda e: e.tensor_tensor_scan(out=out, data0=a, data1=b, initial=init,
                                                        op0=ALU.mult, op1=ALU.add),
                  reads=self._aps(a, b, init), writes=[out])

    def mm(self, out, lhsT, rhs, start=True, stop=True):
        self.S.op("pe", lambda e: e.matmul(out, lhsT=lhsT, rhs=rhs, start=start, stop=stop),
                  reads=[lhsT, rhs], writes=[out])


def build_program():
    nc = bass.Bass("TRN2", target_bir_lowering=False)
    din = lambda n, s, dt=F32: nc.dram_tensor(n, s, dt, kind="ExternalInput")
    dout = lambda n, s: nc.dram_tensor(n, s, F32, kind="ExternalOutput")
    xT_in = din("xT", [1024, NTOK])
    small_in = din("small", [128, NSM])
    wmod = [din(f"wmod{l}", [128, 24576]) for l in range(DEPTH)]
    wp1 = [din(f"wp1{l}", [128, WP1N]) for l in range(DEPTH)]
    wp2 = [din(f"wp2{l}", [128, WP2N]) for l in range(DEPTH)]
    wba = [din(f"wba{l}", [64, 8192]) for l in range(DEPTH)]
    glru_in = din("glru", [128, 4096])
    rmat_in = din("rmat", [128, 128])
    cos_in = din("cosT", [128, 1024])
    sin_in = din("sinT", [128, 1024])
    msk_in = din("masks", [128, 3072])
    ckc_in = [din(f"ckc{l}", [128, 2048]) for l in range(DEPTH)]
    cvc_in = [din(f"cvc{l}", [128, 2048]) for l in range(DEPTH)]
    cka_in = [din(f"cka{l}", [64, 1024]) for l in range(DEPTH)]
    cva_in = [din(f"cva{l}", [128, 512]) for l in range(DEPTH)]

    yT_out = dout("yT", [1024, NTOK])
    o_ka = dout("o_ka", [DEPTH * 2 * 64, 512])
    o_kc = dout("o_kc", [DEPTH * 4 * 128, 512])
    o_va = dout("o_va", [DEPTH * 512, 128])
    o_vc = dout("o_vc", [DEPTH * 512, 512])
    o_st = dout("o_st", [128, 32])

    ga_in = nc.dram_tensor("ga_in", [128, 4096], BF16)
    ga_out = nc.dram_tensor("ga_out", [512, 4096], BF16)
    gb_in = nc.dram_tensor("gb_in", [128, 4096], BF16)
    gb_out = nc.dram_tensor("gb_out", [512, 4096], BF16)
    gc_in = nc.dram_tensor("gc_in", [128, 3072], BF16)
    gc_out = nc.dram_tensor("gc_out", [512, 3072], BF16)
    g2_in = nc.dram_tensor("g2_in", [128, 12], F32)
    g2_out = nc.dram_tensor("g2_out", [512, 12], F32)
    g3_in = nc.dram_tensor("g3_in", [128, 16], F32)
    g3_out = nc.dram_tensor("g3_out", [512, 16], F32)
    spill_a = nc.dram_tensor("spill_a", [128, 8 * 1024], F32)
    spill_b = nc.dram_tensor("spill_b", [128, 8 * 1024], F32)
    GROUPS = [[0, 1, 2, 3], [4, 5, 6, 7]]

    with contextlib.ExitStack() as st:
        S = Sched(nc, st)
        k = K(S)
        _cnt = [0]

        def T(stack, shape, dt=F32, name=None):
            _cnt[0] += 1
            return stack.enter_context(nc.sbuf_tensor(f"{name or 't'}_{_cnt[0]}", shape, dt))

        banks = [st.enter_context(nc.psum_tensor(f"bank{i}", [128, 512], F32)) for i in range(8)]

        XT = T(st, [128, 8, NTOK], F32, "XT")
        OSGB = T(st, [128, 4, NTOK], BF16, "OSGB")
        ONES = T(st, [128, 128], BF16, "ONES")
        BONES = T(st, [128, 128], BF16, "BONES")
        RM = T(st, [128, 128], BF16, "RM")
        COS = T(st, [128, 1024], F32, "COS")
        SIN = T(st, [128, 1024], F32, "SIN")
        MSK = T(st, [128, 6, 512], BF16, "MSK")
        SMALL = T(st, [128, NSM], F32, "SMALL")
        EPST = T(st, [128, 1], F32, "EPST")
        MODV = [T(st, [128, 24, 2], F32, "MODV") for _ in range(DEPTH)]
        GS = [T(st, [128, 8, 2], F32, "GS") for _ in range(DEPTH)]
        DER = [T(st, [128, 64], F32, "DER") for _ in range(DEPTH)]
        KAW = T(st, [128, 2, 1280], BF16, "KAW")
        VAW = T(st, [128, 10, 128], BF16, "VAW")
        VAWS = T(st, [128, 10, 128], BF16, "VAWS")
        KACTX = T(st, [128, 2, 512], BF16, "KACTX")
        KCCTX = T(st, [128, 4, 512], BF16, "KCCTX")
        VACTX = T(st, [128, 4, 128], BF16, "VACTX")
        VACTXS = T(st, [128, 4, 128], BF16, "VACTXS")
        VCCTX = T(st, [128, 4, 512], BF16, "VCCTX")
        STO = T(st, [128, 32], F32, "STO")
        WSL = [T(st, [128, 4096], BF16, "WSL") for _ in range(2)]
        ws_i = [0]

        def sm(name, l=None):
            o, w = SM[(name, l)] if l is not None else SM[name]
            return SMALL[:, o:o + w]

        def wload(src_ap, P, n):
            slot = WSL[ws_i[0] % len(WSL)]
            ws_i[0] += 1
            S.dma("pool", slot[0:P, 0:n], src_ap, max_dma_last_dim=4096)
            return slot

        bank_i = [0]

        def nb(pool=(0, 1, 2, 3)):
            b = banks[pool[bank_i[0] % len(pool)]]
            bank_i[0] += 1
            return b

        D_NBA, D_NBX, D_CL, D_C2, D_ESINK, D_NLAM, D_SUBG = 0, 8, 16, 24, 32, 40, 41

        S.dma("sp", XT[:], xT_in.ap().rearrange("(k p) t -> p k t", p=128))
        S.dma("sp", SMALL[:], small_in.ap())
        S.dma("sp", COS[:], cos_in.ap())
        S.dma("sp", SIN[:], sin_in.ap())
        S.dma("pool", RM[:], rmat_in.ap())
        S.dma("pool", MSK[:].rearrange("p a b -> p (a b)"), msk_in.ap(), max_dma_last_dim=4096)
        k.memset(ONES[:], 1.0)
        k.memset(BONES[:], 0.0)
        k.memset(BONES[0:64, 0:64], 1.0)
        k.memset(BONES[64:128, 64:128], 1.0)
        k.memset(EPST[:], EPS)
        ONE_P = T(st, [128, 1], F32, "ONE_P")
        k.memset(ONE_P[:], 1.0)
        k.memset(KAW[:], 0.0)
        k.memset(VAW[:], 0.0)
        k.memset(VAWS[:], 0.0)
        k.memset(KACTX[:], 0.0)
        k.memset(WSL[0][:, 0:2048], 0.0)
        S.dma("sp", gc_in.ap()[64:128, 0:2048], WSL[0][64:128, 0:2048])

        with contextlib.ExitStack() as ph:
            SCF = T(ph, [128, 16], F32, "SCF")
            SCF2 = T(ph, [128, 16], F32, "SCF2")
            SCB = T(ph, [128, 8, 2], BF16, "SCB")
            TMPS = T(ph, [128, 64], F32, "TMPS")
            TMPS2 = T(ph, [128, 64], F32, "TMPS2")
            cT = sm("c")
            k.act(SCF[:], cT, AF.Exp, scale=-1.0)
            k.ts(SCF[:], SCF[:], 1.0, None, op0=ALU.add)
            k.recip(SCF2[:], SCF[:])
            k.tt(SCB[:].rearrange("p a b -> p (a b)"), cT, SCF2[:], ALU.mult)
            for l in range(DEPTH):
                ps = nb()
                for g in range(6):
                    w = wload(wmod[l].ap()[:, g * 4096:(g + 1) * 4096], 128, 4096)
                    wv = w[:, :].rearrange("p (c k m) -> p c k m", c=4, k=8)
                    for c4 in range(4):
                        cc = g * 4 + c4
                        for kk in range(8):
                            k.mm(ps[:, cc * 2:cc * 2 + 2], wv[:, c4, kk, :], SCB[:, kk, :],
                                 start=(kk == 0), stop=(kk == 7))
                psv = ps[:, 0:48].rearrange("p (c v) -> p c v", v=2)
                for v in range(2):
                    k.tt(MODV[l][:, :, v], psv[:, :, v], sm("modb", l), ALU.add)
                    k.stt(GS[l][:, :, v], MODV[l][:, 8:16, v], 1.0, sm("ng", l), ALU.add, ALU.mult)
                D = DER[l]
                k.ts(D[:, D_NBA:D_NBA + 8], sm("ba", l), -1.0, None, op0=ALU.mult)
                k.ts(D[:, D_NBX:D_NBX + 8], sm("bx", l), -1.0, None, op0=ALU.mult)
                k.act(TMPS[:, 0:8], sm("lam", l), AF.Exp, scale=-1.0)
                k.ts(TMPS[:, 0:8], TMPS[:, 0:8], 1.0, None, op0=ALU.add)
                k.act(TMPS2[:, 0:8], TMPS[:, 0:8], AF.Ln)
                k.ts(D[:, D_CL:D_CL + 8], TMPS2[:, 0:8], -8.0, None, op0=ALU.mult)
                k.ts(D[:, D_C2:D_C2 + 8], TMPS2[:, 0:8], -16.0, None, op0=ALU.mult)
                k.act(D[:, D_ESINK:D_ESINK + 8], sm("sink", l), AF.Exp)
                lam_init = 0.8 - 0.6 * math.exp(-0.3 * l)
                k.tt(TMPS[:, 0:64], sm("lq1", l), sm("lk1", l), ALU.mult)
                k.rsum(TMPS2[:, 8:9], TMPS[:, 0:64])
                k.tt(TMPS[:, 0:64], sm("lq2", l), sm("lk2", l), ALU.mult)
                k.rsum(TMPS2[:, 9:10], TMPS[:, 0:64])
                k.act(TMPS2[:, 10:12], TMPS2[:, 8:10], AF.Exp)
                k.tt(TMPS2[:, 12:13], TMPS2[:, 11:12], TMPS2[:, 10:11], ALU.subtract)
                k.ts(D[:, D_NLAM:D_NLAM + 1], TMPS2[:, 12:13], -lam_init, None, op0=ALU.add)
                k.ts(D[:, D_SUBG:D_SUBG + 1], sm("subln", l), 1.0 - lam_init, None, op0=ALU.mult)
            S.barrier()

        def front(ph_t, l, b):
            HT, SQ, RSTD, FTMP = ph_t["HT"], ph_t["SQ"], ph_t["RSTD"], ph_t["FTMP"]
            v = 0 if b == 0 else 1
            cols = slice(b * 512, (b + 1) * 512)
            ps = banks[7]
            for kk in range(8):
                sq = SQ[kk % 2]
                if kk % 2 == 0:
                    k.act(sq[:], XT[:, kk, cols], AF.Square)
                else:
                    k.tt(sq[:], XT[:, kk, cols], XT[:, kk, cols], ALU.mult, eng=("pool" if kk % 4 == 1 else "dve"))
                k.mm(ps[:], ONES[:], sq[:], start=(kk == 0), stop=(kk == 7))
            k.act(RSTD[:], ps[:], AF.Ln, scale=1.0 / 1024.0, bias=EPST[:])
            k.act(RSTD[:], RSTD[:], AF.Exp, scale=-0.5)
            for kk in range(8):
                ft = FTMP[kk % 2]
                k.tt(ft[:], XT[:, kk, cols], RSTD[:], ALU.mult)
                k.act(HT[:, kk, :], ft[:], AF.Identity, scale=GS[l][:, kk, v:v + 1], bias=MODV[l][:, kk, v:v + 1])

        def proj(ps, w3, nk, M, rhs_fn, N=512, P=128):
            if isinstance(w3, tuple):
                flat, base = w3
                for kk in range(nk):
                    k.mm(ps[0:128, 0:N], flat[0:P, base + kk * 64:base + kk * 64 + 128], rhs_fn(kk),
                         start=(kk == 0), stop=(kk == nk - 1))
                return
            for kk in range(nk):
                k.mm(ps[0:M, 0:N], w3[0:P, kk, 0:M], rhs_fn(kk), start=(kk == 0), stop=(kk == nk - 1))

        qk_i = [0]

        def qknorm(ph_t, ps, P, gain, rope_cols, dest_fn):
            i = qk_i[0] % 2
            qk_i[0] += 1
            X32, SQB, LNV, XN = ph_t["QX"][i], ph_t["QS"][i], ph_t["QL"][i], ph_t["QN"][i]
            k.act(X32[0:P, :], ps[0:P, :], AF.Copy)
            k.act(SQB[0:P, :], ps[0:P, :], AF.Square)
            ss = nb((4, 5))
            k.mm(ss[0:P, :], BONES[0:P, 0:P], SQB[0:P, :])
            k.act(LNV[0:P, :], ss[0:P, :], AF.Ln, scale=1.0 / 64.0, bias=EPST[0:P, :])
            k.act(LNV[0:P, :], LNV[0:P, :], AF.Exp, scale=-0.5)
            k.stt(XN[0:P, :], X32[0:P, :], gain, LNV[0:P, :], ALU.mult, ALU.mult)
            if rope_cols is None:
                dest_fn(XN)
                return
            k.copy(SQB[0:P, :], XN[0:P, :], eng="pool")
            rx = nb((4, 5))
            k.mm(rx[0:P, :], RM[0:P, 0:P], SQB[0:P, :])
            k.tt(X32[0:P, :], XN[0:P, :], COS[0:P, rope_cols], ALU.mult, eng="pool")
            k.tt(LNV[0:P, :], rx[0:P, :], SIN[0:P, rope_cols], ALU.mult)
            dest_fn((X32, LNV))

        def qk_pipeline(ph_t, items, rope_cols, rhs_fn):
            n = len(items)
            st = [None] * n
            base = qk_i[0]
            qk_i[0] += n

            def tiles(i):
                j = (base + i) % 2
                return ph_t["QX"][j], ph_t["QS"][j], ph_t["QL"][j], ph_t["QN"][j]

            def stage_a(i):
                w3, P, gain, dest = items[i]
                ps = nb()
                proj(ps, w3, 8, P, rhs_fn)
                st[i] = ps

            def stage_b(i):
                w3, P, gain, dest = items[i]
                X32, SQB, LNV, XN = tiles(i)
                ps = st[i]
                k.act(SQB[0:P, :], ps[0:P, :], AF.Square)
                ss = nb((4, 5))
                k.mm(ss[0:P, :], BONES[0:P, 0:P], SQB[0:P, :])
                k.act(LNV[0:P, :], ss[0:P, :], AF.Ln, scale=1.0 / 64.0, bias=EPST[0:P, :])
                k.act(LNV[0:P, :], LNV[0:P, :], AF.Exp, scale=-0.5)
                k.stt(XN[0:P, :], ps[0:P, :], gain, LNV[0:P, :], ALU.mult, ALU.mult)
                if rope_cols is None:
                    dest(XN)

            def stage_c0(i):
                w3, P, gain, dest = items[i]
                X32, SQB, LNV, XN = tiles(i)
                k.act(SQB[0:P, :], XN[0:P, :], AF.Copy)

            def stage_c1(i):
                w3, P, gain, dest = items[i]
                X32, SQB, LNV, XN = tiles(i)
                rx = nb((6, 7))
                k.mm(rx[0:P, :], RM[0:P, 0:P], SQB[0:P, :])
                k.tt(X32[0:P, :], XN[0:P, :], COS[0:P, rope_cols], ALU.mult, eng="pool")
                k.tt(LNV[0:P, :], rx[0:P, :], SIN[0:P, rope_cols], ALU.mult)
                dest((X32, LNV))

            for step in range(n + 2):
                if rope_cols is not None and 0 <= step - 2 < n:
                    stage_c0(step - 2)
                if step < n:
                    stage_a(step)
                if 0 <= step - 1 < n:
                    stage_b(step - 1)
                if rope_cols is not None and 0 <= step - 2 < n:
                    stage_c1(step - 2)

        def silu_to(ph_t, ps, P, out_ap):
            i = qk_i[0] % 2
            qk_i[0] += 1
            E = ph_t["QX"][i]
            k.act(E[0:P, :], ps[0:P, :], AF.Exp, scale=-1.0)
            k.act(E[0:P, :], E[0:P, :], AF.Ln, bias=ONE_P[0:P, :])
            k.act(E[0:P, :], E[0:P, :], AF.Exp, scale=-1.0)
            k.tt(out_ap, ps[0:P, :], E[0:P, :], ALU.mult)

        def common_tiles(ph):
            sq = [T(ph, [128, 512], BF16, "SQ") for _ in range(2)]
            qn = [T(ph, [128, 512], F32, "QN") for _ in range(2)]
            return {
                "HT": T(ph, [128, 8, 512], BF16, "HT"),
                "SQ": sq,
                "RSTD": T(ph, [128, 512], F32, "RSTD"),
                "FTMP": qn,
                "QX": [T(ph, [128, 512], F32, "QX") for _ in range(2)],
                "QS": sq,
                "QL": [T(ph, [128, 512], F32, "QL") for _ in range(2)],
                "QN": qn,
            }

        class _Stop(Exception):
            pass

        def layers():
          for l in range(DEPTH):
            D = DER[l]
            if STOP == "setup":
                return
            with contextlib.ExitStack() as ph0:
              XB = T(ph0, [128, 4, XBW], F32, "XB")
              with contextlib.ExitStack() as ph:
                pt = common_tiles(ph)
                HT = pt["HT"]
                KST = [T(ph, [128, 512], BF16, "KST") for _ in range(2)]
                VTMP = [T(ph, [128, 512], F32, "VTMP") for _ in range(1)]
                VST = [T(ph, [128, 512], BF16, "VST") for _ in range(2)]
                VATMP = T(ph, [128, 128], F32, "VATMP")
                k.memset(XB[:], 0.0)
                kst_i = [0]
                W1 = {}
                for (o_, n_) in ((0, 1024), (1024, 4096), (5120, 4096), (9216, 4096), (13312, 4096), (17408, 1024)):
                    t_ = T(ph, [128, n_ + (64 if o_ == 0 else 0)], BF16, "W1")
                    if o_ == 0:
                        k.memset(t_[:, n_:n_ + 64], 0.0)
                    S.dma("pool", t_[:, 0:n_], wp1[l].ap()[:, o_:o_ + n_], max_dma_last_dim=4096)
                    W1[o_] = t_

                class _W1:
                    def get(self, off):
                        return W1[off]
                wpl = _W1()
                for b in range(3):
                    lat = b > 0
                    front(pt, l, b)
                    rope_cols = slice((b - 1) * 512, b * 512) if lat else None
                    hrhs = lambda kk: HT[:, kk, :]
                    if STOP == "front":
                        return
                    w = wpl.get(0)
                    w4 = w[:, 0:1024].rearrange("p (c k m) -> p c k m", c=2, k=8)
                    qitems = []
                    for kv in range(2):
                        if not lat:
                            def dest(XN, kv=kv):
                                S.dma("sp", o_ka.ap()[(l * 2 + kv) * 64:(l * 2 + kv + 1) * 64, :], XN[0:64, :],
                                      is_output=True)
                                k.copy(KACTX[0:64, kv, :], XN[0:64, :])
                        else:
                            def dest(t, kv=kv, b=b):
                                dst = KAW[0:64, kv, 128 + (b - 1) * 512:128 + b * 512]
                                k.tt(dst, t[0][0:64, :], t[1][0:64, :], ALU.add)
                                S.dma("sp", gc_in.ap()[0:64, KA_OFF + kv * 1024 + (b - 1) * 512:KA_OFF + kv * 1024 + b * 512], dst)
                        qitems.append(((w, kv * 512), 64, sm("kna", l)[0:64, :], dest))
                    qk_pipeline(pt, qitems, rope_cols, hrhs)
                    if STOP == "ka":
                        return
                    w = wpl.get(1024)
                    w4 = w[:, 0:4096].rearrange("p (c k m) -> p c k m", c=4, k=8)
                    qitems = []
                    for h in range(4):
                        if not lat:
                            def dest(XN, h=h):
                                S.dma("sp", o_kc.ap()[(l * 4 + h) * 128:(l * 4 + h + 1) * 128, :], XN[:, :], is_output=True)
                                k.copy(KCCTX[:, h, :], XN[:, :])
                        else:
                            def dest(t, h=h, b=b):
                                ks = KST[kst_i[0] % 2]
                                kst_i[0] += 1
                                k.tt(ks[:], t[0][:, :], t[1][:, :], ALU.add)
                                S.dma("sp", ga_in.ap()[:, KC_OFF + h * 1024 + (b - 1) * 512:KC_OFF + h * 1024 + b * 512], ks[:])
                        qitems.append((w4[:, h], 128, sm("knc", l), dest))
                    qk_pipeline(pt, qitems, rope_cols, hrhs)
                    if STOP == "kc":
                        return
                    w = wpl.get(5120)
                    w4 = w[:, 0:4096].rearrange("p (c k m) -> p c k m", c=4, k=8)
                    for ch in range(4):
                        ps = nb()
                        proj(ps, w4[:, ch], 8, 128, hrhs)
                        if b == 0:
                            k.act(XB[:, ch, 2:258], ps[:, 0:256], AF.Copy)
                            k.act(XB[:, ch, 261:517], ps[:, 256:512], AF.Copy)
                        else:
                            c0 = 520 + (b - 1) * 512
                            k.act(XB[:, ch, c0:c0 + 512], ps[:, :], AF.Copy)
                    if STOP == "xb":
                        return
                    w = wpl.get(9216)
                    w4 = w[:, 0:4096].rearrange("p (c k m) -> p c k m", c=4, k=8)
                    for ch in range(4):
                        ps = nb()
                        proj(ps, w4[:, ch], 8, 128, hrhs)
                        silu_to(pt, ps, 128, OSGB[:, ch, b * 512:(b + 1) * 512])
                    if STOP == "gb":
                        return
                    w = wpl.get(13312)
                    w2 = wpl.get(17408)
                    wv = w[:, 0:4096].rearrange("p (k n) -> p k n", k=8)
                    wva = w2[:, 0:1024].rearrange("p (k n) -> p k n", k=8)
                    for tt_ in range(4):
                        tok = slice(tt_ * 128, (tt_ + 1) * 128)
                        psc = nb((4, 5))
                        psa = banks[6]
                        for kk in range(8):
                            k.mm(psc[:, :], HT[:, kk, tok], wv[:, kk, :], start=(kk == 0), stop=(kk == 7))
                        if 'a' in KV:
                            continue
                        for kk in range(8):
                            k.mm(psa[:, 0:128], HT[:, kk, tok], wva[:, kk, :], start=(kk == 0), stop=(kk == 7))
                        if 'b' in KV:
                            continue
                        if not lat:
                            vt = VTMP[0]
                            if 'e' not in KV:
                                k.act(vt[:], psc[:, :], AF.Copy)
                            if 'c' not in KV:
                                S.dma("sp", o_vc.ap()[l * 512 + tt_ * 128:l * 512 + (tt_ + 1) * 128, :], vt[:], is_output=True)
                            if 'f' not in KV:
                                k.copy(VCCTX[:, tt_, :], psc[:, :])
                            if 'd' in KV:
                                continue
                            k.act(VATMP[:], psa[:, 0:128], AF.Copy)
                            if 'c' not in KV:
                                S.dma("sp", o_va.ap()[l * 512 + tt_ * 128:l * 512 + (tt_ + 1) * 128, :], VATMP[:], is_output=True)
                            k.copy(VACTX[:, tt_, :], psa[:, 0:128])
                            k.copy(VACTXS[:, tt_, 0:64], psa[:, 64:128])
                            k.copy(VACTXS[:, tt_, 64:128], psa[:, 0:64])
                        else:
                            Tt = (b - 1) * 4 + tt_
                            vs = VST[tt_ % 2]
                            k.act(vs[:], psc[:, :], AF.Copy)
                            S.dma("sp", gb_in.ap()[:, VC_OFF + Tt * 512:VC_OFF + (Tt + 1) * 512], vs[:])
                            k.copy(VAW[:, 1 + Tt, :], psa[:, 0:128])
                            k.copy(VAWS[:, 1 + Tt, 0:64], psa[:, 64:128])
                            k.copy(VAWS[:, 1 + Tt, 64:128], psa[:, 0:64])
                            S.dma("sp", gc_in.ap()[:, VA_OFF + Tt * 128:VA_OFF + (Tt + 1) * 128], VAW[:, 1 + Tt, :])
                    if STOP == "v":
                        return
                if STOP == "p1":
                    return
                G2S = T(ph, [128, 12], F32, "G2S")
                for ch in range(4):
                    k.copy(G2S[:, ch * 3:ch * 3 + 1], XB[:, ch, 520:521])
                    k.copy(G2S[:, ch * 3 + 1:ch * 3 + 3], XB[:, ch, 520 + 1022:520 + 1024])
                S.dma("sp", g2_in.ap(), G2S[:])
                S.collective(GROUPS, g2_in, g2_out)
                S.collective(GROUPS, gc_in, gc_out)
                S.collective(GROUPS, ga_in, ga_out)
                S.collective(GROUPS, gb_in, gb_out)
                S.barrier(skip_cc=True)
              if STOP == "cc":
                  return
              with contextlib.ExitStack() as ph:
                GL = T(ph, [128, 16, 128], BF16, "GL")
                S.dma("pool", GL[:].rearrange("p a b -> p (a b)"), glru_in.ap()[:, l * 2048:(l + 1) * 2048], max_dma_last_dim=4096)
                Us = [T(ph, [128, 1024], F32, "U") for _ in range(2)]
                UBs = [T(ph, [128, 1024], BF16, "UB") for _ in range(2)]
                AA = [T(ph, [128, 1024], F32, "AA") for _ in range(2)]
                BB = [T(ph, [128, 1024], F32, "BB") for _ in range(2)]
                HH = [T(ph, [128, 1024], F32, "HH") for _ in range(2)]
                LT = [[T(ph, [128, 512], F32, "LT") for _ in range(4)] for _ in range(2)]
                ONE_T = T(ph, [128, 1], F32, "ONE_T")
                k.memset(ONE_T[:], 1.0)
                u_i = [0]
                HAL = T(ph, [128, 4, 12], F32, "HAL")
                TOT = T(ph, [128, 16], F32, "TOT")
                RS = T(ph, [128, 8], F32, "RS")
                TOTG = T(ph, [128, 4, 16], F32, "TOTG")
                CH = T(ph, [128, 40], F32, "CH")
                HIN = T(ph, [128, 8], F32, "HIN")
                sel = sm("sel")

                def lat_halo():
                    S.dma("sp", HAL[:], g2_out.ap().rearrange("(r p) c -> p r c", p=128))
                    for ch in range(4):
                        pre = XB[:, ch, 518:520]
                        post = XB[:, ch, 1544:1545]
                        for r in range(4):
                            k.stt(pre, HAL[:, r, ch * 3 + 1:ch * 3 + 3], sel[:, r:r + 1], pre, ALU.mult, ALU.add)
                            k.stt(post, HAL[:, r, ch * 3:ch * 3 + 1], sel[:, 4 + r:5 + r], post, ALU.mult, ALU.add)

                def lru_gates(ch, s0, Tn, U, UB, want_rsum):
                    npc = (Tn + 511) // 512
                    for pc in range(npc):
                        n = min(512, Tn)
                        cs = slice(pc * 512, pc * 512 + n)
                        seqs = []
                        for dr in range(2):
                            nba = D[:, D_NBA + dr * 4 + ch:D_NBA + dr * 4 + ch + 1]
                            nbx = D[:, D_NBX + dr * 4 + ch:D_NBX + dr * 4 + ch + 1]
                            cl = D[:, D_CL + dr * 4 + ch:D_CL + dr * 4 + ch + 1]
                            c2 = D[:, D_C2 + dr * 4 + ch:D_C2 + dr * 4 + ch + 1]
                            L0, L1, L2, L3 = [t[:, 0:n] for t in LT[dr]]
                            zr = banks[(pc % 2) * 4 + dr * 2]
                            zi = banks[(pc % 2) * 4 + dr * 2 + 1]
                            ops = [
                                lambda zr=zr, dr=dr: k.mm(zr[:, 0:n], GL[:, (dr * 2 + 0) * 4 + ch, :], UB[:, cs]),
                                lambda zi=zi, dr=dr: k.mm(zi[:, 0:n], GL[:, (dr * 2 + 1) * 4 + ch, :], UB[:, cs]),
                                lambda L0=L0, zr=zr, nba=nba: k.act(L0, zr[:, 0:n], AF.Exp, scale=-1.0, bias=nba),
                                lambda L1=L1, zi=zi, nbx=nbx: k.act(L1, zi[:, 0:n], AF.Exp, scale=-1.0, bias=nbx),
                                lambda L0=L0: k.act(L0, L0, AF.Ln, bias=ONE_T[:]),
                                lambda L1=L1: k.act(L1, L1, AF.Ln, bias=ONE_T[:]),
                                lambda L0=L0: k.act(L0, L0, AF.Exp, scale=-1.0),
                                lambda L0=L0, dr=dr, cl=cl: k.act(AA[dr][:, cs], L0, AF.Exp, scale=cl),
                                lambda L2=L2, L0=L0, c2=c2: k.ts(L2, L0, c2, None, op0=ALU.mult),
                                lambda L3=L3, L2=L2: k.ts(L3, L2, 1.0 / 24.0, 1.0 / 6.0, op0=ALU.mult, op1=ALU.add),
                                lambda L3=L3, L2=L2: k.tt(L3, L3, L2, ALU.mult),
                                lambda L3=L3, L2=L2: k.stt(L3, L3, 0.5, L2, ALU.add, ALU.mult),
                                lambda L3=L3, L2=L2: k.stt(L3, L3, 1.0, L2, ALU.add, ALU.mult),
                                lambda L3=L3: k.act(L3, L3, AF.Ln, scale=-1.0),
                                lambda L3=L3, L1=L1: k.stt(L3, L3, 0.5, L1, ALU.mult, ALU.subtract),
                                lambda L3=L3: k.act(L3, L3, AF.Exp),
                                lambda L3=L3, dr=dr: k.tt(BB[dr][:, cs], L3, U[:, cs], ALU.mult),
                            ]
                            if want_rsum:
                                ops.insert(8, lambda L0=L0, dr=dr, pc=pc: k.rsum(RS[:, dr * 2 + pc:dr * 2 + pc + 1], L0))
                            seqs.append(ops)
                        for o0, o1 in zip(*seqs):
                            o0()
                            o1()

                def conv_u(ch, s0, Tn):
                    U = Us[u_i[0] % 2]
                    UB = UBs[u_i[0] % 2]
                    u_i[0] += 1
                    cw = sm("convw", l)
                    k.ts(U[:, 0:Tn], XB[:, ch, s0:s0 + Tn], cw[:, ch * 4:ch * 4 + 1], sm("convb", l)[:, ch:ch + 1],
                         op0=ALU.mult, op1=ALU.add)
                    for j in range(1, 4):
                        k.stt(U[:, 0:Tn], XB[:, ch, s0 + j:s0 + j + Tn], cw[:, ch * 4 + j:ch * 4 + j + 1], U[:, 0:Tn],
                              ALU.mult, ALU.add)
                    k.copy(UB[:, 0:Tn], U[:, 0:Tn])
                    return U, UB

                def do_scan(dr, Tn, init):
                    if dr == 0:
                        k.scan(HH[0][:, 0:Tn], AA[0][:, 0:Tn], BB[0][:, 0:Tn], init)
                    else:
                        k.scan(HH[1][:, 0:Tn][:, ::-1], AA[1][:, 0:Tn][:, ::-1], BB[1][:, 0:Tn][:, ::-1], init)

                def lru_seg(ch, si):
                        s0, Tn, tk0 = SEGS[si]
                        U, UB = conv_u(ch, s0, Tn)
                        lru_gates(ch, s0, Tn, U, UB, si == 2)
                        for dr in range(2):
                            do_scan(dr, Tn, 0.0)
                            endc = Tn - 1 if dr == 0 else 0
                            if si < 2:
                                col = ((l * 2 + si) * 2 + dr) * 4 + ch
                                k.copy(STO[:, col:col + 1], HH[dr][:, endc:endc + 1])
                            else:
                                k.copy(TOT[:, dr * 8 + 4 + ch:dr * 8 + 5 + ch], HH[dr][:, endc:endc + 1])
                                k.tt(RS[:, 4 + dr:5 + dr], RS[:, dr * 2:dr * 2 + 1], RS[:, dr * 2 + 1:dr * 2 + 2], ALU.add)
                                k.act(TOT[:, dr * 8 + ch:dr * 8 + ch + 1], RS[:, 4 + dr:5 + dr], AF.Exp,
                                      scale=D[:, D_CL + dr * 4 + ch:D_CL + dr * 4 + ch + 1])
                                sl = slice((ch * 2 + dr) * 1024, (ch * 2 + dr + 1) * 1024)
                                S.dma("sp", spill_a.ap()[:, sl], AA[dr][:, :])
                                S.dma("sp", spill_b.ap()[:, sl], BB[dr][:, :])
                        if si < 2:
                            k.tt(HH[0][:, 0:Tn], HH[0][:, 0:Tn], HH[1][:, 0:Tn], ALU.add, eng="pool")
                            k.tt(OSGB[:, ch, tk0:tk0 + Tn], HH[0][:, 0:Tn], OSGB[:, ch, tk0:tk0 + Tn], ALU.mult, eng="pool")
                for ch in range(2):
                    lru_seg(ch, 0)
                    lru_seg(ch, 1)
                lat_halo()
                for ch in range(4):
                    lru_seg(ch, 2)
                S.dma("sp", g3_in.ap(), TOT[:])
                S.collective(GROUPS, g3_in, g3_out)
                for ch in range(2, 4):
                    lru_seg(ch, 0)
                    lru_seg(ch, 1)
                S.dma("sp", TOTG[:], g3_out.ap().rearrange("(r p) c -> p r c", p=128))
                h0 = sm("h0", l)
                k.copy(CH[:, 0:4], h0[:, 0:4])
                for r in range(3):
                    k.tt(CH[:, (r + 1) * 4:(r + 2) * 4], TOTG[:, r, 0:4], CH[:, r * 4:(r + 1) * 4], ALU.mult)
                    k.tt(CH[:, (r + 1) * 4:(r + 2) * 4], CH[:, (r + 1) * 4:(r + 2) * 4], TOTG[:, r, 4:8], ALU.add)
                k.copy(CH[:, 16 + 12:16 + 16], h0[:, 4:8])
                for r in (3, 2, 1):
                    k.tt(CH[:, 16 + (r - 1) * 4:16 + r * 4], TOTG[:, r, 8:12], CH[:, 16 + r * 4:16 + (r + 1) * 4], ALU.mult)
                    k.tt(CH[:, 16 + (r - 1) * 4:16 + r * 4], CH[:, 16 + (r - 1) * 4:16 + r * 4], TOTG[:, r, 12:16], ALU.add)
                k.memset(HIN[:], 0.0)
                for r in range(4):
                    k.stt(HIN[:, 0:4], CH[:, r * 4:(r + 1) * 4], sel[:, 8 + r:9 + r], HIN[:, 0:4], ALU.mult, ALU.add)
                    k.stt(HIN[:, 4:8], CH[:, 16 + r * 4:16 + (r + 1) * 4], sel[:, 8 + r:9 + r], HIN[:, 4:8], ALU.mult, ALU.add)
                s0, Tn, tk0 = SEGS[2]
                for ch in range(4):
                    for dr in range(2):
                        sl = slice((ch * 2 + dr) * 1024, (ch * 2 + dr + 1) * 1024)
                        S.dma("sp", AA[dr][:, :], spill_a.ap()[:, sl])
                        S.dma("sp", BB[dr][:, :], spill_b.ap()[:, sl])
                        do_scan(dr, Tn, HIN[:, dr * 4 + ch:dr * 4 + ch + 1])
                    k.tt(HH[0][:, :], HH[0][:, :], HH[1][:, :], ALU.add, eng="pool")
                    k.tt(OSGB[:, ch, tk0:tk0 + Tn], HH[0][:, :], OSGB[:, ch, tk0:tk0 + Tn], ALU.mult, eng="pool")
                S.barrier()

            if STOP == "lru":
                return
            with contextlib.ExitStack() as ph:
                pt = common_tiles(ph)
                HT = pt["HT"]
                QA = T(ph, [128, 8, 512], BF16, "QA")
                OSA = T(ph, [64, 8, 512], BF16, "OSA")
                QCZ = T(ph, [128, 4, 2, 512], BF16, "QCZ")
                OSC = T(ph, [128, 4, 512], BF16, "OSC")
                YT = T(ph, [128, 8, 512], BF16, "YT")
                KCALL = T(ph, [128, 36 * 128], BF16, "KCALL")
                VCALL = T(ph, [128, 36 * 128], BF16, "VCALL")
                plan = []
                for b_ in range(3):
                    for o_ in (0, 2048, 4096, 8192, 10240, 12288):
                        n_ = 2048 if o_ in (0, 2048, 8192, 10240) else 4096
                        plan.append(('L', wp2[l].ap()[:, o_:o_ + n_], 128, n_, o_))
                    plan.append(('KV',))
                    for cc_ in range(8):
                        o_ = WP2_Q + cc_ * 4096
                        plan.append(('L', wp2[l].ap()[:, o_:o_ + 4096], 128, 4096, o_))
                        plan.append(('L', wba[l].ap()[:, cc_ * 1024:(cc_ + 1) * 1024], 64, 1024, -1 - cc_))
                    for g_ in range(2):
                        o_ = WP2_Q + 8 * 4096 + g_ * 4096
                        plan.append(('L', wp2[l].ap()[:, o_:o_ + 4096], 128, 4096, o_))
                S.split[KCALL.name] = 2048
                S.split[VCALL.name] = 2048
                for t_ in WSL + [KCALL, VCALL]:
                    k.memset(t_[:, 2048:2112], 0.0)
                wpl = WPlan(S, WSL, [KCALL, VCALL], plan)
                CKA = T(ph, [128, 2, 512], BF16, "CKA")
                CVA = T(ph, [128, 4, 128], BF16, "CVA")
                CVAS = T(ph, [128, 4, 128], BF16, "CVAS")
                ET = [T(ph, [128, 512], BF16, "ET") for _ in range(3)]
                PT1, PT2 = pt["QX"][0], pt["QX"][1]
                PT3 = pt["RSTD"][:, 0:256]
                PT5 = pt["RSTD"][:, 256:512]
                PT4 = pt["SQ"][0][:, 0:256]
                MT = [pt["QX"][0], pt["QX"][1], pt["QL"][0]]
                MT2 = [pt["QL"][1], pt["QN"][0], pt["QN"][1]]
                YACC = pt["RSTD"]
                k.memset(QCZ[:], 0.0)
                k.memset(CKA[:], 0.0)
                k.memset(QA[64:128, :, :], 0.0)
                S.dma("pool", CKA[0:64, :, :].rearrange("p a b -> p (a b)"), cka_in[l].ap(), max_dma_last_dim=4096)
                S.dma("pool", CVA[:].rearrange("p a b -> p (a b)"), cva_in[l].ap(), max_dma_last_dim=4096)
                cva3 = cva_in[l].ap().rearrange("p (t n) -> p t n", n=128)
                S.dma("pool", CVAS[:, :, 0:64], cva3[:, :, 64:128])
                S.dma("pool", CVAS[:, :, 64:128], cva3[:, :, 0:64])
                g1a, g1b, g1c = ga_out.ap(), gb_out.ap(), gc_out.ap()
                HALK = T(ph, [64, 2, 4, 2, 128], BF16, "HALK")
                HALV = T(ph, [128, 2, 4, 128], BF16, "HALV")
                sel = sm("sel")

                def build_halos():
                    for kv in range(2):
                        base = KA_OFF + kv * 1024
                        S.dma("sp", HALK[:, 0, :, kv, :], g1c[:, base + 896:base + 1024].rearrange("(r p) c -> p r c", p=128)[0:64])
                        S.dma("sp", HALK[:, 1, :, kv, :], g1c[:, base:base + 128].rearrange("(r p) c -> p r c", p=128)[0:64])
                    S.dma("sp", HALV[:, 0, :, :], g1c[:, VA_OFF + 896:VA_OFF + 1024].rearrange("(r p) c -> p r c", p=128))
                    S.dma("sp", HALV[:, 1, :, :], g1c[:, VA_OFF:VA_OFF + 128].rearrange("(r p) c -> p r c", p=128))
                    k.memset(KAW[:, :, 0:128], 0.0)
                    k.memset(KAW[:, :, 1152:1280], 0.0)
                    k.memset(VAW[:, 0, :], 0.0)
                    k.memset(VAW[:, 9, :], 0.0)
                    k.memset(VAWS[:, 0, :], 0.0)
                    k.memset(VAWS[:, 9, :], 0.0)
                    for r in range(4):
                        for kv in range(2):
                            k.stt(KAW[0:64, kv, 0:128], HALK[:, 0, r, kv, :], sel[0:64, r:r + 1], KAW[0:64, kv, 0:128], ALU.mult, ALU.add)
                            k.stt(KAW[0:64, kv, 1152:1280], HALK[:, 1, r, kv, :], sel[0:64, 4 + r:5 + r], KAW[0:64, kv, 1152:1280], ALU.mult, ALU.add)
                        k.stt(VAW[:, 0, :], HALV[:, 0, r, :], sel[:, r:r + 1], VAW[:, 0, :], ALU.mult, ALU.add)
                        k.stt(VAW[:, 9, :], HALV[:, 1, r, :], sel[:, 4 + r:5 + r], VAW[:, 9, :], ALU.mult, ALU.add)
                        for (d0, s0_) in ((0, 64), (64, 0)):
                            k.stt(VAWS[:, 0, d0:d0 + 64], HALV[:, 0, r, s0_:s0_ + 64], sel[:, r:r + 1], VAWS[:, 0, d0:d0 + 64], ALU.mult, ALU.add)
                            k.stt(VAWS[:, 9, d0:d0 + 64], HALV[:, 1, r, s0_:s0_ + 64], sel[:, 4 + r:5 + r], VAWS[:, 9, d0:d0 + 64], ALU.mult, ALU.add)


                et_i = [0]
                acc_i = [0]
                esum_i = [0]
                S.split[HALV.name] = 512
                _hv = HALV[:].rearrange("p a r c -> p (a r c)")
                ESUM = [_hv[:, 0:512], _hv[:, 512:1024]]

                def attend_all(units, order=None):
                    if order is None:
                        order = [(ui, j) for ui, u in enumerate(units) for j in range(len(u[1]))]
                    first = {}
                    last = {}
                    for si_, (ui_, j_) in enumerate(order):
                        first.setdefault(ui_, si_)
                        last[ui_] = si_
                    steps = [(ui_, j_, len(units[ui_][1])) for (ui_, j_) in order]
                    LOOK = 3
                    pend = []
                    deferred = []
                    role = [None] * len(steps)
                    si_ = 0
                    while si_ < len(steps):
                        if si_ + 1 < len(steps) and steps[si_ + 1][0] == steps[si_][0]:
                            role[si_], role[si_ + 1] = 'first', 'second'
                            si_ += 2
                        else:
                            role[si_] = 'single'
                            si_ += 1
                    dfirst, dlast = {}, {}
                    for si_, (ui_, j_, n_) in enumerate(steps):
                        if role[si_] != 'first':
                            dfirst.setdefault(ui_, si_)
                            dlast[ui_] = si_
                    pendD = []
                    prev_et = [None]

                    def issue(si):
                        ui, j, n = steps[si]
                        ps = nb((0, 1, 2, 7))
                        units[ui][0](ps, units[ui][1][j][0])
                        pend.append(ps)
                    for si in range(min(LOOK, len(steps))):
                        issue(si)
                    for si, (ui, j, n) in enumerate(steps):
                        qk_fn, tiles, Mv, post_fn = units[ui]
                        _, vap, mask = tiles[j]
                        ps = pend.pop(0)
                        et = ET[et_i[0] % 3]
                        et_i[0] += 1
                        k.act(et[:], ps[:], AF.Exp, scale=SCALE)
                        if mask is not None:
                            k.tt(et[:], et[:], mask, ALU.mult, eng=("pool" if si % 2 else "dve"))
                        a_ = (acc_i[0] + ui) % 2
                        accO = banks[3 + 2 * a_]
                        accD = banks[4 + 2 * a_]
                        dsrc = None
                        if role[si] == 'second':
                            es = ESUM[esum_i[0] % 2]
                            esum_i[0] += 1
                            k.tt(es, prev_et[0][:], et[:], ALU.add)
                            dsrc = es
                        elif role[si] == 'single':
                            dsrc = et[:]
                        prev_et[0] = et
                        while deferred and deferred[0][0] <= si:
                            deferred.pop(0)[1]()
                        if si + LOOK < len(steps):
                            issue(si + LOOK)
                        k.mm(accO[0:Mv, :], vap, et[:], start=(si == first[ui]), stop=(si == last[ui]))
                        while pendD and pendD[0][0] <= si:
                            pendD.pop(0)[1]()
                        if dsrc is not None:
                            fn = (lambda accD=accD, Mv=Mv, dsrc=dsrc, st_=(si == dfirst[ui]), sp_=(si == dlast[ui]):
                                  k.mm(accD[0:Mv, :], ONES[:, 0:Mv], dsrc, start=st_, stop=sp_))
                            if role[si] == 'single':
                                fn()
                            else:
                                pendD.append((si + 1, fn))
                        if si == last[ui]:
                            deferred.append((si + 3, lambda post_fn=post_fn, accO=accO, accD=accD: post_fn(accO, accD)))
                    for _, fn in pendD:
                        fn()
                    for _, fn in deferred:
                        fn()
                    acc_i[0] += len(units)

                for b in range(3):
                    lat = b > 0
                    v = 1 if lat else 0
                    cols = slice(b * 512, (b + 1) * 512)
                    front(pt, l, b)
                    rope_cols = slice((b - 1) * 512, b * 512) if lat else None
                    hrhs = lambda kk: HT[:, kk, :]
                    for g in range(2):
                        w = wpl.get(g * 2048)
                        w4 = w[:, 0:2048].rearrange("p (c k m) -> p c k m", c=4, k=8)
                        qitems = []
                        for c4 in range(4):
                            h = g * 4 + c4
                            if not lat:
                                def dest(XN, h=h):
                                    k.copy(QA[0:64, h, :], XN[0:64, :])
                            else:
                                def dest(t, h=h):
                                    k.tt(QA[0:64, h, :], t[0][0:64, :], t[1][0:64, :], ALU.add)
                            qitems.append(((w, c4 * 512), 64, sm("qna", l)[0:64, :], dest))
                        qk_pipeline(pt, qitems, rope_cols, hrhs)
                    w = wpl.get(4096)
                    w4 = w[:, 0:4096].rearrange("p (c k m) -> p c k m", c=4, k=8)
                    qitems = []
                    for h in range(4):
                        if not lat:
                            def dest(XN, h=h):
                                k.copy(QCZ[0:64, h, 0, :], XN[0:64, :])
                                k.copy(QCZ[64:128, h, 1, :], XN[64:128, :])
                        else:
                            def dest(t, h=h):
                                k.tt(QCZ[0:64, h, 0, :], t[0][0:64, :], t[1][0:64, :], ALU.add)
                                k.tt(QCZ[64:128, h, 1, :], t[0][64:128, :], t[1][64:128, :], ALU.add)
                        qitems.append((w4[:, h], 128, sm("qnc", l), dest))
                    qk_pipeline(pt, qitems, rope_cols, hrhs)
                    for g in range(2):
                        w = wpl.get(8192 + g * 2048)
                        w4 = w[:, 0:2048].rearrange("p (c k m) -> p c k m", c=4, k=8)
                        for c4 in range(4):
                            h = g * 4 + c4
                            ps = nb()
                            proj(ps, (w, c4 * 512), 8, 64, hrhs)
                            silu_to(pt, ps, 64, OSA[0:64, h, :])
                    w = wpl.get(12288)
                    w4 = w[:, 0:4096].rearrange("p (c k m) -> p c k m", c=4, k=8)
                    for h in range(4):
                        ps = nb()
                        proj(ps, w4[:, h], 8, 128, hrhs)
                        silu_to(pt, ps, 128, OSC[:, h, :])

                    if b == 1:
                        build_halos()
                    wpl.kv_begin()
                    nqb = 2
                    units = []
                    for qb in range(nqb):
                        qc = slice(qb * 256, (qb + 1) * 256)
                        for hp in range(4):
                            kv = hp // 2
                            h0_ = hp * 2
                            tiles = []
                            if not lat:
                                for kt in range(2):
                                    c0 = qb * 256 + kt * 128
                                    tiles.append((KACTX[:, kv, c0:c0 + 128], (VACTX if kv == 0 else VACTXS)[:, qb * 2 + kt, :], None))
                            else:
                                t0 = (b - 1) * 4 + qb * 2
                                for wdx in range(4):
                                    mi = [0 if t0 == 0 else 1, 2, 3, 5 if t0 == 6 else 4][wdx]
                                    tiles.append((KAW[:, kv, (t0 + wdx) * 128:(t0 + wdx + 1) * 128],
                                                  (VAW if kv == 0 else VAWS)[:, t0 + wdx, :], MSK[:, mi, :]))
                                for c in range(4):
                                    tiles.append((CKA[:, kv, c * 128:(c + 1) * 128], (CVA if kv == 0 else CVAS)[:, c, :], None))

                            def qk_a(ps, kap, h0_=h0_, qc=qc):
                                k.mm(ps[:, :], kap, QA[:, h0_:h0_ + 2, qc])

                            def post_a(accO, accD, h0_=h0_, qc=qc):
                                for hh in range(2):
                                    cs = slice(hh * 256, (hh + 1) * 256)
                                    k.act(PT1[0:64, cs], accD[0:64, cs], AF.Ln,
                                          bias=D[0:64, D_ESINK + h0_ + hh:D_ESINK + h0_ + hh + 1])
                                k.act(PT2[0:64, :], PT1[0:64, :], AF.Exp, scale=-1.0)
                                k.tt(PT1[0:64, :], accO[0:64, :], PT2[0:64, :], ALU.mult)
                                k.tt(OSA[0:64, h0_:h0_ + 2, qc], PT1[0:64, :].rearrange("p (a b) -> p a b", a=2),
                                     OSA[0:64, h0_:h0_ + 2, qc], ALU.mult)
                            units.append((qk_a, tiles, 128, post_a))
                    attend_all(units)

                    for h in range(4):
                        units = []
                        if lat:
                            src_k = g1a[:, KC_OFF + h * 1024:KC_OFF + (h + 1) * 1024].rearrange("(r p) c -> p r c", p=128)
                            S.dma("sp", KCALL[:, 0:2048].rearrange("p (r c) -> p r c", r=2), src_k[:, 0:2, :])
                            for r in (0, 1):
                                S.dma("sp", VCALL[:, r * 1024:(r + 1) * 1024].rearrange("p (t n) -> p t n", n=128),
                                      g1b[r * 128:(r + 1) * 128, VC_OFF:VC_OFF + 4096].rearrange("p (t n) -> p t n", n=512)[:, :, h * 128:(h + 1) * 128])
                            S.dma("sp", KCALL[:, 2048:4096].rearrange("p (r c) -> p r c", r=2), src_k[:, 2:4, :])
                            for r in (2, 3):
                                S.dma("sp", VCALL[:, r * 1024:(r + 1) * 1024].rearrange("p (t n) -> p t n", n=128),
                                      g1b[r * 128:(r + 1) * 128, VC_OFF:VC_OFF + 4096].rearrange("p (t n) -> p t n", n=512)[:, :, h * 128:(h + 1) * 128])
                            S.dma("pool", KCALL[:, 4096:4608], ckc_in[l].ap()[:, h * 512:(h + 1) * 512], max_dma_last_dim=2048)
                            S.dma("pool", VCALL[:, 4096:4608].rearrange("p (t n) -> p t n", n=128),
                                  cvc_in[l].ap().rearrange("p (t n) -> p t n", n=512)[:, :, h * 128:(h + 1) * 128])
                        for qb in range(2):
                            qc = slice(qb * 256, (qb + 1) * 256)
                            tiles = []
                            if not lat:
                                for kt in range(2):
                                    c0 = qb * 256 + kt * 128
                                    tiles.append((KCCTX[:, h, c0:c0 + 128], VCCTX[:, qb * 2 + kt, h * 128:(h + 1) * 128], None))
                            else:
                                for j in range(36):
                                    tiles.append((KCALL[:, j * 128:(j + 1) * 128], VCALL[:, j * 128:(j + 1) * 128], None))

                            def qk_c(ps, kap, h=h, qc=qc):
                                k.mm(ps[:, :], kap, QCZ[:, h, :, qc])

                            def post_c(accO, accD, h=h, qc=qc):
                                k.act(PT1[:, :], accD[:, :], AF.Ln)
                                k.act(PT1[:, :], PT1[:, :], AF.Exp, scale=-1.0)
                                k.tt(PT2[:, :], accO[:, :], PT1[:, :], ALU.mult)
                                k.stt(PT3, PT2[:, 256:512], D[:, D_NLAM:D_NLAM + 1], PT2[:, 0:256], ALU.mult, ALU.add)
                                k.tt(PT4, PT3, PT3, ALU.mult)
                                ss = banks[7]
                                k.mm(ss[:, 0:256], ONES[:, :], PT4)
                                k.act(PT5, ss[:, 0:256], AF.Ln, scale=1.0 / 128.0, bias=EPST[:])
                                k.act(PT5, PT5, AF.Exp, scale=-0.5)
                                k.stt(PT3, PT3, D[:, D_SUBG:D_SUBG + 1], PT5, ALU.mult, ALU.mult)
                                k.tt(OSC[:, h, qc], PT3, OSC[:, h, qc], ALU.mult)
                            units.append((qk_c, tiles, 128, post_c))
                        if lat:
                            order = ([(0, j) for j in range(16)] + [(1, j) for j in range(16)]
                                     + [(0, j) for j in range(16, 36)] + [(1, j) for j in range(16, 36)])
                            attend_all(units, order)
                        else:
                            attend_all(units)

                    wpl.kv_end()
                    for cc in range(8):
                        base = WP2_Q + cc * 4096
                        w = wpl.get(base)
                        wa = wpl.get(-1 - cc)
                        wm = w[:, 0:3072].rearrange("p (i k m) -> p i k m", i=3, k=8)
                        wb = w[:, 3072:3584].rearrange("p (k m) -> p k m", k=4)
                        wc = w[:, 3584:4096].rearrange("p (k m) -> p k m", k=4)
                        wa3 = wa[0:64, 0:1024].rearrange("p (k m) -> p k m", k=8)
                        pa, pb, pc = (banks[0], banks[1], banks[2]) if cc % 2 == 0 else (banks[5], banks[6], banks[7])
                        zs = [banks[3], banks[4], banks[3]]
                        proj(zs[0], wm[:, 0], 8, 128, hrhs)
                        proj(zs[1], wm[:, 1], 8, 128, hrhs)
                        k.act(MT[0][:], zs[0][:], AF.Exp, scale=-1.0)
                        proj(zs[2], wm[:, 2], 8, 128, hrhs)
                        k.act(MT[1][:], zs[1][:], AF.Exp, scale=-1.0)
                        k.act(MT[2][:], zs[2][:], AF.Exp, scale=-1.0)
                        proj(pa, wa3, 8, 128, lambda kk: OSA[0:64, kk, :], P=64)
                        proj(pb, wb, 4, 128, lambda kk: OSGB[:, kk, cols])
                        proj(pc, wc, 4, 128, lambda kk: OSC[:, kk, :])
                        for i, pbr in enumerate((pa, pb, pc)):
                            k.act(MT[i][:], MT[i][:], AF.Ln, bias=ONE_P[:])
                            k.act(MT[i][:], MT[i][:], AF.Exp, scale=-1.0)
                            k.tt(MT2[i][:], pbr[:], MT[i][:], ALU.mult)
                        k.tt(YACC[:], MT2[0][:], MT2[1][:], ALU.add)
                        k.tt(YT[:, cc, :], YACC[:], MT2[2][:], ALU.add)
                    for g in range(2):
                        base = WP2_Q + 8 * 4096 + g * 4096
                        w = wpl.get(base)
                        w4 = w[:, 0:4096].rearrange("p (c k m) -> p c k m", c=4, k=8)
                        for c4 in range(4):
                            cc = g * 4 + c4
                            ps = nb((6, 7))
                            proj(ps, w4[:, c4], 8, 128, lambda kk: YT[:, kk, :])
                            k.stt(XT[:, cc, cols], ps[:, :], MODV[l][:, 16 + cc, v:v + 1], XT[:, cc, cols], ALU.mult, ALU.add)
                S.barrier()

        layers()
        S.barrier()
        S.dma("sp", yT_out.ap().rearrange("(k p) t -> p k t", p=128), XT[:], is_output=True)
        S.dma("sp", o_st.ap(), STO[:], is_output=True)
        S.finish()
        S.replay()
    return nc


def _fm(w, M):
    K_, n = w.shape
    a = w.reshape(K_ // 128, 128, n // M, M).transpose(1, 2, 0, 3)
    return np.ascontiguousarray(a).reshape(128, -1)


_PROG = None
import os
STOP = os.environ.get('KSTOP', '')
KV = os.environ.get('KV', '')


def kernel(**inp):
    global _PROG
    f32 = lambda a: np.ascontiguousarray(np.asarray(a, dtype=np.float32))
    I = {k_: f32(v) for k_, v in inp.items()}
    w_in = I["w_in"]
    shared = {}
    for l in range(DEPTH):
        W = w_in[l]
        q_a, k_a, v_a, g_a = W[:, 0:512], W[:, 512:640], W[:, 640:768], W[:, 768:1280]
        x_b, g_b = W[:, 1280:1792], W[:, 1792:2304]
        q_c, k_c, v_c, g_c = W[:, 2304:2816], W[:, 2816:3328], W[:, 3328:3840], W[:, 3840:4352]
        merge = W[:, 4352:7424]
        vtm = lambda w: np.ascontiguousarray(w.reshape(8, 128, -1).transpose(1, 0, 2)).reshape(128, -1)
        shared[f"wp1{l}"] = np.concatenate([_fm(k_a, 64), _fm(k_c, 128), _fm(x_b, 128), _fm(g_b, 128),
                                            vtm(v_c), vtm(v_a)], axis=1)
        parts = [_fm(q_a, 64), _fm(q_c, 128), _fm(g_a, 64), _fm(g_c, 128)]
        for cc in range(8):
            for i in range(3):
                parts.append(_fm(merge[:, i * 1024 + cc * 128:i * 1024 + (cc + 1) * 128], 128))
            parts.append(_fm(I["w_br_b"][l][:, cc * 128:(cc + 1) * 128], 128))
            parts.append(_fm(I["w_br_c"][l][:, cc * 128:(cc + 1) * 128], 128))
        parts.append(_fm(I["w_out"][l], 128))
        shared[f"wp2{l}"] = np.concatenate(parts, axis=1)
        wa = I["w_br_a"][l]
        shared[f"wba{l}"] = np.ascontiguousarray(
            wa.reshape(8, 64, 8, 128).transpose(1, 2, 0, 3)).reshape(64, -1)
        shared[f"wmod{l}"] = _fm(I["mod_w"][l], 128)
    gl = np.zeros((128, DEPTH, 2, 2, 4, 128), np.float32)
    for l in range(DEPTH):
        for dr in range(2):
            for gi, nm in enumerate(("lru_wa", "lru_wx")):
                Wg = I[nm][l, dr]
                for ch in range(4):
                    for hb in range(2):
                        gl[hb * 64:(hb + 1) * 64, l, dr, gi, ch, hb * 64:(hb + 1) * 64] = Wg[ch * 2 + hb]
    shared["glru"] = gl.reshape(128, -1)
    R = np.zeros((128, 128), np.float32)
    for dp in range(128):
        if dp % 32 < 16:
            R[dp + 16, dp] = -1.0
        else:
            R[dp - 16, dp] = 1.0
    shared["rmat"] = R
    inv = np.power(np.float32(10000.0), -np.arange(16, dtype=np.float32) / np.float32(16)).astype(np.float32)
    a_ = np.arange(128)[:, None]
    b_ = np.arange(128)[None, :]
    ge = (a_ >= b_).astype(np.float32)
    le = (a_ <= b_).astype(np.float32)
    one = np.ones((128, 128), np.float32)
    zero = np.zeros((128, 128), np.float32)
    M0 = np.concatenate([ge, zero], 1)
    M1 = np.concatenate([one, ge], 1)
    M2 = np.concatenate([le, one], 1)
    M3 = np.concatenate([zero, le], 1)

    in_maps = []
    for c in range(8):
        s, j = c // 4, c % 4
        m = dict(shared)
        xs = np.concatenate([I["x_prompt"][2 * c], I["x_prompt"][2 * c + 1],
                             I["x_sample"][s, 1024 * j:1024 * (j + 1)]], axis=0)
        m["xT"] = np.ascontiguousarray(xs.T)
        sm_ = np.zeros((128, NSM), np.float32)

        def put(key, arr):
            o, w = SM[key]
            sm_[:, o:o + w] = arr
        col = lambda v, n: np.ascontiguousarray(v.reshape(n, 128).T)
        for l in range(DEPTH):
            put(("ng", l), col(I["norm_g"][l], 8))
            put(("modb", l), col(I["mod_b"][l], 24))
            put(("qna", l), np.tile(I["qn_a"][l], 2)[:, None])
            put(("kna", l), np.tile(I["kn_a"][l], 2)[:, None])
            put(("qnc", l), np.tile(I["qn_c"][l], 2)[:, None])
            put(("knc", l), np.tile(I["kn_c"][l], 2)[:, None])
            cw = I["conv_w"][l]
            put(("convw", l), np.ascontiguousarray(cw.reshape(4, 4, 128).transpose(2, 1, 0)).reshape(128, 16))
            put(("convb", l), col(I["conv_b"][l], 4))
            for nm, key in (("lru_ba", "ba"), ("lru_bx", "bx"), ("lru_lam", "lam")):
                put((key, l), np.ascontiguousarray(I[nm][l].reshape(2, 4, 128).transpose(2, 0, 1)).reshape(128, 8))
            put(("subln", l), I["subln_c"][l][:, None])
            put(("sink", l), np.broadcast_to(I["sink_a"][l][None, :], (128, 8)))
            for nm, key in (("lam_q1", "lq1"), ("lam_k1", "lk1"), ("lam_q2", "lq2"), ("lam_k2", "lk2")):
                put((key, l), np.broadcast_to(I[nm][l][None, :], (128, 64)))
            put(("h0", l), np.ascontiguousarray(I["state_lru"][s, l].reshape(2, 4, 128).transpose(2, 0, 1)).reshape(128, 8))
        cc_ = np.stack([col(I["c_ctx"], 8), col(I["c"][s], 8)], axis=2).reshape(128, 16)
        put("c", cc_)
        sel = np.zeros((128, 12), np.float32)
        if j > 0:
            sel[:, j - 1] = 1.0
        if j < 3:
            sel[:, 4 + j + 1] = 1.0
        sel[:, 8 + j] = 1.0
        put("sel", sel)
        m["small"] = sm_
        t = 1024 * j + np.arange(1024)
        row = (t // 64).astype(np.float32)
        colp = (t % 64).astype(np.float32)
        ang = np.zeros((64, 1024), np.float32)
        for d in range(64):
            ang[d] = (row if d < 32 else colp) * inv[d % 16]
        ang = np.concatenate([ang, ang], 0)
        m["cosT"] = np.cos(ang).astype(np.float32)
        m["sinT"] = np.sin(ang).astype(np.float32)
        first = M0 if j > 0 else np.zeros_like(M0)
        last = M3 if j < 3 else np.zeros_like(M3)
        msk = np.stack([first, M0, M1, M2, M3, last], 0)
        msk = np.concatenate([msk, msk], 2)
        m["masks"] = np.ascontiguousarray(msk.transpose(1, 0, 2)).reshape(128, -1)
        for l in range(DEPTH):
            ck = I["cache_c_k"][s, l]
            m[f"ckc{l}"] = np.ascontiguousarray(ck.transpose(2, 3, 1, 0)).reshape(128, 4 * 512)
            cv = I["cache_c_v"][s, l]
            m[f"cvc{l}"] = np.ascontiguousarray(cv.reshape(4, 128, 512).transpose(1, 0, 2)).reshape(128, -1)
            ka = I["cache_a_k"][s, l]
            m[f"cka{l}"] = np.ascontiguousarray(ka.transpose(2, 1, 0)).reshape(64, -1)
            va = I["cache_a_v"][s, l]
            m[f"cva{l}"] = np.ascontiguousarray(va.reshape(4, 128, 128).transpose(1, 0, 2)).reshape(128, -1)
        in_maps.append(m)

    if _PROG is None:
        _PROG = build_program()
    res = run_bass_kernel_spmd(_PROG, in_maps, core_ids=list(range(8)))
    R_ = res.results

    y_prompt = np.zeros((16, 256, 1024), np.float32)
    y_sample = np.zeros((2, 4096, 1024), np.float32)
    n_ka = np.zeros((16, 2, 256, 2, 64), np.float32)
    n_va = np.zeros((16, 2, 256, 2, 64), np.float32)
    n_kc = np.zeros((16, 2, 256, 4, 2, 64), np.float32)
    n_vc = np.zeros((16, 2, 256, 4, 128), np.float32)
    n_st = np.zeros((16, 2, 2, 512), np.float32)
    for c in range(8):
        s, j = c // 4, c % 4
        r = R_[c]
        y = np.asarray(r["yT"]).T
        y_prompt[2 * c] = y[0:256]
        y_prompt[2 * c + 1] = y[256:512]
        y_sample[s, 1024 * j:1024 * (j + 1)] = y[512:]
        ka = np.asarray(r["o_ka"]).reshape(2, 2, 64, 2, 256)
        kc = np.asarray(r["o_kc"]).reshape(2, 4, 2, 64, 2, 256)
        va = np.asarray(r["o_va"]).reshape(2, 2, 256, 2, 64)
        vc = np.asarray(r["o_vc"]).reshape(2, 2, 256, 4, 128)
        stt_ = np.asarray(r["o_st"]).reshape(128, 2, 2, 2, 4)
        for sq in range(2):
            bi = 2 * c + sq
            n_ka[bi] = ka[:, :, :, sq, :].transpose(0, 3, 1, 2)
            n_kc[bi] = kc[:, :, :, :, sq, :].transpose(0, 4, 1, 2, 3)
            n_va[bi] = va[:, sq]
            n_vc[bi] = vc[:, sq]
            n_st[bi] = stt_[:, :, sq].transpose(1, 2, 3, 0).reshape(2, 2, 512)
    return (y_prompt, y_sample, n_ka, n_va, n_kc, n_vc, n_st)
```

```python
import contextlib
import math
import numpy as np
import concourse.bass as bass
import concourse.mybir as mybir
from concourse.bass_utils import run_bass_kernel_spmd

F32 = mybir.dt.float32
BF16 = mybir.dt.bfloat16
AF = mybir.ActivationFunctionType
ALU = mybir.AluOpType
AX = mybir.AxisListType

ENGS = ("pe", "act", "dve", "pool", "sp")

DEPTH = 2
SCALE = 0.125
EPS = 1e-6
NTOK = 1536
KC_OFF, VC_OFF, KA_OFF, VA_OFF = 0, 0, 0, 2048
XBW = 1545
SEGS = [(0, 256, 0), (259, 256, 256), (518, 1024, 512)]

_SM_LAYER = [("ng", 8), ("modb", 24), ("qna", 1), ("kna", 1), ("qnc", 1), ("knc", 1), ("convw", 16),
             ("convb", 4), ("ba", 8), ("bx", 8), ("lam", 8), ("subln", 1), ("sink", 8),
             ("lq1", 64), ("lk1", 64), ("lq2", 64), ("lk2", 64), ("h0", 8)]
SM = {}
_o = 0
for _l in range(DEPTH):
    for _n, _w in _SM_LAYER:
        SM[(_n, _l)] = (_o, _w)
        _o += _w
SM["c"] = (_o, 16); _o += 16
SM["sel"] = (_o, 12); _o += 12
NSM = _o

WP1N = 1024 + 4096 * 3 + 5120
WP2_Q = 16384
WP2N = WP2_Q + 8 * 4096 + 8192


class Res:
    __slots__ = ("name", "last_w", "readers")

    def __init__(self, name=""):
        self.name = name
        self.last_w = None
        self.readers = {}


class Sched:
    NDMA = 8

    def __init__(self, nc, stack):
        self.nc = nc
        self.streams = {e: [] for e in ENGS}
        self.count = {e: 0 for e in ENGS}
        self.seen = {e: {} for e in ENGS}
        self.sems = {}
        self.resmap = {}
        self.split = {}
        for e in ENGS:
            self.sems[e] = stack.enter_context(nc.semaphore("prog_" + e))
        self.dma_k = {}
        for q in ("sp", "act", "pool"):
            self.dma_k[q] = 0
            for i in range(self.NDMA):
                self.sems[("dma", q, i)] = stack.enter_context(nc.semaphore(f"dma_{q}_{i}"))
        self.sems["cc"] = stack.enter_context(nc.semaphore("cc"))
        self.cc_count = 0
        self.out_events = []

    def _named(self, name):
        r = self.resmap.get(name)
        if r is None:
            r = Res(name)
            self.resmap[name] = r
        return r

    def _res(self, x):
        if isinstance(x, Res):
            return x
        return self._named(x.tensor.name)

    def _resl(self, x):
        if isinstance(x, Res):
            return [x]
        name = x.tensor.name
        b = self.split.get(name)
        if b is None:
            return [self._named(name)]
        try:
            ap = [list(e) for e in x.ap]
            col0 = int(x.offset) % int(ap[0][0])
            hi = col0 + 1
            for st_, cnt in ap[1:]:
                assert st_ >= 0
                hi += (int(cnt) - 1) * int(st_)
        except Exception:
            col0, hi = 0, 1 << 30
        out = []
        if col0 < b:
            out.append(self._named(name + ".A"))
        if hi > b:
            out.append(self._named(name + ".B"))
        return out

    def _collect(self, reads, writes):
        waits = {}

        def add(ev):
            if ev is None:
                return
            k, v = ev
            if waits.get(k, 0) < v:
                waits[k] = v
        for r in reads:
            add(r.last_w)
        for w in writes:
            add(w.last_w)
            for k, v in w.readers.items():
                add((k, v))
        return waits

    def _emit_waits(self, eng, waits):
        for k, v in waits.items():
            if k == eng:
                if eng == "pe":
                    continue
                if v <= self.count[eng] - 6:
                    continue
            if self.seen[eng].get(k, 0) >= v:
                continue
            self.seen[eng][k] = v
            self.streams[eng].append(("wait", k, v))

    def _commit(self, ev, reads, writes):
        k, v = ev
        for r in reads:
            if r.readers.get(k, 0) < v:
                r.readers[k] = v
        for w in writes:
            w.last_w = ev
            w.readers = {}

    def op(self, eng, fn, reads=(), writes=()):
        reads = [r for x in reads for r in self._resl(x)]
        writes = [r for x in writes for r in self._resl(x)]
        writes = writes + [r for r in reads if r.name.startswith("bank") and r not in writes]
        self._emit_waits(eng, self._collect(reads, writes))
        self.count[eng] += 1
        ev = (eng, self.count[eng])
        self.streams[eng].append(("op", fn, eng, 1))
        self._commit(ev, reads, writes)
        return ev

    def dma(self, q, out, in_, is_output=False, extra_reads=(), **kw):
        reads = self._resl(in_) + [r for x in extra_reads for r in self._resl(x)]
        writes = self._resl(out)
        waits = self._collect(reads, writes)
        k = self.dma_k[q]
        self.dma_k[q] += 1
        s = ("dma", q, k % self.NDMA)
        gen = k // self.NDMA
        if gen > 0 and waits.get(s, 0) < 16 * gen:
            waits[s] = 16 * gen
        self._emit_waits(q, waits)
        ev = (s, 16 * (gen + 1))
        self.streams[q].append(("op", lambda e: e.dma_start(out=out, in_=in_, **kw), s, 16))
        self._commit(ev, reads, writes)
        if is_output:
            self.out_events.append(ev)
        return ev

    def collective(self, groups, in_t, out_t):
        reads = [self._res(in_t.ap())]
        writes = [self._res(out_t.ap())]
        self._emit_waits("pool", self._collect(reads, writes))
        self.cc_count += 1
        ev = ("cc", self.cc_count)
        self.streams["pool"].append(("op", lambda e: e.collective_compute(
            "AllGather", ALU.bypass, replica_groups=groups, ins=[in_t.ap().opt()],
            outs=[out_t.ap().opt()]), "cc", 1))
        self._commit(ev, reads, writes)
        return ev

    def _all_events(self):
        waits = {}
        for e in ENGS:
            if self.count[e] > 0:
                waits[e] = self.count[e]
        for q in ("sp", "act", "pool"):
            k = self.dma_k[q]
            for i in range(self.NDMA):
                n = (k // self.NDMA) + (1 if i < k % self.NDMA else 0)
                if n > 0:
                    waits[("dma", q, i)] = 16 * n
        if self.cc_count:
            waits["cc"] = self.cc_count
        return waits

    def barrier(self, skip_cc=False):
        waits = self._all_events()
        if skip_cc:
            waits.pop("cc", None)
        for e in ENGS:
            for k, v in waits.items():
                if k == e:
                    continue
                if self.seen[e].get(k, 0) >= v:
                    continue
                self.seen[e][k] = v
                self.streams[e].append(("wait", k, v))

    def finish(self):
        self.barrier()

    def replay(self):
        nc = self.nc
        sems = self.sems
        streams = self.streams

        def run(engobj, name):
            for item in streams[name]:
                if item[0] == "wait":
                    engobj.wait_ge(sems[item[1]], item[2])
                else:
                    _, fn, semk, inc = item
                    fn(engobj).then_inc(sems[semk], inc)

        with nc.Block() as block:
            @block.tensor
            def _(e):
                run(e, "pe")

            @block.scalar
            def _(e):
                run(e, "act")

            @block.vector
            def _(e):
                run(e, "dve")

            @block.gpsimd
            def _(e):
                run(e, "pool")

            @block.sync
            def _(e):
                run(e, "sp")


class WPlan:
    def __init__(self, S, slotsA, slotsB, plan):
        self.S, self.A, self.B, self.plan = S, list(slotsA), list(slotsB), plan
        self.slot_of = {}
        self.live = {}
        self.cur = 0
        self.got = []
        self.markers_passed = 0

    def _pump(self):
        j = 0
        unpassed = 0
        seen_markers = 0
        for j in range(len(self.plan)):
            e = self.plan[j]
            if e[0] == 'KV':
                seen_markers += 1
                if seen_markers > self.markers_passed:
                    unpassed += 1
                    if unpassed > 1:
                        return
                continue
            if j in self.slot_of or j < self.cur:
                continue
            allowed = self.A + (self.B if unpassed == 0 else [])
            free = [t for t in allowed if t.name not in self.live]
            if not free:
                return
            t = free[0]
            _, src, P, n = e[0:4]
            self.S.dma("pool", t[0:P, 0:n], src, max_dma_last_dim=4096)
            self.slot_of[j] = t
            self.live[t.name] = j

    def _release(self, idx):
        t = self.slot_of.get(idx)
        if t is not None and self.live.get(t.name) == idx:
            del self.live[t.name]

    def get(self, src_off=None):
        while self.plan[self.cur][0] == 'KV':
            self.cur += 1
        i = self.cur
        if src_off is not None:
            assert self.plan[i][4] == src_off, (i, self.plan[i][4], src_off)
        while len(self.got) >= 2:
            self._release(self.got.pop(0))
        if i not in self.slot_of:
            self._pump()
        assert i in self.slot_of, ("no slot for load", i)
        self.cur += 1
        self.got.append(i)
        self._pump()
        return self.slot_of[i]

    def release_all(self):
        while self.got:
            self._release(self.got.pop(0))

    def kv_begin(self):
        self.release_all()
        self._pump()

    def kv_end(self):
        self.markers_passed += 1
        self._pump()


class K:
    def __init__(self, S):
        self.S = S

    @staticmethod
    def _aps(*xs):
        return [x for x in xs if x is not None and not isinstance(x, (int, float))]

    def act(self, out, in_, func, scale=1.0, bias=None):
        kw = {}
        if bias is not None:
            kw["bias"] = bias
        self.S.op("act", lambda e: e.activation(out=out, in_=in_, func=func, scale=scale, **kw),
                  reads=self._aps(in_, scale, bias), writes=[out])

    def ts(self, out, in0, s1, s2=None, op0=ALU.mult, op1=None, eng="dve"):
        kw = {}
        if op1 is not None:
            kw["op1"] = op1
        self.S.op(eng, lambda e: e.tensor_scalar(out=out, in0=in0, scalar1=s1, scalar2=s2, op0=op0, **kw),
                  reads=self._aps(in0, s1, s2), writes=[out])

    def tt(self, out, in0, in1, op, eng="dve"):
        self.S.op(eng, lambda e: e.tensor_tensor(out=out, in0=in0, in1=in1, op=op),
                  reads=[in0, in1], writes=[out])

    def stt(self, out, in0, scalar, in1, op0, op1, eng="dve"):
        self.S.op(eng, lambda e: e.scalar_tensor_tensor(out=out, in0=in0, scalar=scalar, in1=in1, op0=op0, op1=op1),
                  reads=self._aps(in0, scalar, in1), writes=[out])

    def recip(self, out, in_):
        self.S.op("dve", lambda e: e.reciprocal(out=out, in_=in_), reads=[in_], writes=[out])

    def copy(self, out, in_, eng="dve"):
        if in_.tensor.name.startswith("bank"):
            self.ts(out, in_, 1.0, None, op0=ALU.mult, eng=eng)
            return
        self.S.op(eng, lambda e: e.tensor_copy(out=out, in_=in_), reads=[in_], writes=[out])

    def memset(self, ap, val, eng="dve"):
        self.S.op(eng, lambda e: e.memset(ap, val), writes=[ap])

    def rsum(self, out, in_):
        self.S.op("dve", lambda e: e.reduce_sum(out=out, in_=in_, axis=AX.X), reads=[in_], writes=[out])

    def scan(self, out, a, b, init):
        self.S.op("dve", lambda e: e.tensor_tensor_scan(out=out, data0=a, data1=b, initial=init,
                                                        op0=ALU.mult, op1=ALU.add),
                  reads=self._aps(a, b, init), writes=[out])

    def mm(self, out, lhsT, rhs, start=True, stop=True):
        self.S.op("pe", lambda e: e.matmul(out, lhsT=lhsT, rhs=rhs, start=start, stop=stop),
                  reads=[lhsT, rhs], writes=[out])


def build_program():
    nc = bass.Bass("TRN2", target_bir_lowering=False)
    din = lambda n, s, dt=F32: nc.dram_tensor(n, s, dt, kind="ExternalInput")
    dout = lambda n, s: nc.dram_tensor(n, s, F32, kind="ExternalOutput")
    xT_in = din("xT", [1024, NTOK])
    small_in = din("small", [128, NSM])
    wmod = [din(f"wmod{l}", [128, 24576]) for l in range(DEPTH)]
    wp1 = [din(f"wp1{l}", [128, WP1N]) for l in range(DEPTH)]
    wp2 = [din(f"wp2{l}", [128, WP2N]) for l in range(DEPTH)]
    wba = [din(f"wba{l}", [64, 8192]) for l in range(DEPTH)]
    glru_in = din("glru", [128, 4096])
    rmat_in = din("rmat", [128, 128])
    cos_in = din("cosT", [128, 1024])
    sin_in = din("sinT", [128, 1024])
    msk_in = din("masks", [128, 3072])
    ckc_in = [din(f"ckc{l}", [128, 2048]) for l in range(DEPTH)]
    cvc_in = [din(f"cvc{l}", [128, 2048]) for l in range(DEPTH)]
    cka_in = [din(f"cka{l}", [64, 1024]) for l in range(DEPTH)]
    cva_in = [din(f"cva{l}", [128, 512]) for l in range(DEPTH)]

    yT_out = dout("yT", [1024, NTOK])
    o_ka = dout("o_ka", [DEPTH * 2 * 64, 512])
    o_kc = dout("o_kc", [DEPTH * 4 * 128, 512])
    o_va = dout("o_va", [DEPTH * 512, 128])
    o_vc = dout("o_vc", [DEPTH * 512, 512])
    o_st = dout("o_st", [128, 32])

    ga_in = nc.dram_tensor("ga_in", [128, 4096], BF16)
    ga_out = nc.dram_tensor("ga_out", [512, 4096], BF16)
    gb_in = nc.dram_tensor("gb_in", [128, 4096], BF16)
    gb_out = nc.dram_tensor("gb_out", [512, 4096], BF16)
    gc_in = nc.dram_tensor("gc_in", [128, 3072], BF16)
    gc_out = nc.dram_tensor("gc_out", [512, 3072], BF16)
    g2_in = nc.dram_tensor("g2_in", [128, 12], F32)
    g2_out = nc.dram_tensor("g2_out", [512, 12], F32)
    g3_in = nc.dram_tensor("g3_in", [128, 16], F32)
    g3_out = nc.dram_tensor("g3_out", [512, 16], F32)
    spill_a = nc.dram_tensor("spill_a", [128, 8 * 1024], F32)
    spill_b = nc.dram_tensor("spill_b", [128, 8 * 1024], F32)
    GROUPS = [[0, 1, 2, 3], [4, 5, 6, 7]]

    with contextlib.ExitStack() as st:
        S = Sched(nc, st)
        k = K(S)
        _cnt = [0]

        def T(stack, shape, dt=F32, name=None):
            _cnt[0] += 1
            return stack.enter_context(nc.sbuf_tensor(f"{name or 't'}_{_cnt[0]}", shape, dt))

        banks = [st.enter_context(nc.psum_tensor(f"bank{i}", [128, 512], F32)) for i in range(8)]

        XT = T(st, [128, 8, NTOK], F32, "XT")
        OSGB = T(st, [128, 4, NTOK], BF16, "OSGB")
        ONES = T(st, [128, 128], BF16, "ONES")
        BONES = T(st, [128, 128], BF16, "BONES")
        RM = T(st, [128, 128], BF16, "RM")
        COS = T(st, [128, 1024], F32, "COS")
        SIN = T(st, [128, 1024], F32, "SIN")
        MSK = T(st, [128, 6, 512], BF16, "MSK")
        SMALL = T(st, [128, NSM], F32, "SMALL")
        EPST = T(st, [128, 1], F32, "EPST")
        MODV = [T(st, [128, 24, 2], F32, "MODV") for _ in range(DEPTH)]
        GS = [T(st, [128, 8, 2], F32, "GS") for _ in range(DEPTH)]
        DER = [T(st, [128, 64], F32, "DER") for _ in range(DEPTH)]
        KAW = T(st, [128, 2, 1280], BF16, "KAW")
        VAW = T(st, [128, 10, 128], BF16, "VAW")
        VAWS = T(st, [128, 10, 128], BF16, "VAWS")
        KACTX = T(st, [128, 2, 512], BF16, "KACTX")
        KCCTX = T(st, [128, 4, 512], BF16, "KCCTX")
        VACTX = T(st, [128, 4, 128], BF16, "VACTX")
        VACTXS = T(st, [128, 4, 128], BF16, "VACTXS")
        VCCTX = T(st, [128, 4, 512], BF16, "VCCTX")
        STO = T(st, [128, 32], F32, "STO")
        WSL = [T(st, [128, 4096], BF16, "WSL") for _ in range(2)]
        ws_i = [0]

        def sm(name, l=None):
            o, w = SM[(name, l)] if l is not None else SM[name]
            return SMALL[:, o:o + w]

        def wload(src_ap, P, n):
            slot = WSL[ws_i[0] % len(WSL)]
            ws_i[0] += 1
            S.dma("pool", slot[0:P, 0:n], src_ap, max_dma_last_dim=4096)
            return slot

        bank_i = [0]

        def nb(pool=(0, 1, 2, 3)):
            b = banks[pool[bank_i[0] % len(pool)]]
            bank_i[0] += 1
            return b

        D_NBA, D_NBX, D_CL, D_C2, D_ESINK, D_NLAM, D_SUBG = 0, 8, 16, 24, 32, 40, 41

        S.dma("sp", XT[:], xT_in.ap().rearrange("(k p) t -> p k t", p=128))
        S.dma("sp", SMALL[:], small_in.ap())
        S.dma("sp", COS[:], cos_in.ap())
        S.dma("sp", SIN[:], sin_in.ap())
        S.dma("pool", RM[:], rmat_in.ap())
        S.dma("pool", MSK[:].rearrange("p a b -> p (a b)"), msk_in.ap(), max_dma_last_dim=4096)
        k.memset(ONES[:], 1.0)
        k.memset(BONES[:], 0.0)
        k.memset(BONES[0:64, 0:64], 1.0)
        k.memset(BONES[64:128, 64:128], 1.0)
        k.memset(EPST[:], EPS)
        ONE_P = T(st, [128, 1], F32, "ONE_P")
        k.memset(ONE_P[:], 1.0)
        k.memset(KAW[:], 0.0)
        k.memset(VAW[:], 0.0)
        k.memset(VAWS[:], 0.0)
        k.memset(KACTX[:], 0.0)
        k.memset(WSL[0][:, 0:2048], 0.0)
        S.dma("sp", gc_in.ap()[64:128, 0:2048], WSL[0][64:128, 0:2048])

        with contextlib.ExitStack() as ph:
            SCF = T(ph, [128, 16], F32, "SCF")
            SCF2 = T(ph, [128, 16], F32, "SCF2")
            SCB = T(ph, [128, 8, 2], BF16, "SCB")
            TMPS = T(ph, [128, 64], F32, "TMPS")
            TMPS2 = T(ph, [128, 64], F32, "TMPS2")
            cT = sm("c")
            k.act(SCF[:], cT, AF.Exp, scale=-1.0)
            k.ts(SCF[:], SCF[:], 1.0, None, op0=ALU.add)
            k.recip(SCF2[:], SCF[:])
            k.tt(SCB[:].rearrange("p a b -> p (a b)"), cT, SCF2[:], ALU.mult)
            for l in range(DEPTH):
                ps = nb()
                for g in range(6):
                    w = wload(wmod[l].ap()[:, g * 4096:(g + 1) * 4096], 128, 4096)
                    wv = w[:, :].rearrange("p (c k m) -> p c k m", c=4, k=8)
                    for c4 in range(4):
                        cc = g * 4 + c4
                        for kk in range(8):
                            k.mm(ps[:, cc * 2:cc * 2 + 2], wv[:, c4, kk, :], SCB[:, kk, :],
                                 start=(kk == 0), stop=(kk == 7))
                psv = ps[:, 0:48].rearrange("p (c v) -> p c v", v=2)
                for v in range(2):
                    k.tt(MODV[l][:, :, v], psv[:, :, v], sm("modb", l), ALU.add)
                    k.stt(GS[l][:, :, v], MODV[l][:, 8:16, v], 1.0, sm("ng", l), ALU.add, ALU.mult)
                D = DER[l]
                k.ts(D[:, D_NBA:D_NBA + 8], sm("ba", l), -1.0, None, op0=ALU.mult)
                k.ts(D[:, D_NBX:D_NBX + 8], sm("bx", l), -1.0, None, op0=ALU.mult)
                k.act(TMPS[:, 0:8], sm("lam", l), AF.Exp, scale=-1.0)
                k.ts(TMPS[:, 0:8], TMPS[:, 0:8], 1.0, None, op0=ALU.add)
                k.act(TMPS2[:, 0:8], TMPS[:, 0:8], AF.Ln)
                k.ts(D[:, D_CL:D_CL + 8], TMPS2[:, 0:8], -8.0, None, op0=ALU.mult)
                k.ts(D[:, D_C2:D_C2 + 8], TMPS2[:, 0:8], -16.0, None, op0=ALU.mult)
                k.act(D[:, D_ESINK:D_ESINK + 8], sm("sink", l), AF.Exp)
                lam_init = 0.8 - 0.6 * math.exp(-0.3 * l)
                k.tt(TMPS[:, 0:64], sm("lq1", l), sm("lk1", l), ALU.mult)
                k.rsum(TMPS2[:, 8:9], TMPS[:, 0:64])
                k.tt(TMPS[:, 0:64], sm("lq2", l), sm("lk2", l), ALU.mult)
                k.rsum(TMPS2[:, 9:10], TMPS[:, 0:64])
                k.act(TMPS2[:, 10:12], TMPS2[:, 8:10], AF.Exp)
                k.tt(TMPS2[:, 12:13], TMPS2[:, 11:12], TMPS2[:, 10:11], ALU.subtract)
                k.ts(D[:, D_NLAM:D_NLAM + 1], TMPS2[:, 12:13], -lam_init, None, op0=ALU.add)
                k.ts(D[:, D_SUBG:D_SUBG + 1], sm("subln", l), 1.0 - lam_init, None, op0=ALU.mult)
            S.barrier()

        def front(ph_t, l, b):
            HT, SQ, RSTD, FTMP = ph_t["HT"], ph_t["SQ"], ph_t["RSTD"], ph_t["FTMP"]
            v = 0 if b == 0 else 1
            cols = slice(b * 512, (b + 1) * 512)
            ps = banks[7]
            for kk in range(8):
                sq = SQ[kk % 2]
                if kk % 2 == 0:
                    k.act(sq[:], XT[:, kk, cols], AF.Square)
                else:
                    k.tt(sq[:], XT[:, kk, cols], XT[:, kk, cols], ALU.mult, eng=("pool" if kk % 4 == 1 else "dve"))
                k.mm(ps[:], ONES[:], sq[:], start=(kk == 0), stop=(kk == 7))
            k.act(RSTD[:], ps[:], AF.Ln, scale=1.0 / 1024.0, bias=EPST[:])
            k.act(RSTD[:], RSTD[:], AF.Exp, scale=-0.5)
            for kk in range(8):
                ft = FTMP[kk % 2]
                k.tt(ft[:], XT[:, kk, cols], RSTD[:], ALU.mult)
                k.act(HT[:, kk, :], ft[:], AF.Identity, scale=GS[l][:, kk, v:v + 1], bias=MODV[l][:, kk, v:v + 1])

        def proj(ps, w3, nk, M, rhs_fn, N=512, P=128):
            if isinstance(w3, tuple):
                flat, base = w3
                for kk in range(nk):
                    k.mm(ps[0:128, 0:N], flat[0:P, base + kk * 64:base + kk * 64 + 128], rhs_fn(kk),
                         start=(kk == 0), stop=(kk == nk - 1))
                return
            for kk in range(nk):
                k.mm(ps[0:M, 0:N], w3[0:P, kk, 0:M], rhs_fn(kk), start=(kk == 0), stop=(kk == nk - 1))

        qk_i = [0]

        def qknorm(ph_t, ps, P, gain, rope_cols, dest_fn):
            i = qk_i[0] % 2
            qk_i[0] += 1
            X32, SQB, LNV, XN = ph_t["QX"][i], ph_t["QS"][i], ph_t["QL"][i], ph_t["QN"][i]
            k.act(X32[0:P, :], ps[0:P, :], AF.Copy)
            k.act(SQB[0:P, :], ps[0:P, :], AF.Square)
            ss = nb((4, 5))
            k.mm(ss[0:P, :], BONES[0:P, 0:P], SQB[0:P, :])
            k.act(LNV[0:P, :], ss[0:P, :], AF.Ln, scale=1.0 / 64.0, bias=EPST[0:P, :])
            k.act(LNV[0:P, :], LNV[0:P, :], AF.Exp, scale=-0.5)
            k.stt(XN[0:P, :], X32[0:P, :], gain, LNV[0:P, :], ALU.mult, ALU.mult)
            if rope_cols is None:
                dest_fn(XN)
                return
            k.copy(SQB[0:P, :], XN[0:P, :], eng="pool")
            rx = nb((4, 5))
            k.mm(rx[0:P, :], RM[0:P, 0:P], SQB[0:P, :])
            k.tt(X32[0:P, :], XN[0:P, :], COS[0:P, rope_cols], ALU.mult, eng="pool")
            k.tt(LNV[0:P, :], rx[0:P, :], SIN[0:P, rope_cols], ALU.mult)
            dest_fn((X32, LNV))

        def qk_pipeline(ph_t, items, rope_cols, rhs_fn):
            n = len(items)
            st = [None] * n
            base = qk_i[0]
            qk_i[0] += n

            def tiles(i):
                j = (base + i) % 2
                return ph_t["QX"][j], ph_t["QS"][j], ph_t["QL"][j], ph_t["QN"][j]

            def stage_a(i):
                w3, P, gain, dest = items[i]
                ps = nb()
                proj(ps, w3, 8, P, rhs_fn)
                st[i] = ps

            def stage_b(i):
                w3, P, gain, dest = items[i]
                X32, SQB, LNV, XN = tiles(i)
                ps = st[i]
                k.act(SQB[0:P, :], ps[0:P, :], AF.Square)
                ss = nb((4, 5))
                k.mm(ss[0:P, :], BONES[0:P, 0:P], SQB[0:P, :])
                k.act(LNV[0:P, :], ss[0:P, :], AF.Ln, scale=1.0 / 64.0, bias=EPST[0:P, :])
                k.act(LNV[0:P, :], LNV[0:P, :], AF.Exp, scale=-0.5)
                k.stt(XN[0:P, :], ps[0:P, :], gain, LNV[0:P, :], ALU.mult, ALU.mult)
                if rope_cols is None:
                    dest(XN)

            def stage_c0(i):
                w3, P, gain, dest = items[i]
                X32, SQB, LNV, XN = tiles(i)
                k.act(SQB[0:P, :], XN[0:P, :], AF.Copy)

            def stage_c1(i):
                w3, P, gain, dest = items[i]
                X32, SQB, LNV, XN = tiles(i)
                rx = nb((6, 7))
                k.mm(rx[0:P, :], RM[0:P, 0:P], SQB[0:P, :])
                k.tt(X32[0:P, :], XN[0:P, :], COS[0:P, rope_cols], ALU.mult, eng="pool")
                k.tt(LNV[0:P, :], rx[0:P, :], SIN[0:P, rope_cols], ALU.mult)
                dest((X32, LNV))

            for step in range(n + 2):
                if rope_cols is not None and 0 <= step - 2 < n:
                    stage_c0(step - 2)
                if step < n:
                    stage_a(step)
                if 0 <= step - 1 < n:
                    stage_b(step - 1)
                if rope_cols is not None and 0 <= step - 2 < n:
                    stage_c1(step - 2)

        def silu_to(ph_t, ps, P, out_ap):
            i = qk_i[0] % 2
            qk_i[0] += 1
            E = ph_t["QX"][i]
            k.act(E[0:P, :], ps[0:P, :], AF.Exp, scale=-1.0)
            k.act(E[0:P, :], E[0:P, :], AF.Ln, bias=ONE_P[0:P, :])
            k.act(E[0:P, :], E[0:P, :], AF.Exp, scale=-1.0)
            k.tt(out_ap, ps[0:P, :], E[0:P, :], ALU.mult)

        def common_tiles(ph):
            sq = [T(ph, [128, 512], BF16, "SQ") for _ in range(2)]
            qn = [T(ph, [128, 512], F32, "QN") for _ in range(2)]
            return {
                "HT": T(ph, [128, 8, 512], BF16, "HT"),
                "SQ": sq,
                "RSTD": T(ph, [128, 512], F32, "RSTD"),
                "FTMP": qn,
                "QX": [T(ph, [128, 512], F32, "QX") for _ in range(2)],
                "QS": sq,
                "QL": [T(ph, [128, 512], F32, "QL") for _ in range(2)],
                "QN": qn,
            }

        class _Stop(Exception):
            pass

        def layers():
          for l in range(DEPTH):
            D = DER[l]
            if STOP == "setup":
                return
            with contextlib.ExitStack() as ph0:
              XB = T(ph0, [128, 4, XBW], F32, "XB")
              with contextlib.ExitStack() as ph:
                pt = common_tiles(ph)
                HT = pt["HT"]
                KST = [T(ph, [128, 512], BF16, "KST") for _ in range(2)]
                VTMP = [T(ph, [128, 512], F32, "VTMP") for _ in range(1)]
                VST = [T(ph, [128, 512], BF16, "VST") for _ in range(2)]
                VATMP = T(ph, [128, 128], F32, "VATMP")
                k.memset(XB[:], 0.0)
                kst_i = [0]
                W1 = {}
                for (o_, n_) in ((0, 1024), (1024, 4096), (5120, 4096), (9216, 4096), (13312, 4096), (17408, 1024)):
                    t_ = T(ph, [128, n_ + (64 if o_ == 0 else 0)], BF16, "W1")
                    if o_ == 0:
                        k.memset(t_[:, n_:n_ + 64], 0.0)
                    S.dma("pool", t_[:, 0:n_], wp1[l].ap()[:, o_:o_ + n_], max_dma_last_dim=4096)
                    W1[o_] = t_

                class _W1:
                    def get(self, off):
                        return W1[off]
                wpl = _W1()
                for b in range(3):
                    lat = b > 0
                    front(pt, l, b)
                    rope_cols = slice((b - 1) * 512, b * 512) if lat else None
                    hrhs = lambda kk: HT[:, kk, :]
                    if STOP == "front":
                        return
                    w = wpl.get(0)
                    w4 = w[:, 0:1024].rearrange("p (c k m) -> p c k m", c=2, k=8)
                    qitems = []
                    for kv in range(2):
                        if not lat:
                            def dest(XN, kv=kv):
                                S.dma("sp", o_ka.ap()[(l * 2 + kv) * 64:(l * 2 + kv + 1) * 64, :], XN[0:64, :],
                                      is_output=True)
                                k.copy(KACTX[0:64, kv, :], XN[0:64, :])
                        else:
                            def dest(t, kv=kv, b=b):
                                dst = KAW[0:64, kv, 128 + (b - 1) * 512:128 + b * 512]
                                k.tt(dst, t[0][0:64, :], t[1][0:64, :], ALU.add)
                                S.dma("sp", gc_in.ap()[0:64, KA_OFF + kv * 1024 + (b - 1) * 512:KA_OFF + kv * 1024 + b * 512], dst)
                        qitems.append(((w, kv * 512), 64, sm("kna", l)[0:64, :], dest))
                    qk_pipeline(pt, qitems, rope_cols, hrhs)
                    if STOP == "ka":
                        return
                    w = wpl.get(1024)
                    w4 = w[:, 0:4096].rearrange("p (c k m) -> p c k m", c=4, k=8)
                    qitems = []
                    for h in range(4):
                        if not lat:
                            def dest(XN, h=h):
                                S.dma("sp", o_kc.ap()[(l * 4 + h) * 128:(l * 4 + h + 1) * 128, :], XN[:, :], is_output=True)
                                k.copy(KCCTX[:, h, :], XN[:, :])
                        else:
                            def dest(t, h=h, b=b):
                                ks = KST[kst_i[0] % 2]
                                kst_i[0] += 1
                                k.tt(ks[:], t[0][:, :], t[1][:, :], ALU.add)
                                S.dma("sp", ga_in.ap()[:, KC_OFF + h * 1024 + (b - 1) * 512:KC_OFF + h * 1024 + b * 512], ks[:])
                        qitems.append((w4[:, h], 128, sm("knc", l), dest))
                    qk_pipeline(pt, qitems, rope_cols, hrhs)
                    if STOP == "kc":
                        return
                    w = wpl.get(5120)
                    w4 = w[:, 0:4096].rearrange("p (c k m) -> p c k m", c=4, k=8)
                    for ch in range(4):
                        ps = nb()
                        proj(ps, w4[:, ch], 8, 128, hrhs)
                        if b == 0:
                            k.act(XB[:, ch, 2:258], ps[:, 0:256], AF.Copy)
                            k.act(XB[:, ch, 261:517], ps[:, 256:512], AF.Copy)
                        else:
                            c0 = 520 + (b - 1) * 512
                            k.act(XB[:, ch, c0:c0 + 512], ps[:, :], AF.Copy)
                    if STOP == "xb":
                        return
                    w = wpl.get(9216)
                    w4 = w[:, 0:4096].rearrange("p (c k m) -> p c k m", c=4, k=8)
                    for ch in range(4):
                        ps = nb()
                        proj(ps, w4[:, ch], 8, 128, hrhs)
                        silu_to(pt, ps, 128, OSGB[:, ch, b * 512:(b + 1) * 512])
                    if STOP == "gb":
                        return
                    w = wpl.get(13312)
                    w2 = wpl.get(17408)
                    wv = w[:, 0:4096].rearrange("p (k n) -> p k n", k=8)
                    wva = w2[:, 0:1024].rearrange("p (k n) -> p k n", k=8)
                    for tt_ in range(4):
                        tok = slice(tt_ * 128, (tt_ + 1) * 128)
                        psc = nb((4, 5))
                        psa = banks[6]
                        for kk in range(8):
                            k.mm(psc[:, :], HT[:, kk, tok], wv[:, kk, :], start=(kk == 0), stop=(kk == 7))
                        if 'a' in KV:
                            continue
                        for kk in range(8):
                            k.mm(psa[:, 0:128], HT[:, kk, tok], wva[:, kk, :], start=(kk == 0), stop=(kk == 7))
                        if 'b' in KV:
                            continue
                        if not lat:
                            vt = VTMP[0]
                            if 'e' not in KV:
                                k.act(vt[:], psc[:, :], AF.Copy)
                            if 'c' not in KV:
                                S.dma("sp", o_vc.ap()[l * 512 + tt_ * 128:l * 512 + (tt_ + 1) * 128, :], vt[:], is_output=True)
                            if 'f' not in KV:
                                k.copy(VCCTX[:, tt_, :], psc[:, :])
                            if 'd' in KV:
                                continue
                            k.act(VATMP[:], psa[:, 0:128], AF.Copy)
                            if 'c' not in KV:
                                S.dma("sp", o_va.ap()[l * 512 + tt_ * 128:l * 512 + (tt_ + 1) * 128, :], VATMP[:], is_output=True)
                            k.copy(VACTX[:, tt_, :], psa[:, 0:128])
                            k.copy(VACTXS[:, tt_, 0:64], psa[:, 64:128])
                            k.copy(VACTXS[:, tt_, 64:128], psa[:, 0:64])
                        else:
                            Tt = (b - 1) * 4 + tt_
                            vs = VST[tt_ % 2]
                            k.act(vs[:], psc[:, :], AF.Copy)
                            S.dma("sp", gb_in.ap()[:, VC_OFF + Tt * 512:VC_OFF + (Tt + 1) * 512], vs[:])
                            k.copy(VAW[:, 1 + Tt, :], psa[:, 0:128])
                            k.copy(VAWS[:, 1 + Tt, 0:64], psa[:, 64:128])
                            k.copy(VAWS[:, 1 + Tt, 64:128], psa[:, 0:64])
                            S.dma("sp", gc_in.ap()[:, VA_OFF + Tt * 128:VA_OFF + (Tt + 1) * 128], VAW[:, 1 + Tt, :])
                    if STOP == "v":
                        return
                if STOP == "p1":
                    return
                G2S = T(ph, [128, 12], F32, "G2S")
                for ch in range(4):
                    k.copy(G2S[:, ch * 3:ch * 3 + 1], XB[:, ch, 520:521])
                    k.copy(G2S[:, ch * 3 + 1:ch * 3 + 3], XB[:, ch, 520 + 1022:520 + 1024])
                S.dma("sp", g2_in.ap(), G2S[:])
                S.collective(GROUPS, g2_in, g2_out)
                S.collective(GROUPS, gc_in, gc_out)
                S.collective(GROUPS, ga_in, ga_out)
                S.collective(GROUPS, gb_in, gb_out)
                S.barrier(skip_cc=True)
              if STOP == "cc":
                  return
              with contextlib.ExitStack() as ph:
                GL = T(ph, [128, 16, 128], BF16, "GL")
                S.dma("pool", GL[:].rearrange("p a b -> p (a b)"), glru_in.ap()[:, l * 2048:(l + 1) * 2048], max_dma_last_dim=4096)
                Us = [T(ph, [128, 1024], F32, "U") for _ in range(2)]
                UBs = [T(ph, [128, 1024], BF16, "UB") for _ in range(2)]
                AA = [T(ph, [128, 1024], F32, "AA") for _ in range(2)]
                BB = [T(ph, [128, 1024], F32, "BB") for _ in range(2)]
                HH = [T(ph, [128, 1024], F32, "HH") for _ in range(2)]
                LT = [[T(ph, [128, 512], F32, "LT") for _ in range(4)] for _ in range(2)]
                ONE_T = T(ph, [128, 1], F32, "ONE_T")
                k.memset(ONE_T[:], 1.0)
                u_i = [0]
                HAL = T(ph, [128, 4, 12], F32, "HAL")
                TOT = T(ph, [128, 16], F32, "TOT")
                RS = T(ph, [128, 8], F32, "RS")
                TOTG = T(ph, [128, 4, 16], F32, "TOTG")
                CH = T(ph, [128, 40], F32, "CH")
                HIN = T(ph, [128, 8], F32, "HIN")
                sel = sm("sel")

                def lat_halo():
                    S.dma("sp", HAL[:], g2_out.ap().rearrange("(r p) c -> p r c", p=128))
                    for ch in range(4):
                        pre = XB[:, ch, 518:520]
                        post = XB[:, ch, 1544:1545]
                        for r in range(4):
                            k.stt(pre, HAL[:, r, ch * 3 + 1:ch * 3 + 3], sel[:, r:r + 1], pre, ALU.mult, ALU.add)
                            k.stt(post, HAL[:, r, ch * 3:ch * 3 + 1], sel[:, 4 + r:5 + r], post, ALU.mult, ALU.add)

                def lru_gates(ch, s0, Tn, U, UB, want_rsum):
                    npc = (Tn + 511) // 512
                    for pc in range(npc):
                        n = min(512, Tn)
                        cs = slice(pc * 512, pc * 512 + n)
                        seqs = []
                        for dr in range(2):
                            nba = D[:, D_NBA + dr * 4 + ch:D_NBA + dr * 4 + ch + 1]
                            nbx = D[:, D_NBX + dr * 4 + ch:D_NBX + dr * 4 + ch + 1]
                            cl = D[:, D_CL + dr * 4 + ch:D_CL + dr * 4 + ch + 1]
                            c2 = D[:, D_C2 + dr * 4 + ch:D_C2 + dr * 4 + ch + 1]
                            L0, L1, L2, L3 = [t[:, 0:n] for t in LT[dr]]
                            zr = banks[(pc % 2) * 4 + dr * 2]
                            zi = banks[(pc % 2) * 4 + dr * 2 + 1]
                            ops = [
                                lambda zr=zr, dr=dr: k.mm(zr[:, 0:n], GL[:, (dr * 2 + 0) * 4 + ch, :], UB[:, cs]),
                                lambda zi=zi, dr=dr: k.mm(zi[:, 0:n], GL[:, (dr * 2 + 1) * 4 + ch, :], UB[:, cs]),
                                lambda L0=L0, zr=zr, nba=nba: k.act(L0, zr[:, 0:n], AF.Exp, scale=-1.0, bias=nba),
                                lambda L1=L1, zi=zi, nbx=nbx: k.act(L1, zi[:, 0:n], AF.Exp, scale=-1.0, bias=nbx),
                                lambda L0=L0: k.act(L0, L0, AF.Ln, bias=ONE_T[:]),
                                lambda L1=L1: k.act(L1, L1, AF.Ln, bias=ONE_T[:]),
                                lambda L0=L0: k.act(L0, L0, AF.Exp, scale=-1.0),
                                lambda L0=L0, dr=dr, cl=cl: k.act(AA[dr][:, cs], L0, AF.Exp, scale=cl),
                                lambda L2=L2, L0=L0, c2=c2: k.ts(L2, L0, c2, None, op0=ALU.mult),
                                lambda L3=L3, L2=L2: k.ts(L3, L2, 1.0 / 24.0, 1.0 / 6.0, op0=ALU.mult, op1=ALU.add),
                                lambda L3=L3, L2=L2: k.tt(L3, L3, L2, ALU.mult),
                                lambda L3=L3, L2=L2: k.stt(L3, L3, 0.5, L2, ALU.add, ALU.mult),
                                lambda L3=L3, L2=L2: k.stt(L3, L3, 1.0, L2, ALU.add, ALU.mult),
                                lambda L3=L3: k.act(L3, L3, AF.Ln, scale=-1.0),
                                lambda L3=L3, L1=L1: k.stt(L3, L3, 0.5, L1, ALU.mult, ALU.subtract),
                                lambda L3=L3: k.act(L3, L3, AF.Exp),
                                lambda L3=L3, dr=dr: k.tt(BB[dr][:, cs], L3, U[:, cs], ALU.mult),
                            ]
                            if want_rsum:
                                ops.insert(8, lambda L0=L0, dr=dr, pc=pc: k.rsum(RS[:, dr * 2 + pc:dr * 2 + pc + 1], L0))
                            seqs.append(ops)
                        for o0, o1 in zip(*seqs):
                            o0()
                            o1()

                def conv_u(ch, s0, Tn):
                    U = Us[u_i[0] % 2]
                    UB = UBs[u_i[0] % 2]
                    u_i[0] += 1
                    cw = sm("convw", l)
                    k.ts(U[:, 0:Tn], XB[:, ch, s0:s0 + Tn], cw[:, ch * 4:ch * 4 + 1], sm("convb", l)[:, ch:ch + 1],
                         op0=ALU.mult, op1=ALU.add)
                    for j in range(1, 4):
                        k.stt(U[:, 0:Tn], XB[:, ch, s0 + j:s0 + j + Tn], cw[:, ch * 4 + j:ch * 4 + j + 1], U[:, 0:Tn],
                              ALU.mult, ALU.add)
                    k.copy(UB[:, 0:Tn], U[:, 0:Tn])
                    return U, UB

                def do_scan(dr, Tn, init):
                    if dr == 0:
                        k.scan(HH[0][:, 0:Tn], AA[0][:, 0:Tn], BB[0][:, 0:Tn], init)
                    else:
                        k.scan(HH[1][:, 0:Tn][:, ::-1], AA[1][:, 0:Tn][:, ::-1], BB[1][:, 0:Tn][:, ::-1], init)

                def lru_seg(ch, si):
                        s0, Tn, tk0 = SEGS[si]
                        U, UB = conv_u(ch, s0, Tn)
                        lru_gates(ch, s0, Tn, U, UB, si == 2)
                        for dr in range(2):
                            do_scan(dr, Tn, 0.0)
                            endc = Tn - 1 if dr == 0 else 0
                            if si < 2:
                                col = ((l * 2 + si) * 2 + dr) * 4 + ch
                                k.copy(STO[:, col:col + 1], HH[dr][:, endc:endc + 1])
                            else:
                                k.copy(TOT[:, dr * 8 + 4 + ch:dr * 8 + 5 + ch], HH[dr][:, endc:endc + 1])
                                k.tt(RS[:, 4 + dr:5 + dr], RS[:, dr * 2:dr * 2 + 1], RS[:, dr * 2 + 1:dr * 2 + 2], ALU.add)
                                k.act(TOT[:, dr * 8 + ch:dr * 8 + ch + 1], RS[:, 4 + dr:5 + dr], AF.Exp,
                                      scale=D[:, D_CL + dr * 4 + ch:D_CL + dr * 4 + ch + 1])
                                sl = slice((ch * 2 + dr) * 1024, (ch * 2 + dr + 1) * 1024)
                                S.dma("sp", spill_a.ap()[:, sl], AA[dr][:, :])
                                S.dma("sp", spill_b.ap()[:, sl], BB[dr][:, :])
                        if si < 2:
                            k.tt(HH[0][:, 0:Tn], HH[0][:, 0:Tn], HH[1][:, 0:Tn], ALU.add, eng="pool")
                            k.tt(OSGB[:, ch, tk0:tk0 + Tn], HH[0][:, 0:Tn], OSGB[:, ch, tk0:tk0 + Tn], ALU.mult, eng="pool")
                for ch in range(2):
                    lru_seg(ch, 0)
                    lru_seg(ch, 1)
                lat_halo()
                for ch in range(4):
                    lru_seg(ch, 2)
                S.dma("sp", g3_in.ap(), TOT[:])
                S.collective(GROUPS, g3_in, g3_out)
                for ch in range(2, 4):
                    lru_seg(ch, 0)
                    lru_seg(ch, 1)
                S.dma("sp", TOTG[:], g3_out.ap().rearrange("(r p) c -> p r c", p=128))
                h0 = sm("h0", l)
                k.copy(CH[:, 0:4], h0[:, 0:4])
                for r in range(3):
                    k.tt(CH[:, (r + 1) * 4:(r + 2) * 4], TOTG[:, r, 0:4], CH[:, r * 4:(r + 1) * 4], ALU.mult)
                    k.tt(CH[:, (r + 1) * 4:(r + 2) * 4], CH[:, (r + 1) * 4:(r + 2) * 4], TOTG[:, r, 4:8], ALU.add)
                k.copy(CH[:, 16 + 12:16 + 16], h0[:, 4:8])
                for r in (3, 2, 1):
                    k.tt(CH[:, 16 + (r - 1) * 4:16 + r * 4], TOTG[:, r, 8:12], CH[:, 16 + r * 4:16 + (r + 1) * 4], ALU.mult)
                    k.tt(CH[:, 16 + (r - 1) * 4:16 + r * 4], CH[:, 16 + (r - 1) * 4:16 + r * 4], TOTG[:, r, 12:16], ALU.add)
                k.memset(HIN[:], 0.0)
                for r in range(4):
                    k.stt(HIN[:, 0:4], CH[:, r * 4:(r + 1) * 4], sel[:, 8 + r:9 + r], HIN[:, 0:4], ALU.mult, ALU.add)
                    k.stt(HIN[:, 4:8], CH[:, 16 + r * 4:16 + (r + 1) * 4], sel[:, 8 + r:9 + r], HIN[:, 4:8], ALU.mult, ALU.add)
                s0, Tn, tk0 = SEGS[2]
                for ch in range(4):
                    for dr in range(2):
                        sl = slice((ch * 2 + dr) * 1024, (ch * 2 + dr + 1) * 1024)
                        S.dma("sp", AA[dr][:, :], spill_a.ap()[:, sl])
                        S.dma("sp", BB[dr][:, :], spill_b.ap()[:, sl])
                        do_scan(dr, Tn, HIN[:, dr * 4 + ch:dr * 4 + ch + 1])
                    k.tt(HH[0][:, :], HH[0][:, :], HH[1][:, :], ALU.add, eng="pool")
                    k.tt(OSGB[:, ch, tk0:tk0 + Tn], HH[0][:, :], OSGB[:, ch, tk0:tk0 + Tn], ALU.mult, eng="pool")
                S.barrier()

            if STOP == "lru":
                return
            with contextlib.ExitStack() as ph:
                pt = common_tiles(ph)
                HT = pt["HT"]
                QA = T(ph, [128, 8, 512], BF16, "QA")
                OSA = T(ph, [64, 8, 512], BF16, "OSA")
                QCZ = T(ph, [128, 4, 2, 512], BF16, "QCZ")
                OSC = T(ph, [128, 4, 512], BF16, "OSC")
                YT = T(ph, [128, 8, 512], BF16, "YT")
                KCALL = T(ph, [128, 36 * 128], BF16, "KCALL")
                VCALL = T(ph, [128, 36 * 128], BF16, "VCALL")
                plan = []
                for b_ in range(3):
                    for o_ in (0, 2048, 4096, 8192, 10240, 12288):
                        n_ = 2048 if o_ in (0, 2048, 8192, 10240) else 4096
                        plan.append(('L', wp2[l].ap()[:, o_:o_ + n_], 128, n_, o_))
                    plan.append(('KV',))
                    for cc_ in range(8):
                        o_ = WP2_Q + cc_ * 4096
                        plan.append(('L', wp2[l].ap()[:, o_:o_ + 4096], 128, 4096, o_))
                        plan.append(('L', wba[l].ap()[:, cc_ * 1024:(cc_ + 1) * 1024], 64, 1024, -1 - cc_))
                    for g_ in range(2):
                        o_ = WP2_Q + 8 * 4096 + g_ * 4096
                        plan.append(('L', wp2[l].ap()[:, o_:o_ + 4096], 128, 4096, o_))
                S.split[KCALL.name] = 2048
                S.split[VCALL.name] = 2048
                for t_ in WSL + [KCALL, VCALL]:
                    k.memset(t_[:, 2048:2112], 0.0)
                wpl = WPlan(S, WSL, [KCALL, VCALL], plan)
                CKA = T(ph, [128, 2, 512], BF16, "CKA")
                CVA = T(ph, [128, 4, 128], BF16, "CVA")
                CVAS = T(ph, [128, 4, 128], BF16, "CVAS")
                ET = [T(ph, [128, 512], BF16, "ET") for _ in range(3)]
                PT1, PT2 = pt["QX"][0], pt["QX"][1]
                PT3 = pt["RSTD"][:, 0:256]
                PT5 = pt["RSTD"][:, 256:512]
                PT4 = pt["SQ"][0][:, 0:256]
                MT = [pt["QX"][0], pt["QX"][1], pt["QL"][0]]
                MT2 = [pt["QL"][1], pt["QN"][0], pt["QN"][1]]
                YACC = pt["RSTD"]
                k.memset(QCZ[:], 0.0)
                k.memset(CKA[:], 0.0)
                k.memset(QA[64:128, :, :], 0.0)
                S.dma("pool", CKA[0:64, :, :].rearrange("p a b -> p (a b)"), cka_in[l].ap(), max_dma_last_dim=4096)
                S.dma("pool", CVA[:].rearrange("p a b -> p (a b)"), cva_in[l].ap(), max_dma_last_dim=4096)
                cva3 = cva_in[l].ap().rearrange("p (t n) -> p t n", n=128)
                S.dma("pool", CVAS[:, :, 0:64], cva3[:, :, 64:128])
                S.dma("pool", CVAS[:, :, 64:128], cva3[:, :, 0:64])
                g1a, g1b, g1c = ga_out.ap(), gb_out.ap(), gc_out.ap()
                HALK = T(ph, [64, 2, 4, 2, 128], BF16, "HALK")
                HALV = T(ph, [128, 2, 4, 128], BF16, "HALV")
                sel = sm("sel")

                def build_halos():
                    for kv in range(2):
                        base = KA_OFF + kv * 1024
                        S.dma("sp", HALK[:, 0, :, kv, :], g1c[:, base + 896:base + 1024].rearrange("(r p) c -> p r c", p=128)[0:64])
                        S.dma("sp", HALK[:, 1, :, kv, :], g1c[:, base:base + 128].rearrange("(r p) c -> p r c", p=128)[0:64])
                    S.dma("sp", HALV[:, 0, :, :], g1c[:, VA_OFF + 896:VA_OFF + 1024].rearrange("(r p) c -> p r c", p=128))
                    S.dma("sp", HALV[:, 1, :, :], g1c[:, VA_OFF:VA_OFF + 128].rearrange("(r p) c -> p r c", p=128))
                    k.memset(KAW[:, :, 0:128], 0.0)
                    k.memset(KAW[:, :, 1152:1280], 0.0)
                    k.memset(VAW[:, 0, :], 0.0)
                    k.memset(VAW[:, 9, :], 0.0)
                    k.memset(VAWS[:, 0, :], 0.0)
                    k.memset(VAWS[:, 9, :], 0.0)
                    for r in range(4):
                        for kv in range(2):
                            k.stt(KAW[0:64, kv, 0:128], HALK[:, 0, r, kv, :], sel[0:64, r:r + 1], KAW[0:64, kv, 0:128], ALU.mult, ALU.add)
                            k.stt(KAW[0:64, kv, 1152:1280], HALK[:, 1, r, kv, :], sel[0:64, 4 + r:5 + r], KAW[0:64, kv, 1152:1280], ALU.mult, ALU.add)
                        k.stt(VAW[:, 0, :], HALV[:, 0, r, :], sel[:, r:r + 1], VAW[:, 0, :], ALU.mult, ALU.add)
                        k.stt(VAW[:, 9, :], HALV[:, 1, r, :], sel[:, 4 + r:5 + r], VAW[:, 9, :], ALU.mult, ALU.add)
                        for (d0, s0_) in ((0, 64), (64, 0)):
                            k.stt(VAWS[:, 0, d0:d0 + 64], HALV[:, 0, r, s0_:s0_ + 64], sel[:, r:r + 1], VAWS[:, 0, d0:d0 + 64], ALU.mult, ALU.add)
                            k.stt(VAWS[:, 9, d0:d0 + 64], HALV[:, 1, r, s0_:s0_ + 64], sel[:, 4 + r:5 + r], VAWS[:, 9, d0:d0 + 64], ALU.mult, ALU.add)


                et_i = [0]
                acc_i = [0]
                esum_i = [0]
                S.split[HALV.name] = 512
                _hv = HALV[:].rearrange("p a r c -> p (a r c)")
                ESUM = [_hv[:, 0:512], _hv[:, 512:1024]]

                def attend_all(units, order=None):
                    if order is None:
                        order = [(ui, j) for ui, u in enumerate(units) for j in range(len(u[1]))]
                    first = {}
                    last = {}
                    for si_, (ui_, j_) in enumerate(order):
                        first.setdefault(ui_, si_)
                        last[ui_] = si_
                    steps = [(ui_, j_, len(units[ui_][1])) for (ui_, j_) in order]
                    LOOK = 3
                    pend = []
                    deferred = []
                    role = [None] * len(steps)
                    si_ = 0
                    while si_ < len(steps):
                        if si_ + 1 < len(steps) and steps[si_ + 1][0] == steps[si_][0]:
                            role[si_], role[si_ + 1] = 'first', 'second'
                            si_ += 2
                        else:
                            role[si_] = 'single'
                            si_ += 1
                    dfirst, dlast = {}, {}
                    for si_, (ui_, j_, n_) in enumerate(steps):
                        if role[si_] != 'first':
                            dfirst.setdefault(ui_, si_)
                            dlast[ui_] = si_
                    pendD = []
                    prev_et = [None]

                    def issue(si):
                        ui, j, n = steps[si]
                        ps = nb((0, 1, 2, 7))
                        units[ui][0](ps, units[ui][1][j][0])
                        pend.append(ps)
                    for si in range(min(LOOK, len(steps))):
                        issue(si)
                    for si, (ui, j, n) in enumerate(steps):
                        qk_fn, tiles, Mv, post_fn = units[ui]
                        _, vap, mask = tiles[j]
                        ps = pend.pop(0)
                        et = ET[et_i[0] % 3]
                        et_i[0] += 1
                        k.act(et[:], ps[:], AF.Exp, scale=SCALE)
                        if mask is not None:
                            k.tt(et[:], et[:], mask, ALU.mult)
                        a_ = (acc_i[0] + ui) % 2
                        accO = banks[3 + 2 * a_]
                        accD = banks[4 + 2 * a_]
                        dsrc = None
                        if role[si] == 'second':
                            es = ESUM[esum_i[0] % 2]
                            esum_i[0] += 1
                            k.tt(es, prev_et[0][:], et[:], ALU.add)
                            dsrc = es
                        elif role[si] == 'single':
                            dsrc = et[:]
                        prev_et[0] = et
                        while deferred and deferred[0][0] <= si:
                            deferred.pop(0)[1]()
                        if si + LOOK < len(steps):
                            issue(si + LOOK)
                        k.mm(accO[0:Mv, :], vap, et[:], start=(si == first[ui]), stop=(si == last[ui]))
                        while pendD and pendD[0][0] <= si:
                            pendD.pop(0)[1]()
                        if dsrc is not None:
                            fn = (lambda accD=accD, Mv=Mv, dsrc=dsrc, st_=(si == dfirst[ui]), sp_=(si == dlast[ui]):
                                  k.mm(accD[0:Mv, :], ONES[:, 0:Mv], dsrc, start=st_, stop=sp_))
                            if role[si] == 'single':
                                fn()
                            else:
                                pendD.append((si + 1, fn))
                        if si == last[ui]:
                            deferred.append((si + 3, lambda post_fn=post_fn, accO=accO, accD=accD: post_fn(accO, accD)))
                    for _, fn in pendD:
                        fn()
                    for _, fn in deferred:
                        fn()
                    acc_i[0] += len(units)

                for b in range(3):
                    lat = b > 0
                    v = 1 if lat else 0
                    cols = slice(b * 512, (b + 1) * 512)
                    front(pt, l, b)
                    rope_cols = slice((b - 1) * 512, b * 512) if lat else None
                    hrhs = lambda kk: HT[:, kk, :]
                    for g in range(2):
                        w = wpl.get(g * 2048)
                        w4 = w[:, 0:2048].rearrange("p (c k m) -> p c k m", c=4, k=8)
                        qitems = []
                        for c4 in range(4):
                            h = g * 4 + c4
                            if not lat:
                                def dest(XN, h=h):
                                    k.copy(QA[0:64, h, :], XN[0:64, :])
                            else:
                                def dest(t, h=h):
                                    k.tt(QA[0:64, h, :], t[0][0:64, :], t[1][0:64, :], ALU.add)
                            qitems.append(((w, c4 * 512), 64, sm("qna", l)[0:64, :], dest))
                        qk_pipeline(pt, qitems, rope_cols, hrhs)
                    w = wpl.get(4096)
                    w4 = w[:, 0:4096].rearrange("p (c k m) -> p c k m", c=4, k=8)
                    qitems = []
                    for h in range(4):
                        if not lat:
                            def dest(XN, h=h):
                                k.copy(QCZ[0:64, h, 0, :], XN[0:64, :])
                                k.copy(QCZ[64:128, h, 1, :], XN[64:128, :])
                        else:
                            def dest(t, h=h):
                                k.tt(QCZ[0:64, h, 0, :], t[0][0:64, :], t[1][0:64, :], ALU.add)
                                k.tt(QCZ[64:128, h, 1, :], t[0][64:128, :], t[1][64:128, :], ALU.add)
                        qitems.append((w4[:, h], 128, sm("qnc", l), dest))
                    qk_pipeline(pt, qitems, rope_cols, hrhs)
                    for g in range(2):
                        w = wpl.get(8192 + g * 2048)
                        w4 = w[:, 0:2048].rearrange("p (c k m) -> p c k m", c=4, k=8)
                        for c4 in range(4):
                            h = g * 4 + c4
                            ps = nb()
                            proj(ps, (w, c4 * 512), 8, 64, hrhs)
                            silu_to(pt, ps, 64, OSA[0:64, h, :])
                    w = wpl.get(12288)
                    w4 = w[:, 0:4096].rearrange("p (c k m) -> p c k m", c=4, k=8)
                    for h in range(4):
                        ps = nb()
                        proj(ps, w4[:, h], 8, 128, hrhs)
                        silu_to(pt, ps, 128, OSC[:, h, :])

                    if b == 1:
                        build_halos()
                    wpl.kv_begin()
                    nqb = 2
                    units = []
                    for qb in range(nqb):
                        qc = slice(qb * 256, (qb + 1) * 256)
                        for hp in range(4):
                            kv = hp // 2
                            h0_ = hp * 2
                            tiles = []
                            if not lat:
                                for kt in range(2):
                                    c0 = qb * 256 + kt * 128
                                    tiles.append((KACTX[:, kv, c0:c0 + 128], (VACTX if kv == 0 else VACTXS)[:, qb * 2 + kt, :], None))
                            else:
                                t0 = (b - 1) * 4 + qb * 2
                                for wdx in range(4):
                                    mi = [0 if t0 == 0 else 1, 2, 3, 5 if t0 == 6 else 4][wdx]
                                    tiles.append((KAW[:, kv, (t0 + wdx) * 128:(t0 + wdx + 1) * 128],
                                                  (VAW if kv == 0 else VAWS)[:, t0 + wdx, :], MSK[:, mi, :]))
                                for c in range(4):
                                    tiles.append((CKA[:, kv, c * 128:(c + 1) * 128], (CVA if kv == 0 else CVAS)[:, c, :], None))

                            def qk_a(ps, kap, h0_=h0_, qc=qc):
                                k.mm(ps[:, :], kap, QA[:, h0_:h0_ + 2, qc])

                            def post_a(accO, accD, h0_=h0_, qc=qc):
                                for hh in range(2):
                                    cs = slice(hh * 256, (hh + 1) * 256)
                                    k.act(PT1[0:64, cs], accD[0:64, cs], AF.Ln,
                                          bias=D[0:64, D_ESINK + h0_ + hh:D_ESINK + h0_ + hh + 1])
                                k.act(PT2[0:64, :], PT1[0:64, :], AF.Exp, scale=-1.0)
                                k.tt(PT1[0:64, :], accO[0:64, :], PT2[0:64, :], ALU.mult)
                                k.tt(OSA[0:64, h0_:h0_ + 2, qc], PT1[0:64, :].rearrange("p (a b) -> p a b", a=2),
                                     OSA[0:64, h0_:h0_ + 2, qc], ALU.mult)
                            units.append((qk_a, tiles, 128, post_a))
                    attend_all(units)

                    for h in range(4):
                        units = []
                        if lat:
                            src_k = g1a[:, KC_OFF + h * 1024:KC_OFF + (h + 1) * 1024].rearrange("(r p) c -> p r c", p=128)
                            S.dma("sp", KCALL[:, 0:2048].rearrange("p (r c) -> p r c", r=2), src_k[:, 0:2, :])
                            for r in (0, 1):
                                S.dma("sp", VCALL[:, r * 1024:(r + 1) * 1024].rearrange("p (t n) -> p t n", n=128),
                                      g1b[r * 128:(r + 1) * 128, VC_OFF:VC_OFF + 4096].rearrange("p (t n) -> p t n", n=512)[:, :, h * 128:(h + 1) * 128])
                            S.dma("sp", KCALL[:, 2048:4096].rearrange("p (r c) -> p r c", r=2), src_k[:, 2:4, :])
                            for r in (2, 3):
                                S.dma("sp", VCALL[:, r * 1024:(r + 1) * 1024].rearrange("p (t n) -> p t n", n=128),
                                      g1b[r * 128:(r + 1) * 128, VC_OFF:VC_OFF + 4096].rearrange("p (t n) -> p t n", n=512)[:, :, h * 128:(h + 1) * 128])
                            S.dma("pool", KCALL[:, 4096:4608], ckc_in[l].ap()[:, h * 512:(h + 1) * 512], max_dma_last_dim=2048)
                            S.dma("pool", VCALL[:, 4096:4608].rearrange("p (t n) -> p t n", n=128),
                                  cvc_in[l].ap().rearrange("p (t n) -> p t n", n=512)[:, :, h * 128:(h + 1) * 128])
                        for qb in range(2):
                            qc = slice(qb * 256, (qb + 1) * 256)
                            tiles = []
                            if not lat:
                                for kt in range(2):
                                    c0 = qb * 256 + kt * 128
                                    tiles.append((KCCTX[:, h, c0:c0 + 128], VCCTX[:, qb * 2 + kt, h * 128:(h + 1) * 128], None))
                            else:
                                for j in range(36):
                                    tiles.append((KCALL[:, j * 128:(j + 1) * 128], VCALL[:, j * 128:(j + 1) * 128], None))

                            def qk_c(ps, kap, h=h, qc=qc):
                                k.mm(ps[:, :], kap, QCZ[:, h, :, qc])

                            def post_c(accO, accD, h=h, qc=qc):
                                k.act(PT1[:, :], accD[:, :], AF.Ln)
                                k.act(PT1[:, :], PT1[:, :], AF.Exp, scale=-1.0)
                                k.tt(PT2[:, :], accO[:, :], PT1[:, :], ALU.mult)
                                k.stt(PT3, PT2[:, 256:512], D[:, D_NLAM:D_NLAM + 1], PT2[:, 0:256], ALU.mult, ALU.add)
                                k.tt(PT4, PT3, PT3, ALU.mult)
                                ss = banks[7]
                                k.mm(ss[:, 0:256], ONES[:, :], PT4)
                                k.act(PT5, ss[:, 0:256], AF.Ln, scale=1.0 / 128.0, bias=EPST[:])
                                k.act(PT5, PT5, AF.Exp, scale=-0.5)
                                k.stt(PT3, PT3, D[:, D_SUBG:D_SUBG + 1], PT5, ALU.mult, ALU.mult)
                                k.tt(OSC[:, h, qc], PT3, OSC[:, h, qc], ALU.mult)
                            units.append((qk_c, tiles, 128, post_c))
                        if lat:
                            order = ([(0, j) for j in range(16)] + [(1, j) for j in range(16)]
                                     + [(0, j) for j in range(16, 36)] + [(1, j) for j in range(16, 36)])
                            attend_all(units, order)
                        else:
                            attend_all(units)

                    wpl.kv_end()
                    for cc in range(8):
                        base = WP2_Q + cc * 4096
                        w = wpl.get(base)
                        wa = wpl.get(-1 - cc)
                        wm = w[:, 0:3072].rearrange("p (i k m) -> p i k m", i=3, k=8)
                        wb = w[:, 3072:3584].rearrange("p (k m) -> p k m", k=4)
                        wc = w[:, 3584:4096].rearrange("p (k m) -> p k m", k=4)
                        wa3 = wa[0:64, 0:1024].rearrange("p (k m) -> p k m", k=8)
                        pa, pb, pc = (banks[0], banks[1], banks[2]) if cc % 2 == 0 else (banks[5], banks[6], banks[7])
                        zs = [banks[3], banks[4], banks[3]]
                        proj(zs[0], wm[:, 0], 8, 128, hrhs)
                        proj(zs[1], wm[:, 1], 8, 128, hrhs)
                        k.act(MT[0][:], zs[0][:], AF.Exp, scale=-1.0)
                        proj(zs[2], wm[:, 2], 8, 128, hrhs)
                        k.act(MT[1][:], zs[1][:], AF.Exp, scale=-1.0)
                        k.act(MT[2][:], zs[2][:], AF.Exp, scale=-1.0)
                        proj(pa, wa3, 8, 128, lambda kk: OSA[0:64, kk, :], P=64)
                        proj(pb, wb, 4, 128, lambda kk: OSGB[:, kk, cols])
                        proj(pc, wc, 4, 128, lambda kk: OSC[:, kk, :])
                        for i, pbr in enumerate((pa, pb, pc)):
                            k.act(MT[i][:], MT[i][:], AF.Ln, bias=ONE_P[:])
                            k.act(MT[i][:], MT[i][:], AF.Exp, scale=-1.0)
                            k.tt(MT2[i][:], pbr[:], MT[i][:], ALU.mult)
                        k.tt(YACC[:], MT2[0][:], MT2[1][:], ALU.add)
                        k.tt(YT[:, cc, :], YACC[:], MT2[2][:], ALU.add)
                    for g in range(2):
                        base = WP2_Q + 8 * 4096 + g * 4096
                        w = wpl.get(base)
                        w4 = w[:, 0:4096].rearrange("p (c k m) -> p c k m", c=4, k=8)
                        for c4 in range(4):
                            cc = g * 4 + c4
                            ps = nb((6, 7))
                            proj(ps, w4[:, c4], 8, 128, lambda kk: YT[:, kk, :])
                            k.stt(XT[:, cc, cols], ps[:, :], MODV[l][:, 16 + cc, v:v + 1], XT[:, cc, cols], ALU.mult, ALU.add)
                S.barrier()

        layers()
        S.barrier()
        S.dma("sp", yT_out.ap().rearrange("(k p) t -> p k t", p=128), XT[:], is_output=True)
        S.dma("sp", o_st.ap(), STO[:], is_output=True)
        S.finish()
        S.replay()
    return nc


def _fm(w, M):
    K_, n = w.shape
    a = w.reshape(K_ // 128, 128, n // M, M).transpose(1, 2, 0, 3)
    return np.ascontiguousarray(a).reshape(128, -1)


_PROG = None
import os
STOP = os.environ.get('KSTOP', '')
KV = os.environ.get('KV', '')


def kernel(**inp):
    global _PROG
    f32 = lambda a: np.ascontiguousarray(np.asarray(a, dtype=np.float32))
    I = {k_: f32(v) for k_, v in inp.items()}
    w_in = I["w_in"]
    shared = {}
    for l in range(DEPTH):
        W = w_in[l]
        q_a, k_a, v_a, g_a = W[:, 0:512], W[:, 512:640], W[:, 640:768], W[:, 768:1280]
        x_b, g_b = W[:, 1280:1792], W[:, 1792:2304]
        q_c, k_c, v_c, g_c = W[:, 2304:2816], W[:, 2816:3328], W[:, 3328:3840], W[:, 3840:4352]
        merge = W[:, 4352:7424]
        vtm = lambda w: np.ascontiguousarray(w.reshape(8, 128, -1).transpose(1, 0, 2)).reshape(128, -1)
        shared[f"wp1{l}"] = np.concatenate([_fm(k_a, 64), _fm(k_c, 128), _fm(x_b, 128), _fm(g_b, 128),
                                            vtm(v_c), vtm(v_a)], axis=1)
        parts = [_fm(q_a, 64), _fm(q_c, 128), _fm(g_a, 64), _fm(g_c, 128)]
        for cc in range(8):
            for i in range(3):
                parts.append(_fm(merge[:, i * 1024 + cc * 128:i * 1024 + (cc + 1) * 128], 128))
            parts.append(_fm(I["w_br_b"][l][:, cc * 128:(cc + 1) * 128], 128))
            parts.append(_fm(I["w_br_c"][l][:, cc * 128:(cc + 1) * 128], 128))
        parts.append(_fm(I["w_out"][l], 128))
        shared[f"wp2{l}"] = np.concatenate(parts, axis=1)
        wa = I["w_br_a"][l]
        shared[f"wba{l}"] = np.ascontiguousarray(
            wa.reshape(8, 64, 8, 128).transpose(1, 2, 0, 3)).reshape(64, -1)
        shared[f"wmod{l}"] = _fm(I["mod_w"][l], 128)
    gl = np.zeros((128, DEPTH, 2, 2, 4, 128), np.float32)
    for l in range(DEPTH):
        for dr in range(2):
            for gi, nm in enumerate(("lru_wa", "lru_wx")):
                Wg = I[nm][l, dr]
                for ch in range(4):
                    for hb in range(2):
                        gl[hb * 64:(hb + 1) * 64, l, dr, gi, ch, hb * 64:(hb + 1) * 64] = Wg[ch * 2 + hb]
    shared["glru"] = gl.reshape(128, -1)
    R = np.zeros((128, 128), np.float32)
    for dp in range(128):
        if dp % 32 < 16:
            R[dp + 16, dp] = -1.0
        else:
            R[dp - 16, dp] = 1.0
    shared["rmat"] = R
    inv = np.power(np.float32(10000.0), -np.arange(16, dtype=np.float32) / np.float32(16)).astype(np.float32)
    a_ = np.arange(128)[:, None]
    b_ = np.arange(128)[None, :]
    ge = (a_ >= b_).astype(np.float32)
    le = (a_ <= b_).astype(np.float32)
    one = np.ones((128, 128), np.float32)
    zero = np.zeros((128, 128), np.float32)
    M0 = np.concatenate([ge, zero], 1)
    M1 = np.concatenate([one, ge], 1)
    M2 = np.concatenate([le, one], 1)
    M3 = np.concatenate([zero, le], 1)

    in_maps = []
    for c in range(8):
        s, j = c // 4, c % 4
        m = dict(shared)
        xs = np.concatenate([I["x_prompt"][2 * c], I["x_prompt"][2 * c + 1],
                             I["x_sample"][s, 1024 * j:1024 * (j + 1)]], axis=0)
        m["xT"] = np.ascontiguousarray(xs.T)
        sm_ = np.zeros((128, NSM), np.float32)

        def put(key, arr):
            o, w = SM[key]
            sm_[:, o:o + w] = arr
        col = lambda v, n: np.ascontiguousarray(v.reshape(n, 128).T)
        for l in range(DEPTH):
            put(("ng", l), col(I["norm_g"][l], 8))
            put(("modb", l), col(I["mod_b"][l], 24))
            put(("qna", l), np.tile(I["qn_a"][l], 2)[:, None])
            put(("kna", l), np.tile(I["kn_a"][l], 2)[:, None])
            put(("qnc", l), np.tile(I["qn_c"][l], 2)[:, None])
            put(("knc", l), np.tile(I["kn_c"][l], 2)[:, None])
            cw = I["conv_w"][l]
            put(("convw", l), np.ascontiguousarray(cw.reshape(4, 4, 128).transpose(2, 1, 0)).reshape(128, 16))
            put(("convb", l), col(I["conv_b"][l], 4))
            for nm, key in (("lru_ba", "ba"), ("lru_bx", "bx"), ("lru_lam", "lam")):
                put((key, l), np.ascontiguousarray(I[nm][l].reshape(2, 4, 128).transpose(2, 0, 1)).reshape(128, 8))
            put(("subln", l), I["subln_c"][l][:, None])
            put(("sink", l), np.broadcast_to(I["sink_a"][l][None, :], (128, 8)))
            for nm, key in (("lam_q1", "lq1"), ("lam_k1", "lk1"), ("lam_q2", "lq2"), ("lam_k2", "lk2")):
                put((key, l), np.broadcast_to(I[nm][l][None, :], (128, 64)))
            put(("h0", l), np.ascontiguousarray(I["state_lru"][s, l].reshape(2, 4, 128).transpose(2, 0, 1)).reshape(128, 8))
        cc_ = np.stack([col(I["c_ctx"], 8), col(I["c"][s], 8)], axis=2).reshape(128, 16)
        put("c", cc_)
        sel = np.zeros((128, 12), np.float32)
        if j > 0:
            sel[:, j - 1] = 1.0
        if j < 3:
            sel[:, 4 + j + 1] = 1.0
        sel[:, 8 + j] = 1.0
        put("sel", sel)
        m["small"] = sm_
        t = 1024 * j + np.arange(1024)
        row = (t // 64).astype(np.float32)
        colp = (t % 64).astype(np.float32)
        ang = np.zeros((64, 1024), np.float32)
        for d in range(64):
            ang[d] = (row if d < 32 else colp) * inv[d % 16]
        ang = np.concatenate([ang, ang], 0)
        m["cosT"] = np.cos(ang).astype(np.float32)
        m["sinT"] = np.sin(ang).astype(np.float32)
        first = M0 if j > 0 else np.zeros_like(M0)
        last = M3 if j < 3 else np.zeros_like(M3)
        msk = np.stack([first, M0, M1, M2, M3, last], 0)
        msk = np.concatenate([msk, msk], 2)
        m["masks"] = np.ascontiguousarray(msk.transpose(1, 0, 2)).reshape(128, -1)
        for l in range(DEPTH):
            ck = I["cache_c_k"][s, l]
            m[f"ckc{l}"] = np.ascontiguousarray(ck.transpose(2, 3, 1, 0)).reshape(128, 4 * 512)
            cv = I["cache_c_v"][s, l]
            m[f"cvc{l}"] = np.ascontiguousarray(cv.reshape(4, 128, 512).transpose(1, 0, 2)).reshape(128, -1)
            ka = I["cache_a_k"][s, l]
            m[f"cka{l}"] = np.ascontiguousarray(ka.transpose(2, 1, 0)).reshape(64, -1)
            va = I["cache_a_v"][s, l]
            m[f"cva{l}"] = np.ascontiguousarray(va.reshape(4, 128, 128).transpose(1, 0, 2)).reshape(128, -1)
        in_maps.append(m)

    if _PROG is None:
        _PROG = build_program()
    res = run_bass_kernel_spmd(_PROG, in_maps, core_ids=list(range(8)))
    R_ = res.results

    y_prompt = np.zeros((16, 256, 1024), np.float32)
    y_sample = np.zeros((2, 4096, 1024), np.float32)
    n_ka = np.zeros((16, 2, 256, 2, 64), np.float32)
    n_va = np.zeros((16, 2, 256, 2, 64), np.float32)
    n_kc = np.zeros((16, 2, 256, 4, 2, 64), np.float32)
    n_vc = np.zeros((16, 2, 256, 4, 128), np.float32)
    n_st = np.zeros((16, 2, 2, 512), np.float32)
    for c in range(8):
        s, j = c // 4, c % 4
        r = R_[c]
        y = np.asarray(r["yT"]).T
        y_prompt[2 * c] = y[0:256]
        y_prompt[2 * c + 1] = y[256:512]
        y_sample[s, 1024 * j:1024 * (j + 1)] = y[512:]
        ka = np.asarray(r["o_ka"]).reshape(2, 2, 64, 2, 256)
        kc = np.asarray(r["o_kc"]).reshape(2, 4, 2, 64, 2, 256)
        va = np.asarray(r["o_va"]).reshape(2, 2, 256, 2, 64)
        vc = np.asarray(r["o_vc"]).reshape(2, 2, 256, 4, 128)
        stt_ = np.asarray(r["o_st"]).reshape(128, 2, 2, 2, 4)
        for sq in range(2):
            bi = 2 * c + sq
            n_ka[bi] = ka[:, :, :, sq, :].transpose(0, 3, 1, 2)
            n_kc[bi] = kc[:, :, :, :, sq, :].transpose(0, 4, 1, 2, 3)
            n_va[bi] = va[:, sq]
            n_vc[bi] = vc[:, sq]
            n_st[bi] = stt_[:, :, sq].transpose(1, 2, 3, 0).reshape(2, 2, 512)
    return (y_prompt, y_sample, n_ka, n_va, n_kc, n_vc, n_st)
```

```python
import contextlib
import math
import numpy as np
import concourse.bass as bass
import concourse.mybir as mybir
from concourse.bass_utils import run_bass_kernel_spmd

F32 = mybir.dt.float32
BF16 = mybir.dt.bfloat16
AF = mybir.ActivationFunctionType
ALU = mybir.AluOpType
AX = mybir.AxisListType

ENGS = ("pe", "act", "dve", "pool", "sp")

DEPTH = 2
SCALE = 0.125
EPS = 1e-6
NTOK = 1536
KC_OFF, VC_OFF, KA_OFF, VA_OFF = 0, 0, 0, 2048
XBW = 1545
SEGS = [(0, 256, 0), (259, 256, 256), (518, 1024, 512)]

_SM_LAYER = [("ng", 8), ("modb", 24), ("qna", 1), ("kna", 1), ("qnc", 1), ("knc", 1), ("convw", 16),
             ("convb", 4), ("ba", 8), ("bx", 8), ("lam", 8), ("subln", 1), ("sink", 8),
             ("lq1", 64), ("lk1", 64), ("lq2", 64), ("lk2", 64), ("h0", 8)]
SM = {}
_o = 0
for _l in range(DEPTH):
    for _n, _w in _SM_LAYER:
        SM[(_n, _l)] = (_o, _w)
        _o += _w
SM["c"] = (_o, 16); _o += 16
SM["sel"] = (_o, 12); _o += 12
NSM = _o

WP1N = 1024 + 4096 * 3 + 5120
WP2_Q = 16384
WP2N = WP2_Q + 8 * 4096 + 8192


class Res:
    __slots__ = ("name", "last_w", "readers")

    def __init__(self, name=""):
        self.name = name
        self.last_w = None
        self.readers = {}


class Sched:
    NDMA = 8

    def __init__(self, nc, stack):
        self.nc = nc
        self.streams = {e: [] for e in ENGS}
        self.count = {e: 0 for e in ENGS}
        self.seen = {e: {} for e in ENGS}
        self.sems = {}
        self.resmap = {}
        self.split = {}
        for e in ENGS:
            self.sems[e] = stack.enter_context(nc.semaphore("prog_" + e))
        self.dma_k = {}
        for q in ("sp", "act", "pool"):
            self.dma_k[q] = 0
            for i in range(self.NDMA):
                self.sems[("dma", q, i)] = stack.enter_context(nc.semaphore(f"dma_{q}_{i}"))
        self.sems["cc"] = stack.enter_context(nc.semaphore("cc"))
        self.cc_count = 0
        self.out_events = []

    def _named(self, name):
        r = self.resmap.get(name)
        if r is None:
            r = Res(name)
            self.resmap[name] = r
        return r

    def _res(self, x):
        if isinstance(x, Res):
            return x
        return self._named(x.tensor.name)

    def _resl(self, x):
        if isinstance(x, Res):
            return [x]
        name = x.tensor.name
        b = self.split.get(name)
        if b is None:
            return [self._named(name)]
        try:
            ap = [list(e) for e in x.ap]
            col0 = int(x.offset) % int(ap[0][0])
            hi = col0 + 1
            for st_, cnt in ap[1:]:
                assert st_ >= 0
                hi += (int(cnt) - 1) * int(st_)
        except Exception:
            col0, hi = 0, 1 << 30
        out = []
        if col0 < b:
            out.append(self._named(name + ".A"))
        if hi > b:
            out.append(self._named(name + ".B"))
        return out

    def _collect(self, reads, writes):
        waits = {}

        def add(ev):
            if ev is None:
                return
            k, v = ev
            if waits.get(k, 0) < v:
                waits[k] = v
        for r in reads:
            add(r.last_w)
        for w in writes:
            add(w.last_w)
            for k, v in w.readers.items():
                add((k, v))
        return waits

    def _emit_waits(self, eng, waits):
        for k, v in waits.items():
            if k == eng:
                if eng == "pe":
                    continue
                if v <= self.count[eng] - 6:
                    continue
            if self.seen[eng].get(k, 0) >= v:
                continue
            self.seen[eng][k] = v
            self.streams[eng].append(("wait", k, v))

    def _commit(self, ev, reads, writes):
        k, v = ev
        for r in reads:
            if r.readers.get(k, 0) < v:
                r.readers[k] = v
        for w in writes:
            w.last_w = ev
            w.readers = {}

    def op(self, eng, fn, reads=(), writes=()):
        reads = [r for x in reads for r in self._resl(x)]
        writes = [r for x in writes for r in self._resl(x)]
        writes = writes + [r for r in reads if r.name.startswith("bank") and r not in writes]
        self._emit_waits(eng, self._collect(reads, writes))
        self.count[eng] += 1
        ev = (eng, self.count[eng])
        self.streams[eng].append(("op", fn, eng, 1))
        self._commit(ev, reads, writes)
        return ev

    def dma(self, q, out, in_, is_output=False, extra_reads=(), **kw):
        reads = self._resl(in_) + [r for x in extra_reads for r in self._resl(x)]
        writes = self._resl(out)
        waits = self._collect(reads, writes)
        k = self.dma_k[q]
        self.dma_k[q] += 1
        s = ("dma", q, k % self.NDMA)
        gen = k // self.NDMA
        if gen > 0 and waits.get(s, 0) < 16 * gen:
            waits[s] = 16 * gen
        self._emit_waits(q, waits)
        ev = (s, 16 * (gen + 1))
        self.streams[q].append(("op", lambda e: e.dma_start(out=out, in_=in_, **kw), s, 16))
        self._commit(ev, reads, writes)
        if is_output:
            self.out_events.append(ev)
        return ev

    def collective(self, groups, in_t, out_t):
        reads = [self._res(in_t.ap())]
        writes = [self._res(out_t.ap())]
        self._emit_waits("pool", self._collect(reads, writes))
        self.cc_count += 1
        ev = ("cc", self.cc_count)
        self.streams["pool"].append(("op", lambda e: e.collective_compute(
            "AllGather", ALU.bypass, replica_groups=groups, ins=[in_t.ap().opt()],
            outs=[out_t.ap().opt()]), "cc", 1))
        self._commit(ev, reads, writes)
        return ev

    def _all_events(self):
        waits = {}
        for e in ENGS:
            if self.count[e] > 0:
                waits[e] = self.count[e]
        for q in ("sp", "act", "pool"):
            k = self.dma_k[q]
            for i in range(self.NDMA):
                n = (k // self.NDMA) + (1 if i < k % self.NDMA else 0)
                if n > 0:
                    waits[("dma", q, i)] = 16 * n
        if self.cc_count:
            waits["cc"] = self.cc_count
        return waits

    def barrier(self, skip_cc=False):
        waits = self._all_events()
        if skip_cc:
            waits.pop("cc", None)
        for e in ENGS:
            for k, v in waits.items():
                if k == e:
                    continue
                if self.seen[e].get(k, 0) >= v:
                    continue
                self.seen[e][k] = v
                self.streams[e].append(("wait", k, v))

    def finish(self):
        self.barrier()

    def replay(self):
        nc = self.nc
        sems = self.sems
        streams = self.streams

        def run(engobj, name):
            for item in streams[name]:
                if item[0] == "wait":
                    engobj.wait_ge(sems[item[1]], item[2])
                else:
                    _, fn, semk, inc = item
                    fn(engobj).then_inc(sems[semk], inc)

        with nc.Block() as block:
            @block.tensor
            def _(e):
                run(e, "pe")

            @block.scalar
            def _(e):
                run(e, "act")

            @block.vector
            def _(e):
                run(e, "dve")

            @block.gpsimd
            def _(e):
                run(e, "pool")

            @block.sync
            def _(e):
                run(e, "sp")


class WPlan:
    def __init__(self, S, slotsA, slotsB, plan):
        self.S, self.A, self.B, self.plan = S, list(slotsA), list(slotsB), plan
        self.slot_of = {}
        self.live = {}
        self.cur = 0
        self.got = []
        self.markers_passed = 0

    def _pump(self):
        j = 0
        unpassed = 0
        seen_markers = 0
        for j in range(len(self.plan)):
            e = self.plan[j]
            if e[0] == 'KV':
                seen_markers += 1
                if seen_markers > self.markers_passed:
                    unpassed += 1
                    if unpassed > 1:
                        return
                continue
            if j in self.slot_of or j < self.cur:
                continue
            allowed = self.A + (self.B if unpassed == 0 else [])
            free = [t for t in allowed if t.name not in self.live]
            if not free:
                return
            t = free[0]
            _, src, P, n = e[0:4]
            self.S.dma("pool", t[0:P, 0:n], src, max_dma_last_dim=4096)
            self.slot_of[j] = t
            self.live[t.name] = j

    def _release(self, idx):
        t = self.slot_of.get(idx)
        if t is not None and self.live.get(t.name) == idx:
            del self.live[t.name]

    def get(self, src_off=None):
        while self.plan[self.cur][0] == 'KV':
            self.cur += 1
        i = self.cur
        if src_off is not None:
            assert self.plan[i][4] == src_off, (i, self.plan[i][4], src_off)
        while len(self.got) >= 2:
            self._release(self.got.pop(0))
        if i not in self.slot_of:
            self._pump()
        assert i in self.slot_of, ("no slot for load", i)
        self.cur += 1
        self.got.append(i)
        self._pump()
        return self.slot_of[i]

    def release_all(self):
        while self.got:
            self._release(self.got.pop(0))

    def kv_begin(self):
        self.release_all()
        self._pump()

    def kv_end(self):
        self.markers_passed += 1
        self._pump()


class K:
    def __init__(self, S):
        self.S = S

    @staticmethod
    def _aps(*xs):
        return [x for x in xs if x is not None and not isinstance(x, (int, float))]

    def act(self, out, in_, func, scale=1.0, bias=None):
        kw = {}
        if bias is not None:
            kw["bias"] = bias
        self.S.op("act", lambda e: e.activation(out=out, in_=in_, func=func, scale=scale, **kw),
                  reads=self._aps(in_, scale, bias), writes=[out])

    def ts(self, out, in0, s1, s2=None, op0=ALU.mult, op1=None, eng="dve"):
        kw = {}
        if op1 is not None:
            kw["op1"] = op1
        self.S.op(eng, lambda e: e.tensor_scalar(out=out, in0=in0, scalar1=s1, scalar2=s2, op0=op0, **kw),
                  reads=self._aps(in0, s1, s2), writes=[out])

    def tt(self, out, in0, in1, op, eng="dve"):
        self.S.op(eng, lambda e: e.tensor_tensor(out=out, in0=in0, in1=in1, op=op),
                  reads=[in0, in1], writes=[out])

    def stt(self, out, in0, scalar, in1, op0, op1, eng="dve"):
        self.S.op(eng, lambda e: e.scalar_tensor_tensor(out=out, in0=in0, scalar=scalar, in1=in1, op0=op0, op1=op1),
                  reads=self._aps(in0, scalar, in1), writes=[out])

    def recip(self, out, in_):
        self.S.op("dve", lambda e: e.reciprocal(out=out, in_=in_), reads=[in_], writes=[out])

    def copy(self, out, in_, eng="dve"):
        if in_.tensor.name.startswith("bank"):
            self.ts(out, in_, 1.0, None, op0=ALU.mult, eng=eng)
            return
        self.S.op(eng, lambda e: e.tensor_copy(out=out, in_=in_), reads=[in_], writes=[out])

    def memset(self, ap, val, eng="dve"):
        self.S.op(eng, lambda e: e.memset(ap, val), writes=[ap])

    def rsum(self, out, in_):
        self.S.op("dve", lambda e: e.reduce_sum(out=out, in_=in_, axis=AX.X), reads=[in_], writes=[out])

    def scan(self, out, a, b, init):
        self.S.op("dve", lambda e: e.tensor_tensor_scan(out=out, data0=a, data1=b, initial=init,
                                                        op0=ALU.mult, op1=ALU.add),
                  reads=self._aps(a, b, init), writes=[out])

    def mm(self, out, lhsT, rhs, start=True, stop=True):
        self.S.op("pe", lambda e: e.matmul(out, lhsT=lhsT, rhs=rhs, start=start, stop=stop),
                  reads=[lhsT, rhs], writes=[out])


def build_program():
    nc = bass.Bass("TRN2", target_bir_lowering=False)
    din = lambda n, s, dt=F32: nc.dram_tensor(n, s, dt, kind="ExternalInput")
    dout = lambda n, s: nc.dram_tensor(n, s, F32, kind="ExternalOutput")
    xT_in = din("xT", [1024, NTOK])
    small_in = din("small", [128, NSM])
    wmod = [din(f"wmod{l}", [128, 24576]) for l in range(DEPTH)]
    wp1 = [din(f"wp1{l}", [128, WP1N]) for l in range(DEPTH)]
    wp2 = [din(f"wp2{l}", [128, WP2N]) for l in range(DEPTH)]
    wba = [din(f"wba{l}", [64, 8192]) for l in range(DEPTH)]
    glru_in = din("glru", [128, 4096])
    rmat_in = din("rmat", [128, 128])
    cos_in = din("cosT", [128, 1024])
    sin_in = din("sinT", [128, 1024])
    msk_in = din("masks", [128, 3072])
    ckc_in = [din(f"ckc{l}", [128, 2048]) for l in range(DEPTH)]
    cvc_in = [din(f"cvc{l}", [128, 2048]) for l in range(DEPTH)]
    cka_in = [din(f"cka{l}", [64, 1024]) for l in range(DEPTH)]
    cva_in = [din(f"cva{l}", [128, 512]) for l in range(DEPTH)]

    yT_out = dout("yT", [1024, NTOK])
    o_ka = dout("o_ka", [DEPTH * 2 * 64, 512])
    o_kc = dout("o_kc", [DEPTH * 4 * 128, 512])
    o_va = dout("o_va", [DEPTH * 512, 128])
    o_vc = dout("o_vc", [DEPTH * 512, 512])
    o_st = dout("o_st", [128, 32])

    ga_in = nc.dram_tensor("ga_in", [128, 4096], BF16)
    ga_out = nc.dram_tensor("ga_out", [512, 4096], BF16)
    gb_in = nc.dram_tensor("gb_in", [128, 4096], BF16)
    gb_out = nc.dram_tensor("gb_out", [512, 4096], BF16)
    gc_in = nc.dram_tensor("gc_in", [128, 3072], BF16)
    gc_out = nc.dram_tensor("gc_out", [512, 3072], BF16)
    g2_in = nc.dram_tensor("g2_in", [128, 12], F32)
    g2_out = nc.dram_tensor("g2_out", [512, 12], F32)
    g3_in = nc.dram_tensor("g3_in", [128, 16], F32)
    g3_out = nc.dram_tensor("g3_out", [512, 16], F32)
    spill_a = nc.dram_tensor("spill_a", [128, 8 * 1024], F32)
    spill_b = nc.dram_tensor("spill_b", [128, 8 * 1024], F32)
    GROUPS = [[0, 1, 2, 3], [4, 5, 6, 7]]

    with contextlib.ExitStack() as st:
        S = Sched(nc, st)
        k = K(S)
        _cnt = [0]

        def T(stack, shape, dt=F32, name=None):
            _cnt[0] += 1
            return stack.enter_context(nc.sbuf_tensor(f"{name or 't'}_{_cnt[0]}", shape, dt))

        banks = [st.enter_context(nc.psum_tensor(f"bank{i}", [128, 512], F32)) for i in range(8)]

        XT = T(st, [128, 8, NTOK], F32, "XT")
        OSGB = T(st, [128, 4, NTOK], BF16, "OSGB")
        ONES = T(st, [128, 128], BF16, "ONES")
        BONES = T(st, [128, 128], BF16, "BONES")
        RM = T(st, [128, 128], BF16, "RM")
        COS = T(st, [128, 1024], F32, "COS")
        SIN = T(st, [128, 1024], F32, "SIN")
        MSK = T(st, [128, 6, 512], BF16, "MSK")
        SMALL = T(st, [128, NSM], F32, "SMALL")
        EPST = T(st, [128, 1], F32, "EPST")
        MODV = [T(st, [128, 24, 2], F32, "MODV") for _ in range(DEPTH)]
        GS = [T(st, [128, 8, 2], F32, "GS") for _ in range(DEPTH)]
        DER = [T(st, [128, 64], F32, "DER") for _ in range(DEPTH)]
        KAW = T(st, [128, 2, 1280], BF16, "KAW")
        VAW = T(st, [128, 10, 128], BF16, "VAW")
        VAWS = T(st, [128, 10, 128], BF16, "VAWS")
        KACTX = T(st, [128, 2, 512], BF16, "KACTX")
        KCCTX = T(st, [128, 4, 512], BF16, "KCCTX")
        VACTX = T(st, [128, 4, 128], BF16, "VACTX")
        VACTXS = T(st, [128, 4, 128], BF16, "VACTXS")
        VCCTX = T(st, [128, 4, 512], BF16, "VCCTX")
        STO = T(st, [128, 32], F32, "STO")
        WSL = [T(st, [128, 4096], BF16, "WSL") for _ in range(2)]
        ws_i = [0]

        def sm(name, l=None):
            o, w = SM[(name, l)] if l is not None else SM[name]
            return SMALL[:, o:o + w]

        def wload(src_ap, P, n):
            slot = WSL[ws_i[0] % len(WSL)]
            ws_i[0] += 1
            S.dma("pool", slot[0:P, 0:n], src_ap, max_dma_last_dim=4096)
            return slot

        bank_i = [0]

        def nb(pool=(0, 1, 2, 3)):
            b = banks[pool[bank_i[0] % len(pool)]]
            bank_i[0] += 1
            return b

        D_NBA, D_NBX, D_CL, D_C2, D_ESINK, D_NLAM, D_SUBG = 0, 8, 16, 24, 32, 40, 41

        S.dma("sp", XT[:], xT_in.ap().rearrange("(k p) t -> p k t", p=128))
        S.dma("sp", SMALL[:], small_in.ap())
        S.dma("sp", COS[:], cos_in.ap())
        S.dma("sp", SIN[:], sin_in.ap())
        S.dma("pool", RM[:], rmat_in.ap())
        S.dma("pool", MSK[:].rearrange("p a b -> p (a b)"), msk_in.ap(), max_dma_last_dim=4096)
        k.memset(ONES[:], 1.0)
        k.memset(BONES[:], 0.0)
        k.memset(BONES[0:64, 0:64], 1.0)
        k.memset(BONES[64:128, 64:128], 1.0)
        k.memset(EPST[:], EPS)
        ONE_P = T(st, [128, 1], F32, "ONE_P")
        k.memset(ONE_P[:], 1.0)
        k.memset(KAW[:], 0.0)
        k.memset(VAW[:], 0.0)
        k.memset(VAWS[:], 0.0)
        k.memset(KACTX[:], 0.0)
        k.memset(WSL[0][:, 0:2048], 0.0)
        S.dma("sp", gc_in.ap()[64:128, 0:2048], WSL[0][64:128, 0:2048])

        with contextlib.ExitStack() as ph:
            SCF = T(ph, [128, 16], F32, "SCF")
            SCF2 = T(ph, [128, 16], F32, "SCF2")
            SCB = T(ph, [128, 8, 2], BF16, "SCB")
            TMPS = T(ph, [128, 64], F32, "TMPS")
            TMPS2 = T(ph, [128, 64], F32, "TMPS2")
            cT = sm("c")
            k.act(SCF[:], cT, AF.Exp, scale=-1.0)
            k.ts(SCF[:], SCF[:], 1.0, None, op0=ALU.add)
            k.recip(SCF2[:], SCF[:])
            k.tt(SCB[:].rearrange("p a b -> p (a b)"), cT, SCF2[:], ALU.mult)
            for l in range(DEPTH):
                ps = nb()
                for g in range(6):
                    w = wload(wmod[l].ap()[:, g * 4096:(g + 1) * 4096], 128, 4096)
                    wv = w[:, :].rearrange("p (c k m) -> p c k m", c=4, k=8)
                    for c4 in range(4):
                        cc = g * 4 + c4
                        for kk in range(8):
                            k.mm(ps[:, cc * 2:cc * 2 + 2], wv[:, c4, kk, :], SCB[:, kk, :],
                                 start=(kk == 0), stop=(kk == 7))
                psv = ps[:, 0:48].rearrange("p (c v) -> p c v", v=2)
                for v in range(2):
                    k.tt(MODV[l][:, :, v], psv[:, :, v], sm("modb", l), ALU.add)
                    k.stt(GS[l][:, :, v], MODV[l][:, 8:16, v], 1.0, sm("ng", l), ALU.add, ALU.mult)
                D = DER[l]
                k.ts(D[:, D_NBA:D_NBA + 8], sm("ba", l), -1.0, None, op0=ALU.mult)
                k.ts(D[:, D_NBX:D_NBX + 8], sm("bx", l), -1.0, None, op0=ALU.mult)
                k.act(TMPS[:, 0:8], sm("lam", l), AF.Exp, scale=-1.0)
                k.ts(TMPS[:, 0:8], TMPS[:, 0:8], 1.0, None, op0=ALU.add)
                k.act(TMPS2[:, 0:8], TMPS[:, 0:8], AF.Ln)
                k.ts(D[:, D_CL:D_CL + 8], TMPS2[:, 0:8], -8.0, None, op0=ALU.mult)
                k.ts(D[:, D_C2:D_C2 + 8], TMPS2[:, 0:8], -16.0, None, op0=ALU.mult)
                k.act(D[:, D_ESINK:D_ESINK + 8], sm("sink", l), AF.Exp)
                lam_init = 0.8 - 0.6 * math.exp(-0.3 * l)
                k.tt(TMPS[:, 0:64], sm("lq1", l), sm("lk1", l), ALU.mult)
                k.rsum(TMPS2[:, 8:9], TMPS[:, 0:64])
                k.tt(TMPS[:, 0:64], sm("lq2", l), sm("lk2", l), ALU.mult)
                k.rsum(TMPS2[:, 9:10], TMPS[:, 0:64])
                k.act(TMPS2[:, 10:12], TMPS2[:, 8:10], AF.Exp)
                k.tt(TMPS2[:, 12:13], TMPS2[:, 11:12], TMPS2[:, 10:11], ALU.subtract)
                k.ts(D[:, D_NLAM:D_NLAM + 1], TMPS2[:, 12:13], -lam_init, None, op0=ALU.add)
                k.ts(D[:, D_SUBG:D_SUBG + 1], sm("subln", l), 1.0 - lam_init, None, op0=ALU.mult)
            S.barrier()

        def front(ph_t, l, b):
            HT, SQ, RSTD, FTMP = ph_t["HT"], ph_t["SQ"], ph_t["RSTD"], ph_t["FTMP"]
            v = 0 if b == 0 else 1
            cols = slice(b * 512, (b + 1) * 512)
            ps = banks[7]
            for kk in range(8):
                sq = SQ[kk % 2]
                if kk % 2 == 0:
                    k.act(sq[:], XT[:, kk, cols], AF.Square)
                else:
                    k.tt(sq[:], XT[:, kk, cols], XT[:, kk, cols], ALU.mult, eng=("pool" if kk % 4 == 1 else "dve"))
                k.mm(ps[:], ONES[:], sq[:], start=(kk == 0), stop=(kk == 7))
            k.act(RSTD[:], ps[:], AF.Ln, scale=1.0 / 1024.0, bias=EPST[:])
            k.act(RSTD[:], RSTD[:], AF.Exp, scale=-0.5)
            for kk in range(8):
                ft = FTMP[kk % 2]
                k.tt(ft[:], XT[:, kk, cols], RSTD[:], ALU.mult)
                k.act(HT[:, kk, :], ft[:], AF.Identity, scale=GS[l][:, kk, v:v + 1], bias=MODV[l][:, kk, v:v + 1])

        def proj(ps, w3, nk, M, rhs_fn, N=512, P=128):
            if isinstance(w3, tuple):
                flat, base = w3
                for kk in range(nk):
                    k.mm(ps[0:128, 0:N], flat[0:P, base + kk * 64:base + kk * 64 + 128], rhs_fn(kk),
                         start=(kk == 0), stop=(kk == nk - 1))
                return
            for kk in range(nk):
                k.mm(ps[0:M, 0:N], w3[0:P, kk, 0:M], rhs_fn(kk), start=(kk == 0), stop=(kk == nk - 1))

        qk_i = [0]

        def qknorm(ph_t, ps, P, gain, rope_cols, dest_fn):
            i = qk_i[0] % 2
            qk_i[0] += 1
            X32, SQB, LNV, XN = ph_t["QX"][i], ph_t["QS"][i], ph_t["QL"][i], ph_t["QN"][i]
            k.act(X32[0:P, :], ps[0:P, :], AF.Copy)
            k.act(SQB[0:P, :], ps[0:P, :], AF.Square)
            ss = nb((4, 5))
            k.mm(ss[0:P, :], BONES[0:P, 0:P], SQB[0:P, :])
            k.act(LNV[0:P, :], ss[0:P, :], AF.Ln, scale=1.0 / 64.0, bias=EPST[0:P, :])
            k.act(LNV[0:P, :], LNV[0:P, :], AF.Exp, scale=-0.5)
            k.stt(XN[0:P, :], X32[0:P, :], gain, LNV[0:P, :], ALU.mult, ALU.mult)
            if rope_cols is None:
                dest_fn(XN)
                return
            k.copy(SQB[0:P, :], XN[0:P, :], eng="pool")
            rx = nb((4, 5))
            k.mm(rx[0:P, :], RM[0:P, 0:P], SQB[0:P, :])
            k.tt(X32[0:P, :], XN[0:P, :], COS[0:P, rope_cols], ALU.mult, eng="pool")
            k.tt(LNV[0:P, :], rx[0:P, :], SIN[0:P, rope_cols], ALU.mult)
            dest_fn((X32, LNV))

        def qk_pipeline(ph_t, items, rope_cols, rhs_fn):
            n = len(items)
            st = [None] * n
            base = qk_i[0]
            qk_i[0] += n

            def tiles(i):
                j = (base + i) % 2
                return ph_t["QX"][j], ph_t["QS"][j], ph_t["QL"][j], ph_t["QN"][j]

            def stage_a(i):
                w3, P, gain, dest = items[i]
                ps = nb()
                proj(ps, w3, 8, P, rhs_fn)
                st[i] = ps

            def stage_b(i):
                w3, P, gain, dest = items[i]
                X32, SQB, LNV, XN = tiles(i)
                ps = st[i]
                k.act(SQB[0:P, :], ps[0:P, :], AF.Square)
                ss = nb((4, 5))
                k.mm(ss[0:P, :], BONES[0:P, 0:P], SQB[0:P, :])
                k.act(LNV[0:P, :], ss[0:P, :], AF.Ln, scale=1.0 / 64.0, bias=EPST[0:P, :])
                k.act(LNV[0:P, :], LNV[0:P, :], AF.Exp, scale=-0.5)
                k.stt(XN[0:P, :], ps[0:P, :], gain, LNV[0:P, :], ALU.mult, ALU.mult)
                if rope_cols is None:
                    dest(XN)

            def stage_c0(i):
                w3, P, gain, dest = items[i]
                X32, SQB, LNV, XN = tiles(i)
                k.act(SQB[0:P, :], XN[0:P, :], AF.Copy)

            def stage_c1(i):
                w3, P, gain, dest = items[i]
                X32, SQB, LNV, XN = tiles(i)
                rx = nb((6, 7))
                k.mm(rx[0:P, :], RM[0:P, 0:P], SQB[0:P, :])
                k.tt(X32[0:P, :], XN[0:P, :], COS[0:P, rope_cols], ALU.mult, eng="pool")
                k.tt(LNV[0:P, :], rx[0:P, :], SIN[0:P, rope_cols], ALU.mult)
                dest((X32, LNV))

            for step in range(n + 2):
                if rope_cols is not None and 0 <= step - 2 < n:
                    stage_c0(step - 2)
                if step < n:
                    stage_a(step)
                if 0 <= step - 1 < n:
                    stage_b(step - 1)
                if rope_cols is not None and 0 <= step - 2 < n:
                    stage_c1(step - 2)

        def silu_to(ph_t, ps, P, out_ap):
            i = qk_i[0] % 2
            qk_i[0] += 1
            E = ph_t["QX"][i]
            k.act(E[0:P, :], ps[0:P, :], AF.Exp, scale=-1.0)
            k.act(E[0:P, :], E[0:P, :], AF.Ln, bias=ONE_P[0:P, :])
            k.act(E[0:P, :], E[0:P, :], AF.Exp, scale=-1.0)
            k.tt(out_ap, ps[0:P, :], E[0:P, :], ALU.mult)

        def common_tiles(ph):
            sq = [T(ph, [128, 512], BF16, "SQ") for _ in range(2)]
            qn = [T(ph, [128, 512], F32, "QN") for _ in range(2)]
            return {
                "HT": T(ph, [128, 8, 512], BF16, "HT"),
                "SQ": sq,
                "RSTD": T(ph, [128, 512], F32, "RSTD"),
                "FTMP": qn,
                "QX": [T(ph, [128, 512], F32, "QX") for _ in range(2)],
                "QS": sq,
                "QL": [T(ph, [128, 512], F32, "QL") for _ in range(2)],
                "QN": qn,
            }

        class _Stop(Exception):
            pass

        def layers():
          for l in range(DEPTH):
            D = DER[l]
            if STOP == "setup":
                return
            with contextlib.ExitStack() as ph0:
              XB = T(ph0, [128, 4, XBW], F32, "XB")
              with contextlib.ExitStack() as ph:
                pt = common_tiles(ph)
                HT = pt["HT"]
                KST = [T(ph, [128, 512], BF16, "KST") for _ in range(2)]
                VTMP = [T(ph, [128, 512], F32, "VTMP") for _ in range(1)]
                VST = [T(ph, [128, 512], BF16, "VST") for _ in range(2)]
                VATMP = T(ph, [128, 128], F32, "VATMP")
                k.memset(XB[:], 0.0)
                kst_i = [0]
                W1 = {}
                for (o_, n_) in ((0, 1024), (1024, 4096), (5120, 4096), (9216, 4096), (13312, 4096), (17408, 1024)):
                    t_ = T(ph, [128, n_ + (64 if o_ == 0 else 0)], BF16, "W1")
                    if o_ == 0:
                        k.memset(t_[:, n_:n_ + 64], 0.0)
                    S.dma("pool", t_[:, 0:n_], wp1[l].ap()[:, o_:o_ + n_], max_dma_last_dim=4096)
                    W1[o_] = t_

                class _W1:
                    def get(self, off):
                        return W1[off]
                wpl = _W1()
                for b in range(3):
                    lat = b > 0
                    front(pt, l, b)
                    rope_cols = slice((b - 1) * 512, b * 512) if lat else None
                    hrhs = lambda kk: HT[:, kk, :]
                    if STOP == "front":
                        return
                    w = wpl.get(0)
                    w4 = w[:, 0:1024].rearrange("p (c k m) -> p c k m", c=2, k=8)
                    qitems = []
                    for kv in range(2):
                        if not lat:
                            def dest(XN, kv=kv):
                                S.dma("sp", o_ka.ap()[(l * 2 + kv) * 64:(l * 2 + kv + 1) * 64, :], XN[0:64, :],
                                      is_output=True)
                                k.copy(KACTX[0:64, kv, :], XN[0:64, :])
                        else:
                            def dest(t, kv=kv, b=b):
                                dst = KAW[0:64, kv, 128 + (b - 1) * 512:128 + b * 512]
                                k.tt(dst, t[0][0:64, :], t[1][0:64, :], ALU.add)
                                S.dma("sp", gc_in.ap()[0:64, KA_OFF + kv * 1024 + (b - 1) * 512:KA_OFF + kv * 1024 + b * 512], dst)
                        qitems.append(((w, kv * 512), 64, sm("kna", l)[0:64, :], dest))
                    qk_pipeline(pt, qitems, rope_cols, hrhs)
                    if STOP == "ka":
                        return
                    w = wpl.get(1024)
                    w4 = w[:, 0:4096].rearrange("p (c k m) -> p c k m", c=4, k=8)
                    qitems = []
                    for h in range(4):
                        if not lat:
                            def dest(XN, h=h):
                                S.dma("sp", o_kc.ap()[(l * 4 + h) * 128:(l * 4 + h + 1) * 128, :], XN[:, :], is_output=True)
                                k.copy(KCCTX[:, h, :], XN[:, :])
                        else:
                            def dest(t, h=h, b=b):
                                ks = KST[kst_i[0] % 2]
                                kst_i[0] += 1
                                k.tt(ks[:], t[0][:, :], t[1][:, :], ALU.add)
                                S.dma("sp", ga_in.ap()[:, KC_OFF + h * 1024 + (b - 1) * 512:KC_OFF + h * 1024 + b * 512], ks[:])
                        qitems.append((w4[:, h], 128, sm("knc", l), dest))
                    qk_pipeline(pt, qitems, rope_cols, hrhs)
                    if STOP == "kc":
                        return
                    w = wpl.get(5120)
                    w4 = w[:, 0:4096].rearrange("p (c k m) -> p c k m", c=4, k=8)
                    for ch in range(4):
                        ps = nb()
                        proj(ps, w4[:, ch], 8, 128, hrhs)
                        if b == 0:
                            k.act(XB[:, ch, 2:258], ps[:, 0:256], AF.Copy)
                            k.act(XB[:, ch, 261:517], ps[:, 256:512], AF.Copy)
                        else:
                            c0 = 520 + (b - 1) * 512
                            k.act(XB[:, ch, c0:c0 + 512], ps[:, :], AF.Copy)
                    if STOP == "xb":
                        return
                    w = wpl.get(9216)
                    w4 = w[:, 0:4096].rearrange("p (c k m) -> p c k m", c=4, k=8)
                    for ch in range(4):
                        ps = nb()
                        proj(ps, w4[:, ch], 8, 128, hrhs)
                        silu_to(pt, ps, 128, OSGB[:, ch, b * 512:(b + 1) * 512])
                    if STOP == "gb":
                        return
                    w = wpl.get(13312)
                    w2 = wpl.get(17408)
                    wv = w[:, 0:4096].rearrange("p (k n) -> p k n", k=8)
                    wva = w2[:, 0:1024].rearrange("p (k n) -> p k n", k=8)
                    for tt_ in range(4):
                        tok = slice(tt_ * 128, (tt_ + 1) * 128)
                        psc = nb((4, 5))
                        psa = banks[6]
                        for kk in range(8):
                            k.mm(psc[:, :], HT[:, kk, tok], wv[:, kk, :], start=(kk == 0), stop=(kk == 7))
                        if 'a' in KV:
                            continue
                        for kk in range(8):
                            k.mm(psa[:, 0:128], HT[:, kk, tok], wva[:, kk, :], start=(kk == 0), stop=(kk == 7))
                        if 'b' in KV:
                            continue
                        if not lat:
                            vt = VTMP[0]
                            if 'e' not in KV:
                                k.act(vt[:], psc[:, :], AF.Copy)
                            if 'c' not in KV:
                                S.dma("sp", o_vc.ap()[l * 512 + tt_ * 128:l * 512 + (tt_ + 1) * 128, :], vt[:], is_output=True)
                            if 'f' not in KV:
                                k.copy(VCCTX[:, tt_, :], psc[:, :])
                            if 'd' in KV:
                                continue
                            k.act(VATMP[:], psa[:, 0:128], AF.Copy)
                            if 'c' not in KV:
                                S.dma("sp", o_va.ap()[l * 512 + tt_ * 128:l * 512 + (tt_ + 1) * 128, :], VATMP[:], is_output=True)
                            k.copy(VACTX[:, tt_, :], psa[:, 0:128])
                            k.copy(VACTXS[:, tt_, 0:64], psa[:, 64:128])
                            k.copy(VACTXS[:, tt_, 64:128], psa[:, 0:64])
                        else:
                            Tt = (b - 1) * 4 + tt_
                            vs = VST[tt_ % 2]
                            k.act(vs[:], psc[:, :], AF.Copy)
                            S.dma("sp", gb_in.ap()[:, VC_OFF + Tt * 512:VC_OFF + (Tt + 1) * 512], vs[:])
                            k.copy(VAW[:, 1 + Tt, :], psa[:, 0:128])
                            k.copy(VAWS[:, 1 + Tt, 0:64], psa[:, 64:128])
                            k.copy(VAWS[:, 1 + Tt, 64:128], psa[:, 0:64])
                            S.dma("sp", gc_in.ap()[:, VA_OFF + Tt * 128:VA_OFF + (Tt + 1) * 128], VAW[:, 1 + Tt, :])
                    if STOP == "v":
                        return
                if STOP == "p1":
                    return
                G2S = T(ph, [128, 12], F32, "G2S")
                for ch in range(4):
                    k.copy(G2S[:, ch * 3:ch * 3 + 1], XB[:, ch, 520:521])
                    k.copy(G2S[:, ch * 3 + 1:ch * 3 + 3], XB[:, ch, 520 + 1022:520 + 1024])
                S.dma("sp", g2_in.ap(), G2S[:])
                S.collective(GROUPS, g2_in, g2_out)
                S.collective(GROUPS, gc_in, gc_out)
                S.collective(GROUPS, ga_in, ga_out)
                S.collective(GROUPS, gb_in, gb_out)
                S.barrier(skip_cc=True)
              if STOP == "cc":
                  return
              with contextlib.ExitStack() as ph:
                GL = T(ph, [128, 16, 128], BF16, "GL")
                S.dma("pool", GL[:].rearrange("p a b -> p (a b)"), glru_in.ap()[:, l * 2048:(l + 1) * 2048], max_dma_last_dim=4096)
                Us = [T(ph, [128, 1024], F32, "U") for _ in range(2)]
                UBs = [T(ph, [128, 1024], BF16, "UB") for _ in range(2)]
                AA = [T(ph, [128, 1024], F32, "AA") for _ in range(2)]
                BB = [T(ph, [128, 1024], F32, "BB") for _ in range(2)]
                HH = [T(ph, [128, 1024], F32, "HH") for _ in range(2)]
                LT = [[T(ph, [128, 512], F32, "LT") for _ in range(4)] for _ in range(2)]
                ONE_T = T(ph, [128, 1], F32, "ONE_T")
                k.memset(ONE_T[:], 1.0)
                u_i = [0]
                HAL = T(ph, [128, 4, 12], F32, "HAL")
                TOT = T(ph, [128, 16], F32, "TOT")
                RS = T(ph, [128, 8], F32, "RS")
                TOTG = T(ph, [128, 4, 16], F32, "TOTG")
                CH = T(ph, [128, 40], F32, "CH")
                HIN = T(ph, [128, 8], F32, "HIN")
                sel = sm("sel")

                def lat_halo():
                    S.dma("sp", HAL[:], g2_out.ap().rearrange("(r p) c -> p r c", p=128))
                    for ch in range(4):
                        pre = XB[:, ch, 518:520]
                        post = XB[:, ch, 1544:1545]
                        for r in range(4):
                            k.stt(pre, HAL[:, r, ch * 3 + 1:ch * 3 + 3], sel[:, r:r + 1], pre, ALU.mult, ALU.add)
                            k.stt(post, HAL[:, r, ch * 3:ch * 3 + 1], sel[:, 4 + r:5 + r], post, ALU.mult, ALU.add)

                def lru_gates(ch, s0, Tn, U, UB, want_rsum):
                    npc = (Tn + 511) // 512
                    for pc in range(npc):
                        n = min(512, Tn)
                        cs = slice(pc * 512, pc * 512 + n)
                        seqs = []
                        for dr in range(2):
                            nba = D[:, D_NBA + dr * 4 + ch:D_NBA + dr * 4 + ch + 1]
                            nbx = D[:, D_NBX + dr * 4 + ch:D_NBX + dr * 4 + ch + 1]
                            cl = D[:, D_CL + dr * 4 + ch:D_CL + dr * 4 + ch + 1]
                            c2 = D[:, D_C2 + dr * 4 + ch:D_C2 + dr * 4 + ch + 1]
                            L0, L1, L2, L3 = [t[:, 0:n] for t in LT[dr]]
                            zr = banks[(pc % 2) * 4 + dr * 2]
                            zi = banks[(pc % 2) * 4 + dr * 2 + 1]
                            ops = [
                                lambda zr=zr, dr=dr: k.mm(zr[:, 0:n], GL[:, (dr * 2 + 0) * 4 + ch, :], UB[:, cs]),
                                lambda zi=zi, dr=dr: k.mm(zi[:, 0:n], GL[:, (dr * 2 + 1) * 4 + ch, :], UB[:, cs]),
                                lambda L0=L0, zr=zr, nba=nba: k.act(L0, zr[:, 0:n], AF.Exp, scale=-1.0, bias=nba),
                                lambda L1=L1, zi=zi, nbx=nbx: k.act(L1, zi[:, 0:n], AF.Exp, scale=-1.0, bias=nbx),
                                lambda L0=L0: k.act(L0, L0, AF.Ln, bias=ONE_T[:]),
                                lambda L1=L1: k.act(L1, L1, AF.Ln, bias=ONE_T[:]),
                                lambda L0=L0: k.act(L0, L0, AF.Exp, scale=-1.0),
                                lambda L0=L0, dr=dr, cl=cl: k.act(AA[dr][:, cs], L0, AF.Exp, scale=cl),
                                lambda L2=L2, L0=L0, c2=c2: k.ts(L2, L0, c2, None, op0=ALU.mult),
                                lambda L3=L3, L2=L2: k.ts(L3, L2, 1.0 / 24.0, 1.0 / 6.0, op0=ALU.mult, op1=ALU.add),
                                lambda L3=L3, L2=L2: k.tt(L3, L3, L2, ALU.mult),
                                lambda L3=L3, L2=L2: k.stt(L3, L3, 0.5, L2, ALU.add, ALU.mult),
                                lambda L3=L3, L2=L2: k.stt(L3, L3, 1.0, L2, ALU.add, ALU.mult),
                                lambda L3=L3: k.act(L3, L3, AF.Ln, scale=-1.0),
                                lambda L3=L3, L1=L1: k.stt(L3, L3, 0.5, L1, ALU.mult, ALU.subtract),
                                lambda L3=L3: k.act(L3, L3, AF.Exp),
                                lambda L3=L3, dr=dr: k.tt(BB[dr][:, cs], L3, U[:, cs], ALU.mult),
                            ]
                            if want_rsum:
                                ops.insert(8, lambda L0=L0, dr=dr, pc=pc: k.rsum(RS[:, dr * 2 + pc:dr * 2 + pc + 1], L0))
                            seqs.append(ops)
                        for o0, o1 in zip(*seqs):
                            o0()
                            o1()

                def conv_u(ch, s0, Tn):
                    U = Us[u_i[0] % 2]
                    UB = UBs[u_i[0] % 2]
                    u_i[0] += 1
                    cw = sm("convw", l)
                    k.act(U[:, 0:Tn], XB[:, ch, s0:s0 + Tn], AF.Identity, scale=cw[:, ch * 4:ch * 4 + 1],
                          bias=sm("convb", l)[:, ch:ch + 1])
                    for j in range(1, 4):
                        k.stt(U[:, 0:Tn], XB[:, ch, s0 + j:s0 + j + Tn], cw[:, ch * 4 + j:ch * 4 + j + 1], U[:, 0:Tn],
                              ALU.mult, ALU.add)
                    k.act(UB[:, 0:Tn], U[:, 0:Tn], AF.Copy)
                    return U, UB

                def do_scan(dr, Tn, init):
                    if dr == 0:
                        k.scan(HH[0][:, 0:Tn], AA[0][:, 0:Tn], BB[0][:, 0:Tn], init)
                    else:
                        k.scan(HH[1][:, 0:Tn][:, ::-1], AA[1][:, 0:Tn][:, ::-1], BB[1][:, 0:Tn][:, ::-1], init)

                def lru_seg(ch, si):
                        s0, Tn, tk0 = SEGS[si]
                        U, UB = conv_u(ch, s0, Tn)
                        lru_gates(ch, s0, Tn, U, UB, si == 2)
                        for dr in range(2):
                            do_scan(dr, Tn, 0.0)
                            endc = Tn - 1 if dr == 0 else 0
                            if si < 2:
                                col = ((l * 2 + si) * 2 + dr) * 4 + ch
                                k.copy(STO[:, col:col + 1], HH[dr][:, endc:endc + 1])
                            else:
                                k.copy(TOT[:, dr * 8 + 4 + ch:dr * 8 + 5 + ch], HH[dr][:, endc:endc + 1])
                                k.tt(RS[:, 4 + dr:5 + dr], RS[:, dr * 2:dr * 2 + 1], RS[:, dr * 2 + 1:dr * 2 + 2], ALU.add)
                                k.act(TOT[:, dr * 8 + ch:dr * 8 + ch + 1], RS[:, 4 + dr:5 + dr], AF.Exp,
                                      scale=D[:, D_CL + dr * 4 + ch:D_CL + dr * 4 + ch + 1])
                                sl = slice((ch * 2 + dr) * 1024, (ch * 2 + dr + 1) * 1024)
                                S.dma("sp", spill_a.ap()[:, sl], AA[dr][:, :])
                                S.dma("sp", spill_b.ap()[:, sl], BB[dr][:, :])
                        if si < 2:
                            k.tt(HH[0][:, 0:Tn], HH[0][:, 0:Tn], HH[1][:, 0:Tn], ALU.add, eng="pool")
                            k.tt(OSGB[:, ch, tk0:tk0 + Tn], HH[0][:, 0:Tn], OSGB[:, ch, tk0:tk0 + Tn], ALU.mult, eng="pool")
                for ch in range(2):
                    lru_seg(ch, 0)
                    lru_seg(ch, 1)
                lat_halo()
                for ch in range(4):
                    lru_seg(ch, 2)
                S.dma("sp", g3_in.ap(), TOT[:])
                S.collective(GROUPS, g3_in, g3_out)
                for ch in range(2, 4):
                    lru_seg(ch, 0)
                    lru_seg(ch, 1)
                S.dma("sp", TOTG[:], g3_out.ap().rearrange("(r p) c -> p r c", p=128))
                h0 = sm("h0", l)
                k.copy(CH[:, 0:4], h0[:, 0:4])
                for r in range(3):
                    k.tt(CH[:, (r + 1) * 4:(r + 2) * 4], TOTG[:, r, 0:4], CH[:, r * 4:(r + 1) * 4], ALU.mult)
                    k.tt(CH[:, (r + 1) * 4:(r + 2) * 4], CH[:, (r + 1) * 4:(r + 2) * 4], TOTG[:, r, 4:8], ALU.add)
                k.copy(CH[:, 16 + 12:16 + 16], h0[:, 4:8])
                for r in (3, 2, 1):
                    k.tt(CH[:, 16 + (r - 1) * 4:16 + r * 4], TOTG[:, r, 8:12], CH[:, 16 + r * 4:16 + (r + 1) * 4], ALU.mult)
                    k.tt(CH[:, 16 + (r - 1) * 4:16 + r * 4], CH[:, 16 + (r - 1) * 4:16 + r * 4], TOTG[:, r, 12:16], ALU.add)
                k.memset(HIN[:], 0.0)
                for r in range(4):
                    k.stt(HIN[:, 0:4], CH[:, r * 4:(r + 1) * 4], sel[:, 8 + r:9 + r], HIN[:, 0:4], ALU.mult, ALU.add)
                    k.stt(HIN[:, 4:8], CH[:, 16 + r * 4:16 + (r + 1) * 4], sel[:, 8 + r:9 + r], HIN[:, 4:8], ALU.mult, ALU.add)
                s0, Tn, tk0 = SEGS[2]
                for ch in range(4):
                    for dr in range(2):
                        sl = slice((ch * 2 + dr) * 1024, (ch * 2 + dr + 1) * 1024)
                        S.dma("sp", AA[dr][:, :], spill_a.ap()[:, sl])
                        S.dma("sp", BB[dr][:, :], spill_b.ap()[:, sl])
                        do_scan(dr, Tn, HIN[:, dr * 4 + ch:dr * 4 + ch + 1])
                    k.tt(HH[0][:, :], HH[0][:, :], HH[1][:, :], ALU.add, eng="pool")
                    k.tt(OSGB[:, ch, tk0:tk0 + Tn], HH[0][:, :], OSGB[:, ch, tk0:tk0 + Tn], ALU.mult, eng="pool")
                S.barrier()

            if STOP == "lru":
                return
            with contextlib.ExitStack() as ph:
                pt = common_tiles(ph)
                HT = pt["HT"]
                QA = T(ph, [128, 8, 512], BF16, "QA")
                OSA = T(ph, [64, 8, 512], BF16, "OSA")
                QCZ = T(ph, [128, 4, 2, 512], BF16, "QCZ")
                OSC = T(ph, [128, 4, 512], BF16, "OSC")
                YT = T(ph, [128, 8, 512], BF16, "YT")
                KCALL = T(ph, [128, 36 * 128], BF16, "KCALL")
                VCALL = T(ph, [128, 36 * 128], BF16, "VCALL")
                plan = []
                for b_ in range(3):
                    for o_ in (0, 2048, 4096, 8192, 10240, 12288):
                        n_ = 2048 if o_ in (0, 2048, 8192, 10240) else 4096
                        plan.append(('L', wp2[l].ap()[:, o_:o_ + n_], 128, n_, o_))
                    plan.append(('KV',))
                    for cc_ in range(8):
                        o_ = WP2_Q + cc_ * 4096
                        plan.append(('L', wp2[l].ap()[:, o_:o_ + 4096], 128, 4096, o_))
                        plan.append(('L', wba[l].ap()[:, cc_ * 1024:(cc_ + 1) * 1024], 64, 1024, -1 - cc_))
                    for g_ in range(2):
                        o_ = WP2_Q + 8 * 4096 + g_ * 4096
                        plan.append(('L', wp2[l].ap()[:, o_:o_ + 4096], 128, 4096, o_))
                S.split[KCALL.name] = 2048
                S.split[VCALL.name] = 2048
                for t_ in WSL + [KCALL, VCALL]:
                    k.memset(t_[:, 2048:2112], 0.0)
                wpl = WPlan(S, WSL, [KCALL, VCALL], plan)
                CKA = T(ph, [128, 2, 512], BF16, "CKA")
                CVA = T(ph, [128, 4, 128], BF16, "CVA")
                CVAS = T(ph, [128, 4, 128], BF16, "CVAS")
                ET = [T(ph, [128, 512], BF16, "ET") for _ in range(3)]
                PT1, PT2 = pt["QX"][0], pt["QX"][1]
                PT3 = pt["RSTD"][:, 0:256]
                PT5 = pt["RSTD"][:, 256:512]
                PT4 = pt["SQ"][0][:, 0:256]
                MT = [pt["QX"][0], pt["QX"][1], pt["QL"][0]]
                MT2 = [pt["QL"][1], pt["QN"][0], pt["QN"][1]]
                YACC = pt["RSTD"]
                k.memset(QCZ[:], 0.0)
                k.memset(CKA[:], 0.0)
                k.memset(QA[64:128, :, :], 0.0)
                S.dma("pool", CKA[0:64, :, :].rearrange("p a b -> p (a b)"), cka_in[l].ap(), max_dma_last_dim=4096)
                S.dma("pool", CVA[:].rearrange("p a b -> p (a b)"), cva_in[l].ap(), max_dma_last_dim=4096)
                cva3 = cva_in[l].ap().rearrange("p (t n) -> p t n", n=128)
                S.dma("pool", CVAS[:, :, 0:64], cva3[:, :, 64:128])
                S.dma("pool", CVAS[:, :, 64:128], cva3[:, :, 0:64])
                g1a, g1b, g1c = ga_out.ap(), gb_out.ap(), gc_out.ap()
                HALK = T(ph, [64, 2, 4, 2, 128], BF16, "HALK")
                HALV = T(ph, [128, 2, 4, 128], BF16, "HALV")
                sel = sm("sel")

                def build_halos():
                    for kv in range(2):
                        base = KA_OFF + kv * 1024
                        S.dma("sp", HALK[:, 0, :, kv, :], g1c[:, base + 896:base + 1024].rearrange("(r p) c -> p r c", p=128)[0:64])
                        S.dma("sp", HALK[:, 1, :, kv, :], g1c[:, base:base + 128].rearrange("(r p) c -> p r c", p=128)[0:64])
                    S.dma("sp", HALV[:, 0, :, :], g1c[:, VA_OFF + 896:VA_OFF + 1024].rearrange("(r p) c -> p r c", p=128))
                    S.dma("sp", HALV[:, 1, :, :], g1c[:, VA_OFF:VA_OFF + 128].rearrange("(r p) c -> p r c", p=128))
                    k.memset(KAW[:, :, 0:128], 0.0)
                    k.memset(KAW[:, :, 1152:1280], 0.0)
                    k.memset(VAW[:, 0, :], 0.0)
                    k.memset(VAW[:, 9, :], 0.0)
                    k.memset(VAWS[:, 0, :], 0.0)
                    k.memset(VAWS[:, 9, :], 0.0)
                    for r in range(4):
                        for kv in range(2):
                            k.stt(KAW[0:64, kv, 0:128], HALK[:, 0, r, kv, :], sel[0:64, r:r + 1], KAW[0:64, kv, 0:128], ALU.mult, ALU.add)
                            k.stt(KAW[0:64, kv, 1152:1280], HALK[:, 1, r, kv, :], sel[0:64, 4 + r:5 + r], KAW[0:64, kv, 1152:1280], ALU.mult, ALU.add)
                        k.stt(VAW[:, 0, :], HALV[:, 0, r, :], sel[:, r:r + 1], VAW[:, 0, :], ALU.mult, ALU.add)
                        k.stt(VAW[:, 9, :], HALV[:, 1, r, :], sel[:, 4 + r:5 + r], VAW[:, 9, :], ALU.mult, ALU.add)
                        for (d0, s0_) in ((0, 64), (64, 0)):
                            k.stt(VAWS[:, 0, d0:d0 + 64], HALV[:, 0, r, s0_:s0_ + 64], sel[:, r:r + 1], VAWS[:, 0, d0:d0 + 64], ALU.mult, ALU.add)
                            k.stt(VAWS[:, 9, d0:d0 + 64], HALV[:, 1, r, s0_:s0_ + 64], sel[:, 4 + r:5 + r], VAWS[:, 9, d0:d0 + 64], ALU.mult, ALU.add)


                et_i = [0]
                acc_i = [0]
                esum_i = [0]
                S.split[HALV.name] = 512
                _hv = HALV[:].rearrange("p a r c -> p (a r c)")
                ESUM = [_hv[:, 0:512], _hv[:, 512:1024]]

                def attend_all(units, order=None):
                    if order is None:
                        order = [(ui, j) for ui, u in enumerate(units) for j in range(len(u[1]))]
                    first = {}
                    last = {}
                    for si_, (ui_, j_) in enumerate(order):
                        first.setdefault(ui_, si_)
                        last[ui_] = si_
                    steps = [(ui_, j_, len(units[ui_][1])) for (ui_, j_) in order]
                    LOOK = 3
                    pend = []
                    deferred = []
                    role = [None] * len(steps)
                    si_ = 0
                    while si_ < len(steps):
                        if si_ + 1 < len(steps) and steps[si_ + 1][0] == steps[si_][0]:
                            role[si_], role[si_ + 1] = 'first', 'second'
                            si_ += 2
                        else:
                            role[si_] = 'single'
                            si_ += 1
                    dfirst, dlast = {}, {}
                    for si_, (ui_, j_, n_) in enumerate(steps):
                        if role[si_] != 'first':
                            dfirst.setdefault(ui_, si_)
                            dlast[ui_] = si_
                    pendD = []
                    prev_et = [None]

                    def issue(si):
                        ui, j, n = steps[si]
                        ps = nb((0, 1, 2, 7))
                        units[ui][0](ps, units[ui][1][j][0])
                        pend.append(ps)
                    for si in range(min(LOOK, len(steps))):
                        issue(si)
                    for si, (ui, j, n) in enumerate(steps):
                        qk_fn, tiles, Mv, post_fn = units[ui]
                        _, vap, mask = tiles[j]
                        ps = pend.pop(0)
                        et = ET[et_i[0] % 3]
                        et_i[0] += 1
                        k.act(et[:], ps[:], AF.Exp, scale=SCALE)
                        if mask is not None:
                            k.tt(et[:], et[:], mask, ALU.mult)
                        a_ = (acc_i[0] + ui) % 2
                        accO = banks[3 + 2 * a_]
                        accD = banks[4 + 2 * a_]
                        dsrc = None
                        if role[si] == 'second':
                            es = ESUM[esum_i[0] % 2]
                            esum_i[0] += 1
                            k.tt(es, prev_et[0][:], et[:], ALU.add)
                            dsrc = es
                        elif role[si] == 'single':
                            dsrc = et[:]
                        prev_et[0] = et
                        while deferred and deferred[0][0] <= si:
                            deferred.pop(0)[1]()
                        if si + LOOK < len(steps):
                            issue(si + LOOK)
                        k.mm(accO[0:Mv, :], vap, et[:], start=(si == first[ui]), stop=(si == last[ui]))
                        while pendD and pendD[0][0] <= si:
                            pendD.pop(0)[1]()
                        if dsrc is not None:
                            fn = (lambda accD=accD, Mv=Mv, dsrc=dsrc, st_=(si == dfirst[ui]), sp_=(si == dlast[ui]):
                                  k.mm(accD[0:Mv, :], ONES[:, 0:Mv], dsrc, start=st_, stop=sp_))
                            if role[si] == 'single':
                                fn()
                            else:
                                pendD.append((si + 1, fn))
                        if si == last[ui]:
                            deferred.append((si + 3, lambda post_fn=post_fn, accO=accO, accD=accD: post_fn(accO, accD)))
                    for _, fn in pendD:
                        fn()
                    for _, fn in deferred:
                        fn()
                    acc_i[0] += len(units)

                for b in range(3):
                    lat = b > 0
                    v = 1 if lat else 0
                    cols = slice(b * 512, (b + 1) * 512)
                    front(pt, l, b)
                    rope_cols = slice((b - 1) * 512, b * 512) if lat else None
                    hrhs = lambda kk: HT[:, kk, :]
                    for g in range(2):
                        w = wpl.get(g * 2048)
                        w4 = w[:, 0:2048].rearrange("p (c k m) -> p c k m", c=4, k=8)
                        qitems = []
                        for c4 in range(4):
                            h = g * 4 + c4
                            if not lat:
                                def dest(XN, h=h):
                                    k.copy(QA[0:64, h, :], XN[0:64, :])
                            else:
                                def dest(t, h=h):
                                    k.tt(QA[0:64, h, :], t[0][0:64, :], t[1][0:64, :], ALU.add)
                            qitems.append(((w, c4 * 512), 64, sm("qna", l)[0:64, :], dest))
                        qk_pipeline(pt, qitems, rope_cols, hrhs)
                    w = wpl.get(4096)
                    w4 = w[:, 0:4096].rearrange("p (c k m) -> p c k m", c=4, k=8)
                    qitems = []
                    for h in range(4):
                        if not lat:
                            def dest(XN, h=h):
                                k.copy(QCZ[0:64, h, 0, :], XN[0:64, :])
                                k.copy(QCZ[64:128, h, 1, :], XN[64:128, :])
                        else:
                            def dest(t, h=h):
                                k.tt(QCZ[0:64, h, 0, :], t[0][0:64, :], t[1][0:64, :], ALU.add)
                                k.tt(QCZ[64:128, h, 1, :], t[0][64:128, :], t[1][64:128, :], ALU.add)
                        qitems.append((w4[:, h], 128, sm("qnc", l), dest))
                    qk_pipeline(pt, qitems, rope_cols, hrhs)
                    for g in range(2):
                        w = wpl.get(8192 + g * 2048)
                        w4 = w[:, 0:2048].rearrange("p (c k m) -> p c k m", c=4, k=8)
                        for c4 in range(4):
                            h = g * 4 + c4
                            ps = nb()
                            proj(ps, (w, c4 * 512), 8, 64, hrhs)
                            silu_to(pt, ps, 64, OSA[0:64, h, :])
                    w = wpl.get(12288)
                    w4 = w[:, 0:4096].rearrange("p (c k m) -> p c k m", c=4, k=8)
                    for h in range(4):
                        ps = nb()
                        proj(ps, w4[:, h], 8, 128, hrhs)
                        silu_to(pt, ps, 128, OSC[:, h, :])

                    if b == 1:
                        build_halos()
                    wpl.kv_begin()
                    nqb = 2
                    units = []
                    for qb in range(nqb):
                        qc = slice(qb * 256, (qb + 1) * 256)
                        for hp in range(4):
                            kv = hp // 2
                            h0_ = hp * 2
                            tiles = []
                            if not lat:
                                for kt in range(2):
                                    c0 = qb * 256 + kt * 128
                                    tiles.append((KACTX[:, kv, c0:c0 + 128], (VACTX if kv == 0 else VACTXS)[:, qb * 2 + kt, :], None))
                            else:
                                t0 = (b - 1) * 4 + qb * 2
                                for wdx in range(4):
                                    mi = [0 if t0 == 0 else 1, 2, 3, 5 if t0 == 6 else 4][wdx]
                                    tiles.append((KAW[:, kv, (t0 + wdx) * 128:(t0 + wdx + 1) * 128],
                                                  (VAW if kv == 0 else VAWS)[:, t0 + wdx, :], MSK[:, mi, :]))
                                for c in range(4):
                                    tiles.append((CKA[:, kv, c * 128:(c + 1) * 128], (CVA if kv == 0 else CVAS)[:, c, :], None))

                            def qk_a(ps, kap, h0_=h0_, qc=qc):
                                k.mm(ps[:, :], kap, QA[:, h0_:h0_ + 2, qc])

                            def post_a(accO, accD, h0_=h0_, qc=qc):
                                for hh in range(2):
                                    cs = slice(hh * 256, (hh + 1) * 256)
                                    k.act(PT1[0:64, cs], accD[0:64, cs], AF.Ln,
                                          bias=D[0:64, D_ESINK + h0_ + hh:D_ESINK + h0_ + hh + 1])
                                k.act(PT2[0:64, :], PT1[0:64, :], AF.Exp, scale=-1.0)
                                k.tt(PT1[0:64, :], accO[0:64, :], PT2[0:64, :], ALU.mult)
                                k.tt(OSA[0:64, h0_:h0_ + 2, qc], PT1[0:64, :].rearrange("p (a b) -> p a b", a=2),
                                     OSA[0:64, h0_:h0_ + 2, qc], ALU.mult)
                            units.append((qk_a, tiles, 128, post_a))
                    attend_all(units)

                    for h in range(4):
                        units = []
                        if lat:
                            src_k = g1a[:, KC_OFF + h * 1024:KC_OFF + (h + 1) * 1024].rearrange("(r p) c -> p r c", p=128)
                            S.dma("sp", KCALL[:, 0:2048].rearrange("p (r c) -> p r c", r=2), src_k[:, 0:2, :])
                            for r in (0, 1):
                                S.dma("sp", VCALL[:, r * 1024:(r + 1) * 1024].rearrange("p (t n) -> p t n", n=128),
                                      g1b[r * 128:(r + 1) * 128, VC_OFF:VC_OFF + 4096].rearrange("p (t n) -> p t n", n=512)[:, :, h * 128:(h + 1) * 128])
                            S.dma("sp", KCALL[:, 2048:4096].rearrange("p (r c) -> p r c", r=2), src_k[:, 2:4, :])
                            for r in (2, 3):
                                S.dma("sp", VCALL[:, r * 1024:(r + 1) * 1024].rearrange("p (t n) -> p t n", n=128),
                                      g1b[r * 128:(r + 1) * 128, VC_OFF:VC_OFF + 4096].rearrange("p (t n) -> p t n", n=512)[:, :, h * 128:(h + 1) * 128])
                            S.dma("pool", KCALL[:, 4096:4608], ckc_in[l].ap()[:, h * 512:(h + 1) * 512], max_dma_last_dim=2048)
                            S.dma("pool", VCALL[:, 4096:4608].rearrange("p (t n) -> p t n", n=128),
                                  cvc_in[l].ap().rearrange("p (t n) -> p t n", n=512)[:, :, h * 128:(h + 1) * 128])
                        for qb in range(2):
                            qc = slice(qb * 256, (qb + 1) * 256)
                            tiles = []
                            if not lat:
                                for kt in range(2):
                                    c0 = qb * 256 + kt * 128
                                    tiles.append((KCCTX[:, h, c0:c0 + 128], VCCTX[:, qb * 2 + kt, h * 128:(h + 1) * 128], None))
                            else:
                                for j in range(36):
                                    tiles.append((KCALL[:, j * 128:(j + 1) * 128], VCALL[:, j * 128:(j + 1) * 128], None))

                            def qk_c(ps, kap, h=h, qc=qc):
                                k.mm(ps[:, :], kap, QCZ[:, h, :, qc])

                            def post_c(accO, accD, h=h, qc=qc):
                                k.act(PT1[:, :], accD[:, :], AF.Ln)
                                k.act(PT1[:, :], PT1[:, :], AF.Exp, scale=-1.0)
                                k.tt(PT2[:, :], accO[:, :], PT1[:, :], ALU.mult)
                                k.stt(PT3, PT2[:, 256:512], D[:, D_NLAM:D_NLAM + 1], PT2[:, 0:256], ALU.mult, ALU.add)
                                k.tt(PT4, PT3, PT3, ALU.mult)
                                ss = banks[7]
                                k.mm(ss[:, 0:256], ONES[:, :], PT4)
                                k.act(PT5, ss[:, 0:256], AF.Ln, scale=1.0 / 128.0, bias=EPST[:])
                                k.act(PT5, PT5, AF.Exp, scale=-0.5)
                                k.stt(PT3, PT3, D[:, D_SUBG:D_SUBG + 1], PT5, ALU.mult, ALU.mult)
                                k.tt(OSC[:, h, qc], PT3, OSC[:, h, qc], ALU.mult)
                            units.append((qk_c, tiles, 128, post_c))
                        if lat:
                            order = ([(0, j) for j in range(16)] + [(1, j) for j in range(16)]
                                     + [(0, j) for j in range(16, 36)] + [(1, j) for j in range(16, 36)])
                            attend_all(units, order)
                        else:
                            attend_all(units)

                    wpl.kv_end()
                    for cc in range(8):
                        base = WP2_Q + cc * 4096
                        w = wpl.get(base)
                        wa = wpl.get(-1 - cc)
                        wm = w[:, 0:3072].rearrange("p (i k m) -> p i k m", i=3, k=8)
                        wb = w[:, 3072:3584].rearrange("p (k m) -> p k m", k=4)
                        wc = w[:, 3584:4096].rearrange("p (k m) -> p k m", k=4)
                        wa3 = wa[0:64, 0:1024].rearrange("p (k m) -> p k m", k=8)
                        pa, pb, pc = (banks[0], banks[1], banks[2]) if cc % 2 == 0 else (banks[5], banks[6], banks[7])
                        zs = [banks[3], banks[4], banks[3]]
                        proj(zs[0], wm[:, 0], 8, 128, hrhs)
                        proj(zs[1], wm[:, 1], 8, 128, hrhs)
                        k.act(MT[0][:], zs[0][:], AF.Exp, scale=-1.0)
                        proj(zs[2], wm[:, 2], 8, 128, hrhs)
                        k.act(MT[1][:], zs[1][:], AF.Exp, scale=-1.0)
                        k.act(MT[2][:], zs[2][:], AF.Exp, scale=-1.0)
                        proj(pa, wa3, 8, 128, lambda kk: OSA[0:64, kk, :], P=64)
                        proj(pb, wb, 4, 128, lambda kk: OSGB[:, kk, cols])
                        proj(pc, wc, 4, 128, lambda kk: OSC[:, kk, :])
                        for i, pbr in enumerate((pa, pb, pc)):
                            k.act(MT[i][:], MT[i][:], AF.Ln, bias=ONE_P[:])
                            k.act(MT[i][:], MT[i][:], AF.Exp, scale=-1.0)
                            k.tt(MT2[i][:], pbr[:], MT[i][:], ALU.mult)
                        k.tt(YACC[:], MT2[0][:], MT2[1][:], ALU.add)
                        k.tt(YT[:, cc, :], YACC[:], MT2[2][:], ALU.add)
                    for g in range(2):
                        base = WP2_Q + 8 * 4096 + g * 4096
                        w = wpl.get(base)
                        w4 = w[:, 0:4096].rearrange("p (c k m) -> p c k m", c=4, k=8)
                        for c4 in range(4):
                            cc = g * 4 + c4
                            ps = nb((6, 7))
                            proj(ps, w4[:, c4], 8, 128, lambda kk: YT[:, kk, :])
                            k.stt(XT[:, cc, cols], ps[:, :], MODV[l][:, 16 + cc, v:v + 1], XT[:, cc, cols], ALU.mult, ALU.add)
                S.barrier()

        layers()
        S.barrier()
        S.dma("sp", yT_out.ap().rearrange("(k p) t -> p k t", p=128), XT[:], is_output=True)
        S.dma("sp", o_st.ap(), STO[:], is_output=True)
        S.finish()
        S.replay()
    return nc


def _fm(w, M):
    K_, n = w.shape
    a = w.reshape(K_ // 128, 128, n // M, M).transpose(1, 2, 0, 3)
    return np.ascontiguousarray(a).reshape(128, -1)


_PROG = None
import os
STOP = os.environ.get('KSTOP', '')
KV = os.environ.get('KV', '')


def kernel(**inp):
    global _PROG
    f32 = lambda a: np.ascontiguousarray(np.asarray(a, dtype=np.float32))
    I = {k_: f32(v) for k_, v in inp.items()}
    w_in = I["w_in"]
    shared = {}
    for l in range(DEPTH):
        W = w_in[l]
        q_a, k_a, v_a, g_a = W[:, 0:512], W[:, 512:640], W[:, 640:768], W[:, 768:1280]
        x_b, g_b = W[:, 1280:1792], W[:, 1792:2304]
        q_c, k_c, v_c, g_c = W[:, 2304:2816], W[:, 2816:3328], W[:, 3328:3840], W[:, 3840:4352]
        merge = W[:, 4352:7424]
        vtm = lambda w: np.ascontiguousarray(w.reshape(8, 128, -1).transpose(1, 0, 2)).reshape(128, -1)
        shared[f"wp1{l}"] = np.concatenate([_fm(k_a, 64), _fm(k_c, 128), _fm(x_b, 128), _fm(g_b, 128),
                                            vtm(v_c), vtm(v_a)], axis=1)
        parts = [_fm(q_a, 64), _fm(q_c, 128), _fm(g_a, 64), _fm(g_c, 128)]
        for cc in range(8):
            for i in range(3):
                parts.append(_fm(merge[:, i * 1024 + cc * 128:i * 1024 + (cc + 1) * 128], 128))
            parts.append(_fm(I["w_br_b"][l][:, cc * 128:(cc + 1) * 128], 128))
            parts.append(_fm(I["w_br_c"][l][:, cc * 128:(cc + 1) * 128], 128))
        parts.append(_fm(I["w_out"][l], 128))
        shared[f"wp2{l}"] = np.concatenate(parts, axis=1)
        wa = I["w_br_a"][l]
        shared[f"wba{l}"] = np.ascontiguousarray(
            wa.reshape(8, 64, 8, 128).transpose(1, 2, 0, 3)).reshape(64, -1)
        shared[f"wmod{l}"] = _fm(I["mod_w"][l], 128)
    gl = np.zeros((128, DEPTH, 2, 2, 4, 128), np.float32)
    for l in range(DEPTH):
        for dr in range(2):
            for gi, nm in enumerate(("lru_wa", "lru_wx")):
                Wg = I[nm][l, dr]
                for ch in range(4):
                    for hb in range(2):
                        gl[hb * 64:(hb + 1) * 64, l, dr, gi, ch, hb * 64:(hb + 1) * 64] = Wg[ch * 2 + hb]
    shared["glru"] = gl.reshape(128, -1)
    R = np.zeros((128, 128), np.float32)
    for dp in range(128):
        if dp % 32 < 16:
            R[dp + 16, dp] = -1.0
        else:
            R[dp - 16, dp] = 1.0
    shared["rmat"] = R
    inv = np.power(np.float32(10000.0), -np.arange(16, dtype=np.float32) / np.float32(16)).astype(np.float32)
    a_ = np.arange(128)[:, None]
    b_ = np.arange(128)[None, :]
    ge = (a_ >= b_).astype(np.float32)
    le = (a_ <= b_).astype(np.float32)
    one = np.ones((128, 128), np.float32)
    zero = np.zeros((128, 128), np.float32)
    M0 = np.concatenate([ge, zero], 1)
    M1 = np.concatenate([one, ge], 1)
    M2 = np.concatenate([le, one], 1)
    M3 = np.concatenate([zero, le], 1)

    in_maps = []
    for c in range(8):
        s, j = c // 4, c % 4
        m = dict(shared)
        xs = np.concatenate([I["x_prompt"][2 * c], I["x_prompt"][2 * c + 1],
                             I["x_sample"][s, 1024 * j:1024 * (j + 1)]], axis=0)
        m["xT"] = np.ascontiguousarray(xs.T)
        sm_ = np.zeros((128, NSM), np.float32)

        def put(key, arr):
            o, w = SM[key]
            sm_[:, o:o + w] = arr
        col = lambda v, n: np.ascontiguousarray(v.reshape(n, 128).T)
        for l in range(DEPTH):
            put(("ng", l), col(I["norm_g"][l], 8))
            put(("modb", l), col(I["mod_b"][l], 24))
            put(("qna", l), np.tile(I["qn_a"][l], 2)[:, None])
            put(("kna", l), np.tile(I["kn_a"][l], 2)[:, None])
            put(("qnc", l), np.tile(I["qn_c"][l], 2)[:, None])
            put(("knc", l), np.tile(I["kn_c"][l], 2)[:, None])
            cw = I["conv_w"][l]
            put(("convw", l), np.ascontiguousarray(cw.reshape(4, 4, 128).transpose(2, 1, 0)).reshape(128, 16))
            put(("convb", l), col(I["conv_b"][l], 4))
            for nm, key in (("lru_ba", "ba"), ("lru_bx", "bx"), ("lru_lam", "lam")):
                put((key, l), np.ascontiguousarray(I[nm][l].reshape(2, 4, 128).transpose(2, 0, 1)).reshape(128, 8))
            put(("subln", l), I["subln_c"][l][:, None])
            put(("sink", l), np.broadcast_to(I["sink_a"][l][None, :], (128, 8)))
            for nm, key in (("lam_q1", "lq1"), ("lam_k1", "lk1"), ("lam_q2", "lq2"), ("lam_k2", "lk2")):
                put((key, l), np.broadcast_to(I[nm][l][None, :], (128, 64)))
            put(("h0", l), np.ascontiguousarray(I["state_lru"][s, l].reshape(2, 4, 128).transpose(2, 0, 1)).reshape(128, 8))
        cc_ = np.stack([col(I["c_ctx"], 8), col(I["c"][s], 8)], axis=2).reshape(128, 16)
        put("c", cc_)
        sel = np.zeros((128, 12), np.float32)
        if j > 0:
            sel[:, j - 1] = 1.0
        if j < 3:
            sel[:, 4 + j + 1] = 1.0
        sel[:, 8 + j] = 1.0
        put("sel", sel)
        m["small"] = sm_
        t = 1024 * j + np.arange(1024)
        row = (t // 64).astype(np.float32)
        colp = (t % 64).astype(np.float32)
        ang = np.zeros((64, 1024), np.float32)
        for d in range(64):
            ang[d] = (row if d < 32 else colp) * inv[d % 16]
        ang = np.concatenate([ang, ang], 0)
        m["cosT"] = np.cos(ang).astype(np.float32)
        m["sinT"] = np.sin(ang).astype(np.float32)
        first = M0 if j > 0 else np.zeros_like(M0)
        last = M3 if j < 3 else np.zeros_like(M3)
        msk = np.stack([first, M0, M1, M2, M3, last], 0)
        msk = np.concatenate([msk, msk], 2)
        m["masks"] = np.ascontiguousarray(msk.transpose(1, 0, 2)).reshape(128, -1)
        for l in range(DEPTH):
            ck = I["cache_c_k"][s, l]
            m[f"ckc{l}"] = np.ascontiguousarray(ck.transpose(2, 3, 1, 0)).reshape(128, 4 * 512)
            cv = I["cache_c_v"][s, l]
            m[f"cvc{l}"] = np.ascontiguousarray(cv.reshape(4, 128, 512).transpose(1, 0, 2)).reshape(128, -1)
            ka = I["cache_a_k"][s, l]
            m[f"cka{l}"] = np.ascontiguousarray(ka.transpose(2, 1, 0)).reshape(64, -1)
            va = I["cache_a_v"][s, l]
            m[f"cva{l}"] = np.ascontiguousarray(va.reshape(4, 128, 128).transpose(1, 0, 2)).reshape(128, -1)
        in_maps.append(m)

    if _PROG is None:
        _PROG = build_program()
    res = run_bass_kernel_spmd(_PROG, in_maps, core_ids=list(range(8)))
    R_ = res.results

    y_prompt = np.zeros((16, 256, 1024), np.float32)
    y_sample = np.zeros((2, 4096, 1024), np.float32)
    n_ka = np.zeros((16, 2, 256, 2, 64), np.float32)
    n_va = np.zeros((16, 2, 256, 2, 64), np.float32)
    n_kc = np.zeros((16, 2, 256, 4, 2, 64), np.float32)
    n_vc = np.zeros((16, 2, 256, 4, 128), np.float32)
    n_st = np.zeros((16, 2, 2, 512), np.float32)
    for c in range(8):
        s, j = c // 4, c % 4
        r = R_[c]
        y = np.asarray(r["yT"]).T
        y_prompt[2 * c] = y[0:256]
        y_prompt[2 * c + 1] = y[256:512]
        y_sample[s, 1024 * j:1024 * (j + 1)] = y[512:]
        ka = np.asarray(r["o_ka"]).reshape(2, 2, 64, 2, 256)
        kc = np.asarray(r["o_kc"]).reshape(2, 4, 2, 64, 2, 256)
        va = np.asarray(r["o_va"]).reshape(2, 2, 256, 2, 64)
        vc = np.asarray(r["o_vc"]).reshape(2, 2, 256, 4, 128)
        stt_ = np.asarray(r["o_st"]).reshape(128, 2, 2, 2, 4)
        for sq in range(2):
            bi = 2 * c + sq
            n_ka[bi] = ka[:, :, :, sq, :].transpose(0, 3, 1, 2)
            n_kc[bi] = kc[:, :, :, :, sq, :].transpose(0, 4, 1, 2, 3)
            n_va[bi] = va[:, sq]
            n_vc[bi] = vc[:, sq]
            n_st[bi] = stt_[:, :, sq].transpose(1, 2, 3, 0).reshape(2, 2, 512)
    return (y_prompt, y_sample, n_ka, n_va, n_kc, n_vc, n_st)
```
